# Optimizing a Trainium2 kernel written in Bass

```python
import jax
import jax.numpy as jnp
from jax import lax
import numpy as np

D_MODEL = 1024
BATCH = 16
SEQ = 2048
DEPTH = 4

GRID_W = 64
CTX_LEN = 256
N_MIXERS = 3

CONV_WIDTH = 31

NA_HEADS = 16
NA_HEAD_DIM = D_MODEL // NA_HEADS
NA_WIN_ROWS = 8
NA_WIN_COLS = 16
NA_QBLOCK_COLS = 16
NA_KBLOCK_COLS = NA_QBLOCK_COLS + NA_WIN_COLS

RWKV_HEAD_DIM = 64
RWKV_HEADS = D_MODEL // RWKV_HEAD_DIM
DECAY_LORA = 64
ICLR_LORA = 64
GATE_LORA = 128
N_LERP = 6
N_DIR = 2

N_EXPERTS = 16
EC_CAPACITY = 2
D_EXPERT = 2 * D_MODEL

DN_ALPHA = (2 * DEPTH) ** 0.25
DN_BETA = (8 * DEPTH) ** -0.25
LN_EPS = 1e-5
GN_EPS = 64e-5

N_CONV_LAYERS = (DEPTH + 2) // 3
N_NA_LAYERS = (DEPTH + 1) // 3
N_RWKV_LAYERS = DEPTH // 3

kernel_name = 'hybrid_conv_natten_rwkv7_ecmoe_diffusion'


def _layer_norm(x, g, b, eps=LN_EPS):
    xf = x.astype(jnp.float32)
    mu = jnp.mean(xf, -1, keepdims=True)
    var = jnp.mean(jnp.square(xf - mu), -1, keepdims=True)
    return ((xf - mu) * lax.rsqrt(var + eps)).astype(x.dtype) * g + b


def _ada(cond, w, b):
    m = jax.nn.silu(cond) @ w + b
    return jnp.split(m[..., None, :], 6, axis=-1)


def _modulate(h, shift, scale):
    return h * (1 + scale) + shift


def _post_norm(h, y, gate, g, b):
    return _layer_norm(DN_ALPHA * h + gate * y, g, b)


def _depthwise_conv(u, w, b):
    width = w.shape[0]
    y = lax.conv_general_dilated(u, w[:, None, :], window_strides=(1,),
                                 padding=[(width // 2, width // 2)],
                                 dimension_numbers=('NWC', 'WIO', 'NWC'),
                                 feature_group_count=u.shape[-1])
    return y + b


def _conformer_conv(h, w_in, b_in, w_dw, b_dw, n_g, n_b, w_out, b_out):
    u = h @ w_in + b_in
    val, gt = jnp.split(u, 2, axis=-1)
    u = val * jax.nn.sigmoid(gt)
    u = _depthwise_conv(u, w_dw, b_dw)
    u = jax.nn.silu(_layer_norm(u, n_g, n_b))
    return u @ w_out + b_out


def _na_latent(q, k, v, k_ctx, v_ctx, rpb):
    B, N, H, Dh = q.shape
    rows = N // GRID_W
    kh = min(NA_WIN_ROWS, rows)
    ncb = GRID_W // NA_QBLOCK_COLS
    qcol = np.arange(GRID_W).reshape(ncb, NA_QBLOCK_COLS)
    win_c0 = np.clip(qcol - NA_WIN_COLS // 2, 0, GRID_W - NA_WIN_COLS)
    blk_c0 = np.clip(np.arange(ncb) * NA_QBLOCK_COLS - NA_WIN_COLS // 2, 0, GRID_W - NA_KBLOCK_COLS)
    kcol = blk_c0[:, None] + np.arange(NA_KBLOCK_COLS)
    col_ok = (kcol[:, None, :] >= win_c0[:, :, None]) & (kcol[:, None, :] < win_c0[:, :, None] + NA_WIN_COLS)
    cidx = np.clip(kcol[:, None, :] - qcol[:, :, None] + NA_WIN_COLS - 1, 0, 2 * NA_WIN_COLS - 2)
    row0 = np.clip(np.arange(rows) - kh // 2, 0, rows - kh)
    ridx = row0[:, None] + np.arange(kh)[None, :] - np.arange(rows)[:, None] + NA_WIN_ROWS - 1
    kg = k.reshape(B, rows, GRID_W, H, Dh)
    vg = v.reshape(B, rows, GRID_W, H, Dh)
    qg = jnp.moveaxis(q.reshape(B, rows, ncb, NA_QBLOCK_COLS, H, Dh), 1, 0) * (Dh ** -0.5)
    mask = jnp.asarray(col_ok)[None, None, :, :, None, :]
    n_loc = kh * NA_KBLOCK_COLS

    def row_block(args):
        q_r, r0, ridx_r = args
        k_band = lax.dynamic_slice_in_dim(kg, r0, kh, axis=1)[:, :, kcol]
        v_band = lax.dynamic_slice_in_dim(vg, r0, kh, axis=1)[:, :, kcol]
        bias = jnp.transpose(rpb[:, ridx_r][:, :, cidx], (0, 2, 3, 1, 4)).astype(jnp.float32)
        s_loc = jnp.einsum('bjqhd,bajmhd->bhjqam', q_r, k_band).astype(jnp.float32) + bias
        s_loc = jnp.where(mask, s_loc, -jnp.inf)
        s_ctx = jnp.einsum('bjqhd,blhd->bhjql', q_r, k_ctx).astype(jnp.float32)
        s = jnp.concatenate([s_loc.reshape(s_loc.shape[:4] + (n_loc,)), s_ctx], axis=-1)
        p = jax.nn.softmax(s, axis=-1).astype(v.dtype)
        p_loc = p[..., :n_loc].reshape(s_loc.shape)
        return (jnp.einsum('bhjqam,bajmhd->bjqhd', p_loc, v_band)
                + jnp.einsum('bhjql,blhd->bjqhd', p[..., n_loc:], v_ctx))

    o = lax.map(row_block, (qg, jnp.asarray(row0, jnp.int32), jnp.asarray(ridx, jnp.int32)))
    return jnp.moveaxis(o, 0, 1).reshape(B, N, H * Dh)


def _na_mixer(u, uc, w_qkv, w_o, rpb, ctx_out):
    B, N, D = u.shape
    L = uc.shape[1]
    H, Dh = NA_HEADS, NA_HEAD_DIM
    qkv = (u @ w_qkv).reshape(B, N, 3, H, Dh)
    if ctx_out:
        qkv_c = (uc @ w_qkv).reshape(B, L, 3, H, Dh)
        k_c, v_c = qkv_c[:, :, 1], qkv_c[:, :, 2]
    else:
        kv_c = (uc @ w_qkv[:, D:]).reshape(B, L, 2, H, Dh)
        k_c, v_c = kv_c[:, :, 0], kv_c[:, :, 1]
    y = _na_latent(qkv[:, :, 0], qkv[:, :, 1], qkv[:, :, 2], k_c, v_c, rpb) @ w_o
    yc = None
    if ctx_out:
        s = jnp.einsum('blhd,bmhd->bhlm', qkv_c[:, :, 0] * (Dh ** -0.5), k_c).astype(jnp.float32)
        p = jax.nn.softmax(s, axis=-1).astype(v_c.dtype)
        yc = jnp.einsum('bhlm,bmhd->blhd', p, v_c).reshape(B, L, D) @ w_o
    return y, yc


def _shift_prev(h):
    return jnp.pad(h, ((0, 0), (1, 0), (0, 0)))[:, :-1]


def _shift_next(h):
    return jnp.pad(h, ((0, 0), (0, 1), (0, 0)))[:, 1:]


def _wkv7_scan(s0, w, k, v, a, b, r, reverse):
    seq = [jnp.swapaxes(t, 0, 1).astype(jnp.float32) for t in (w, k, v, a, b)]
    emit = r is not None
    if emit:
        seq.append(jnp.swapaxes(r, 0, 1).astype(jnp.float32))

    def step(S, inp):
        w_t, k_t, v_t, a_t, b_t = inp[:5]
        sa = jnp.einsum('bhvk,bhk->bhv', S, a_t)
        S = S * w_t[:, :, None, :] + sa[..., None] * b_t[:, :, None, :] + v_t[..., None] * k_t[:, :, None, :]
        return S, (jnp.einsum('bhvk,bhk->bhv', S, inp[5]) if emit else None)

    s_fin, ys = lax.scan(step, s0, tuple(seq), reverse=reverse)
    return s_fin, (jnp.swapaxes(ys, 0, 1) if emit else None)


def _head_norm(y, g, b):
    H, Dh = y.shape[-2:]
    mu = jnp.mean(y, -1, keepdims=True)
    var = jnp.mean(jnp.square(y - mu), -1, keepdims=True)
    return (y - mu) * lax.rsqrt(var + GN_EPS) * g.reshape(H, Dh) + b.reshape(H, Dh)


def _rwkv7_mixer(h, s0, emit, mu_prev, mu_next, w_r, w_k, w_v, w0, w1, w2, a0, a1, a2,
                 k_k, k_a, r_k, g1, g2, gn_g, gn_b, w_o):
    B, T, D = h.shape
    H, Dh = RWKV_HEADS, RWKV_HEAD_DIM
    d_prev = _shift_prev(h) - h
    d_next = _shift_next(h) - h

    def lerp(n):
        return h + d_prev * mu_prev[n] + d_next * mu_next[n]

    def heads(t):
        return t.reshape(B, T, H, Dh)

    xw, xa = lerp(1), lerp(4)
    k = heads(lerp(2) @ w_k)
    v = heads(lerp(3) @ w_v)
    kk = (k * k_k.reshape(H, Dh)).astype(jnp.float32)
    kk = kk * lax.rsqrt(jnp.maximum(jnp.sum(kk * kk, -1, keepdims=True), 1e-24))
    r = heads(lerp(0) @ w_r) if emit else None
    states, reads = [], []
    for d in range(N_DIR):
        w_log = -jax.nn.softplus(-(w0[d] + jnp.tanh(xw @ w1[d]) @ w2[d]).astype(jnp.float32)) - 0.5
        decay = heads(jnp.exp(-jnp.exp(w_log)))
        a = heads(jax.nn.sigmoid(a0[d] + (xa @ a1[d]) @ a2[d]))
        k_d = k * (1 + (a - 1) * k_a.reshape(H, Dh))
        init = jnp.zeros((B, H, Dh, Dh), jnp.float32) if s0 is None else s0[d]
        s_fin, y = _wkv7_scan(init, decay, k_d, v, -kk, kk * a, r, reverse=(d == 1))
        states.append(s_fin)
        if emit:
            bonus = jnp.sum(r * k_d * r_k, -1, keepdims=True) * v
            reads.append(_head_norm(y, gn_g, gn_b).astype(h.dtype) + bonus)
    if not emit:
        return None, states
    g = jax.nn.sigmoid(lerp(5) @ g1) @ g2
    return ((reads[0] + reads[1]).reshape(B, T, D) * g) @ w_o, states


def _ec_moe(h, w_router, w_gate, w_up, w_down):
    B, T, D = h.shape
    cap = max(1, EC_CAPACITY * T // N_EXPERTS)
    aff = jax.nn.softmax((h @ w_router).astype(jnp.float32), axis=-1)
    gates, idx = lax.top_k(jnp.swapaxes(aff, 1, 2), cap)
    xs = jax.vmap(lambda hb, ib: hb[ib])(h, idx)
    hid = jax.nn.silu(jnp.einsum('becd,edf->becf', xs, w_gate)) * jnp.einsum('becd,edf->becf', xs, w_up)
    y = jnp.einsum('becf,efd->becd', hid, w_down) * gates[..., None].astype(h.dtype)
    return jax.vmap(lambda yb, ib: jnp.zeros((T, D), yb.dtype).at[ib.reshape(-1)].add(yb.reshape(-1, D)))(y, idx)


def setup_inputs(seed: int = 0) -> dict:
    key = jax.random.key(seed)
    ks = iter(jax.random.split(key, 48))
    D = D_MODEL
    f = D ** -0.5

    def nrm(shape, s):
        return s * jax.random.normal(next(ks), shape, jnp.float32)

    def uni(shape, lo, hi):
        return jax.random.uniform(next(ks), shape, jnp.float32, lo, hi)

    NC, NN, NR = N_CONV_LAYERS, N_NA_LAYERS, N_RWKV_LAYERS
    return {
        'x': nrm((BATCH, SEQ, D), 1.0),
        'c': nrm((BATCH, D), 1.0),
        'ctx': nrm((BATCH, CTX_LEN, D), 1.0),
        'c_ctx': nrm((D,), 1.0),
        'ada_w': nrm((DEPTH, D, 6 * D), 0.5 * f),
        'ada_b': nrm((DEPTH, 6 * D), 0.02),
        'ln_g': 1.0 + nrm((DEPTH, 2, D), 0.02),
        'ln_b': nrm((DEPTH, 2, D), 0.02),
        'conv_w_in': nrm((NC, D, 2 * D), f),
        'conv_b_in': nrm((NC, 2 * D), 0.02),
        'conv_w_dw': nrm((NC, CONV_WIDTH, D), CONV_WIDTH ** -0.5),
        'conv_b_dw': nrm((NC, D), 0.02),
        'conv_ln_g': 1.0 + nrm((NC, D), 0.02),
        'conv_ln_b': nrm((NC, D), 0.02),
        'conv_w_out': nrm((NC, D, D), f * DN_BETA),
        'conv_b_out': nrm((NC, D), 0.02),
        'na_w_qkv': nrm((NN, D, 3 * D), f),
        'na_w_o': nrm((NN, D, D), f * DN_BETA),
        'na_rpb': nrm((NN, NA_HEADS, 2 * NA_WIN_ROWS - 1, 2 * NA_WIN_COLS - 1), 0.05),
        'rw_mu_prev': uni((NR, N_LERP, D), 0.0, 1.0),
        'rw_mu_next': uni((NR, N_LERP, D), 0.0, 1.0),
        'rw_w_r': nrm((NR, D, D), f),
        'rw_w_k': nrm((NR, D, D), f),
        'rw_w_v': nrm((NR, D, D), f),
        'rw_w0': uni((NR, N_DIR, D), -4.0, 1.0),
        'rw_w1': nrm((NR, N_DIR, D, DECAY_LORA), f),
        'rw_w2': nrm((NR, N_DIR, DECAY_LORA, D), 0.1 * DECAY_LORA ** -0.5),
        'rw_a0': nrm((NR, N_DIR, D), 0.1),
        'rw_a1': nrm((NR, N_DIR, D, ICLR_LORA), f),
        'rw_a2': nrm((NR, N_DIR, ICLR_LORA, D), 0.1 * ICLR_LORA ** -0.5),
        'rw_k_k': 0.85 + nrm((NR, D), 0.05),
        'rw_k_a': 1.0 + nrm((NR, D), 0.05),
        'rw_r_k': nrm((NR, RWKV_HEADS, RWKV_HEAD_DIM), 0.1),
        'rw_g1': nrm((NR, D, GATE_LORA), f),
        'rw_g2': nrm((NR, GATE_LORA, D), GATE_LORA ** -0.5),
        'rw_gn_g': 1.0 + nrm((NR, D), 0.02),
        'rw_gn_b': nrm((NR, D), 0.02),
        'rw_w_o': nrm((NR, D, D), f * DN_BETA),
        'moe_router': nrm((DEPTH, D, N_EXPERTS), f),
        'moe_w_gate': nrm((DEPTH, N_EXPERTS, D, D_EXPERT), f),
        'moe_w_up': nrm((DEPTH, N_EXPERTS, D, D_EXPERT), f),
        'moe_w_down': nrm((DEPTH, N_EXPERTS, D_EXPERT, D), D_EXPERT ** -0.5 * DN_BETA),
    }


def reference(x, c, ctx, c_ctx, ada_w, ada_b, ln_g, ln_b,
              conv_w_in, conv_b_in, conv_w_dw, conv_b_dw, conv_ln_g, conv_ln_b, conv_w_out, conv_b_out,
              na_w_qkv, na_w_o, na_rpb,
              rw_mu_prev, rw_mu_next, rw_w_r, rw_w_k, rw_w_v, rw_w0, rw_w1, rw_w2, rw_a0, rw_a1, rw_a2,
              rw_k_k, rw_k_a, rw_r_k, rw_g1, rw_g2, rw_gn_g, rw_gn_b, rw_w_o,
              moe_router, moe_w_gate, moe_w_up, moe_w_down):
    ctx_layers = [i for i in range(DEPTH) if i % N_MIXERS != 0]
    last_ctx = max(ctx_layers) if ctx_layers else -1
    h, hc = x, ctx
    for i in range(DEPTH):
        kind, j = i % N_MIXERS, i // N_MIXERS
        ctx_read = i <= last_ctx
        ctx_live = i < last_ctx
        sh1, sc1, ga1, sh2, sc2, ga2 = _ada(c, ada_w[i], ada_b[i])
        u = _modulate(h, sh1, sc1)
        uc = None
        if ctx_read:
            csh1, csc1, cga1, csh2, csc2, cga2 = _ada(c_ctx, ada_w[i], ada_b[i])
            uc = _modulate(hc, csh1, csc1)
        if kind == 0:
            conv_p = (conv_w_in[j], conv_b_in[j], conv_w_dw[j], conv_b_dw[j],
                      conv_ln_g[j], conv_ln_b[j], conv_w_out[j], conv_b_out[j])
            y = _conformer_conv(u, *conv_p)
            yc = _conformer_conv(uc, *conv_p) if ctx_live else None
        elif kind == 1:
            y, yc = _na_mixer(u, uc, na_w_qkv[j], na_w_o[j], na_rpb[j], ctx_live)
        else:
            rw_p = (rw_mu_prev[j], rw_mu_next[j], rw_w_r[j], rw_w_k[j], rw_w_v[j],
                    rw_w0[j], rw_w1[j], rw_w2[j], rw_a0[j], rw_a1[j], rw_a2[j],
                    rw_k_k[j], rw_k_a[j], rw_r_k[j], rw_g1[j], rw_g2[j],
                    rw_gn_g[j], rw_gn_b[j], rw_w_o[j])
            yc, ctx_states = _rwkv7_mixer(uc, None, ctx_live, *rw_p)
            y, _ = _rwkv7_mixer(u, ctx_states, True, *rw_p)
        moe_p = (moe_router[i], moe_w_gate[i], moe_w_up[i], moe_w_down[i])
        h = _post_norm(h, y, ga1, ln_g[i, 0], ln_b[i, 0])
        h = _post_norm(h, _ec_moe(_modulate(h, sh2, sc2), *moe_p), ga2, ln_g[i, 1], ln_b[i, 1])
        if ctx_live:
            hc = _post_norm(hc, yc, cga1, ln_g[i, 0], ln_b[i, 0])
            hc = _post_norm(hc, _ec_moe(_modulate(hc, csh2, csc2), *moe_p), cga2, ln_g[i, 1], ln_b[i, 1])
    return h
```

```python
import numpy as np
import concourse.bass as bass
import concourse.mybir as mybir
from concourse.bass_utils import run_bass_kernel_spmd
from contextlib import ExitStack

F32 = mybir.dt.float32
BF16 = mybir.dt.bfloat16
U32 = mybir.dt.uint32
I32 = mybir.dt.int32
ALU = mybir.AluOpType
AF = mybir.ActivationFunctionType
AX = mybir.AxisListType

ENGS = ('pe', 'act', 'dve', 'pool', 'sp')
DMAQ = ('sp', 'act', 'pool')


class Em:
    def __init__(self, nc, ndma=8):
        self.nc = nc
        self.K = ndma
        self.stack = ExitStack()
        ent = self.stack.enter_context
        self.csem = {e: ent(nc.semaphore("c_" + e)) for e in ENGS}
        self.dsem = {q: [ent(nc.semaphore("d_%s%d" % (q, i))) for i in range(ndma)] for q in DMAQ}
        self.ccnt = {e: 0 for e in ENGS}
        self.dcnt = {q: 0 for q in DMAQ}
        self.seen = {e: {} for e in ENGS}
        self.lastw = {}
        self.rd = {}
        self.prog = {e: [] for e in ENGS}
        self.ninstr = 0

    def sem(self, key):
        if key[0] == 'c':
            return self.csem[key[1]]
        return self.dsem[key[1]][key[2]]

    def _deps(self, eng, reads, writes):
        need = {}
        for r in reads:
            t = self.lastw.get(r)
            if t is not None and need.get(t[0], 0) < t[1]:
                need[t[0]] = t[1]
        for w in writes:
            t = self.lastw.get(w)
            if t is not None and need.get(t[0], 0) < t[1]:
                need[t[0]] = t[1]
            for k, v in self.rd.get(w, {}).items():
                if need.get(k, 0) < v:
                    need[k] = v
        waits = []
        seen = self.seen[eng]
        for k, v in need.items():
            if seen.get(k, 0) < v:
                seen[k] = v
                waits.append((k, v))
        return waits

    def _reg(self, tok, reads, writes):
        for r in reads:
            self.rd.setdefault(r, {})[tok[0]] = tok[1]
        for w in writes:
            self.lastw[w] = tok
            self.rd[w] = {}

    def op(self, eng, fn, reads=(), writes=()):
        waits = self._deps(eng, reads, writes)
        self.ccnt[eng] += 1
        tok = (('c', eng), self.ccnt[eng])
        self.prog[eng].append((waits, fn, tok, 1))
        self._reg(tok, reads, writes)
        self.ninstr += 1

    def dma(self, q, fn, reads=(), writes=()):
        waits = self._deps(q, reads, writes)
        n = self.dcnt[q]
        self.dcnt[q] += 1
        slot = n % self.K
        val = 16 * (n // self.K + 1)
        key = ('d', q, slot)
        if val > 16 and self.seen[q].get(key, 0) < val - 16:
            self.seen[q][key] = val - 16
            waits.append((key, val - 16))
        tok = (key, val)
        self.prog[q].append((waits, fn, tok, 16))
        self._reg(tok, reads, writes)
        self.ninstr += 1

    def barrier(self):
        for e in ENGS:
            waits = []
            seen = self.seen[e]
            for e2 in ENGS:
                k = ('c', e2)
                v = self.ccnt[e2]
                if e2 != e and v > 0 and seen.get(k, 0) < v:
                    seen[k] = v
                    waits.append((k, v))
            for q in DMAQ:
                n = self.dcnt[q]
                for slot in range(self.K):
                    cnt = (n - slot + self.K - 1) // self.K if n > slot else 0
                    if cnt > 0:
                        k = ('d', q, slot)
                        v = 16 * cnt
                        if seen.get(k, 0) < v:
                            seen[k] = v
                            waits.append((k, v))
            if waits:
                self.prog[e].append((waits, None, None, 0))
        self.lastw = {}
        self.rd = {}

    def flush(self):
        self.barrier()
        nc = self.nc
        with nc.Block() as block:
            for e, meth in (('pe', block.tensor), ('act', block.scalar), ('dve', block.vector),
                            ('pool', block.gpsimd), ('sp', block.sync)):
                prog = self.prog[e]
                self.prog[e] = []

                def body(eng, prog=prog):
                    for waits, fn, tok, inc in prog:
                        if fn is None:
                            for k, v in waits:
                                eng.wait_ge(self.sem(k), v)
                            continue
                        for k, v in waits[:-1]:
                            eng.wait_ge(self.sem(k), v)
                        ins = fn(eng)
                        if waits:
                            k, v = waits[-1]
                            ins._wait_ge(self.sem(k), v)
                        ins.then_inc(self.sem(tok[0]), inc)
                meth(body)

    def close(self):
        self.stack.close()


D = 1024
T = 2048
L = 256
NB = 2
KC = 8
NE = 16
CAP = 256
CAPC = 32
DE = 2048
DEPTH = 4
NROW = NB * T + NB * L
CTX0 = NB * T
ALPHA = float(8 ** 0.25)
LN_EPS = 1e-5
GRID_W = 64
CONVW = 31
NCORES = 8


class KB:
    def __init__(self, layers=(0, 1, 2, 3), n_exp=NE, debug=False):
        self.layers = list(layers)
        self.n_exp = n_exp
        self.debug = debug
        nc = bass.Bass("TRN2", target_bir_lowering=False)
        self.nc = nc
        self.em = Em(nc)
        self.dr = {}
        self._psi = 0

    def din(self, name, shape, dt=F32):
        t = self.nc.dram_tensor(name, list(shape), dt, kind="ExternalInput").ap()
        self.dr[name] = t
        return t

    def dout(self, name, shape, dt=F32):
        t = self.nc.dram_tensor(name, list(shape), dt, kind="ExternalOutput").ap()
        self.dr[name] = t
        return t

    def dint(self, name, shape, dt=F32):
        t = self.nc.dram_tensor(name, list(shape), dt, kind="ExternalOutput" if self.debug else "Internal").ap()
        self.dr[name] = t
        return t

    def sbt(self, name, shape, dt):
        self._uid = getattr(self, '_uid', 0) + 1
        return self.nc.sbuf_tensor("%s_%d" % (name, self._uid), shape, dt)

    def bank(self):
        i = self._psi
        self._psi = (i + 1) % 8
        return i

    def mm(self, out, lhsT, rhs, start, stop, r, w):
        self.em.op('pe', lambda e: e.matmul(out, lhsT=lhsT, rhs=rhs, start=start, stop=stop), r, w)

    def tr(self, out, in_, r, w, ident=None):
        idn = self.ident if ident is None else ident
        np_ = in_.shape[0]
        self.em.op('pe', lambda e: e.transpose(out=out, in_=in_, identity=idn[0:np_, 0:np_]), list(r) + ['ident'], w)

    def act(self, out, in_, func, r, w, bias=None, scale=None, accum_out=None):
        kw = {}
        if bias is not None:
            kw['bias'] = bias
        if scale is not None:
            kw['scale'] = scale
        if accum_out is not None:
            kw['accum_out'] = accum_out
        self.em.op('act', lambda e: e.activation(out=out, in_=in_, func=func, **kw), r, w)

    def tt(self, eng, out, in0, in1, op, r, w):
        self.em.op(eng, lambda e: e.tensor_tensor(out=out, in0=in0, in1=in1, op=op), r, w)

    def ts(self, eng, out, in0, s1, s2, op0, op1, r, w):
        if op1 is None:
            self.em.op(eng, lambda e: e.tensor_scalar(out=out, in0=in0, scalar1=s1, scalar2=None, op0=op0), r, w)
        else:
            self.em.op(eng, lambda e: e.tensor_scalar(out=out, in0=in0, scalar1=s1, scalar2=s2, op0=op0, op1=op1), r, w)

    def stt(self, eng, out, in0, scalar, in1, op0, op1, r, w):
        self.em.op(eng, lambda e: e.scalar_tensor_tensor(out=out, in0=in0, scalar=scalar, in1=in1, op0=op0, op1=op1), r, w)

    def cp(self, eng, out, in_, r, w):
        if eng == 'act':
            self.em.op('act', lambda e: e.copy(out=out, in_=in_), r, w)
        else:
            self.em.op(eng, lambda e: e.tensor_copy(out=out, in_=in_), r, w)

    def ms(self, eng, ap, val, w):
        self.em.op(eng, lambda e: e.memset(ap, val), (), w)

    def dma(self, q, out, in_, r, w, **kw):
        self.em.dma(q, lambda e: e.dma_start(out=out, in_=in_, **kw), r, w)

    def build(self):
        nc = self.nc
        em = self.em
        st = ExitStack()
        self.st = st
        E = st.enter_context
        self.hin = self.din("hin", [NROW, D])
        self.cT = self.din("cT", [128, KC, 3])
        self.ident_d = self.din("ident", [128, 128])
        self.ada_w = self.din("ada_w", [DEPTH, D, 6 * D])
        self.ada_bT = self.din("ada_bT", [DEPTH, 128, 48])
        self.ada_b = self.din("ada_b", [DEPTH, 6 * D])
        self.ln_g = self.din("ln_g", [DEPTH, 2, D])
        self.ln_b = self.din("ln_b", [DEPTH, 2, D])
        self.conv_w_in = self.din("conv_w_in", [2, D, 2 * D])
        self.conv_b_inT = self.din("conv_b_inT", [2, 128, 16])
        self.conv_w_dwT = self.din("conv_w_dwT", [2, 128, KC, CONVW])
        self.conv_b_dwT = self.din("conv_b_dwT", [2, 128, KC])
        self.conv_ln_gT = self.din("conv_ln_gT", [2, 128, KC])
        self.conv_ln_bT = self.din("conv_ln_bT", [2, 128, KC])
        self.conv_w_out = self.din("conv_w_out", [2, D, D])
        self.conv_b_out = self.din("conv_b_out", [2, D])
        self.moe_router = self.din("moe_router", [DEPTH, D, NE])
        self.moe_w_gate = self.din("moe_w_gate", [DEPTH, NE, D, DE])
        self.moe_w_up = self.din("moe_w_up", [DEPTH, NE, D, DE])
        self.moe_w_down = self.din("moe_w_down", [DEPTH, NE, DE, D])
        self.out = self.dout("out", [NB * T, D])
        self.H = self.dint("H", [NROW, D])
        self.Y = self.dint("Y", [NROW, D])
        self.extra_io()

        self.ident = E(self.sbt("ident_s", [128, 128], F32))
        self.ones = E(self.sbt("ones_s", [128, 128], F32))
        self.scT = E(self.sbt("scT", [128, KC, 3], BF16))
        self.modT = E(self.sbt("modT", [128, 32, 3], F32))
        self.ga_bc = E(self.sbt("ga_bc", [128, 2, 3, D], BF16))
        self.lnp = E(self.sbt("lnp", [128, 2, D], F32))
        self.affT = E(self.sbt("affT", [48, T], F32))
        self.affTc = E(self.sbt("affTc", [16, NB * L], F32))
        self.wr = E(self.sbt("wr", [128, KC, NE], F32))
        self.PS = [E(nc.psum_tensor("ps%d" % i, [128, 512], F32)) for i in range(8)]

        self.dma('sp', self.ident[:], self.ident_d, (), ['ident'])
        self.ms('dve', self.ones[:], 1.0, ['ones'])
        self.epsc = E(self.sbt("epsc", [128, 2], F32))
        self.ms('dve', self.epsc[:, 0:1], LN_EPS, ['epsc'])
        self.ms('dve', self.epsc[:, 1:2], 64e-5, ['epsc'])
        self.ms('dve', self.affT[:], 0.0, ['affT'])
        self.ms('dve', self.affTc[:], 0.0, ['affTc'])
        with ExitStack() as s2:
            cTs = s2.enter_context(self.sbt("cTs", [128, KC, 3], F32))
            self.dma('sp', cTs[:], self.cT, (), ['cTs'])
            self.act(self.scT[:], cTs[:], AF.Silu, ['cTs'], ['scT'])
            em.flush()

        for i in self.layers:
            kind, jj = i % 3, i // 3
            ctx_read = i <= 2
            ctx_live = i < 2
            self.ada_phase(i)
            if kind == 0:
                self.conv_phase(i, jj, ctx_live)
            elif kind == 1:
                self.na_phase(i, jj, ctx_live)
            else:
                self.rwkv_phase(i, jj)
            self.moe_phase(i, ctx_live, last=(i == self.layers[-1]))
        em.flush()
        st.close()
        em.close()
        return nc

    def extra_io(self):
        self.na_w_qkv = self.din("na_w_qkv", [1, D, 3 * D])
        self.na_w_o = self.din("na_w_o", [1, D, D])
        self.bm_tab = self.din("bm_tab", [16, 5, 128, 640])
        self.V_d = self.dint("V_d", [T + L, D], BF16)
        NT = T + L
        self.rw_mu_prevT = self.din("rw_mu_prevT", [128, 6, KC])
        self.rw_mu_nextT = self.din("rw_mu_nextT", [128, 6, KC])
        for nm in ("rw_w_r", "rw_w_k", "rw_w_v", "rw_w_o"):
            setattr(self, nm, self.din(nm, [D, D]))
        self.rw_w0 = self.din("rw_w0", [2, D])
        self.rw_a0 = self.din("rw_a0", [2, D])
        self.rw_w1 = self.din("rw_w1", [2, D, 64])
        self.rw_a1 = self.din("rw_a1", [2, D, 64])
        self.rw_w2 = self.din("rw_w2", [2, 64, D])
        self.rw_a2 = self.din("rw_a2", [2, 64, D])
        self.rw_g1 = self.din("rw_g1", [D, 128])
        self.rw_g2 = self.din("rw_g2", [128, D])
        for nm in ("rw_k_k", "rw_k_a", "rw_r_k", "rw_gn_g", "rw_gn_b"):
            setattr(self, nm, self.din(nm, [1, D]))
        self.masks = self.din("masks", [4, 128, 128])
        self.bdmask = self.din("bdmask", [128, 128])
        self.RK = self.dint("RK", [NT, D])
        self.RV = self.dint("RV", [NT, D])
        self.RR = self.dint("RR", [NT, D])
        self.RG = self.dint("RG", [NT, D])
        self.KKN = self.dint("KKN", [NT, D])
        self.RLW = [self.dint("RLW%d" % d, [NT, D]) for d in range(2)]
        self.RAI = [self.dint("RAI%d" % d, [NT, D]) for d in range(2)]
        self.KD = [self.dint("KD%d" % d, [NT, D]) for d in range(2)]
        self.BV = [self.dint("BV%d" % d, [NT, D]) for d in range(2)]
        self.YS = [self.dint("YS%d" % d, [T, D]) for d in range(2)]

    def load_lnp(self, i, g):
        for r_, src in ((0, self.ln_g), (1, self.ln_b)):
            self.dma('sp', self.lnp[:, r_, :], src[i, g:g + 1, :].to_broadcast([128, D]), (), [('lnp', r_)])

    def hsrc(self, i):
        return self.hin if i == self.layers[0] else self.H

    def ada_phase(self, i):
        nc = self.nc
        PS = self.PS
        with ExitStack() as s:
            E = s.enter_context
            wA = [E(self.sbt("wA%d" % k, [128, KC, D], BF16)) for k in range(2)]
            abT = E(self.sbt("abT", [128, 48], F32))
            gb = E(self.sbt("gb", [128, 2, D], F32))
            self.scR = E(self.sbt("scR", [128, KC, 3, 128], BF16))
            for j in range(3):
                self.cp('dve', self.scR[:, :, j, :], self.scT[:, :, j:j + 1].to_broadcast([128, KC, 128]), ['scT'], ['scR'])
            self.dma('sp', abT[:], self.ada_bT[i], (), ['abT'])
            for g, cb in enumerate((2, 5)):
                self.dma('sp', gb[:, g, :], self.ada_b[i:i + 1, cb * D:(cb + 1) * D].to_broadcast([128, D]), (), [('gb', g)])
            self.load_lnp(i, 0)
            self.dma('sp', self.wr[:], self.moe_router[i].rearrange("(k p) e -> p k e", p=128), (), ['wr'])
            for cb in range(6):
                wb = wA[cb % 2]
                src = self.ada_w[i][:, cb * D:(cb + 1) * D].rearrange("(k p) n -> p k n", p=128)
                for hh in range(2):
                    self.dma('pool', wb[:, :, hh * 512:(hh + 1) * 512], src[:, :, hh * 512:(hh + 1) * 512], (), [('wA', cb % 2, hh)])
                rw = [('wA', cb % 2, 0), ('wA', cb % 2, 1)]
                if cb in (0, 1, 3, 4):
                    v = (0, 1, None, 2, 3)[cb]
                    b = self.bank()
                    for fc in range(KC):
                        for k in range(KC):
                            self.mm(PS[b][:, fc * 3:fc * 3 + 3], wb[:, k, fc * 128:(fc + 1) * 128], self.scT[:, k, :],
                                    k == 0, k == KC - 1, rw + ['scT'], [('ps', b)])
                    pv = PS[b][:, 0:24].rearrange("p (f j) -> p f j", j=3)
                    self.tt('dve', self.modT[:, v * 8:(v + 1) * 8, :], pv,
                            abT[:, cb * 8:(cb + 1) * 8].unsqueeze(2).to_broadcast([128, 8, 3]), ALU.add,
                            [('ps', b), 'abT'], [('modT', v)])
                    if cb in (1, 4):
                        self.ts('dve', self.modT[:, v * 8:(v + 1) * 8, :], self.modT[:, v * 8:(v + 1) * 8, :], 1.0, None, ALU.add, None,
                                [('modT', v)], [('modT', v)])
                else:
                    g = 0 if cb == 2 else 1
                    for j in range(3):
                        for nh in range(2):
                            b = self.bank()
                            for k in range(KC):
                                self.mm(PS[b][:, :], self.scR[:, k, j, :], wb[:, k, nh * 512:(nh + 1) * 512],
                                        k == 0, k == KC - 1, rw + ['scR'], [('ps', b)])
                            self.tt('dve', self.ga_bc[:, g, j, nh * 512:(nh + 1) * 512], PS[b][:, :], gb[:, g, nh * 512:(nh + 1) * 512], ALU.add,
                                    [('ps', b), ('gb', g)], [('ga', g, j)])
            self.em.flush()

    def streams(self, ctx):
        s = [(0, T, 0, 0), (T, T, 1, 1)]
        if ctx:
            s += [(CTX0, L, 2, 2), (CTX0 + L, L, 2, 3)]
        return s

    def load_modT_to(self, i):
        pass

    def postnorm_tile(self, tl, row0, cond, g, hsrc, ysrc, yr, dst, dst_res, bias_bc=None, router=None):
        PS = self.PS
        ht, t1, t2, st6, mv, sm = tl['ht'], tl['t1'], tl['t2'], tl['st6'], tl['mv'], tl['sm']
        n = tl['n']
        tl['n'] += 1
        p = n % 2
        ht, t1 = ht[p], t1[p]
        R = lambda nm: (nm, p)
        self.dma('sp', ht[:], hsrc[row0:row0 + 128, :], [('H', row0 // 128)] if hsrc is self.H else [], [R('ht')])
        for nh in range(2):
            sl = slice(nh * 512, (nh + 1) * 512)
            if bias_bc is not None:
                self.tt('dve', t1[:, sl], ysrc[nh], bias_bc[:, sl], ALU.add, list(yr[nh]) + ['bias_bc'], [R('t1')])
                self.tt('pool', t1[:, sl], t1[:, sl], self.ga_bc[:, g, cond, sl], ALU.mult, [R('t1'), ('ga', g, cond)], [R('t1')])
            else:
                self.tt('dve', t1[:, sl], ysrc[nh], self.ga_bc[:, g, cond, sl], ALU.mult, list(yr[nh]) + [('ga', g, cond)], [R('t1')])
        self.stt('dve', t1[:], ht[:], ALPHA, t1[:], ALU.mult, ALU.add, [R('ht'), R('t1')], [R('t1')])
        for c in range(2):
            self.em.op('dve', lambda e, c=c: e.bn_stats(out=st6[p][:, c, :], in_=t1[:, c * 512:(c + 1) * 512]), [R('t1')], [R('st6')])
        self.em.op('dve', lambda e: e.bn_aggr(out=mv[p][:], in_=st6[p][:]), [R('st6')], [R('mv')])
        self.act(sm[p][:, 0:1], mv[p][:, 1:2], AF.Ln, [R('mv')], [R('sm')], bias=self.epsc[:, 0:1])
        self.act(sm[p][:, 0:1], sm[p][:, 0:1], AF.Exp, [R('sm')], [R('sm')], scale=-0.5)
        self.stt('dve', sm[p][:, 1:2], mv[p][:, 0:1], -1.0, sm[p][:, 0:1], ALU.mult, ALU.mult, [R('mv'), R('sm')], [R('sm')])
        self.act(t1[:], t1[:], AF.Identity, [R('t1'), R('sm')], [R('t1')], bias=sm[p][:, 1:2], scale=sm[p][:, 0:1])
        self.tt('pool', t1[:], t1[:], self.lnp[:, 0, :], ALU.mult, [R('t1'), ('lnp', 0)], [R('t1')])
        self.tt('dve', t1[:], t1[:], self.lnp[:, 1, :], ALU.add, [R('t1'), ('lnp', 1)], [R('t1')])
        self.dma('sp', dst[row0:row0 + 128, :] if dst is not self.out else dst[row0:row0 + 128, :], t1[:], [R('t1')], [(dst_res, row0 // 128)])
        if router is not None:
            self.router_tile(tl, t1, R('t1'), row0, cond, p)

    def router_tile(self, tl, hnew, hres, row0, cond, p):
        PS = self.PS
        hmT = tl['hmT'][p]
        lg = tl['lg'][p]
        R = lambda nm: (nm, p)
        for half in range(2):
            b = self.bank()
            for kk in range(4):
                k = half * 4 + kk
                self.tr(PS[b][:, kk * 128:(kk + 1) * 128], hnew[:, k * 128:(k + 1) * 128], [hres], [('ps', b)])
            for kk in range(4):
                k = half * 4 + kk
                eng = 'act' if kk % 2 == 0 else 'dve'
                if eng == 'act':
                    self.act(hmT[:, k, :], PS[b][:, kk * 128:(kk + 1) * 128], AF.Identity, [('ps', b), ('modT', 2), ('modT', 3)], [R('hmT')],
                             bias=self.modT[:, 16 + k, cond:cond + 1], scale=self.modT[:, 24 + k, cond:cond + 1])
                else:
                    self.ts('dve', hmT[:, k, :], PS[b][:, kk * 128:(kk + 1) * 128], self.modT[:, 24 + k, cond:cond + 1],
                            self.modT[:, 16 + k, cond:cond + 1], ALU.mult, ALU.add, [('ps', b), ('modT', 2), ('modT', 3)], [R('hmT')])
        b = self.bank()
        for k in range(KC):
            self.mm(PS[b][:, 0:NE], hmT[:, k, :], self.wr[:, k, :], k == 0, k == KC - 1, [R('hmT'), 'wr'], [('ps', b)])
        sm = tl['sm2'][p]
        self.em.op('dve', lambda e: e.tensor_reduce(out=sm[:, 0:1], in_=PS[b][:, 0:NE], axis=AX.X, op=ALU.max, negate=True), [('ps', b)], [R('sm2')])
        is_ctx = row0 >= CTX0
        bsel = (row0 - CTX0) // L if is_ctx else row0 // T
        c0 = 32 if (bsel == 1 and not is_ctx) else 0
        self.act(lg[:, c0:c0 + NE], PS[b][:, 0:NE], AF.Exp, [('ps', b), R('sm2')], [R('lg')], bias=sm[:, 0:1], accum_out=sm[:, 1:2])
        self.em.op('dve', lambda e: e.reciprocal(out=sm[:, 2:3], in_=sm[:, 1:2]), [R('sm2')], [R('sm2')])
        self.ts('dve', lg[:, c0:c0 + NE], lg[:, c0:c0 + NE], sm[:, 2:3], None, ALU.mult, None, [R('lg'), R('sm2')], [R('lg')])
        b2 = self.bank()
        npart = c0 + NE
        self.tr(PS[b2][0:npart, 0:128], lg[:, 0:npart], [R('lg')], [('ps', b2)])
        if is_ctx:
            col = (row0 - CTX0)
            self.cp('act', self.affTc[0:16, col:col + 128], PS[b2][0:16, 0:128], [('ps', b2)], ['affTc'])
        else:
            col = row0 - bsel * T
            self.cp('act', self.affT[c0:c0 + 16, col:col + 128], PS[b2][c0:c0 + 16, 0:128], [('ps', b2)], ['affT'])

    def alloc_pn_tiles(self, E, router):
        nc = self.nc
        tl = {'n': 0}
        tl['ht'] = [E(self.sbt("pn_ht%d" % k, [128, D], F32)) for k in range(2)]
        tl['t1'] = [E(self.sbt("pn_t1%d" % k, [128, D], F32)) for k in range(2)]
        tl['t2'] = None
        tl['st6'] = [E(self.sbt("pn_st%d" % k, [128, 2, 6], F32)) for k in range(2)]
        tl['mv'] = [E(self.sbt("pn_mv%d" % k, [128, 2], F32)) for k in range(2)]
        tl['sm'] = [E(self.sbt("pn_sm%d" % k, [128, 2], F32)) for k in range(2)]
        if router:
            tl['hmT'] = [E(self.sbt("pn_hmT%d" % k, [128, KC, 128], F32)) for k in range(2)]
            tl['lg'] = [E(self.sbt("pn_lg%d" % k, [128, 48], F32)) for k in range(2)]
            tl['sm2'] = [E(self.sbt("pn_sm2%d" % k, [128, 4], F32)) for k in range(2)]
            for k in range(2):
                self.ms('dve', tl['lg'][k][:], 0.0, [('lg', k)])
        return tl

    def conv_phase(self, i, jj, ctx_live):
        nc = self.nc
        PS = self.PS
        em = self.em
        hsrc = self.hsrc(i)
        with ExitStack() as so:
            convout = so.enter_context(self.sbt("convout", [128, KC, T], F32))
            vec = so.enter_context(self.sbt("cvec", [128, 16 + KC * CONVW + 3 * KC], F32))
            b_in = vec[:, 0:16]
            w_dw = vec[:, 16:16 + KC * CONVW].rearrange("p (k j) -> p k j", j=CONVW)
            o = 16 + KC * CONVW
            b_dw = vec[:, o:o + KC]
            cg = vec[:, o + KC:o + 2 * KC]
            cb_ = vec[:, o + 2 * KC:o + 3 * KC]
            self.dma('sp', b_in, self.conv_b_inT[jj], (), ['cvec'])
            self.dma('sp', w_dw, self.conv_w_dwT[jj], (), ['cvec'])
            self.dma('sp', b_dw, self.conv_b_dwT[jj], (), ['cvec'])
            self.dma('sp', cg, self.conv_ln_gT[jj], (), ['cvec'])
            self.dma('sp', cb_, self.conv_ln_bT[jj], (), ['cvec'])
            for (row0, nt, cond, sid) in self.streams(ctx_live):
                ntile = nt // 128
                CH = 512 if nt >= 512 else nt
                nch = nt // CH
                with ExitStack() as s:
                    E = s.enter_context
                    uT = E(self.sbt("uT", [128, KC, nt], BF16))
                    w_in = E(self.sbt("w_in", [128, KC, 2 * D], BF16))
                    xpad = [E(self.sbt("xpad%d" % k, [128, nt + 30], F32)) for k in range(2)]
                    sg = [E(self.sbt("sg%d" % k, [128, 512], F32)) for k in range(2)]
                    ht = [E(self.sbt("cht%d" % k, [128, D], F32)) for k in range(2)]
                    src = self.conv_w_in[jj].rearrange("(k p) n -> p k n", p=128)
                    for hh in range(4):
                        self.dma('pool', w_in[:, :, hh * 512:(hh + 1) * 512], src[:, :, hh * 512:(hh + 1) * 512], (), [('w_in', hh)])
                    for k in range(2):
                        self.ms('pool', xpad[k][:, 0:15], 0.0, [('xpad', k)])
                        self.ms('pool', xpad[k][:, nt + 15:nt + 30], 0.0, [('xpad', k)])
                    for tt_ in range(ntile):
                        p = tt_ % 2
                        r0 = row0 + tt_ * 128
                        self.dma('sp', ht[p][:], hsrc[r0:r0 + 128, :], [('H', r0 // 128)] if hsrc is self.H else [], [('cht', p)])
                        for half in range(2):
                            b = self.bank()
                            for kk in range(4):
                                k = half * 4 + kk
                                self.tr(PS[b][:, kk * 128:(kk + 1) * 128], ht[p][:, k * 128:(k + 1) * 128], [('cht', p)], [('ps', b)])
                            for kk in range(4):
                                k = half * 4 + kk
                                if kk % 2 == 0:
                                    self.act(uT[:, k, tt_ * 128:(tt_ + 1) * 128], PS[b][:, kk * 128:(kk + 1) * 128], AF.Identity,
                                             [('ps', b), ('modT', 0), ('modT', 1)], [('uT', tt_ * 128 // CH)],
                                             bias=self.modT[:, k, cond:cond + 1], scale=self.modT[:, 8 + k, cond:cond + 1])
                                else:
                                    self.ts('dve', uT[:, k, tt_ * 128:(tt_ + 1) * 128], PS[b][:, kk * 128:(kk + 1) * 128],
                                            self.modT[:, 8 + k, cond:cond + 1], self.modT[:, k, cond:cond + 1], ALU.mult, ALU.add,
                                            [('ps', b), ('modT', 0), ('modT', 1)], [('uT', tt_ * 128 // CH)])
                    for j in range(KC):
                        xp = xpad[j % 2]
                        for cc in range(nch):
                            tsl = slice(cc * CH, (cc + 1) * CH)
                            bg = self.bank()
                            for k in range(KC):
                                self.mm(PS[bg][:, 0:CH], w_in[:, k, (8 + j) * 128:(9 + j) * 128], uT[:, k, tsl], k == 0, k == KC - 1,
                                        [('w_in', (8 + j) // 4), ('uT', cc)], [('ps', bg)])
                            bv = self.bank()
                            for k in range(KC):
                                self.mm(PS[bv][:, 0:CH], w_in[:, k, j * 128:(j + 1) * 128], uT[:, k, tsl], k == 0, k == KC - 1,
                                        [('w_in', j // 4), ('uT', cc)], [('ps', bv)])
                            q = cc % 2
                            self.act(sg[q][:, 0:CH], PS[bg][:, 0:CH], AF.Sigmoid, [('ps', bg), 'cvec'], [('sg', q)], bias=b_in[:, 8 + j:9 + j])
                            self.stt('dve', xp[:, 15 + cc * CH:15 + (cc + 1) * CH], PS[bv][:, 0:CH], b_in[:, j:j + 1], sg[q][:, 0:CH],
                                     ALU.add, ALU.mult, [('ps', bv), ('sg', q), 'cvec'], [('xpad', j % 2)])
                        acc = convout[:, j, 0:nt]
                        self.ts('dve', acc, xp[:, 0:nt], w_dw[:, j, 0:1], b_dw[:, j:j + 1], ALU.mult, ALU.add,
                                [('xpad', j % 2), 'cvec'], [('convout', j)])
                        for tap in range(1, CONVW):
                            self.stt('dve', acc, xp[:, tap:tap + nt], w_dw[:, j, tap:tap + 1], acc, ALU.mult, ALU.add,
                                     [('xpad', j % 2), ('convout', j), 'cvec'], [('convout', j)])
                    em.flush()
                with ExitStack() as s:
                    E = s.enter_context
                    sT = E(self.sbt("sT", [128, KC, nt], BF16))
                    w_out = E(self.sbt("w_out", [128, KC, D], BF16))
                    bo_bc = E(self.sbt("bo_bc", [128, D], F32))
                    sq = [E(self.sbt("sq%d" % k, [128, 512], F32)) for k in range(2)]
                    mean = E(self.sbt("cmean", [128, 512], F32))
                    rstd = E(self.sbt("crstd", [128, 512], F32))
                    xn = [E(self.sbt("cxn%d" % k, [128, 512], F32)) for k in range(2)]
                    tl = self.alloc_pn_tiles(E, True)
                    src = self.conv_w_out[jj].rearrange("(k p) n -> p k n", p=128)
                    for hh in range(2):
                        self.dma('pool', w_out[:, :, hh * 512:(hh + 1) * 512], src[:, :, hh * 512:(hh + 1) * 512], (), [('w_out', hh)])
                    self.dma('sp', bo_bc[:], self.conv_b_out[jj:jj + 1, :].to_broadcast([128, D]), (), ['bias_bc'])
                    for cc in range(nch):
                        tsl = slice(cc * CH, (cc + 1) * CH)
                        bm = self.bank()
                        for k in range(KC):
                            self.mm(PS[bm][:, 0:CH], self.ones[:], convout[:, k, tsl], k == 0, k == KC - 1, ['ones', ('convout', k)], [('ps', bm)])
                        bq = self.bank()
                        for k in range(KC):
                            q = k % 2
                            self.act(sq[q][:, 0:CH], convout[:, k, tsl], AF.Square, [('convout', k)], [('sq', q)])
                            self.mm(PS[bq][:, 0:CH], self.ones[:], sq[q][:, 0:CH], k == 0, k == KC - 1, ['ones', ('sq', q)], [('ps', bq)])
                        self.act(mean[:, 0:CH], PS[bm][:, 0:CH], AF.Identity, [('ps', bm)], ['cmean'], scale=1.0 / D)
                        self.tt('dve', rstd[:, 0:CH], mean[:, 0:CH], mean[:, 0:CH], ALU.mult, ['cmean'], ['crstd'])
                        self.stt('dve', rstd[:, 0:CH], PS[bq][:, 0:CH], 1.0 / D, rstd[:, 0:CH], ALU.mult, ALU.subtract, [('ps', bq), 'crstd'], ['crstd'])
                        self.act(rstd[:, 0:CH], rstd[:, 0:CH], AF.Ln, ['crstd'], ['crstd'], bias=self.epsc[:, 0:1])
                        self.act(rstd[:, 0:CH], rstd[:, 0:CH], AF.Exp, ['crstd'], ['crstd'], scale=-0.5)
                        for k in range(KC):
                            q = k % 2
                            self.tt('dve', xn[q][:, 0:CH], convout[:, k, tsl], mean[:, 0:CH], ALU.subtract, [('convout', k), 'cmean'], [('cxn', q)])
                            self.tt('pool', xn[q][:, 0:CH], xn[q][:, 0:CH], rstd[:, 0:CH], ALU.mult, [('cxn', q), 'crstd'], [('cxn', q)])
                            self.act(sT[:, k, tsl], xn[q][:, 0:CH], AF.Silu, [('cxn', q), 'cvec'], [('sT', cc)],
                                     bias=cb_[:, k:k + 1], scale=cg[:, k:k + 1])
                    for tt_ in range(ntile):
                        r0 = row0 + tt_ * 128
                        bs = []
                        for nh in range(2):
                            b = self.bank()
                            bs.append(b)
                            for k in range(KC):
                                self.mm(PS[b][:, :], sT[:, k, tt_ * 128:(tt_ + 1) * 128], w_out[:, k, nh * 512:(nh + 1) * 512], k == 0, k == KC - 1,
                                        [('sT', tt_ * 128 // CH), ('w_out', nh)], [('ps', b)])
                        self.postnorm_tile(tl, r0, cond, 0, hsrc, [PS[bs[0]][:, :], PS[bs[1]][:, :]], [[('ps', bs[0])], [('ps', bs[1])]],
                                           self.H, 'H', bias_bc=bo_bc, router=True)
                    em.flush()

    def moe_phase(self, i, ctx_live, last):
        nc = self.nc
        PS = self.PS
        em = self.em
        nexp = self.n_exp
        with ExitStack() as so:
            EO = so.enter_context
            idxT = EO(self.sbt("idxT", [128, 2, 48], I32))
            gateT = EO(self.sbt("gateT", [128, 2, 48], F32))
            idxTc = EO(self.sbt("idxTc", [64, 16], I32))
            gateTc = EO(self.sbt("gateTc", [64, 16], F32))
            self.zero = EO(self.sbt("zero", [128, D], F32))
            self.ms('dve', self.zero[:], 0.0, ['zero'])
            self.load_lnp(i, 1)
            nrows = NROW if ctx_live else NB * T
            for r0 in range(0, nrows, 128):
                self.dma('sp', self.Y[r0:r0 + 128, :], self.zero[:], ['zero'], ['Y'])
            with ExitStack() as s:
                E = s.enter_context
                wk = E(self.sbt("wk", [48, T], F32))
                vals = E(self.sbt("vals", [48, CAP], F32))
                idxu = E(self.sbt("idxu", [48, CAP], U32))
                idxf = E(self.sbt("idxf", [48, CAP], F32))
                self.cp('dve', wk[:], self.affT[:], ['affT'], ['wk'])
                for it in range(CAP // 8):
                    sl = slice(it * 8, (it + 1) * 8)
                    self.em.op('dve', lambda e, sl=sl: e.max(out=vals[:, sl], in_=wk[:]), ['wk'], ['vals'])
                    self.em.op('dve', lambda e, sl=sl: e.max_index(out=idxu[:, sl], in_max=vals[:, sl], in_values=wk[:]), ['wk', 'vals'], ['idxu'])
                    if it < CAP // 8 - 1:
                        self.em.op('dve', lambda e, sl=sl: e.match_replace(out=wk[:], in_to_replace=vals[:, sl], in_values=wk[:], imm_value=-1.0),
                                   ['wk', 'vals'], ['wk'])
                self.cp('dve', idxf[:], idxu[:], ['idxu'], ['idxf'])
                self.ts('dve', idxf[32:48, :], idxf[32:48, :], float(T), None, ALU.add, None, ['idxf'], ['idxf'])
                for j in range(2):
                    b = self.bank()
                    self.tr(PS[b][:, 0:48], idxf[:, j * 128:(j + 1) * 128], ['idxf'], [('ps', b)])
                    self.cp('dve', idxT[:, j, :], PS[b][:, 0:48], [('ps', b)], ['idxT'])
                    b = self.bank()
                    self.tr(PS[b][:, 0:48], vals[:, j * 128:(j + 1) * 128], ['vals'], [('ps', b)])
                    self.cp('act', gateT[:, j, :], PS[b][:, 0:48], [('ps', b)], ['gateT'])
                if ctx_live:
                    wkc = E(self.sbt("wkc", [16, NB * L], F32))
                    valc = E(self.sbt("valc", [16, NB * CAPC], F32))
                    idxcu = E(self.sbt("idxcu", [16, NB * CAPC], U32))
                    idxcf = E(self.sbt("idxcf", [16, NB * CAPC], F32))
                    self.cp('dve', wkc[:], self.affTc[:], ['affTc'], ['wkc'])
                    for bb in range(NB):
                        for it in range(CAPC // 8):
                            sl = slice(bb * CAPC + it * 8, bb * CAPC + (it + 1) * 8)
                            wsl = slice(bb * L, (bb + 1) * L)
                            self.em.op('dve', lambda e, sl=sl, wsl=wsl: e.max(out=valc[:, sl], in_=wkc[:, wsl]), ['wkc'], ['valc'])
                            self.em.op('dve', lambda e, sl=sl, wsl=wsl: e.max_index(out=idxcu[:, sl], in_max=valc[:, sl], in_values=wkc[:, wsl]),
                                       ['wkc', 'valc'], ['idxcu'])
                            if it < CAPC // 8 - 1:
                                self.em.op('dve', lambda e, sl=sl, wsl=wsl: e.match_replace(out=wkc[:, wsl], in_to_replace=valc[:, sl],
                                                                                          in_values=wkc[:, wsl], imm_value=-1.0),
                                           ['wkc', 'valc'], ['wkc'])
                    self.cp('dve', idxcf[:], idxcu[:], ['idxcu'], ['idxcf'])
                    for bb in range(NB):
                        self.ts('dve', idxcf[:, bb * CAPC:(bb + 1) * CAPC], idxcf[:, bb * CAPC:(bb + 1) * CAPC], float(CTX0 + bb * L), None,
                                ALU.add, None, ['idxcf'], ['idxcf'])
                    b = self.bank()
                    self.tr(PS[b][0:64, 0:16], idxcf[:, :], ['idxcf'], [('ps', b)])
                    self.cp('dve', idxTc[:], PS[b][0:64, 0:16], [('ps', b)], ['idxTc'])
                    b = self.bank()
                    self.tr(PS[b][0:64, 0:16], valc[:, :], ['valc'], [('ps', b)])
                    self.cp('act', gateTc[:], PS[b][0:64, 0:16], [('ps', b)], ['gateTc'])
                em.flush()
            NTOK = 4 * 128 + (64 if ctx_live else 0)
            with ExitStack() as s:
                E = s.enter_context
                NS = 3
                G = [E(self.sbt("Gw%d" % k, [128, KC, 512], BF16)) for k in range(NS)]
                U = [E(self.sbt("Uw%d" % k, [128, KC, 512], BF16)) for k in range(NS)]
                Wd = [E(self.sbt("Wd%d" % k, [128, 16, D], BF16)) for k in range(2)]
                xg = [E(self.sbt("xg%d" % k, [128, D], F32)) for k in range(2)]
                xsT = [E(self.sbt("xsT%d" % k, [128, KC, NTOK], BF16)) for k in range(1)]
                hidT = [E(self.sbt("hidT%d" % k, [128, 16, NTOK], BF16)) for k in range(1)]
                sg = [E(self.sbt("msg%d" % k, [128, NTOK], F32)) for k in range(2)]
                ys = [E(self.sbt("ys%d" % k, [128, D], F32)) for k in range(2)]

                def load_gu(sq_):
                    if sq_ >= nexp * 4:
                        return
                    e, c = sq_ // 4, sq_ % 4
                    sl_ = sq_ % NS
                    sgw = self.moe_w_gate[i, e].rearrange("(k p) n -> p k n", p=128)
                    suw = self.moe_w_up[i, e].rearrange("(k p) n -> p k n", p=128)
                    self.dma('pool', G[sl_][:, :, :], sgw[:, :, c * 512:(c + 1) * 512], (), [('G', sl_)])
                    self.dma('pool', U[sl_][:, :, :], suw[:, :, c * 512:(c + 1) * 512], (), [('U', sl_)])

                def load_d(e):
                    sdw = self.moe_w_down[i, e].rearrange("(f p) n -> p f n", p=128)
                    for hh in range(4):
                        self.dma('pool', Wd[e % 2][:, hh * 4:(hh + 1) * 4, :], sdw[:, hh * 4:(hh + 1) * 4, :], (), [('Wd', e % 2, hh)])

                for sq_ in range(NS):
                    load_gu(sq_)
                load_d(0)
                ngath = 0
                for e in range(nexp):
                    xs = xsT[0]
                    hd = hidT[0]
                    tiles = [(0, 0), (0, 1), (1, 0), (1, 1)]
                    for bb in range(NB):
                        bks = [self.bank() for _ in range(4)]
                        for j in range(2):
                            g = xg[ngath % 2]
                            gr = ('xg', ngath % 2)
                            ngath += 1
                            col = bb * 32 + e
                            self.em.dma('pool', lambda en, g=g, j=j, col=col: en.indirect_dma_start(
                                out=g[:], out_offset=None, in_=self.H,
                                in_offset=bass.IndirectOffsetOnAxis(ap=idxT[:, j, col:col + 1], axis=0)),
                                ['idxT'] + [('H', r) for r in range(NROW // 128)], [gr])
                            for k in range(KC):
                                bk = bks[k // 2]
                                off = (k % 2) * 256 + j * 128
                                self.tr(PS[bk][:, off:off + 128], g[:, k * 128:(k + 1) * 128], [gr], [('ps', bk)])
                        for k in range(KC):
                            bk = bks[k // 2]
                            off = (k % 2) * 256
                            dst = xs[:, k, bb * 256:(bb + 1) * 256]
                            if k % 2 == 0:
                                self.act(dst, PS[bk][:, off:off + 256], AF.Identity, [('ps', bk), ('modT', 2), ('modT', 3)], [('xsT', 0)],
                                         bias=self.modT[:, 16 + k, bb:bb + 1], scale=self.modT[:, 24 + k, bb:bb + 1])
                            else:
                                self.ts('dve', dst, PS[bk][:, off:off + 256], self.modT[:, 24 + k, bb:bb + 1], self.modT[:, 16 + k, bb:bb + 1],
                                        ALU.mult, ALU.add, [('ps', bk), ('modT', 2), ('modT', 3)], [('xsT', 0)])
                    if ctx_live:
                        g = xg[ngath % 2]
                        gr = ('xg', ngath % 2)
                        ngath += 1
                        self.em.dma('pool', lambda en, g=g, e=e: en.indirect_dma_start(
                            out=g[0:64, :], out_offset=None, in_=self.H,
                            in_offset=bass.IndirectOffsetOnAxis(ap=idxTc[:, e:e + 1], axis=0)),
                            ['idxTc'] + [('H', r) for r in range(NROW // 128)], [gr])
                        bks = [self.bank() for _ in range(1)]
                        for k in range(KC):
                            self.tr(PS[bks[0]][:, k * 64:(k + 1) * 64], g[0:64, k * 128:(k + 1) * 128], [gr], [('ps', bks[0])])
                        for k in range(KC):
                            dst = xs[:, k, 512:576]
                            self.ts('dve', dst, PS[bks[0]][:, k * 64:(k + 1) * 64], self.modT[:, 24 + k, 2:3], self.modT[:, 16 + k, 2:3],
                                    ALU.mult, ALU.add, [('ps', bks[0]), ('modT', 2), ('modT', 3)], [('xsT', 0)])
                    if e + 1 < nexp:
                        load_d(e + 1)
                    for c in range(4):
                        for fl in range(4):
                            ft = c * 4 + fl
                            bg = self.bank()
                            bu = self.bank()
                            sl_ = (e * 4 + c) % NS
                            for (bk_, W, wr_) in ((bg, G[sl_], ('G', sl_)), (bu, U[sl_], ('U', sl_))):
                                for k in range(KC):
                                    self.mm(PS[bk_][:, 0:512], W[:, k, fl * 128:(fl + 1) * 128], xs[:, k, 0:512], k == 0, k == KC - 1,
                                            [wr_, ('xsT', 0)], [('ps', bk_)])
                            q = ft % 2
                            self.act(sg[q][:, 0:512], PS[bg][:, 0:512], AF.Silu, [('ps', bg)], [('msg', q)])
                            self.tt('dve', hd[:, ft, 0:512], sg[q][:, 0:512], PS[bu][:, 0:512], ALU.mult, [('msg', q), ('ps', bu)], [('hidT', ft)])
                            if ctx_live:
                                bc = self.bank()
                                for gi, (W, wr_) in enumerate(((G[sl_], ('G', sl_)), (U[sl_], ('U', sl_)))):
                                    for k in range(KC):
                                        self.mm(PS[bc][:, gi * 64:(gi + 1) * 64], W[:, k, fl * 128:(fl + 1) * 128], xs[:, k, 512:576], k == 0, k == KC - 1,
                                                [wr_, ('xsT', 0)], [('ps', bc)])
                                self.act(sg[q][:, 512:576], PS[bc][:, 0:64], AF.Silu, [('ps', bc)], [('msg', q)])
                                self.tt('dve', hd[:, ft, 512:576], sg[q][:, 512:576], PS[bc][:, 64:128], ALU.mult, [('msg', q), ('ps', bc)], [('hidT', ft)])
                        load_gu(e * 4 + c + NS)
                    ttiles = [(bb, j, 128) for bb in range(NB) for j in range(2)] + ([(2, 0, 64)] if ctx_live else [])
                    for ti, (bb, j, np_) in enumerate(ttiles):
                        y_ = ys[ti % 2]
                        yr = ('ys', ti % 2)
                        tok0 = ti * 128
                        for nh in range(2):
                            b = self.bank()
                            for ft in range(16):
                                self.mm(PS[b][0:np_, :], hd[:, ft, tok0:tok0 + np_], Wd[e % 2][:, ft, nh * 512:(nh + 1) * 512], ft == 0, ft == 15,
                                        [('hidT', ft), ('Wd', e % 2, ft // 4)], [('ps', b)])
                            gcol = gateTc[:, e:e + 1] if bb == 2 else gateT[:, j, bb * 32 + e:bb * 32 + e + 1]
                            gres = 'gateTc' if bb == 2 else 'gateT'
                            if nh == 0:
                                self.ts('dve', y_[0:np_, 0:512], PS[b][0:np_, :], gcol[0:np_, :], None, ALU.mult, None, [('ps', b), gres], [yr])
                            else:
                                self.act(y_[0:np_, 512:1024], PS[b][0:np_, :], AF.Identity, [('ps', b), gres], [yr], scale=gcol[0:np_, :])
                        if bb == 2:
                            self.em.dma('pool', lambda en, y_=y_, e=e: en.indirect_dma_start(
                                out=self.Y, out_offset=bass.IndirectOffsetOnAxis(ap=idxTc[:, e:e + 1], axis=0),
                                in_=y_[0:64, :], in_offset=None, compute_op=ALU.add), [yr, 'idxTc'], ['Y'])
                        else:
                            col = bb * 32 + e
                            self.em.dma('pool', lambda en, y_=y_, j=j, col=col: en.indirect_dma_start(
                                out=self.Y, out_offset=bass.IndirectOffsetOnAxis(ap=idxT[:, j, col:col + 1], axis=0),
                                in_=y_[:, :], in_offset=None, compute_op=ALU.add), [yr, 'idxT'], ['Y'])
                em.flush()
            with ExitStack() as s:
                E = s.enter_context
                tl = self.alloc_pn_tiles(E, False)
                yt = [E(self.sbt("pn_y%d" % k, [128, D], F32)) for k in range(2)]
                for (row0, nt, cond, sid) in self.streams(ctx_live):
                    for tt_ in range(nt // 128):
                        r0 = row0 + tt_ * 128
                        p = tl['n'] % 2
                        self.dma('act', yt[p][:], self.Y[r0:r0 + 128, :], ['Y'], [('pn_y', p)])
                        if last and row0 < CTX0:
                            dst, dres = self.out, 'out'
                        else:
                            dst, dres = self.H, 'H'
                        self.postnorm_tile(tl, r0, cond, 1, self.H, [yt[p][:, 0:512], yt[p][:, 512:1024]], [[('pn_y', p)], [('pn_y', p)]],
                                           dst, dres, bias_bc=None, router=None)
                em.flush()

    def modT_tiles(self, uT, hsrc, ht, row0, nt, cond, c0, CH, res_name='uT'):
        PS = self.PS
        for tt_ in range(nt // 128):
            p = self._mt % 2
            self._mt += 1
            r0 = row0 + tt_ * 128
            col = c0 + tt_ * 128
            self.dma('sp', ht[p][:], hsrc[r0:r0 + 128, :], [('H', r0 // 128)] if hsrc is self.H else [], [('cht', p)])
            for half in range(2):
                b = self.bank()
                for kk in range(4):
                    k = half * 4 + kk
                    self.tr(PS[b][:, kk * 128:(kk + 1) * 128], ht[p][:, k * 128:(k + 1) * 128], [('cht', p)], [('ps', b)])
                for kk in range(4):
                    k = half * 4 + kk
                    if kk % 2 == 0:
                        self.act(uT[:, k, col:col + 128], PS[b][:, kk * 128:(kk + 1) * 128], AF.Identity,
                                 [('ps', b), ('modT', 0), ('modT', 1)], [(res_name, col // CH)],
                                 bias=self.modT[:, k, cond:cond + 1], scale=self.modT[:, 8 + k, cond:cond + 1])
                    else:
                        self.ts('dve', uT[:, k, col:col + 128], PS[b][:, kk * 128:(kk + 1) * 128],
                                self.modT[:, 8 + k, cond:cond + 1], self.modT[:, k, cond:cond + 1], ALU.mult, ALU.add,
                                [('ps', b), ('modT', 0), ('modT', 1)], [(res_name, col // CH)])

    def na_phase(self, i, jj, ctx_live):
        nc = self.nc
        PS = self.PS
        em = self.em
        hsrc = self.hsrc(i)
        NT = T + L
        chunks = [(0, 512), (512, 512), (1024, 512), (1536, 512), (2048, 256)]
        classes = ((0, [0]), (1, [1]), (2, list(range(2, 14))), (3, [14]), (4, [15]))
        with ExitStack() as so:
            EO = so.enter_context
            qT = EO(self.sbt("qT", [128, 8, NT], BF16))
            kT = EO(self.sbt("kT", [128, 8, NT], BF16))
            identb = EO(self.sbt("identb", [128, 128], BF16))
            self.cp('dve', identb[:], self.ident[:], ['ident'], ['identb'])
            for b in range(NB):
                with ExitStack() as s:
                    E = s.enter_context
                    uT = E(self.sbt("uT", [128, KC, NT], BF16))
                    wq = [E(self.sbt("wq%d" % k, [128, KC, D], BF16)) for k in range(2)]
                    vst = [E(self.sbt("vst%d" % k, [128, D], BF16)) for k in range(2)]
                    ht = [E(self.sbt("cht%d" % k, [128, D], F32)) for k in range(2)]

                    def loadw(blk, slot):
                        src = self.na_w_qkv[jj][:, blk * D:(blk + 1) * D].rearrange("(k p) n -> p k n", p=128)
                        for hh in range(2):
                            self.dma('pool', wq[slot][:, :, hh * 512:(hh + 1) * 512], src[:, :, hh * 512:(hh + 1) * 512], (), [('wq', slot, hh)])
                    loadw(0, 0)
                    loadw(1, 1)
                    self._mt = 0
                    self.modT_tiles(uT, hsrc, ht, b * T, T, b, 0, 512)
                    self.modT_tiles(uT, hsrc, ht, CTX0 + b * L, L, 2, T, 512)
                    for blk, dstT, nm in ((0, qT, 'qT'), (1, kT, 'kT')):
                        w = wq[blk]
                        for hp in range(8):
                            for (c0, n) in chunks:
                                bk = self.bank()
                                for k in range(KC):
                                    self.mm(PS[bk][:, 0:n], w[:, k, hp * 128:(hp + 1) * 128], uT[:, k, c0:c0 + n], k == 0, k == KC - 1,
                                            [('wq', blk, hp // 4), ('uT', c0 // 512)], [('ps', bk)])
                                if blk == 0:
                                    self.act(dstT[:, hp, c0:c0 + n], PS[bk][:, 0:n], AF.Identity, [('ps', bk)], [(nm, hp)], scale=0.125)
                                else:
                                    self.cp('dve', dstT[:, hp, c0:c0 + n], PS[bk][:, 0:n], [('ps', bk)], [(nm, hp)])
                    loadw(2, 0)
                    for t in range(NT // 128):
                        p = t % 2
                        for nh in range(2):
                            bk = self.bank()
                            for k in range(KC):
                                self.mm(PS[bk][:, :], uT[:, k, t * 128:(t + 1) * 128], wq[0][:, k, nh * 512:(nh + 1) * 512], k == 0, k == KC - 1,
                                        [('wq', 0, nh), ('uT', t * 128 // 512)], [('ps', bk)])
                            if nh == 0:
                                self.cp('act', vst[p][:, 0:512], PS[bk][:, :], [('ps', bk)], [('vst', p)])
                            else:
                                self.cp('dve', vst[p][:, 512:1024], PS[bk][:, :], [('ps', bk)], [('vst', p)])
                        self.dma('sp', self.V_d[t * 128:(t + 1) * 128, :], vst[p][:], [('vst', p)], [('V_d', t)])
                    em.flush()
                with ExitStack() as sm_:
                    oT = sm_.enter_context(self.sbt("oT", [128, 8, NT], BF16))
                    with ExitStack() as s:
                        E = s.enter_context
                        Vh = [E(self.sbt("Vh%d" % k, [128, NT // 128, 128], BF16)) for k in range(2)]
                        BM = [E(self.sbt("BM%d" % k, [128, 640], F32)) for k in range(2)]
                        S = [E(self.sbt("S%d" % k, [128, 896], F32)) for k in range(2)]
                        Pn = [E(self.sbt("Pn%d" % k, [128, 896], BF16)) for k in range(2)]
                        PT = [E(self.sbt("PT%d" % k, [128, 896], BF16)) for k in range(2)]
                        smx = [E(self.sbt("smx%d" % k, [128, 4], F32)) for k in range(2)]
                        nbm = 0
                        nq = 0
                        vsrc = self.V_d.rearrange("(t p) c -> p t c", p=128)
                        for hp in range(8):
                            vh = Vh[hp % 2]
                            self.dma('sp', vh[:], vsrc[:, :, hp * 128:(hp + 1) * 128], [('V_d', t) for t in range(NT // 128)], [('Vh', hp % 2)])
                            for hh in range(2):
                                h = hp * 2 + hh
                                pr = slice(hh * 64, (hh + 1) * 64)

                                def attend(qcol, segs, bm, nkeys, vts):
                                    nonlocal nq
                                    m = nq % 2
                                    nq += 1
                                    S_, Pn_, PT_, sx = S[m], Pn[m], PT[m], smx[m]
                                    R = lambda nm: (nm, m)
                                    bks = {}
                                    scol = 0
                                    for (bi, pc0, kc0, n, loc) in segs:
                                        if bi not in bks:
                                            bks[bi] = self.bank()
                                        bk = bks[bi]
                                        self.mm(PS[bk][:, pc0:pc0 + n], qT[pr, hp, qcol:qcol + 128], kT[pr, hp, kc0:kc0 + n], True, True,
                                                [('qT', hp), ('kT', hp)], [('ps', bk)])
                                    for (bi, pc0, kc0, n, loc) in segs:
                                        bk = bks[bi]
                                        if loc:
                                            self.tt('dve', S_[:, scol:scol + n], PS[bk][:, pc0:pc0 + n], bm[0][:, scol:scol + n], ALU.add,
                                                    [('ps', bk), bm[1]], [R('S')])
                                        else:
                                            self.cp('act', S_[:, scol:scol + n], PS[bk][:, pc0:pc0 + n], [('ps', bk)], [R('S')])
                                        scol += n
                                    self.em.op('dve', lambda e: e.tensor_reduce(out=sx[:, 0:1], in_=S_[:, 0:nkeys], axis=AX.X, op=ALU.max, negate=True),
                                               [R('S')], [R('smx')])
                                    self.act(S_[:, 0:nkeys], S_[:, 0:nkeys], AF.Exp, [R('S'), R('smx')], [R('S')], bias=sx[:, 0:1], accum_out=sx[:, 1:2])
                                    self.em.op('dve', lambda e: e.reciprocal(out=sx[:, 2:3], in_=sx[:, 1:2]), [R('smx')], [R('smx')])
                                    self.ts('dve', Pn_[:, 0:nkeys], S_[:, 0:nkeys], sx[:, 2:3], None, ALU.mult, None, [R('S'), R('smx')], [R('Pn')])
                                    bT = self.bank()
                                    PTv = PS[bT][:, :].bitcast(BF16)
                                    nch = nkeys // 128
                                    for c in range(nch):
                                        self.tr(PTv[:, c * 128:(c + 1) * 128], Pn_[:, c * 128:(c + 1) * 128], [R('Pn'), 'identb'], [('ps', bT)], ident=identb)
                                    if m == 0:
                                        self.cp('act', PT_[:, 0:nkeys], PTv[:, 0:nkeys], [('ps', bT)], [R('PT')])
                                    else:
                                        self.cp('dve', PT_[:, 0:nkeys], PTv[:, 0:nkeys], [('ps', bT)], [R('PT')])
                                    bO = self.bank()
                                    for c in range(nch):
                                        self.mm(PS[bO][pr, 0:128], vh[:, vts[c], hh * 64:(hh + 1) * 64], PT_[:, c * 128:(c + 1) * 128], c == 0, c == nch - 1,
                                                [('Vh', hp % 2), R('PT')], [('ps', bO)])
                                    if m == 0:
                                        self.cp('dve', oT[pr, hp, qcol:qcol + 128], PS[bO][pr, 0:128], [('ps', bO)], [('oT', qcol // 128)])
                                    else:
                                        self.cp('act', oT[pr, hp, qcol:qcol + 128], PS[bO][pr, 0:128], [('ps', bO)], [('oT', qcol // 128)])

                                for cls, tiles in classes:
                                    bmt = BM[nbm % 2]
                                    bmr = ('BM', nbm % 2)
                                    nbm += 1
                                    self.dma('sp', bmt[:], self.bm_tab[h, cls], (), [bmr])
                                    for i_ in tiles:
                                        ks = min(max(2 * i_ - 4, 0), 22)
                                        k0 = ks * 64
                                        segs = [(0, 0, k0, 512, True), (1, 0, k0 + 512, 128, True), (1, 128, T, 256, False)]
                                        vts = [ks // 2 + c for c in range(5)] + [16, 17]
                                        attend(i_ * 128, segs, (bmt, bmr), 896, vts)
                                for j in range(2):
                                    attend(T + j * 128, [(0, 0, T, 256, False)], None, 256, [16, 17])
                        em.flush()
                    with ExitStack() as s:
                        E = s.enter_context
                        w_o = E(self.sbt("w_o", [128, KC, D], BF16))
                        tl = self.alloc_pn_tiles(E, True)
                        src = self.na_w_o[jj].rearrange("(k p) n -> p k n", p=128)
                        for hh in range(2):
                            self.dma('pool', w_o[:, :, hh * 512:(hh + 1) * 512], src[:, :, hh * 512:(hh + 1) * 512], (), [('w_o', hh)])
                        for t in range(NT // 128):
                            if t < 16:
                                r0, cond = b * T + t * 128, b
                            else:
                                r0, cond = CTX0 + b * L + (t - 16) * 128, 2
                            bs = []
                            for nh in range(2):
                                bk = self.bank()
                                bs.append(bk)
                                for k in range(KC):
                                    self.mm(PS[bk][:, :], oT[:, k, t * 128:(t + 1) * 128], w_o[:, k, nh * 512:(nh + 1) * 512], k == 0, k == KC - 1,
                                            [('oT', t), ('w_o', nh)], [('ps', bk)])
                            self.postnorm_tile(tl, r0, cond, 0, hsrc, [PS[bs[0]][:, :], PS[bs[1]][:, :]], [[('ps', bs[0])], [('ps', bs[1])]],
                                               self.H, 'H', bias_bc=None, router=True)
                        em.flush()


    def rwkv_phase(self, i, jj):
        nc = self.nc
        PS = self.PS
        em = self.em
        hsrc = self.hsrc(i)
        NT = T + L
        C64 = 0.6065306597126334
        R3 = lambda ap, dd=64: ap.rearrange("p (h d) -> p h d", d=dd)

        def bc64(ap16):
            return ap16.unsqueeze(2).to_broadcast([128, 16, 64])

        for b in range(NB):
            with ExitStack() as s:
                E = s.enter_context
                wbuf = [E(self.sbt("rwW%d" % k, [128, KC, D], BF16)) for k in range(1)]
                w1c = E(self.sbt("w1c", [128, KC, 128], BF16))
                a1c = E(self.sbt("a1c", [128, KC, 128], BF16))
                g1s = E(self.sbt("g1s", [128, KC, 128], BF16))
                w2s = E(self.sbt("w2s", [128, D], BF16))
                a2s = E(self.sbt("a2s", [128, D], BF16))
                g2s = E(self.sbt("g2s", [128, D], BF16))
                muT = E(self.sbt("muT", [128, 2, 6, KC], F32))
                bc0 = E(self.sbt("bc0", [128, 2, D], F32))
                u32 = E(self.sbt("u32", [128, KC, T + 2], F32))
                lerp = E(self.sbt("lerp", [128, KC, T], BF16))
                tmp = [E(self.sbt("ltmp%d" % k, [128, T], F32)) for k in range(2)]
                loT = E(self.sbt("loT", [128, T], BF16))
                stg = [E(self.sbt("stg%d" % k, [128, D], F32)) for k in range(2)]
                ht = [E(self.sbt("cht%d" % k, [128, D], F32)) for k in range(2)]
                self.dma('sp', muT[:, 0], self.rw_mu_prevT, (), ['muT'])
                self.dma('sp', muT[:, 1], self.rw_mu_nextT, (), ['muT'])
                for d in range(2):
                    self.dma('pool', w1c[:, :, d * 64:(d + 1) * 64], self.rw_w1[d].rearrange("(k p) n -> p k n", p=128), (), ['w1c'])
                    self.dma('pool', a1c[:, :, d * 64:(d + 1) * 64], self.rw_a1[d].rearrange("(k p) n -> p k n", p=128), (), ['a1c'])
                self.dma('pool', g1s[:], self.rw_g1.rearrange("(k p) n -> p k n", p=128), (), ['g1s'])
                self.dma('pool', w2s[:], self.rw_w2.rearrange("d j n -> (d j) n"), (), ['w2s'])
                self.dma('pool', a2s[:], self.rw_a2.rearrange("d j n -> (d j) n"), (), ['a2s'])
                self.dma('pool', g2s[:], self.rw_g2, (), ['g2s'])
                nst = [0]

                def loadW(src, slot):
                    v = src.rearrange("(k p) n -> p k n", p=128)
                    for hh in range(2):
                        self.dma('pool', wbuf[slot][:, :, hh * 512:(hh + 1) * 512], v[:, :, hh * 512:(hh + 1) * 512], (), [('rwW', slot, hh)])

                def stage_out(dst, row, evs):
                    p = nst[0] % 2
                    nst[0] += 1
                    for nh in range(2):
                        evs[nh](stg[p][:, nh * 512:(nh + 1) * 512], ('stg', p))
                    self.dma('sp', dst[row:row + 128, :], stg[p][:], [('stg', p)], [(dst.tensor.name, row // 128)])

                for (row0, nt, cond, roff, is_lat) in ((b * T, T, b, 0, True), (CTX0 + b * L, L, 2, T, False)):
                    ntile = nt // 128
                    CH = min(512, nt)
                    nch = nt // CH
                    for k in range(KC):
                        self.ms('pool', u32[:, k, 0:1], 0.0, [('u32', 0)])
                        self.ms('pool', u32[:, k, nt + 1:nt + 2], 0.0, [('u32', 0)])
                    self._mt = 0
                    self.modT_tiles(u32, hsrc, ht, row0, nt, cond, 1, 1 << 30, res_name='u32')
                    lerps = (0, 1, 2, 3, 4, 5) if is_lat else (1, 2, 3, 4)
                    wsrc = {0: self.rw_w_r, 2: self.rw_w_k, 3: self.rw_w_v}
                    wslot = {0: 0, 2: 0, 3: 0}
                    for n in lerps:
                        if n in wsrc:
                            loadW(wsrc[n], wslot[n])
                        for k in range(KC):
                            uk = u32[:, k, 1:nt + 1]
                            t0, t1 = tmp[0][:, 0:nt], tmp[1][:, 0:nt]
                            self.tt('dve', t0, u32[:, k, 0:nt], uk, ALU.subtract, [('u32', 0)], [('ltmp', 0)])
                            self.stt('dve', t0, t0, muT[:, 0, n, k:k + 1], uk, ALU.mult, ALU.add, [('ltmp', 0), 'muT', ('u32', 0)], [('ltmp', 0)])
                            self.tt('pool', t1, u32[:, k, 2:nt + 2], uk, ALU.subtract, [('u32', 0)], [('ltmp', 1)])
                            self.stt('dve', lerp[:, k, 0:nt], t1, muT[:, 1, n, k:k + 1], t0, ALU.mult, ALU.add,
                                     [('ltmp', 0), ('ltmp', 1), 'muT'], [('lerp', k)])
                        lr = [('lerp', k) for k in range(KC)]
                        if n in wsrc:
                            dst = {0: self.RR, 2: self.RK, 3: self.RV}[n]
                            W = wbuf[wslot[n]]
                            for t in range(ntile):
                                bks = [self.bank(), self.bank()]
                                for nh in range(2):
                                    for k in range(KC):
                                        self.mm(PS[bks[nh]][:, :], lerp[:, k, t * 128:(t + 1) * 128], W[:, k, nh * 512:(nh + 1) * 512], k == 0, k == KC - 1,
                                                lr + [('rwW', wslot[n], nh)], [('ps', bks[nh])])
                                stage_out(dst, roff + t * 128, [
                                    lambda o, r_, bk=bks[0]: self.cp('act', o, PS[bk][:, :], [('ps', bk)], [r_]),
                                    lambda o, r_, bk=bks[1]: self.cp('dve', o, PS[bk][:, :], [('ps', bk)], [r_])])
                        else:
                            l1 = {1: w1c, 4: a1c, 5: g1s}[n]
                            l1n = {1: 'w1c', 4: 'a1c', 5: 'g1s'}[n]
                            for cc in range(nch):
                                bk = self.bank()
                                for k in range(KC):
                                    self.mm(PS[bk][:, 0:CH], l1[:, k, :], lerp[:, k, cc * CH:(cc + 1) * CH], k == 0, k == KC - 1, lr + [l1n], [('ps', bk)])
                                fn = {1: AF.Tanh, 4: AF.Identity, 5: AF.Sigmoid}[n]
                                self.act(loT[:, cc * CH:(cc + 1) * CH], PS[bk][:, 0:CH], fn, [('ps', bk)], ['loT'])
                            if n == 5:
                                for t in range(ntile):
                                    bks = [self.bank(), self.bank()]
                                    for nh in range(2):
                                        self.mm(PS[bks[nh]][:, :], loT[:, t * 128:(t + 1) * 128], g2s[:, nh * 512:(nh + 1) * 512], True, True, ['loT', 'g2s'], [('ps', bks[nh])])
                                    stage_out(self.RG, roff + t * 128, [
                                        lambda o, r_, bk=bks[0]: self.cp('act', o, PS[bk][:, :], [('ps', bk)], [r_]),
                                        lambda o, r_, bk=bks[1]: self.cp('dve', o, PS[bk][:, :], [('ps', bk)], [r_])])
                            else:
                                l2, l2n = (w2s, 'w2s') if n == 1 else (a2s, 'a2s')
                                for d in range(2):
                                    self.dma('sp', bc0[:, d, :], (self.rw_w0 if n == 1 else self.rw_a0)[d:d + 1, :].to_broadcast([128, D]), (), ['bc0'])
                                for d in range(2):
                                    pr = slice(d * 64, (d + 1) * 64)
                                    dst = (self.RLW if n == 1 else self.RAI)[d]
                                    bci = d
                                    for t in range(ntile):
                                        bks = [self.bank(), self.bank()]
                                        for nh in range(2):
                                            self.mm(PS[bks[nh]][:, :], loT[pr, t * 128:(t + 1) * 128], l2[pr, nh * 512:(nh + 1) * 512], True, True, ['loT', l2n], [('ps', bks[nh])])

                                        def ev(o, r_, bk, nh, n=n, bci=bci):
                                            self.tt('dve', o, PS[bk][:, :], bc0[:, bci, nh * 512:(nh + 1) * 512], ALU.add, [('ps', bk), 'bc0'], [r_])
                                            self.act(o, o, AF.Sigmoid, [r_], [r_])
                                            if n == 1:
                                                self.ts('pool', o, o, -C64, None, ALU.mult, None, [r_], [r_])
                                        stage_out(dst, roff + t * 128, [
                                            lambda o, r_, bk=bks[0]: ev(o, r_, bk, 0),
                                            lambda o, r_, bk=bks[1]: ev(o, r_, bk, 1)])
                em.flush()
            with ExitStack() as s:
                E = s.enter_context
                bc1 = E(self.sbt("bc1", [128, 2, D], F32))
                self.dma('sp', bc1[:, 0, :], self.rw_k_k.to_broadcast([128, D]), (), ['bc1'])
                self.dma('sp', bc1[:, 1, :], self.rw_k_a.to_broadcast([128, D]), (), ['bc1'])
                kt = [E(self.sbt("ekt%d" % k, [128, D], F32)) for k in range(2)]
                at = [[E(self.sbt("eat%d_%d" % (d, k), [128, D], F32)) for k in range(2)] for d in range(2)]
                kk = [E(self.sbt("ekk%d" % k, [128, D], F32)) for k in range(2)]
                sq = [E(self.sbt("esq%d" % k, [128, D], F32)) for k in range(2)]
                o1 = [[E(self.sbt("eo1%d_%d" % (d, k), [128, D], F32)) for k in range(2)] for d in range(2)]
                o2 = [[E(self.sbt("eo2%d_%d" % (d, k), [128, D], F32)) for k in range(2)] for d in range(2)]
                ss = [E(self.sbt("ess%d" % k, [128, 16], F32)) for k in range(2)]
                for t in range(NT // 128):
                    p = t % 2
                    row = t * 128
                    RR_ = lambda nm: (nm, p)
                    self.dma('sp', kt[p][:], self.RK[row:row + 128, :], [('RK', t)], [RR_('ekt')])
                    for d in range(2):
                        self.dma('act', at[d][p][:], self.RAI[d][row:row + 128, :], [(self.RAI[d].tensor.name, t)], [RR_('eat%d' % d)])
                    self.tt('dve', kk[p][:], kt[p][:], bc1[:, 0, :], ALU.mult, [RR_('ekt'), 'bc1'], [RR_('ekk')])
                    self.act(sq[p][:], kk[p][:], AF.Square, [RR_('ekk')], [RR_('esq')])
                    self.em.op('dve', lambda e, p=p: e.tensor_reduce(out=ss[p][:], in_=R3(sq[p][:]), axis=AX.X, op=ALU.add), [RR_('esq')], [RR_('ess')])
                    self.ts('dve', ss[p][:], ss[p][:], 1e-24, None, ALU.max, None, [RR_('ess')], [RR_('ess')])
                    self.act(ss[p][:], ss[p][:], AF.Ln, [RR_('ess')], [RR_('ess')])
                    self.act(ss[p][:], ss[p][:], AF.Exp, [RR_('ess')], [RR_('ess')], scale=-0.5)
                    self.tt('dve', R3(kk[p][:]), R3(kk[p][:]), bc64(ss[p][:]), ALU.mult, [RR_('ekk'), RR_('ess')], [RR_('ekk')])
                    self.dma('sp', self.KKN[row:row + 128, :], kk[p][:], [RR_('ekk')], [('KKN', t)])
                    for d in range(2):
                        a_ = at[d][p]
                        self.stt('dve', o1[d][p][:], a_[:], -1.0, bc1[:, 1, :], ALU.add, ALU.mult, [RR_('eat%d' % d), 'bc1'], [RR_('eo1%d' % d)])
                        self.stt('dve', o1[d][p][:], o1[d][p][:], 1.0, kt[p][:], ALU.add, ALU.mult, [RR_('eo1%d' % d), RR_('ekt')], [RR_('eo1%d' % d)])
                        self.dma('sp', self.KD[d][row:row + 128, :], o1[d][p][:], [RR_('eo1%d' % d)], [(self.KD[d].tensor.name, t)])
                        self.tt('pool', o2[d][p][:], kk[p][:], a_[:], ALU.mult, [RR_('ekk'), RR_('eat%d' % d)], [RR_('eo2%d' % d)])
                        self.dma('sp', self.BV[d][row:row + 128, :], o2[d][p][:], [RR_('eo2%d' % d)], [(self.BV[d].tensor.name, t)])
                em.flush()
            with ExitStack() as s:
                E = s.enter_context
                msk = E(self.sbt("msk", [128, 4, 128], F32))
                bdm = E(self.sbt("bdm", [128, 128], F32))
                self.dma('sp', msk[:], self.masks.rearrange("m p t -> p m t"), (), ['msk'])
                self.dma('sp', bdm[:], self.bdmask, (), ['bdm'])
                MK4 = E(self.sbt("MK4", [128, 4, 128], F32))
                ST2 = E(self.sbt("ST2", [128, 8, 128], F32))
                ld = {nm: E(self.sbt("ld_" + nm, [128, D], F32)) for nm in ('lw', 'kd', 'bv', 'kkn', 'v', 'r')}
                cum = E(self.sbt("cum", [128, D], F32))
                tA = [E(self.sbt("tA%d" % k, [128, D], F32)) for k in range(2)]
                kb = E(self.sbt("kbar", [128, D], F32))
                bb = E(self.sbt("bbar", [128, D], F32))
                QT2 = E(self.sbt("QT2", [128, 8, 2, 128], F32))
                khT = E(self.sbt("khT", [128, 8, 128], F32))
                bhT = E(self.sbt("bhT", [128, 8, 128], F32))
                AT3 = E(self.sbt("AT3", [128, 16, 3, 128], F32))
                Lb = [E(self.sbt("Lb%d" % k, [128, 16, 128], F32)) for k in range(2)]
                Ub = [E(self.sbt("Ub%d" % k, [128, 16, 128], F32)) for k in range(2)]
                XT = E(self.sbt("XT", [128, 16, 128], F32))
                rhs_sb = E(self.sbt("rhs_sb", [128, D], F32))
                sa_sb = E(self.sbt("sa_sb", [128, D], F32))
                y_sb = E(self.sbt("y_sb", [128, D], F32))
                gC = E(self.sbt("gC", [128, 8], F32))
                sttmp = E(self.sbt("sttmp", [128, 4, 128], F32))
                for d in range(2):
                    mS, mI, mL = (0, 1, 2) if d == 0 else (2, 3, 0)
                    for q_, mi_ in enumerate((mS, mI, mS, mI)):
                        self.cp('dve', MK4[:, q_, :], msk[:, mi_, :], ['msk'], ['MK4'])
                    self.ms('dve', ST2[:], 0.0, ['ST2'])
                    order = [16, 17] + list(range(16)) if d == 0 else [17, 16] + list(range(15, -1, -1))
                    for ci in order:
                        row = ci * 128
                        emit = ci < 16
                        names = ['lw', 'kd', 'bv', 'kkn', 'v'] + (['r'] if emit else [])
                        srcs = {'lw': self.RLW[d], 'kd': self.KD[d], 'bv': self.BV[d], 'kkn': self.KKN, 'v': self.RV, 'r': self.RR}
                        for qi, nm in enumerate(names):
                            self.dma('sp' if qi % 2 == 0 else 'act', ld[nm][:], srcs[nm][row:row + 128, :], [(srcs[nm].tensor.name, ci)], [('ld', nm)])
                        lw = ld['lw']
                        bks = [self.bank(), self.bank()]
                        for nh in range(2):
                            self.mm(PS[bks[nh]][:, :], msk[:, mI, :], lw[:, nh * 512:(nh + 1) * 512], True, True, ['msk', ('ld', 'lw')], [('ps', bks[nh])])
                            self.cp('act' if nh == 0 else 'dve', cum[:, nh * 512:(nh + 1) * 512], PS[bks[nh]][:, :], [('ps', bks[nh])], ['cum'])
                        bt = [self.bank(), self.bank()]
                        for nh in range(2):
                            self.mm(PS[bt[nh]][:, :], self.ones[:], lw[:, nh * 512:(nh + 1) * 512], True, True, ['ones', ('ld', 'lw')], [('ps', bt[nh])])
                        bg = self.bank()
                        for hp in range(8):
                            self.mm(PS[bg][:, hp:hp + 1], lw[:, hp * 128:(hp + 1) * 128], self.ones[:, 0:1], True, True, ['ones', ('ld', 'lw')], [('ps', bg)])
                        self.act(gC[:], PS[bg][:, 0:8], AF.Exp, [('ps', bg)], ['gC'])
                        for nh in range(2):
                            sl = slice(nh * 512, (nh + 1) * 512)
                            self.tt('dve', tA[0][:, sl], PS[bt[nh]][:, :], cum[:, sl], ALU.subtract, [('ps', bt[nh]), 'cum'], [('tA', 0)])
                        self.act(tA[0][:], tA[0][:], AF.Exp, [('tA', 0)], [('tA', 0)])
                        self.tt('dve', kb[:], ld['kd'][:], tA[0][:], ALU.mult, [('ld', 'kd'), ('tA', 0)], ['kbar'])
                        self.tt('pool', bb[:], ld['bv'][:], tA[0][:], ALU.mult, [('ld', 'bv'), ('tA', 0)], ['bbar'])

                        def transp(src, srcr, dst_fn, dstr):
                            for g4 in range(2):
                                bk = self.bank()
                                for q4 in range(4):
                                    hp = g4 * 4 + q4
                                    self.tr(PS[bk][:, q4 * 128:(q4 + 1) * 128], src[:, hp * 128:(hp + 1) * 128], [srcr], [('ps', bk)])
                                dst_fn(g4, PS[bk][:, :].rearrange("p (q t) -> p q t", t=128), bk, dstr)
                        self.tt('dve', tA[0][:], cum[:], lw[:], ALU.subtract, ['cum', ('ld', 'lw')], [('tA', 0)])
                        self.act(tA[0][:], tA[0][:], AF.Exp, [('tA', 0)], [('tA', 0)])
                        self.stt('dve', tA[0][:], ld['kkn'][:], -1.0, tA[0][:], ALU.mult, ALU.mult, [('ld', 'kkn'), ('tA', 0)], [('tA', 0)])
                        transp(tA[0], ('tA', 0), lambda g4, pv, bk, dr: self.cp('act', QT2[:, g4 * 4:(g4 + 1) * 4, 0, :], pv, [('ps', bk)], [dr]), 'QT2')
                        if emit:
                            self.act(tA[1][:], cum[:], AF.Exp, ['cum'], [('tA', 1)])
                            self.tt('dve', tA[1][:], tA[1][:], ld['r'][:], ALU.mult, [('tA', 1), ('ld', 'r')], [('tA', 1)])
                            transp(tA[1], ('tA', 1), lambda g4, pv, bk, dr: self.cp('dve', QT2[:, g4 * 4:(g4 + 1) * 4, 1, :], pv, [('ps', bk)], [dr]), 'QT2')
                        self.act(tA[0][:], cum[:], AF.Exp, ['cum'], [('tA', 0)], scale=-1.0)
                        self.tt('dve', tA[1][:], tA[0][:], ld['kd'][:], ALU.mult, [('tA', 0), ('ld', 'kd')], [('tA', 1)])
                        transp(tA[1], ('tA', 1), lambda g4, pv, bk, dr: self.cp('act', khT[:, g4 * 4:(g4 + 1) * 4, :], pv, [('ps', bk)], [dr]), 'khT')
                        self.tt('pool', tA[0][:], tA[0][:], ld['bv'][:], ALU.mult, [('tA', 0), ('ld', 'bv')], [('tA', 0)])
                        transp(tA[0], ('tA', 0), lambda g4, pv, bk, dr: self.cp('dve', bhT[:, g4 * 4:(g4 + 1) * 4, :], pv, [('ps', bk)], [dr]), 'bhT')
                        NQ = 256 if emit else 128
                        for h in range(16):
                            hp, hh = h // 2, h % 2
                            pr = slice(hh * 64, (hh + 1) * 64)
                            bk = self.bank()
                            q2 = QT2[pr, hp, :, :].rearrange("p a t -> p (a t)")
                            self.mm(PS[bk][:, 0:NQ], khT[pr, hp, :], q2[:, 0:NQ], True, True, ['khT', 'QT2'], [('ps', bk)])
                            self.mm(PS[bk][:, 256:256 + NQ], bhT[pr, hp, :], q2[:, 0:NQ], True, True, ['bhT', 'QT2'], [('ps', bk)])
                            pv = PS[bk][:, :].rearrange("p (q t) -> p q t", t=128)
                            eng = 'dve'
                            if emit:
                                self.tt(eng, AT3[:, h, 0:2, :], pv[:, 0:2, :], MK4[:, 0:2, :], ALU.mult, [('ps', bk), 'MK4'], [('AT3', h)])
                                self.tt(eng, Ub[0][:, h, :], pv[:, 2, :], MK4[:, 2, :], ALU.mult, [('ps', bk), 'MK4'], [('Ub', 0, h // 4)])
                                self.tt(eng, AT3[:, h, 2, :], pv[:, 3, :], MK4[:, 3, :], ALU.mult, [('ps', bk), 'MK4'], [('AT3', h)])
                            else:
                                self.tt(eng, AT3[:, h, 0, :], pv[:, 0, :], MK4[:, 0, :], ALU.mult, [('ps', bk), 'MK4'], [('AT3', h)])
                                self.tt(eng, Ub[0][:, h, :], pv[:, 2, :], MK4[:, 2, :], ALU.mult, [('ps', bk), 'MK4'], [('Ub', 0, h // 4)])
                        for g4 in range(4):
                            bk = self.bank()
                            for q4 in range(4):
                                h = g4 * 4 + q4
                                hp, hh = h // 2, h % 2
                                pr = slice(hh * 64, (hh + 1) * 64)
                                self.mm(PS[bk][:, q4 * 128:(q4 + 1) * 128], QT2[pr, hp, 0, :], bhT[pr, hp, :], True, True, ['QT2', 'bhT'], [('ps', bk)])
                            pv = PS[bk][:, :].rearrange("p (q t) -> p q t", t=128)
                            self.tt('dve', Lb[0][:, g4 * 4:(g4 + 1) * 4, :], pv, msk[:, mL, :].unsqueeze(1).to_broadcast([128, 4, 128]), ALU.mult,
                                    [('ps', bk), 'msk'], [('Lb', 0, g4)])
                            self.tt('pool', XT[:, g4 * 4:(g4 + 1) * 4, :], Ub[0][:, g4 * 4:(g4 + 1) * 4, :],
                                    self.ident[:, :].unsqueeze(1).to_broadcast([128, 4, 128]), ALU.add, [('Ub', 0, g4), 'ident'], [('XT', g4)])
                        cur = 0
                        for lev in range(6):
                            nxt = 1 - cur
                            last = lev == 5
                            for g4 in range(4):
                                bL = self.bank()
                                for q4 in range(4):
                                    h = g4 * 4 + q4
                                    self.mm(PS[bL][:, q4 * 128:(q4 + 1) * 128], Ub[cur][:, h, :], Lb[cur][:, h, :], True, True,
                                            [('Ub', cur, g4), ('Lb', cur, g4)], [('ps', bL)])
                                self.cp('act', Lb[nxt][:, g4 * 4:(g4 + 1) * 4, :], PS[bL][:, :].rearrange("p (q t) -> p q t", t=128), [('ps', bL)], [('Lb', nxt, g4)])
                                if not last:
                                    bU = self.bank()
                                    for q4 in range(4):
                                        h = g4 * 4 + q4
                                        self.mm(PS[bU][:, q4 * 128:(q4 + 1) * 128], Lb[cur][:, h, :], Ub[cur][:, h, :], True, True,
                                                [('Ub', cur, g4), ('Lb', cur, g4)], [('ps', bU)])
                                    self.cp('dve', Ub[nxt][:, g4 * 4:(g4 + 1) * 4, :], PS[bU][:, :].rearrange("p (q t) -> p q t", t=128), [('ps', bU)], [('Ub', nxt, g4)])
                                bX = self.bank()
                                for q4 in range(4):
                                    h = g4 * 4 + q4
                                    self.mm(PS[bX][:, q4 * 128:(q4 + 1) * 128], Lb[nxt][:, h, :], XT[:, h, :], True, True,
                                            [('Lb', nxt, g4), ('XT', g4)], [('ps', bX)])
                                self.tt('dve', XT[:, g4 * 4:(g4 + 1) * 4, :], XT[:, g4 * 4:(g4 + 1) * 4, :], PS[bX][:, :].rearrange("p (q t) -> p q t", t=128), ALU.add,
                                        [('ps', bX), ('XT', g4)], [('XT', g4)])
                            cur = nxt
                        v_ = ld['v']
                        br = [self.bank(), self.bank()]
                        for hp in range(8):
                            bk = br[hp // 4]
                            c0 = (hp % 4) * 128
                            self.mm(PS[bk][:, c0:c0 + 128], QT2[:, hp, 0, :], ST2[:, hp, :], True, False, ['QT2', 'ST2'], [('ps', bk)])
                            for hh in range(2):
                                h = hp * 2 + hh
                                self.mm(PS[bk][:, c0 + hh * 64:c0 + (hh + 1) * 64], AT3[:, h, 0, :], v_[:, h * 64:(h + 1) * 64], False, hh == 1,
                                        [('AT3', h), ('ld', 'v')], [('ps', bk)])
                        self.cp('act', rhs_sb[:, 0:512], PS[br[0]][:, :], [('ps', br[0])], ['rhs_sb'])
                        self.cp('dve', rhs_sb[:, 512:1024], PS[br[1]][:, :], [('ps', br[1])], ['rhs_sb'])
                        bs_ = [self.bank(), self.bank()]
                        for h in range(16):
                            bk = bs_[h // 8]
                            c0 = (h % 8) * 64
                            self.mm(PS[bk][:, c0:c0 + 64], XT[:, h, :], rhs_sb[:, h * 64:(h + 1) * 64], True, True, [('XT', h // 4), 'rhs_sb'], [('ps', bk)])
                        self.cp('act', sa_sb[:, 0:512], PS[bs_[0]][:, :], [('ps', bs_[0])], ['sa_sb'])
                        self.cp('dve', sa_sb[:, 512:1024], PS[bs_[1]][:, :], [('ps', bs_[1])], ['sa_sb'])
                        if emit:
                            by = [self.bank(), self.bank()]
                            for hp in range(8):
                                bk = by[hp // 4]
                                c0 = (hp % 4) * 128
                                self.mm(PS[bk][:, c0:c0 + 128], QT2[:, hp, 1, :], ST2[:, hp, :], True, False, ['QT2', 'ST2'], [('ps', bk)])
                                for hh in range(2):
                                    h = hp * 2 + hh
                                    self.mm(PS[bk][:, c0 + hh * 64:c0 + (hh + 1) * 64], AT3[:, h, 1, :], v_[:, h * 64:(h + 1) * 64], False, False,
                                            [('AT3', h), ('ld', 'v')], [('ps', bk)])
                                    self.mm(PS[bk][:, c0 + hh * 64:c0 + (hh + 1) * 64], AT3[:, h, 2, :], sa_sb[:, h * 64:(h + 1) * 64], False, hh == 1,
                                            [('AT3', h), 'sa_sb'], [('ps', bk)])
                            self.cp('act', y_sb[:, 0:512], PS[by[0]][:, :], [('ps', by[0])], ['y_sb'])
                            self.cp('dve', y_sb[:, 512:1024], PS[by[1]][:, :], [('ps', by[1])], ['y_sb'])
                            self.dma('sp', self.YS[d][row:row + 128, :], y_sb[:], ['y_sb'], [(self.YS[d].tensor.name, ci)])
                        bst = [self.bank(), self.bank()]
                        for hp in range(8):
                            bk = bst[hp // 4]
                            c0 = (hp % 4) * 128
                            self.mm(PS[bk][:, c0:c0 + 128], kb[:, hp * 128:(hp + 1) * 128], v_[:, hp * 128:(hp + 1) * 128], True, False,
                                    ['kbar', ('ld', 'v')], [('ps', bk)])
                            self.mm(PS[bk][:, c0:c0 + 128], bb[:, hp * 128:(hp + 1) * 128], sa_sb[:, hp * 128:(hp + 1) * 128], False, True,
                                    ['bbar', 'sa_sb'], [('ps', bk)])
                        for g4 in range(2):
                            self.tt('dve', sttmp[:], PS[bst[g4]][:, :].rearrange("p (q t) -> p q t", t=128),
                                    bdm[:, :].unsqueeze(1).to_broadcast([128, 4, 128]), ALU.mult, [('ps', bst[g4]), 'bdm'], ['sttmp'])
                            self.tt('pool', ST2[:, g4 * 4:(g4 + 1) * 4, :], ST2[:, g4 * 4:(g4 + 1) * 4, :],
                                    gC[:, g4 * 4:(g4 + 1) * 4].unsqueeze(2).to_broadcast([128, 4, 128]), ALU.mult, ['ST2', 'gC'], ['ST2'])
                            self.tt('dve', ST2[:, g4 * 4:(g4 + 1) * 4, :], ST2[:, g4 * 4:(g4 + 1) * 4, :], sttmp[:], ALU.add, ['ST2', 'sttmp'], ['ST2'])
                em.flush()
            with ExitStack() as s:
                E = s.enter_context
                w_o = E(self.sbt("rw_wo", [128, KC, D], BF16))
                bc2 = E(self.sbt("bc2", [128, 3, D], F32))
                self.dma('sp', bc2[:, 0, :], self.rw_r_k.to_broadcast([128, D]), (), ['bc2'])
                self.dma('sp', bc2[:, 1, :], self.rw_gn_g.to_broadcast([128, D]), (), ['bc2'])
                self.dma('sp', bc2[:, 2, :], self.rw_gn_b.to_broadcast([128, D]), (), ['bc2'])
                src = self.rw_w_o.rearrange("(k p) n -> p k n", p=128)
                for hh in range(2):
                    self.dma('pool', w_o[:, :, hh * 512:(hh + 1) * 512], src[:, :, hh * 512:(hh + 1) * 512], (), [('w_o', hh)])
                tl = self.alloc_pn_tiles(E, True)
                L_ = {nm: E(self.sbt("d_" + nm, [128, D], F32)) for nm in ('y0', 'y1', 'r', 'kd0', 'kd1', 'v', 'g')}
                acc = E(self.sbt("d_acc", [128, D], F32))
                tq = E(self.sbt("d_tq", [128, D], F32))
                st16 = E(self.sbt("d_st", [128, 4, 16], F32))
                zT = E(self.sbt("d_zT", [128, KC, 128], BF16))
                for t in range(T // 128):
                    row = t * 128
                    srcs = {'y0': self.YS[0], 'y1': self.YS[1], 'r': self.RR, 'kd0': self.KD[0], 'kd1': self.KD[1], 'v': self.RV, 'g': self.RG}
                    for qi, nm in enumerate(L_):
                        self.dma('sp' if qi % 2 == 0 else 'act', L_[nm][:], srcs[nm][row:row + 128, :], [(srcs[nm].tensor.name, t)], [('dl', nm)])
                    for d in range(2):
                        y = L_['y%d' % d]
                        yr = ('dl', 'y%d' % d)
                        self.em.op('dve', lambda e, y=y: e.tensor_reduce(out=st16[:, 0, :], in_=R3(y[:]), axis=AX.X, op=ALU.add), [yr], ['st16'])
                        self.act(tq[:], y[:], AF.Square, [yr], ['tq'])
                        self.em.op('dve', lambda e: e.tensor_reduce(out=st16[:, 1, :], in_=R3(tq[:]), axis=AX.X, op=ALU.add), ['tq'], ['st16'])
                        self.ts('dve', st16[:, 0, :], st16[:, 0, :], 1.0 / 64, None, ALU.mult, None, ['st16'], ['st16'])
                        self.tt('dve', st16[:, 2, :], st16[:, 0, :], st16[:, 0, :], ALU.mult, ['st16'], ['st16'])
                        self.stt('dve', st16[:, 1, :], st16[:, 1, :], 1.0 / 64, st16[:, 2, :], ALU.mult, ALU.subtract, ['st16'], ['st16'])
                        self.act(st16[:, 1, :], st16[:, 1, :], AF.Ln, ['st16', 'epsc'], ['st16'], bias=self.epsc[:, 1:2])
                        self.act(st16[:, 1, :], st16[:, 1, :], AF.Exp, ['st16'], ['st16'], scale=-0.5)
                        self.tt('dve', R3(y[:]), R3(y[:]), bc64(st16[:, 0, :]), ALU.subtract, [yr, 'st16'], [yr])
                        self.tt('dve', R3(y[:]), R3(y[:]), bc64(st16[:, 1, :]), ALU.mult, [yr, 'st16'], [yr])
                        self.tt('pool', y[:], y[:], bc2[:, 1, :], ALU.mult, [yr, 'bc2'], [yr])
                        if d == 0:
                            self.tt('pool', acc[:], y[:], bc2[:, 2, :], ALU.add, [yr, 'bc2'], ['acc'])
                        else:
                            self.tt('pool', y[:], y[:], bc2[:, 2, :], ALU.add, [yr, 'bc2'], [yr])
                            self.tt('pool', acc[:], acc[:], y[:], ALU.add, [yr, 'acc'], ['acc'])
                        kd_ = L_['kd%d' % d]
                        kr_ = ('dl', 'kd%d' % d)
                        self.tt('dve', tq[:], L_['r'][:], kd_[:], ALU.mult, [('dl', 'r'), kr_], ['tq'])
                        self.tt('dve', tq[:], tq[:], bc2[:, 0, :], ALU.mult, ['tq', 'bc2'], ['tq'])
                        self.em.op('dve', lambda e: e.tensor_reduce(out=st16[:, 3, :], in_=R3(tq[:]), axis=AX.X, op=ALU.add), ['tq'], ['st16'])
                        self.tt('dve', R3(tq[:]), R3(L_['v'][:]), bc64(st16[:, 3, :]), ALU.mult, [('dl', 'v'), 'st16'], ['tq'])
                        self.tt('dve', acc[:], acc[:], tq[:], ALU.add, ['acc', 'tq'], ['acc'])
                    self.tt('dve', acc[:], acc[:], L_['g'][:], ALU.mult, ['acc', ('dl', 'g')], ['acc'])
                    for half in range(2):
                        bk = self.bank()
                        for kk_ in range(4):
                            k = half * 4 + kk_
                            self.tr(PS[bk][:, kk_ * 128:(kk_ + 1) * 128], acc[:, k * 128:(k + 1) * 128], ['acc'], [('ps', bk)])
                        self.cp('act' if half == 0 else 'dve', zT[:, half * 4:(half + 1) * 4, :], PS[bk][:, :].rearrange("p (q t) -> p q t", t=128),
                                [('ps', bk)], ['zT'])
                    bs = []
                    for nh in range(2):
                        bk = self.bank()
                        bs.append(bk)
                        for k in range(KC):
                            self.mm(PS[bk][:, :], zT[:, k, :], w_o[:, k, nh * 512:(nh + 1) * 512], k == 0, k == KC - 1, ['zT', ('w_o', nh)], [('ps', bk)])
                    self.postnorm_tile(tl, b * T + row, b, 0, hsrc, [PS[bs[0]][:, :], PS[bs[1]][:, :]], [[('ps', bs[0])], [('ps', bs[1])]],
                                       self.H, 'H', bias_bc=None, router=True)
                em.flush()


def tr128(v, nchunk):
    return np.ascontiguousarray(v.reshape(nchunk, 128).T)


def make_bm_tab(rpb):
    NEG = np.float32(-30000.0)
    out = np.full((16, 5, 2, 64, 10, 64), NEG, np.float32)
    rep = {0: 0, 1: 1, 2: 2, 3: 14, 4: 15}
    c = np.arange(64)
    kc = np.arange(64)
    win_c0 = np.clip(c - 8, 0, 48)
    col_ok = (kc[None, :] >= win_c0[:, None]) & (kc[None, :] < win_c0[:, None] + 16)
    cidx = np.clip(kc[None, :] - c[:, None] + 15, 0, 30)
    for cls, i_ in rep.items():
        ks = min(max(2 * i_ - 4, 0), 22)
        for rr in range(2):
            r = 2 * i_ + rr
            row0 = min(max(r - 4, 0), 24)
            for krel in range(10):
                kr = ks + krel
                if not (row0 <= kr < row0 + 8):
                    continue
                ridx = kr - r + 7
                vals = rpb[:, ridx, :][:, cidx]
                out[:, cls, rr, :, krel, :] = np.where(col_ok[None], vals, NEG)
    return np.ascontiguousarray(out.reshape(16, 5, 128, 640))


def prep_shared(inp):
    f = lambda a: np.ascontiguousarray(np.asarray(a, dtype=np.float32))
    sh = {}
    sh['ident'] = np.eye(128, dtype=np.float32)
    sh['ada_w'] = f(inp['ada_w'])
    sh['ada_b'] = f(inp['ada_b'])
    sh['ada_bT'] = np.stack([tr128(f(inp['ada_b'])[i], 48) for i in range(DEPTH)])
    sh['ln_g'] = f(inp['ln_g'])
    sh['ln_b'] = f(inp['ln_b'])
    sh['conv_w_in'] = f(inp['conv_w_in'])
    sh['conv_b_inT'] = np.stack([tr128(f(inp['conv_b_in'])[i], 16) for i in range(2)])
    wdw = f(inp['conv_w_dw'])
    sh['conv_w_dwT'] = np.ascontiguousarray(wdw.reshape(2, CONVW, KC, 128).transpose(0, 3, 2, 1))
    sh['conv_b_dwT'] = np.stack([tr128(f(inp['conv_b_dw'])[i], KC) for i in range(2)])
    sh['conv_ln_gT'] = np.stack([tr128(f(inp['conv_ln_g'])[i], KC) for i in range(2)])
    sh['conv_ln_bT'] = np.stack([tr128(f(inp['conv_ln_b'])[i], KC) for i in range(2)])
    sh['conv_w_out'] = f(inp['conv_w_out'])
    sh['conv_b_out'] = f(inp['conv_b_out'])
    sh['na_w_qkv'] = f(inp['na_w_qkv'])
    sh['na_w_o'] = f(inp['na_w_o'])
    sh['bm_tab'] = make_bm_tab(f(inp['na_rpb'])[0])
    mp = f(inp['rw_mu_prev'])[0]
    mn = f(inp['rw_mu_next'])[0]
    sh['rw_mu_prevT'] = np.ascontiguousarray(mp.reshape(6, KC, 128).transpose(2, 0, 1))
    sh['rw_mu_nextT'] = np.ascontiguousarray(mn.reshape(6, KC, 128).transpose(2, 0, 1))
    for nm in ("rw_w_r", "rw_w_k", "rw_w_v", "rw_w_o", "rw_w0", "rw_a0", "rw_w1", "rw_a1", "rw_w2", "rw_a2", "rw_g1", "rw_g2"):
        sh[nm] = np.ascontiguousarray(f(inp[nm])[0])
    for nm in ("rw_k_k", "rw_k_a", "rw_r_k", "rw_gn_g", "rw_gn_b"):
        sh[nm] = np.ascontiguousarray(f(inp[nm])[0].reshape(1, D))
    tri = np.triu(np.ones((128, 128), np.float32))
    sh['masks'] = np.stack([np.triu(tri, 1), tri, np.tril(tri.T, -1), tri.T]).astype(np.float32)
    bd = np.zeros((128, 128), np.float32)
    bd[:64, :64] = 1
    bd[64:, 64:] = 1
    sh['bdmask'] = bd
    sh['moe_router'] = f(inp['moe_router'])
    sh['moe_w_gate'] = f(inp['moe_w_gate'])
    sh['moe_w_up'] = f(inp['moe_w_up'])
    sh['moe_w_down'] = f(inp['moe_w_down'])
    return sh


def prep_core(inp, core):
    f = lambda a: np.asarray(a, dtype=np.float32)
    b0 = core * NB
    x = f(inp['x'])[b0:b0 + NB].reshape(NB * T, D)
    cx = f(inp['ctx'])[b0:b0 + NB].reshape(NB * L, D)
    hin = np.ascontiguousarray(np.concatenate([x, cx], axis=0))
    conds = np.stack([f(inp['c'])[b0], f(inp['c'])[b0 + 1], f(inp['c_ctx'])], axis=1)
    cT = np.ascontiguousarray(conds.reshape(KC, 128, 3).transpose(1, 0, 2))
    return {'hin': hin, 'cT': cT}


_NC_CACHE = {}


def kernel(**inputs):
    kb = KB()
    nc = kb.build()
    sh = prep_shared(inputs)
    in_maps = []
    for c in range(NCORES):
        m = dict(sh)
        m.update(prep_core(inputs, c))
        in_maps.append(m)
    res = run_bass_kernel_spmd(nc, in_maps, core_ids=list(range(NCORES)))
    outs = [r["out"].reshape(NB, T, D) for r in res.results]
    return np.concatenate(outs, axis=0).astype(np.float32)
```

```python
import numpy as np
import concourse.bass as bass
import concourse.mybir as mybir
from concourse.bass_utils import run_bass_kernel_spmd
from contextlib import ExitStack

F32 = mybir.dt.float32
BF16 = mybir.dt.bfloat16
U32 = mybir.dt.uint32
I32 = mybir.dt.int32
ALU = mybir.AluOpType
AF = mybir.ActivationFunctionType
AX = mybir.AxisListType

ENGS = ('pe', 'act', 'dve', 'pool', 'sp')
DMAQ = ('sp', 'act', 'pool')


class Em:
    def __init__(self, nc, ndma=8):
        self.nc = nc
        self.K = ndma
        self.stack = ExitStack()
        ent = self.stack.enter_context
        self.csem = {e: ent(nc.semaphore("c_" + e)) for e in ENGS}
        self.dsem = {q: [ent(nc.semaphore("d_%s%d" % (q, i))) for i in range(ndma)] for q in DMAQ}
        self.ccnt = {e: 0 for e in ENGS}
        self.dcnt = {q: 0 for q in DMAQ}
        self.seen = {e: {} for e in ENGS}
        self.lastw = {}
        self.rd = {}
        self.prog = {e: [] for e in ENGS}
        self.ninstr = 0
        self.sigcnt = {}
        self.pe_sync_toks = set()

    def sem(self, key):
        if key[0] == 'c':
            return self.csem[key[1]]
        return self.dsem[key[1]][key[2]]

    def _deps(self, eng, reads, writes, pe_sync=False):
        need = {}
        pe_keep = pe_sync
        for r in reads:
            t = self.lastw.get(r)
            if t is not None and need.get(t[0], 0) < t[1]:
                need[t[0]] = t[1]
        for w in writes:
            t = self.lastw.get(w)
            if t is not None:
                if t in self.pe_sync_toks:
                    pe_keep = True
                if need.get(t[0], 0) < t[1]:
                    need[t[0]] = t[1]
            for k, v in self.rd.get(w, {}).items():
                if need.get(k, 0) < v:
                    need[k] = v
        waits = []
        seen = self.seen[eng]
        for k, v in need.items():
            if eng == 'pe' and k == ('c', 'pe') and not pe_keep:
                continue
            if seen.get(k, 0) < v:
                seen[k] = v
                waits.append((k, v))
        return waits

    def _reg(self, tok, reads, writes):
        for r in reads:
            self.rd.setdefault(r, {})[tok[0]] = tok[1]
        for w in writes:
            self.lastw[w] = tok
            self.rd[w] = {}

    def op(self, eng, fn, reads=(), writes=(), pe_sync=False):
        waits = self._deps(eng, reads, writes, pe_sync)
        self.ccnt[eng] += 1
        tok = (('c', eng), self.ccnt[eng])
        if pe_sync:
            self.pe_sync_toks.add(tok)
        self.prog[eng].append((waits, fn, tok, 1))
        self._reg(tok, reads, writes)
        self.ninstr += 1

    def dma(self, q, fn, reads=(), writes=()):
        waits = self._deps(q, reads, writes)
        n = self.dcnt[q]
        self.dcnt[q] += 1
        slot = n % self.K
        val = 16 * (n // self.K + 1)
        key = ('d', q, slot)
        if val > 16 and self.seen[q].get(key, 0) < val - 16:
            self.seen[q][key] = val - 16
            waits.append((key, val - 16))
        tok = (key, val)
        self.prog[q].append((waits, fn, tok, 16))
        self._reg(tok, reads, writes)
        self.ninstr += 1

    def barrier(self):
        for e in ENGS:
            waits = []
            seen = self.seen[e]
            for e2 in ENGS:
                k = ('c', e2)
                v = self.ccnt[e2]
                if e2 != e and v > 0 and seen.get(k, 0) < v:
                    seen[k] = v
                    waits.append((k, v))
            for q in DMAQ:
                n = self.dcnt[q]
                for slot in range(self.K):
                    cnt = (n - slot + self.K - 1) // self.K if n > slot else 0
                    if cnt > 0:
                        k = ('d', q, slot)
                        v = 16 * cnt
                        if seen.get(k, 0) < v:
                            seen[k] = v
                            waits.append((k, v))
            if waits:
                self.prog[e].append((waits, None, None, 0))
        self.lastw = {}
        self.rd = {}

    def flush(self):
        self.barrier()
        nc = self.nc
        waited = {e: set() for e in ENGS}
        for e in ENGS:
            for waits, fn, tok, inc in self.prog[e]:
                for k, v in waits:
                    if k[0] == 'c':
                        waited[k[1]].add(v)
        newval = {e: {} for e in ENGS}
        for e in ENGS:
            sc = self.sigcnt.get(e, 0)
            for waits, fn, tok, inc in self.prog[e]:
                if fn is not None and tok[0][0] == 'c' and tok[1] in waited[e]:
                    sc += 1
                    newval[e][tok[1]] = sc
            self.sigcnt[e] = sc

        def tr_(k, v):
            if k[0] == 'c':
                return newval[k[1]][v]
            return v
        with nc.Block() as block:
            for e, meth in (('pe', block.tensor), ('act', block.scalar), ('dve', block.vector),
                            ('pool', block.gpsimd), ('sp', block.sync)):
                prog = self.prog[e]
                self.prog[e] = []

                def body(eng, prog=prog, e=e):
                    for waits, fn, tok, inc in prog:
                        if fn is None:
                            for k, v in waits:
                                eng.wait_ge(self.sem(k), tr_(k, v))
                            continue
                        for k, v in waits[:-1]:
                            eng.wait_ge(self.sem(k), tr_(k, v))
                        ins = fn(eng)
                        if waits:
                            k, v = waits[-1]
                            ins._wait_ge(self.sem(k), tr_(k, v))
                        if tok[0][0] == 'c':
                            if tok[1] in newval[e]:
                                ins.then_inc(self.sem(tok[0]), 1)
                        else:
                            ins.then_inc(self.sem(tok[0]), inc)
                meth(body)

    def close(self):
        self.stack.close()


D = 1024
T = 2048
L = 256
NB = 2
KC = 8
NE = 16
CAP = 256
CAPC = 32
DE = 2048
DEPTH = 4
NROW = NB * T + NB * L
CTX0 = NB * T
ALPHA = float(8 ** 0.25)
LN_EPS = 1e-5
GRID_W = 64
CONVW = 31
NCORES = 8


class KB:
    def __init__(self, layers=(0, 1, 2, 3), n_exp=NE, debug=False):
        self.layers = list(layers)
        self.n_exp = n_exp
        self.debug = debug
        nc = bass.Bass("TRN2", target_bir_lowering=False)
        self.nc = nc
        self.em = Em(nc)
        self.dr = {}
        self._psi = 0

    def din(self, name, shape, dt=F32):
        t = self.nc.dram_tensor(name, list(shape), dt, kind="ExternalInput").ap()
        self.dr[name] = t
        return t

    def dout(self, name, shape, dt=F32):
        t = self.nc.dram_tensor(name, list(shape), dt, kind="ExternalOutput").ap()
        self.dr[name] = t
        return t

    def dint(self, name, shape, dt=F32):
        t = self.nc.dram_tensor(name, list(shape), dt, kind="ExternalOutput" if self.debug else "Internal").ap()
        self.dr[name] = t
        return t

    def sbt(self, name, shape, dt):
        self._uid = getattr(self, '_uid', 0) + 1
        return self.nc.sbuf_tensor("%s_%d" % (name, self._uid), shape, dt)

    def bank(self):
        i = self._psi
        self._psi = (i + 1) % 8
        return i

    def mm(self, out, lhsT, rhs, start, stop, r, w):
        self.em.op('pe', lambda e: e.matmul(out, lhsT=lhsT, rhs=rhs, start=start, stop=stop), r, w, pe_sync=(lhsT.dtype == F32))

    def tr(self, out, in_, r, w, ident=None):
        idn = self.ident if ident is None else ident
        np_ = in_.shape[0]
        self.em.op('pe', lambda e: e.transpose(out=out, in_=in_, identity=idn[0:np_, 0:np_]), list(r) + ['ident'], w)

    def act(self, out, in_, func, r, w, bias=None, scale=None, accum_out=None):
        kw = {}
        if bias is not None:
            kw['bias'] = bias
        if scale is not None:
            kw['scale'] = scale
        if accum_out is not None:
            kw['accum_out'] = accum_out
        self.em.op('act', lambda e: e.activation(out=out, in_=in_, func=func, **kw), r, w)

    def tt(self, eng, out, in0, in1, op, r, w):
        self.em.op(eng, lambda e: e.tensor_tensor(out=out, in0=in0, in1=in1, op=op), r, w)

    def ts(self, eng, out, in0, s1, s2, op0, op1, r, w):
        if op1 is None:
            self.em.op(eng, lambda e: e.tensor_scalar(out=out, in0=in0, scalar1=s1, scalar2=None, op0=op0), r, w)
        else:
            self.em.op(eng, lambda e: e.tensor_scalar(out=out, in0=in0, scalar1=s1, scalar2=s2, op0=op0, op1=op1), r, w)

    def stt(self, eng, out, in0, scalar, in1, op0, op1, r, w):
        self.em.op(eng, lambda e: e.scalar_tensor_tensor(out=out, in0=in0, scalar=scalar, in1=in1, op0=op0, op1=op1), r, w)

    def cp(self, eng, out, in_, r, w):
        if eng == 'act':
            self.em.op('act', lambda e: e.copy(out=out, in_=in_), r, w)
        else:
            self.em.op(eng, lambda e: e.tensor_copy(out=out, in_=in_), r, w)

    def ms(self, eng, ap, val, w):
        self.em.op(eng, lambda e: e.memset(ap, val), (), w)

    def dma(self, q, out, in_, r, w, **kw):
        self.em.dma(q, lambda e: e.dma_start(out=out, in_=in_, **kw), r, w)

    def build(self):
        nc = self.nc
        em = self.em
        st = ExitStack()
        self.st = st
        E = st.enter_context
        self.hin = self.din("hin", [NROW, D])
        self.cT = self.din("cT", [128, KC, 3])
        self.ident_d = self.din("ident", [128, 128])
        self.ada_w = self.din("ada_w", [DEPTH, D, 6 * D])
        self.ada_bT = self.din("ada_bT", [DEPTH, 128, 48])
        self.ada_b = self.din("ada_b", [DEPTH, 6 * D])
        self.ln_g = self.din("ln_g", [DEPTH, 2, D])
        self.ln_b = self.din("ln_b", [DEPTH, 2, D])
        self.conv_w_in = self.din("conv_w_in", [2, D, 2 * D])
        self.conv_b_inT = self.din("conv_b_inT", [2, 128, 16])
        self.conv_w_dwT = self.din("conv_w_dwT", [2, 128, KC, CONVW])
        self.conv_b_dwT = self.din("conv_b_dwT", [2, 128, KC])
        self.conv_ln_gT = self.din("conv_ln_gT", [2, 128, KC])
        self.conv_ln_bT = self.din("conv_ln_bT", [2, 128, KC])
        self.conv_w_out = self.din("conv_w_out", [2, D, D])
        self.conv_b_out = self.din("conv_b_out", [2, D])
        self.moe_router = self.din("moe_router", [DEPTH, D, NE])
        self.moe_w_gate = self.din("moe_w_gate", [DEPTH, NE, D, DE])
        self.moe_w_up = self.din("moe_w_up", [DEPTH, NE, D, DE])
        self.moe_w_down = self.din("moe_w_down", [DEPTH, NE, DE, D])
        self.out = self.dout("out", [NB * T, D])
        self.H = self.dint("H", [NROW, D])
        self.Y = self.dint("Y", [NROW, D])
        self.extra_io()

        self.ident = E(self.sbt("ident_s", [128, 128], F32))
        self.ones = E(self.sbt("ones_s", [128, 128], F32))
        self.scT = E(self.sbt("scT", [128, KC, 3], BF16))
        self.modT = E(self.sbt("modT", [128, 32, 3], F32))
        self.ga_bc = E(self.sbt("ga_bc", [128, 2, 3, D], BF16))
        self.lnp = E(self.sbt("lnp", [128, 2, D], F32))
        self.affT = E(self.sbt("affT", [48, T], F32))
        self.affTc = E(self.sbt("affTc", [16, NB * L], F32))
        self.wr = E(self.sbt("wr", [128, KC, NE], F32))
        self.PS = [E(nc.psum_tensor("ps%d" % i, [128, 512], F32)) for i in range(8)]

        self.dma('sp', self.ident[:], self.ident_d, (), ['ident'])
        self.ms('dve', self.ones[:], 1.0, ['ones'])
        self.epsc = E(self.sbt("epsc", [128, 2], F32))
        self.ms('dve', self.epsc[:, 0:1], LN_EPS, ['epsc'])
        self.ms('dve', self.epsc[:, 1:2], 64e-5, ['epsc'])
        self.ms('dve', self.affT[:], 0.0, ['affT'])
        self.ms('dve', self.affTc[:], 0.0, ['affTc'])
        with ExitStack() as s2:
            cTs = s2.enter_context(self.sbt("cTs", [128, KC, 3], F32))
            self.dma('sp', cTs[:], self.cT, (), ['cTs'])
            self.act(self.scT[:], cTs[:], AF.Silu, ['cTs'], ['scT'])
            em.flush()

        for i in self.layers:
            kind, jj = i % 3, i // 3
            ctx_read = i <= 2
            ctx_live = i < 2
            self.ada_phase(i)
            if kind == 0:
                self.conv_phase(i, jj, ctx_live)
            elif kind == 1:
                self.na_phase(i, jj, ctx_live)
            else:
                self.rwkv_phase(i, jj)
            self.moe_phase(i, ctx_live, last=(i == self.layers[-1]))
        em.flush()
        st.close()
        em.close()
        return nc

    def extra_io(self):
        self.na_w_qkv = self.din("na_w_qkv", [1, D, 3 * D])
        self.na_w_o = self.din("na_w_o", [1, D, D])
        self.bm_tab = self.din("bm_tab", [16, 5, 128, 640])
        self.V_d = self.dint("V_d", [T + L, D], BF16)
        NT = T + L
        self.rw_mu_prevT = self.din("rw_mu_prevT", [128, 6, KC])
        self.rw_mu_nextT = self.din("rw_mu_nextT", [128, 6, KC])
        for nm in ("rw_w_r", "rw_w_k", "rw_w_v", "rw_w_o"):
            setattr(self, nm, self.din(nm, [D, D]))
        self.rw_w0 = self.din("rw_w0", [2, D])
        self.rw_a0 = self.din("rw_a0", [2, D])
        self.rw_w1 = self.din("rw_w1", [2, D, 64])
        self.rw_a1 = self.din("rw_a1", [2, D, 64])
        self.rw_w2 = self.din("rw_w2", [2, 64, D])
        self.rw_a2 = self.din("rw_a2", [2, 64, D])
        self.rw_g1 = self.din("rw_g1", [D, 128])
        self.rw_g2 = self.din("rw_g2", [128, D])
        for nm in ("rw_k_k", "rw_k_a", "rw_r_k", "rw_gn_g", "rw_gn_b"):
            setattr(self, nm, self.din(nm, [1, D]))
        self.masks = self.din("masks", [4, 128, 128])
        self.bdmask = self.din("bdmask", [128, 128])
        self.RK = self.dint("RK", [NT, D])
        self.RV = self.dint("RV", [NT, D])
        self.RR = self.dint("RR", [NT, D])
        self.RG = self.dint("RG", [NT, D])
        self.KKN = self.dint("KKN", [NT, D])
        self.RLW = [self.dint("RLW%d" % d, [NT, D]) for d in range(2)]
        self.RAI = [self.dint("RAI%d" % d, [NT, D]) for d in range(2)]
        self.KD = [self.dint("KD%d" % d, [NT, D]) for d in range(2)]
        self.BV = [self.dint("BV%d" % d, [NT, D]) for d in range(2)]
        self.YS = [self.dint("YS%d" % d, [T, D]) for d in range(2)]

    def load_lnp(self, i, g):
        for r_, src in ((0, self.ln_g), (1, self.ln_b)):
            self.dma('sp', self.lnp[:, r_, :], src[i, g:g + 1, :].to_broadcast([128, D]), (), [('lnp', r_)])

    def hsrc(self, i):
        return self.hin if i == self.layers[0] else self.H

    def ada_phase(self, i):
        nc = self.nc
        PS = self.PS
        with ExitStack() as s:
            E = s.enter_context
            wA = [E(self.sbt("wA%d" % k, [128, KC, D], BF16)) for k in range(2)]
            abT = E(self.sbt("abT", [128, 48], F32))
            gb = E(self.sbt("gb", [128, 2, D], F32))
            self.scR = E(self.sbt("scR", [128, KC, 3, 128], BF16))
            for j in range(3):
                self.cp('dve', self.scR[:, :, j, :], self.scT[:, :, j:j + 1].to_broadcast([128, KC, 128]), ['scT'], ['scR'])
            self.dma('sp', abT[:], self.ada_bT[i], (), ['abT'])
            for g, cb in enumerate((2, 5)):
                self.dma('sp', gb[:, g, :], self.ada_b[i:i + 1, cb * D:(cb + 1) * D].to_broadcast([128, D]), (), [('gb', g)])
            self.load_lnp(i, 0)
            self.dma('sp', self.wr[:], self.moe_router[i].rearrange("(k p) e -> p k e", p=128), (), ['wr'])
            for cb in range(6):
                wb = wA[cb % 2]
                src = self.ada_w[i][:, cb * D:(cb + 1) * D].rearrange("(k p) n -> p k n", p=128)
                for hh in range(2):
                    self.dma('pool', wb[:, :, hh * 512:(hh + 1) * 512], src[:, :, hh * 512:(hh + 1) * 512], (), [('wA', cb % 2, hh)])
                rw = [('wA', cb % 2, 0), ('wA', cb % 2, 1)]
                if cb in (0, 1, 3, 4):
                    v = (0, 1, None, 2, 3)[cb]
                    b = self.bank()
                    for fc in range(KC):
                        for k in range(KC):
                            self.mm(PS[b][:, fc * 3:fc * 3 + 3], wb[:, k, fc * 128:(fc + 1) * 128], self.scT[:, k, :],
                                    k == 0, k == KC - 1, rw + ['scT'], [('ps', b)])
                    pv = PS[b][:, 0:24].rearrange("p (f j) -> p f j", j=3)
                    self.tt('dve', self.modT[:, v * 8:(v + 1) * 8, :], pv,
                            abT[:, cb * 8:(cb + 1) * 8].unsqueeze(2).to_broadcast([128, 8, 3]), ALU.add,
                            [('ps', b), 'abT'], [('modT', v)])
                    if cb in (1, 4):
                        self.ts('dve', self.modT[:, v * 8:(v + 1) * 8, :], self.modT[:, v * 8:(v + 1) * 8, :], 1.0, None, ALU.add, None,
                                [('modT', v)], [('modT', v)])
                else:
                    g = 0 if cb == 2 else 1
                    for j in range(3):
                        for nh in range(2):
                            b = self.bank()
                            for k in range(KC):
                                self.mm(PS[b][:, :], self.scR[:, k, j, :], wb[:, k, nh * 512:(nh + 1) * 512],
                                        k == 0, k == KC - 1, rw + ['scR'], [('ps', b)])
                            self.tt('dve', self.ga_bc[:, g, j, nh * 512:(nh + 1) * 512], PS[b][:, :], gb[:, g, nh * 512:(nh + 1) * 512], ALU.add,
                                    [('ps', b), ('gb', g)], [('ga', g, j)])
            self.em.flush()

    def streams(self, ctx):
        s = [(0, T, 0, 0), (T, T, 1, 1)]
        if ctx:
            s += [(CTX0, L, 2, 2), (CTX0 + L, L, 2, 3)]
        return s

    def load_modT_to(self, i):
        pass

    def postnorm_tile(self, tl, row0, cond, g, hsrc, ysrc, yr, dst, dst_res, bias_bc=None, router=None):
        PS = self.PS
        ht, t1, t2, st6, mv, sm = tl['ht'], tl['t1'], tl['t2'], tl['st6'], tl['mv'], tl['sm']
        n = tl['n']
        tl['n'] += 1
        p = n % 2
        ht, t1 = ht[p], t1[p]
        R = lambda nm: (nm, p)
        self.dma('sp', ht[:], hsrc[row0:row0 + 128, :], [('H', row0 // 128)] if hsrc is self.H else [], [R('ht')])
        for nh in range(2):
            sl = slice(nh * 512, (nh + 1) * 512)
            if bias_bc is not None:
                self.tt('dve', t1[:, sl], ysrc[nh], bias_bc[:, sl], ALU.add, list(yr[nh]) + ['bias_bc'], [R('t1')])
                self.tt('pool', t1[:, sl], t1[:, sl], self.ga_bc[:, g, cond, sl], ALU.mult, [R('t1'), ('ga', g, cond)], [R('t1')])
            else:
                self.tt('dve', t1[:, sl], ysrc[nh], self.ga_bc[:, g, cond, sl], ALU.mult, list(yr[nh]) + [('ga', g, cond)], [R('t1')])
        self.stt('dve', t1[:], ht[:], ALPHA, t1[:], ALU.mult, ALU.add, [R('ht'), R('t1')], [R('t1')])
        for c in range(2):
            self.em.op('dve', lambda e, c=c: e.bn_stats(out=st6[p][:, c, :], in_=t1[:, c * 512:(c + 1) * 512]), [R('t1')], [R('st6')])
        self.em.op('dve', lambda e: e.bn_aggr(out=mv[p][:], in_=st6[p][:]), [R('st6')], [R('mv')])
        self.act(sm[p][:, 0:1], mv[p][:, 1:2], AF.Ln, [R('mv')], [R('sm')], bias=self.epsc[:, 0:1])
        self.act(sm[p][:, 0:1], sm[p][:, 0:1], AF.Exp, [R('sm')], [R('sm')], scale=-0.5)
        self.stt('dve', sm[p][:, 1:2], mv[p][:, 0:1], -1.0, sm[p][:, 0:1], ALU.mult, ALU.mult, [R('mv'), R('sm')], [R('sm')])
        self.act(t1[:], t1[:], AF.Identity, [R('t1'), R('sm')], [R('t1')], bias=sm[p][:, 1:2], scale=sm[p][:, 0:1])
        self.tt('pool', t1[:], t1[:], self.lnp[:, 0, :], ALU.mult, [R('t1'), ('lnp', 0)], [R('t1')])
        self.tt('dve', t1[:], t1[:], self.lnp[:, 1, :], ALU.add, [R('t1'), ('lnp', 1)], [R('t1')])
        self.dma('sp', dst[row0:row0 + 128, :] if dst is not self.out else dst[row0:row0 + 128, :], t1[:], [R('t1')], [(dst_res, row0 // 128)])
        if router is not None:
            self.router_tile(tl, t1, R('t1'), row0, cond, p)

    def router_tile(self, tl, hnew, hres, row0, cond, p):
        PS = self.PS
        hmT = tl['hmT'][p]
        lg = tl['lg'][p]
        R = lambda nm: (nm, p)
        for half in range(2):
            b = self.bank()
            for kk in range(4):
                k = half * 4 + kk
                self.tr(PS[b][:, kk * 128:(kk + 1) * 128], hnew[:, k * 128:(k + 1) * 128], [hres], [('ps', b)])
            for kk in range(4):
                k = half * 4 + kk
                eng = 'act' if kk % 2 == 0 else 'dve'
                if eng == 'act':
                    self.act(hmT[:, k, :], PS[b][:, kk * 128:(kk + 1) * 128], AF.Identity, [('ps', b), ('modT', 2), ('modT', 3)], [R('hmT')],
                             bias=self.modT[:, 16 + k, cond:cond + 1], scale=self.modT[:, 24 + k, cond:cond + 1])
                else:
                    self.ts('dve', hmT[:, k, :], PS[b][:, kk * 128:(kk + 1) * 128], self.modT[:, 24 + k, cond:cond + 1],
                            self.modT[:, 16 + k, cond:cond + 1], ALU.mult, ALU.add, [('ps', b), ('modT', 2), ('modT', 3)], [R('hmT')])
        b = self.bank()
        for k in range(KC):
            self.mm(PS[b][:, 0:NE], hmT[:, k, :], self.wr[:, k, :], k == 0, k == KC - 1, [R('hmT'), 'wr'], [('ps', b)])
        sm = tl['sm2'][p]
        self.em.op('dve', lambda e: e.tensor_reduce(out=sm[:, 0:1], in_=PS[b][:, 0:NE], axis=AX.X, op=ALU.max, negate=True), [('ps', b)], [R('sm2')])
        is_ctx = row0 >= CTX0
        bsel = (row0 - CTX0) // L if is_ctx else row0 // T
        c0 = 32 if (bsel == 1 and not is_ctx) else 0
        self.act(lg[:, c0:c0 + NE], PS[b][:, 0:NE], AF.Exp, [('ps', b), R('sm2')], [R('lg')], bias=sm[:, 0:1], accum_out=sm[:, 1:2])
        self.em.op('dve', lambda e: e.reciprocal(out=sm[:, 2:3], in_=sm[:, 1:2]), [R('sm2')], [R('sm2')])
        self.ts('dve', lg[:, c0:c0 + NE], lg[:, c0:c0 + NE], sm[:, 2:3], None, ALU.mult, None, [R('lg'), R('sm2')], [R('lg')])
        b2 = self.bank()
        npart = c0 + NE
        self.tr(PS[b2][0:npart, 0:128], lg[:, 0:npart], [R('lg')], [('ps', b2)])
        if is_ctx:
            col = (row0 - CTX0)
            self.cp('act', self.affTc[0:16, col:col + 128], PS[b2][0:16, 0:128], [('ps', b2)], ['affTc'])
        else:
            col = row0 - bsel * T
            self.cp('act', self.affT[c0:c0 + 16, col:col + 128], PS[b2][c0:c0 + 16, 0:128], [('ps', b2)], ['affT'])

    def alloc_pn_tiles(self, E, router):
        nc = self.nc
        tl = {'n': 0}
        tl['ht'] = [E(self.sbt("pn_ht%d" % k, [128, D], F32)) for k in range(2)]
        tl['t1'] = [E(self.sbt("pn_t1%d" % k, [128, D], F32)) for k in range(2)]
        tl['t2'] = None
        tl['st6'] = [E(self.sbt("pn_st%d" % k, [128, 2, 6], F32)) for k in range(2)]
        tl['mv'] = [E(self.sbt("pn_mv%d" % k, [128, 2], F32)) for k in range(2)]
        tl['sm'] = [E(self.sbt("pn_sm%d" % k, [128, 2], F32)) for k in range(2)]
        if router:
            tl['hmT'] = [E(self.sbt("pn_hmT%d" % k, [128, KC, 128], F32)) for k in range(2)]
            tl['lg'] = [E(self.sbt("pn_lg%d" % k, [128, 48], F32)) for k in range(2)]
            tl['sm2'] = [E(self.sbt("pn_sm2%d" % k, [128, 4], F32)) for k in range(2)]
            for k in range(2):
                self.ms('dve', tl['lg'][k][:], 0.0, [('lg', k)])
        return tl

    def conv_phase(self, i, jj, ctx_live):
        nc = self.nc
        PS = self.PS
        em = self.em
        hsrc = self.hsrc(i)
        with ExitStack() as so:
            convout = so.enter_context(self.sbt("convout", [128, KC, T], F32))
            vec = so.enter_context(self.sbt("cvec", [128, 16 + KC * CONVW + 3 * KC], F32))
            b_in = vec[:, 0:16]
            w_dw = vec[:, 16:16 + KC * CONVW].rearrange("p (k j) -> p k j", j=CONVW)
            o = 16 + KC * CONVW
            b_dw = vec[:, o:o + KC]
            cg = vec[:, o + KC:o + 2 * KC]
            cb_ = vec[:, o + 2 * KC:o + 3 * KC]
            self.dma('sp', b_in, self.conv_b_inT[jj], (), ['cvec'])
            self.dma('sp', w_dw, self.conv_w_dwT[jj], (), ['cvec'])
            self.dma('sp', b_dw, self.conv_b_dwT[jj], (), ['cvec'])
            self.dma('sp', cg, self.conv_ln_gT[jj], (), ['cvec'])
            self.dma('sp', cb_, self.conv_ln_bT[jj], (), ['cvec'])
            for (row0, nt, cond, sid) in self.streams(ctx_live):
                ntile = nt // 128
                CH = 512 if nt >= 512 else nt
                nch = nt // CH
                with ExitStack() as s:
                    E = s.enter_context
                    uT = E(self.sbt("uT", [128, KC, nt], BF16))
                    w_in = E(self.sbt("w_in", [128, KC, 2 * D], BF16))
                    xpad = [E(self.sbt("xpad%d" % k, [128, nt + 30], F32)) for k in range(2)]
                    sg = [E(self.sbt("sg%d" % k, [128, 512], F32)) for k in range(2)]
                    ht = [E(self.sbt("cht%d" % k, [128, D], F32)) for k in range(2)]
                    src = self.conv_w_in[jj].rearrange("(k p) n -> p k n", p=128)
                    for hh in range(4):
                        self.dma('pool', w_in[:, :, hh * 512:(hh + 1) * 512], src[:, :, hh * 512:(hh + 1) * 512], (), [('w_in', hh)])
                    for k in range(2):
                        self.ms('pool', xpad[k][:, 0:15], 0.0, [('xpad', k)])
                        self.ms('pool', xpad[k][:, nt + 15:nt + 30], 0.0, [('xpad', k)])
                    for tt_ in range(ntile):
                        p = tt_ % 2
                        r0 = row0 + tt_ * 128
                        self.dma('sp', ht[p][:], hsrc[r0:r0 + 128, :], [('H', r0 // 128)] if hsrc is self.H else [], [('cht', p)])
                        for half in range(2):
                            b = self.bank()
                            for kk in range(4):
                                k = half * 4 + kk
                                self.tr(PS[b][:, kk * 128:(kk + 1) * 128], ht[p][:, k * 128:(k + 1) * 128], [('cht', p)], [('ps', b)])
                            for kk in range(4):
                                k = half * 4 + kk
                                if kk % 2 == 0:
                                    self.act(uT[:, k, tt_ * 128:(tt_ + 1) * 128], PS[b][:, kk * 128:(kk + 1) * 128], AF.Identity,
                                             [('ps', b), ('modT', 0), ('modT', 1)], [('uT', tt_ * 128 // CH)],
                                             bias=self.modT[:, k, cond:cond + 1], scale=self.modT[:, 8 + k, cond:cond + 1])
                                else:
                                    self.ts('dve', uT[:, k, tt_ * 128:(tt_ + 1) * 128], PS[b][:, kk * 128:(kk + 1) * 128],
                                            self.modT[:, 8 + k, cond:cond + 1], self.modT[:, k, cond:cond + 1], ALU.mult, ALU.add,
                                            [('ps', b), ('modT', 0), ('modT', 1)], [('uT', tt_ * 128 // CH)])
                    for j in range(KC):
                        xp = xpad[j % 2]
                        for cc in range(nch):
                            tsl = slice(cc * CH, (cc + 1) * CH)
                            bg = self.bank()
                            for k in range(KC):
                                self.mm(PS[bg][:, 0:CH], w_in[:, k, (8 + j) * 128:(9 + j) * 128], uT[:, k, tsl], k == 0, k == KC - 1,
                                        [('w_in', (8 + j) // 4), ('uT', cc)], [('ps', bg)])
                            bv = self.bank()
                            for k in range(KC):
                                self.mm(PS[bv][:, 0:CH], w_in[:, k, j * 128:(j + 1) * 128], uT[:, k, tsl], k == 0, k == KC - 1,
                                        [('w_in', j // 4), ('uT', cc)], [('ps', bv)])
                            q = cc % 2
                            self.act(sg[q][:, 0:CH], PS[bg][:, 0:CH], AF.Sigmoid, [('ps', bg), 'cvec'], [('sg', q)], bias=b_in[:, 8 + j:9 + j])
                            self.stt('dve', xp[:, 15 + cc * CH:15 + (cc + 1) * CH], PS[bv][:, 0:CH], b_in[:, j:j + 1], sg[q][:, 0:CH],
                                     ALU.add, ALU.mult, [('ps', bv), ('sg', q), 'cvec'], [('xpad', j % 2)])
                        acc = convout[:, j, 0:nt]
                        self.ts('dve', acc, xp[:, 0:nt], w_dw[:, j, 0:1], b_dw[:, j:j + 1], ALU.mult, ALU.add,
                                [('xpad', j % 2), 'cvec'], [('convout', j)])
                        for tap in range(1, CONVW):
                            self.stt('dve', acc, xp[:, tap:tap + nt], w_dw[:, j, tap:tap + 1], acc, ALU.mult, ALU.add,
                                     [('xpad', j % 2), ('convout', j), 'cvec'], [('convout', j)])
                    em.flush()
                with ExitStack() as s:
                    E = s.enter_context
                    sT = E(self.sbt("sT", [128, KC, nt], BF16))
                    w_out = E(self.sbt("w_out", [128, KC, D], BF16))
                    bo_bc = E(self.sbt("bo_bc", [128, D], F32))
                    sq = [E(self.sbt("sq%d" % k, [128, 512], F32)) for k in range(2)]
                    mean = E(self.sbt("cmean", [128, 512], F32))
                    rstd = E(self.sbt("crstd", [128, 512], F32))
                    xn = [E(self.sbt("cxn%d" % k, [128, 512], F32)) for k in range(2)]
                    tl = self.alloc_pn_tiles(E, True)
                    src = self.conv_w_out[jj].rearrange("(k p) n -> p k n", p=128)
                    for hh in range(2):
                        self.dma('pool', w_out[:, :, hh * 512:(hh + 1) * 512], src[:, :, hh * 512:(hh + 1) * 512], (), [('w_out', hh)])
                    self.dma('sp', bo_bc[:], self.conv_b_out[jj:jj + 1, :].to_broadcast([128, D]), (), ['bias_bc'])
                    for cc in range(nch):
                        tsl = slice(cc * CH, (cc + 1) * CH)
                        bm = self.bank()
                        for k in range(KC):
                            self.mm(PS[bm][:, 0:CH], self.ones[:], convout[:, k, tsl], k == 0, k == KC - 1, ['ones', ('convout', k)], [('ps', bm)])
                        bq = self.bank()
                        for k in range(KC):
                            q = k % 2
                            self.act(sq[q][:, 0:CH], convout[:, k, tsl], AF.Square, [('convout', k)], [('sq', q)])
                            self.mm(PS[bq][:, 0:CH], self.ones[:], sq[q][:, 0:CH], k == 0, k == KC - 1, ['ones', ('sq', q)], [('ps', bq)])
                        self.act(mean[:, 0:CH], PS[bm][:, 0:CH], AF.Identity, [('ps', bm)], ['cmean'], scale=1.0 / D)
                        self.tt('dve', rstd[:, 0:CH], mean[:, 0:CH], mean[:, 0:CH], ALU.mult, ['cmean'], ['crstd'])
                        self.stt('dve', rstd[:, 0:CH], PS[bq][:, 0:CH], 1.0 / D, rstd[:, 0:CH], ALU.mult, ALU.subtract, [('ps', bq), 'crstd'], ['crstd'])
                        self.act(rstd[:, 0:CH], rstd[:, 0:CH], AF.Ln, ['crstd'], ['crstd'], bias=self.epsc[:, 0:1])
                        self.act(rstd[:, 0:CH], rstd[:, 0:CH], AF.Exp, ['crstd'], ['crstd'], scale=-0.5)
                        for k in range(KC):
                            q = k % 2
                            self.tt('dve', xn[q][:, 0:CH], convout[:, k, tsl], mean[:, 0:CH], ALU.subtract, [('convout', k), 'cmean'], [('cxn', q)])
                            self.tt('pool', xn[q][:, 0:CH], xn[q][:, 0:CH], rstd[:, 0:CH], ALU.mult, [('cxn', q), 'crstd'], [('cxn', q)])
                            self.act(sT[:, k, tsl], xn[q][:, 0:CH], AF.Silu, [('cxn', q), 'cvec'], [('sT', cc)],
                                     bias=cb_[:, k:k + 1], scale=cg[:, k:k + 1])
                    for tt_ in range(ntile):
                        r0 = row0 + tt_ * 128
                        bs = []
                        for nh in range(2):
                            b = self.bank()
                            bs.append(b)
                            for k in range(KC):
                                self.mm(PS[b][:, :], sT[:, k, tt_ * 128:(tt_ + 1) * 128], w_out[:, k, nh * 512:(nh + 1) * 512], k == 0, k == KC - 1,
                                        [('sT', tt_ * 128 // CH), ('w_out', nh)], [('ps', b)])
                        self.postnorm_tile(tl, r0, cond, 0, hsrc, [PS[bs[0]][:, :], PS[bs[1]][:, :]], [[('ps', bs[0])], [('ps', bs[1])]],
                                           self.H, 'H', bias_bc=bo_bc, router=True)
                    em.flush()

    def moe_phase(self, i, ctx_live, last):
        nc = self.nc
        PS = self.PS
        em = self.em
        nexp = self.n_exp
        with ExitStack() as so:
            EO = so.enter_context
            idxT = EO(self.sbt("idxT", [128, 2, 48], I32))
            gateT = EO(self.sbt("gateT", [128, 2, 48], F32))
            idxTc = EO(self.sbt("idxTc", [64, 16], I32))
            gateTc = EO(self.sbt("gateTc", [64, 16], F32))
            self.zero = EO(self.sbt("zero", [128, D], F32))
            self.ms('dve', self.zero[:], 0.0, ['zero'])
            self.load_lnp(i, 1)
            nrows = NROW if ctx_live else NB * T
            for r0 in range(0, nrows, 128):
                self.dma('sp', self.Y[r0:r0 + 128, :], self.zero[:], ['zero'], ['Y'])
            with ExitStack() as s:
                E = s.enter_context
                wk = E(self.sbt("wk", [48, T], F32))
                vals = E(self.sbt("vals", [48, CAP], F32))
                idxu = E(self.sbt("idxu", [48, CAP], U32))
                idxf = E(self.sbt("idxf", [48, CAP], F32))
                self.cp('dve', wk[:], self.affT[:], ['affT'], ['wk'])
                for it in range(CAP // 8):
                    sl = slice(it * 8, (it + 1) * 8)
                    self.em.op('dve', lambda e, sl=sl: e.max(out=vals[:, sl], in_=wk[:]), ['wk'], ['vals'])
                    self.em.op('dve', lambda e, sl=sl: e.max_index(out=idxu[:, sl], in_max=vals[:, sl], in_values=wk[:]), ['wk', 'vals'], ['idxu'])
                    if it < CAP // 8 - 1:
                        self.em.op('dve', lambda e, sl=sl: e.match_replace(out=wk[:], in_to_replace=vals[:, sl], in_values=wk[:], imm_value=-1.0),
                                   ['wk', 'vals'], ['wk'])
                self.cp('dve', idxf[:], idxu[:], ['idxu'], ['idxf'])
                self.ts('dve', idxf[32:48, :], idxf[32:48, :], float(T), None, ALU.add, None, ['idxf'], ['idxf'])
                for j in range(2):
                    b = self.bank()
                    self.tr(PS[b][:, 0:48], idxf[:, j * 128:(j + 1) * 128], ['idxf'], [('ps', b)])
                    self.cp('dve', idxT[:, j, :], PS[b][:, 0:48], [('ps', b)], ['idxT'])
                    b = self.bank()
                    self.tr(PS[b][:, 0:48], vals[:, j * 128:(j + 1) * 128], ['vals'], [('ps', b)])
                    self.cp('act', gateT[:, j, :], PS[b][:, 0:48], [('ps', b)], ['gateT'])
                if ctx_live:
                    wkc = E(self.sbt("wkc", [16, NB * L], F32))
                    valc = E(self.sbt("valc", [16, NB * CAPC], F32))
                    idxcu = E(self.sbt("idxcu", [16, NB * CAPC], U32))
                    idxcf = E(self.sbt("idxcf", [16, NB * CAPC], F32))
                    self.cp('dve', wkc[:], self.affTc[:], ['affTc'], ['wkc'])
                    for bb in range(NB):
                        for it in range(CAPC // 8):
                            sl = slice(bb * CAPC + it * 8, bb * CAPC + (it + 1) * 8)
                            wsl = slice(bb * L, (bb + 1) * L)
                            self.em.op('dve', lambda e, sl=sl, wsl=wsl: e.max(out=valc[:, sl], in_=wkc[:, wsl]), ['wkc'], ['valc'])
                            self.em.op('dve', lambda e, sl=sl, wsl=wsl: e.max_index(out=idxcu[:, sl], in_max=valc[:, sl], in_values=wkc[:, wsl]),
                                       ['wkc', 'valc'], ['idxcu'])
                            if it < CAPC // 8 - 1:
                                self.em.op('dve', lambda e, sl=sl, wsl=wsl: e.match_replace(out=wkc[:, wsl], in_to_replace=valc[:, sl],
                                                                                          in_values=wkc[:, wsl], imm_value=-1.0),
                                           ['wkc', 'valc'], ['wkc'])
                    self.cp('dve', idxcf[:], idxcu[:], ['idxcu'], ['idxcf'])
                    for bb in range(NB):
                        self.ts('dve', idxcf[:, bb * CAPC:(bb + 1) * CAPC], idxcf[:, bb * CAPC:(bb + 1) * CAPC], float(CTX0 + bb * L), None,
                                ALU.add, None, ['idxcf'], ['idxcf'])
                    b = self.bank()
                    self.tr(PS[b][0:64, 0:16], idxcf[:, :], ['idxcf'], [('ps', b)])
                    self.cp('dve', idxTc[:], PS[b][0:64, 0:16], [('ps', b)], ['idxTc'])
                    b = self.bank()
                    self.tr(PS[b][0:64, 0:16], valc[:, :], ['valc'], [('ps', b)])
                    self.cp('act', gateTc[:], PS[b][0:64, 0:16], [('ps', b)], ['gateTc'])
                em.flush()
            NTOK = 4 * 128 + (64 if ctx_live else 0)
            with ExitStack() as s:
                E = s.enter_context
                NS = 3
                G = [E(self.sbt("Gw%d" % k, [128, KC, 512], BF16)) for k in range(NS)]
                U = [E(self.sbt("Uw%d" % k, [128, KC, 512], BF16)) for k in range(NS)]
                Wd = [E(self.sbt("Wd%d" % k, [128, 16, D], BF16)) for k in range(2)]
                xg = [E(self.sbt("xg%d" % k, [128, D], F32)) for k in range(2)]
                xsT = [E(self.sbt("xsT%d" % k, [128, KC, NTOK], BF16)) for k in range(1)]
                hidT = [E(self.sbt("hidT%d" % k, [128, 16, NTOK], BF16)) for k in range(1)]
                sg = [E(self.sbt("msg%d" % k, [128, NTOK], F32)) for k in range(2)]
                ys = [E(self.sbt("ys%d" % k, [128, D], F32)) for k in range(2)]

                def load_gu(sq_):
                    if sq_ >= nexp * 4:
                        return
                    e, c = sq_ // 4, sq_ % 4
                    sl_ = sq_ % NS
                    sgw = self.moe_w_gate[i, e].rearrange("(k p) n -> p k n", p=128)
                    suw = self.moe_w_up[i, e].rearrange("(k p) n -> p k n", p=128)
                    self.dma('pool', G[sl_][:, :, :], sgw[:, :, c * 512:(c + 1) * 512], (), [('G', sl_)])
                    self.dma('pool', U[sl_][:, :, :], suw[:, :, c * 512:(c + 1) * 512], (), [('U', sl_)])

                def load_d(e):
                    sdw = self.moe_w_down[i, e].rearrange("(f p) n -> p f n", p=128)
                    for hh in range(4):
                        self.dma('pool', Wd[e % 2][:, hh * 4:(hh + 1) * 4, :], sdw[:, hh * 4:(hh + 1) * 4, :], (), [('Wd', e % 2, hh)])

                for sq_ in range(NS):
                    load_gu(sq_)
                load_d(0)
                ngath = 0
                for e in range(nexp):
                    xs = xsT[0]
                    hd = hidT[0]
                    tiles = [(0, 0), (0, 1), (1, 0), (1, 1)]
                    for bb in range(NB):
                        bks = [self.bank() for _ in range(4)]
                        for j in range(2):
                            g = xg[ngath % 2]
                            gr = ('xg', ngath % 2)
                            ngath += 1
                            col = bb * 32 + e
                            self.em.dma('pool', lambda en, g=g, j=j, col=col: en.indirect_dma_start(
                                out=g[:], out_offset=None, in_=self.H,
                                in_offset=bass.IndirectOffsetOnAxis(ap=idxT[:, j, col:col + 1], axis=0)),
                                ['idxT'] + [('H', r) for r in range(NROW // 128)], [gr])
                            for k in range(KC):
                                bk = bks[k // 2]
                                off = (k % 2) * 256 + j * 128
                                self.tr(PS[bk][:, off:off + 128], g[:, k * 128:(k + 1) * 128], [gr], [('ps', bk)])
                        for k in range(KC):
                            bk = bks[k // 2]
                            off = (k % 2) * 256
                            dst = xs[:, k, bb * 256:(bb + 1) * 256]
                            if k % 2 == 0:
                                self.act(dst, PS[bk][:, off:off + 256], AF.Identity, [('ps', bk), ('modT', 2), ('modT', 3)], [('xsT', 0)],
                                         bias=self.modT[:, 16 + k, bb:bb + 1], scale=self.modT[:, 24 + k, bb:bb + 1])
                            else:
                                self.ts('dve', dst, PS[bk][:, off:off + 256], self.modT[:, 24 + k, bb:bb + 1], self.modT[:, 16 + k, bb:bb + 1],
                                        ALU.mult, ALU.add, [('ps', bk), ('modT', 2), ('modT', 3)], [('xsT', 0)])
                    if ctx_live:
                        g = xg[ngath % 2]
                        gr = ('xg', ngath % 2)
                        ngath += 1
                        self.em.dma('pool', lambda en, g=g, e=e: en.indirect_dma_start(
                            out=g[0:64, :], out_offset=None, in_=self.H,
                            in_offset=bass.IndirectOffsetOnAxis(ap=idxTc[:, e:e + 1], axis=0)),
                            ['idxTc'] + [('H', r) for r in range(NROW // 128)], [gr])
                        bks = [self.bank() for _ in range(1)]
                        for k in range(KC):
                            self.tr(PS[bks[0]][:, k * 64:(k + 1) * 64], g[0:64, k * 128:(k + 1) * 128], [gr], [('ps', bks[0])])
                        for k in range(KC):
                            dst = xs[:, k, 512:576]
                            self.ts('dve', dst, PS[bks[0]][:, k * 64:(k + 1) * 64], self.modT[:, 24 + k, 2:3], self.modT[:, 16 + k, 2:3],
                                    ALU.mult, ALU.add, [('ps', bks[0]), ('modT', 2), ('modT', 3)], [('xsT', 0)])
                    if e + 1 < nexp:
                        load_d(e + 1)
                    for c in range(4):
                        for fl in range(4):
                            ft = c * 4 + fl
                            bg = self.bank()
                            bu = self.bank()
                            sl_ = (e * 4 + c) % NS
                            for (bk_, W, wr_) in ((bg, G[sl_], ('G', sl_)), (bu, U[sl_], ('U', sl_))):
                                for k in range(KC):
                                    self.mm(PS[bk_][:, 0:512], W[:, k, fl * 128:(fl + 1) * 128], xs[:, k, 0:512], k == 0, k == KC - 1,
                                            [wr_, ('xsT', 0)], [('ps', bk_)])
                            q = ft % 2
                            self.act(sg[q][:, 0:512], PS[bg][:, 0:512], AF.Silu, [('ps', bg)], [('msg', q)])
                            self.tt('dve', hd[:, ft, 0:512], sg[q][:, 0:512], PS[bu][:, 0:512], ALU.mult, [('msg', q), ('ps', bu)], [('hidT', ft)])
                            if ctx_live:
                                bc = self.bank()
                                for gi, (W, wr_) in enumerate(((G[sl_], ('G', sl_)), (U[sl_], ('U', sl_)))):
                                    for k in range(KC):
                                        self.mm(PS[bc][:, gi * 64:(gi + 1) * 64], W[:, k, fl * 128:(fl + 1) * 128], xs[:, k, 512:576], k == 0, k == KC - 1,
                                                [wr_, ('xsT', 0)], [('ps', bc)])
                                self.act(sg[q][:, 512:576], PS[bc][:, 0:64], AF.Silu, [('ps', bc)], [('msg', q)])
                                self.tt('dve', hd[:, ft, 512:576], sg[q][:, 512:576], PS[bc][:, 64:128], ALU.mult, [('msg', q), ('ps', bc)], [('hidT', ft)])
                        load_gu(e * 4 + c + NS)
                    ttiles = [(bb, j, 128) for bb in range(NB) for j in range(2)] + ([(2, 0, 64)] if ctx_live else [])
                    for ti, (bb, j, np_) in enumerate(ttiles):
                        y_ = ys[ti % 2]
                        yr = ('ys', ti % 2)
                        tok0 = ti * 128
                        for nh in range(2):
                            b = self.bank()
                            for ft in range(16):
                                self.mm(PS[b][0:np_, :], hd[:, ft, tok0:tok0 + np_], Wd[e % 2][:, ft, nh * 512:(nh + 1) * 512], ft == 0, ft == 15,
                                        [('hidT', ft), ('Wd', e % 2, ft // 4)], [('ps', b)])
                            gcol = gateTc[:, e:e + 1] if bb == 2 else gateT[:, j, bb * 32 + e:bb * 32 + e + 1]
                            gres = 'gateTc' if bb == 2 else 'gateT'
                            if nh == 0:
                                self.ts('dve', y_[0:np_, 0:512], PS[b][0:np_, :], gcol[0:np_, :], None, ALU.mult, None, [('ps', b), gres], [yr])
                            else:
                                self.act(y_[0:np_, 512:1024], PS[b][0:np_, :], AF.Identity, [('ps', b), gres], [yr], scale=gcol[0:np_, :])
                        if bb == 2:
                            self.em.dma('pool', lambda en, y_=y_, e=e: en.indirect_dma_start(
                                out=self.Y, out_offset=bass.IndirectOffsetOnAxis(ap=idxTc[:, e:e + 1], axis=0),
                                in_=y_[0:64, :], in_offset=None, compute_op=ALU.add), [yr, 'idxTc'], ['Y'])
                        else:
                            col = bb * 32 + e
                            self.em.dma('pool', lambda en, y_=y_, j=j, col=col: en.indirect_dma_start(
                                out=self.Y, out_offset=bass.IndirectOffsetOnAxis(ap=idxT[:, j, col:col + 1], axis=0),
                                in_=y_[:, :], in_offset=None, compute_op=ALU.add), [yr, 'idxT'], ['Y'])
                em.flush()
            with ExitStack() as s:
                E = s.enter_context
                tl = self.alloc_pn_tiles(E, False)
                yt = [E(self.sbt("pn_y%d" % k, [128, D], F32)) for k in range(2)]
                for (row0, nt, cond, sid) in self.streams(ctx_live):
                    for tt_ in range(nt // 128):
                        r0 = row0 + tt_ * 128
                        p = tl['n'] % 2
                        self.dma('act', yt[p][:], self.Y[r0:r0 + 128, :], ['Y'], [('pn_y', p)])
                        if last and row0 < CTX0:
                            dst, dres = self.out, 'out'
                        else:
                            dst, dres = self.H, 'H'
                        self.postnorm_tile(tl, r0, cond, 1, self.H, [yt[p][:, 0:512], yt[p][:, 512:1024]], [[('pn_y', p)], [('pn_y', p)]],
                                           dst, dres, bias_bc=None, router=None)
                em.flush()

    def modT_tiles(self, uT, hsrc, ht, row0, nt, cond, c0, CH, res_name='uT'):
        PS = self.PS
        for tt_ in range(nt // 128):
            p = self._mt % 2
            self._mt += 1
            r0 = row0 + tt_ * 128
            col = c0 + tt_ * 128
            self.dma('sp', ht[p][:], hsrc[r0:r0 + 128, :], [('H', r0 // 128)] if hsrc is self.H else [], [('cht', p)])
            for half in range(2):
                b = self.bank()
                for kk in range(4):
                    k = half * 4 + kk
                    self.tr(PS[b][:, kk * 128:(kk + 1) * 128], ht[p][:, k * 128:(k + 1) * 128], [('cht', p)], [('ps', b)])
                for kk in range(4):
                    k = half * 4 + kk
                    if kk % 2 == 0:
                        self.act(uT[:, k, col:col + 128], PS[b][:, kk * 128:(kk + 1) * 128], AF.Identity,
                                 [('ps', b), ('modT', 0), ('modT', 1)], [(res_name, col // CH)],
                                 bias=self.modT[:, k, cond:cond + 1], scale=self.modT[:, 8 + k, cond:cond + 1])
                    else:
                        self.ts('dve', uT[:, k, col:col + 128], PS[b][:, kk * 128:(kk + 1) * 128],
                                self.modT[:, 8 + k, cond:cond + 1], self.modT[:, k, cond:cond + 1], ALU.mult, ALU.add,
                                [('ps', b), ('modT', 0), ('modT', 1)], [(res_name, col // CH)])

    def na_phase(self, i, jj, ctx_live):
        nc = self.nc
        PS = self.PS
        em = self.em
        hsrc = self.hsrc(i)
        NT = T + L
        chunks = [(0, 512), (512, 512), (1024, 512), (1536, 512), (2048, 256)]
        classes = ((0, [0]), (1, [1]), (2, list(range(2, 14))), (3, [14]), (4, [15]))
        with ExitStack() as so:
            EO = so.enter_context
            qT = EO(self.sbt("qT", [128, 8, NT], BF16))
            kT = EO(self.sbt("kT", [128, 8, NT], BF16))
            identb = EO(self.sbt("identb", [128, 128], BF16))
            self.cp('dve', identb[:], self.ident[:], ['ident'], ['identb'])
            for b in range(NB):
                with ExitStack() as s:
                    E = s.enter_context
                    uT = E(self.sbt("uT", [128, KC, NT], BF16))
                    wq = [E(self.sbt("wq%d" % k, [128, KC, D], BF16)) for k in range(2)]
                    vst = [E(self.sbt("vst%d" % k, [128, D], BF16)) for k in range(2)]
                    ht = [E(self.sbt("cht%d" % k, [128, D], F32)) for k in range(2)]

                    def loadw(blk, slot):
                        src = self.na_w_qkv[jj][:, blk * D:(blk + 1) * D].rearrange("(k p) n -> p k n", p=128)
                        for hh in range(2):
                            self.dma('pool', wq[slot][:, :, hh * 512:(hh + 1) * 512], src[:, :, hh * 512:(hh + 1) * 512], (), [('wq', slot, hh)])
                    loadw(0, 0)
                    loadw(1, 1)
                    self._mt = 0
                    self.modT_tiles(uT, hsrc, ht, b * T, T, b, 0, 512)
                    self.modT_tiles(uT, hsrc, ht, CTX0 + b * L, L, 2, T, 512)
                    for blk, dstT, nm in ((0, qT, 'qT'), (1, kT, 'kT')):
                        w = wq[blk]
                        for hp in range(8):
                            for (c0, n) in chunks:
                                bk = self.bank()
                                for k in range(KC):
                                    self.mm(PS[bk][:, 0:n], w[:, k, hp * 128:(hp + 1) * 128], uT[:, k, c0:c0 + n], k == 0, k == KC - 1,
                                            [('wq', blk, hp // 4), ('uT', c0 // 512)], [('ps', bk)])
                                if blk == 0:
                                    self.act(dstT[:, hp, c0:c0 + n], PS[bk][:, 0:n], AF.Identity, [('ps', bk)], [(nm, hp)], scale=0.125)
                                else:
                                    self.cp('dve', dstT[:, hp, c0:c0 + n], PS[bk][:, 0:n], [('ps', bk)], [(nm, hp)])
                    loadw(2, 0)
                    for t in range(NT // 128):
                        p = t % 2
                        for nh in range(2):
                            bk = self.bank()
                            for k in range(KC):
                                self.mm(PS[bk][:, :], uT[:, k, t * 128:(t + 1) * 128], wq[0][:, k, nh * 512:(nh + 1) * 512], k == 0, k == KC - 1,
                                        [('wq', 0, nh), ('uT', t * 128 // 512)], [('ps', bk)])
                            if nh == 0:
                                self.cp('act', vst[p][:, 0:512], PS[bk][:, :], [('ps', bk)], [('vst', p)])
                            else:
                                self.cp('dve', vst[p][:, 512:1024], PS[bk][:, :], [('ps', bk)], [('vst', p)])
                        self.dma('sp', self.V_d[t * 128:(t + 1) * 128, :], vst[p][:], [('vst', p)], [('V_d', t)])
                    em.flush()
                with ExitStack() as sm_:
                    oT = sm_.enter_context(self.sbt("oT", [128, 8, NT], BF16))
                    with ExitStack() as s:
                        E = s.enter_context
                        Vh = [E(self.sbt("Vh%d" % k, [128, NT // 128, 128], BF16)) for k in range(2)]
                        BM = [E(self.sbt("BM%d" % k, [128, 640], F32)) for k in range(2)]
                        S = [E(self.sbt("S%d" % k, [128, 896], F32)) for k in range(2)]
                        Pn = [E(self.sbt("Pn%d" % k, [128, 896], BF16)) for k in range(2)]
                        PT = [E(self.sbt("PT%d" % k, [128, 896], BF16)) for k in range(2)]
                        smx = [E(self.sbt("smx%d" % k, [128, 4], F32)) for k in range(2)]
                        nbm = 0
                        nq = 0
                        vsrc = self.V_d.rearrange("(t p) c -> p t c", p=128)
                        for hp in range(8):
                            vh = Vh[hp % 2]
                            self.dma('sp', vh[:], vsrc[:, :, hp * 128:(hp + 1) * 128], [('V_d', t) for t in range(NT // 128)], [('Vh', hp % 2)])
                            for hh in range(2):
                                h = hp * 2 + hh
                                pr = slice(hh * 64, (hh + 1) * 64)

                                def attend(qcol, segs, bm, nkeys, vts):
                                    nonlocal nq
                                    m = nq % 2
                                    nq += 1
                                    S_, Pn_, PT_, sx = S[m], Pn[m], PT[m], smx[m]
                                    R = lambda nm: (nm, m)
                                    bks = {}
                                    scol = 0
                                    for (bi, pc0, kc0, n, loc) in segs:
                                        if bi not in bks:
                                            bks[bi] = self.bank()
                                        bk = bks[bi]
                                        self.mm(PS[bk][:, pc0:pc0 + n], qT[pr, hp, qcol:qcol + 128], kT[pr, hp, kc0:kc0 + n], True, True,
                                                [('qT', hp), ('kT', hp)], [('ps', bk)])
                                    for (bi, pc0, kc0, n, loc) in segs:
                                        bk = bks[bi]
                                        if loc:
                                            self.tt('dve', S_[:, scol:scol + n], PS[bk][:, pc0:pc0 + n], bm[0][:, scol:scol + n], ALU.add,
                                                    [('ps', bk), bm[1]], [R('S')])
                                        else:
                                            self.cp('act', S_[:, scol:scol + n], PS[bk][:, pc0:pc0 + n], [('ps', bk)], [R('S')])
                                        scol += n
                                    self.em.op('dve', lambda e: e.tensor_reduce(out=sx[:, 0:1], in_=S_[:, 0:nkeys], axis=AX.X, op=ALU.max, negate=True),
                                               [R('S')], [R('smx')])
                                    self.act(S_[:, 0:nkeys], S_[:, 0:nkeys], AF.Exp, [R('S'), R('smx')], [R('S')], bias=sx[:, 0:1], accum_out=sx[:, 1:2])
                                    self.em.op('dve', lambda e: e.reciprocal(out=sx[:, 2:3], in_=sx[:, 1:2]), [R('smx')], [R('smx')])
                                    self.ts('dve', Pn_[:, 0:nkeys], S_[:, 0:nkeys], sx[:, 2:3], None, ALU.mult, None, [R('S'), R('smx')], [R('Pn')])
                                    bT = self.bank()
                                    PTv = PS[bT][:, :].bitcast(BF16)
                                    nch = nkeys // 128
                                    for c in range(nch):
                                        self.tr(PTv[:, c * 128:(c + 1) * 128], Pn_[:, c * 128:(c + 1) * 128], [R('Pn'), 'identb'], [('ps', bT)], ident=identb)
                                    if m == 0:
                                        self.cp('act', PT_[:, 0:nkeys], PTv[:, 0:nkeys], [('ps', bT)], [R('PT')])
                                    else:
                                        self.cp('dve', PT_[:, 0:nkeys], PTv[:, 0:nkeys], [('ps', bT)], [R('PT')])
                                    bO = self.bank()
                                    for c in range(nch):
                                        self.mm(PS[bO][pr, 0:128], vh[:, vts[c], hh * 64:(hh + 1) * 64], PT_[:, c * 128:(c + 1) * 128], c == 0, c == nch - 1,
                                                [('Vh', hp % 2), R('PT')], [('ps', bO)])
                                    if m == 0:
                                        self.cp('dve', oT[pr, hp, qcol:qcol + 128], PS[bO][pr, 0:128], [('ps', bO)], [('oT', qcol // 128)])
                                    else:
                                        self.cp('act', oT[pr, hp, qcol:qcol + 128], PS[bO][pr, 0:128], [('ps', bO)], [('oT', qcol // 128)])

                                for cls, tiles in classes:
                                    bmt = BM[nbm % 2]
                                    bmr = ('BM', nbm % 2)
                                    nbm += 1
                                    self.dma('sp', bmt[:], self.bm_tab[h, cls], (), [bmr])
                                    for i_ in tiles:
                                        ks = min(max(2 * i_ - 4, 0), 22)
                                        k0 = ks * 64
                                        segs = [(0, 0, k0, 512, True), (1, 0, k0 + 512, 128, True), (1, 128, T, 256, False)]
                                        vts = [ks // 2 + c for c in range(5)] + [16, 17]
                                        attend(i_ * 128, segs, (bmt, bmr), 896, vts)
                                for j in range(2):
                                    attend(T + j * 128, [(0, 0, T, 256, False)], None, 256, [16, 17])
                        em.flush()
                    with ExitStack() as s:
                        E = s.enter_context
                        w_o = E(self.sbt("w_o", [128, KC, D], BF16))
                        tl = self.alloc_pn_tiles(E, True)
                        src = self.na_w_o[jj].rearrange("(k p) n -> p k n", p=128)
                        for hh in range(2):
                            self.dma('pool', w_o[:, :, hh * 512:(hh + 1) * 512], src[:, :, hh * 512:(hh + 1) * 512], (), [('w_o', hh)])
                        for t in range(NT // 128):
                            if t < 16:
                                r0, cond = b * T + t * 128, b
                            else:
                                r0, cond = CTX0 + b * L + (t - 16) * 128, 2
                            bs = []
                            for nh in range(2):
                                bk = self.bank()
                                bs.append(bk)
                                for k in range(KC):
                                    self.mm(PS[bk][:, :], oT[:, k, t * 128:(t + 1) * 128], w_o[:, k, nh * 512:(nh + 1) * 512], k == 0, k == KC - 1,
                                            [('oT', t), ('w_o', nh)], [('ps', bk)])
                            self.postnorm_tile(tl, r0, cond, 0, hsrc, [PS[bs[0]][:, :], PS[bs[1]][:, :]], [[('ps', bs[0])], [('ps', bs[1])]],
                                               self.H, 'H', bias_bc=None, router=True)
                        em.flush()


    def rwkv_phase(self, i, jj):
        nc = self.nc
        PS = self.PS
        em = self.em
        hsrc = self.hsrc(i)
        NT = T + L
        C64 = 0.6065306597126334
        R3 = lambda ap, dd=64: ap.rearrange("p (h d) -> p h d", d=dd)

        def bc64(ap16):
            return ap16.unsqueeze(2).to_broadcast([128, 16, 64])

        for b in range(NB):
            with ExitStack() as s:
                E = s.enter_context
                wbuf = [E(self.sbt("rwW%d" % k, [128, KC, D], BF16)) for k in range(1)]
                w1c = E(self.sbt("w1c", [128, KC, 128], BF16))
                a1c = E(self.sbt("a1c", [128, KC, 128], BF16))
                g1s = E(self.sbt("g1s", [128, KC, 128], BF16))
                w2s = E(self.sbt("w2s", [128, D], BF16))
                a2s = E(self.sbt("a2s", [128, D], BF16))
                g2s = E(self.sbt("g2s", [128, D], BF16))
                muT = E(self.sbt("muT", [128, 2, 6, KC], F32))
                bc0 = E(self.sbt("bc0", [128, 2, D], F32))
                u32 = E(self.sbt("u32", [128, KC, T + 2], F32))
                lerp = E(self.sbt("lerp", [128, KC, T], BF16))
                tmp = [E(self.sbt("ltmp%d" % k, [128, T], F32)) for k in range(2)]
                loT = E(self.sbt("loT", [128, T], BF16))
                stg = [E(self.sbt("stg%d" % k, [128, D], F32)) for k in range(2)]
                ht = [E(self.sbt("cht%d" % k, [128, D], F32)) for k in range(2)]
                self.dma('sp', muT[:, 0], self.rw_mu_prevT, (), ['muT'])
                self.dma('sp', muT[:, 1], self.rw_mu_nextT, (), ['muT'])
                for d in range(2):
                    self.dma('pool', w1c[:, :, d * 64:(d + 1) * 64], self.rw_w1[d].rearrange("(k p) n -> p k n", p=128), (), ['w1c'])
                    self.dma('pool', a1c[:, :, d * 64:(d + 1) * 64], self.rw_a1[d].rearrange("(k p) n -> p k n", p=128), (), ['a1c'])
                self.dma('pool', g1s[:], self.rw_g1.rearrange("(k p) n -> p k n", p=128), (), ['g1s'])
                self.dma('pool', w2s[:], self.rw_w2.rearrange("d j n -> (d j) n"), (), ['w2s'])
                self.dma('pool', a2s[:], self.rw_a2.rearrange("d j n -> (d j) n"), (), ['a2s'])
                self.dma('pool', g2s[:], self.rw_g2, (), ['g2s'])
                nst = [0]

                def loadW(src, slot):
                    v = src.rearrange("(k p) n -> p k n", p=128)
                    for hh in range(2):
                        self.dma('pool', wbuf[slot][:, :, hh * 512:(hh + 1) * 512], v[:, :, hh * 512:(hh + 1) * 512], (), [('rwW', slot, hh)])

                def stage_out(dst, row, evs):
                    p = nst[0] % 2
                    nst[0] += 1
                    for nh in range(2):
                        evs[nh](stg[p][:, nh * 512:(nh + 1) * 512], ('stg', p))
                    self.dma('sp', dst[row:row + 128, :], stg[p][:], [('stg', p)], [(dst.tensor.name, row // 128)])

                for (row0, nt, cond, roff, is_lat) in ((b * T, T, b, 0, True), (CTX0 + b * L, L, 2, T, False)):
                    ntile = nt // 128
                    CH = min(512, nt)
                    nch = nt // CH
                    for k in range(KC):
                        self.ms('pool', u32[:, k, 0:1], 0.0, [('u32', 0)])
                        self.ms('pool', u32[:, k, nt + 1:nt + 2], 0.0, [('u32', 0)])
                    self._mt = 0
                    self.modT_tiles(u32, hsrc, ht, row0, nt, cond, 1, 1 << 30, res_name='u32')
                    lerps = (0, 1, 2, 3, 4, 5) if is_lat else (1, 2, 3, 4)
                    wsrc = {0: self.rw_w_r, 2: self.rw_w_k, 3: self.rw_w_v}
                    wslot = {0: 0, 2: 0, 3: 0}
                    for n in lerps:
                        if n in wsrc:
                            loadW(wsrc[n], wslot[n])
                        for k in range(KC):
                            uk = u32[:, k, 1:nt + 1]
                            t0, t1 = tmp[0][:, 0:nt], tmp[1][:, 0:nt]
                            self.tt('dve', t0, u32[:, k, 0:nt], uk, ALU.subtract, [('u32', 0)], [('ltmp', 0)])
                            self.stt('dve', t0, t0, muT[:, 0, n, k:k + 1], uk, ALU.mult, ALU.add, [('ltmp', 0), 'muT', ('u32', 0)], [('ltmp', 0)])
                            self.tt('pool', t1, u32[:, k, 2:nt + 2], uk, ALU.subtract, [('u32', 0)], [('ltmp', 1)])
                            self.stt('dve', lerp[:, k, 0:nt], t1, muT[:, 1, n, k:k + 1], t0, ALU.mult, ALU.add,
                                     [('ltmp', 0), ('ltmp', 1), 'muT'], [('lerp', k)])
                        lr = [('lerp', k) for k in range(KC)]
                        if n in wsrc:
                            dst = {0: self.RR, 2: self.RK, 3: self.RV}[n]
                            W = wbuf[wslot[n]]
                            for t in range(ntile):
                                bks = [self.bank(), self.bank()]
                                for nh in range(2):
                                    for k in range(KC):
                                        self.mm(PS[bks[nh]][:, :], lerp[:, k, t * 128:(t + 1) * 128], W[:, k, nh * 512:(nh + 1) * 512], k == 0, k == KC - 1,
                                                lr + [('rwW', wslot[n], nh)], [('ps', bks[nh])])
                                stage_out(dst, roff + t * 128, [
                                    lambda o, r_, bk=bks[0]: self.cp('act', o, PS[bk][:, :], [('ps', bk)], [r_]),
                                    lambda o, r_, bk=bks[1]: self.cp('dve', o, PS[bk][:, :], [('ps', bk)], [r_])])
                        else:
                            l1 = {1: w1c, 4: a1c, 5: g1s}[n]
                            l1n = {1: 'w1c', 4: 'a1c', 5: 'g1s'}[n]
                            for cc in range(nch):
                                bk = self.bank()
                                for k in range(KC):
                                    self.mm(PS[bk][:, 0:CH], l1[:, k, :], lerp[:, k, cc * CH:(cc + 1) * CH], k == 0, k == KC - 1, lr + [l1n], [('ps', bk)])
                                fn = {1: AF.Tanh, 4: AF.Identity, 5: AF.Sigmoid}[n]
                                self.act(loT[:, cc * CH:(cc + 1) * CH], PS[bk][:, 0:CH], fn, [('ps', bk)], ['loT'])
                            if n == 5:
                                for t in range(ntile):
                                    bks = [self.bank(), self.bank()]
                                    for nh in range(2):
                                        self.mm(PS[bks[nh]][:, :], loT[:, t * 128:(t + 1) * 128], g2s[:, nh * 512:(nh + 1) * 512], True, True, ['loT', 'g2s'], [('ps', bks[nh])])
                                    stage_out(self.RG, roff + t * 128, [
                                        lambda o, r_, bk=bks[0]: self.cp('act', o, PS[bk][:, :], [('ps', bk)], [r_]),
                                        lambda o, r_, bk=bks[1]: self.cp('dve', o, PS[bk][:, :], [('ps', bk)], [r_])])
                            else:
                                l2, l2n = (w2s, 'w2s') if n == 1 else (a2s, 'a2s')
                                for d in range(2):
                                    self.dma('sp', bc0[:, d, :], (self.rw_w0 if n == 1 else self.rw_a0)[d:d + 1, :].to_broadcast([128, D]), (), ['bc0'])
                                for d in range(2):
                                    pr = slice(d * 64, (d + 1) * 64)
                                    dst = (self.RLW if n == 1 else self.RAI)[d]
                                    bci = d
                                    for t in range(ntile):
                                        bks = [self.bank(), self.bank()]
                                        for nh in range(2):
                                            self.mm(PS[bks[nh]][:, :], loT[pr, t * 128:(t + 1) * 128], l2[pr, nh * 512:(nh + 1) * 512], True, True, ['loT', l2n], [('ps', bks[nh])])

                                        def ev(o, r_, bk, nh, n=n, bci=bci):
                                            self.tt('dve', o, PS[bk][:, :], bc0[:, bci, nh * 512:(nh + 1) * 512], ALU.add, [('ps', bk), 'bc0'], [r_])
                                            self.act(o, o, AF.Sigmoid, [r_], [r_])
                                            if n == 1:
                                                self.ts('pool', o, o, -C64, None, ALU.mult, None, [r_], [r_])
                                        stage_out(dst, roff + t * 128, [
                                            lambda o, r_, bk=bks[0]: ev(o, r_, bk, 0),
                                            lambda o, r_, bk=bks[1]: ev(o, r_, bk, 1)])
                em.flush()
            with ExitStack() as s:
                E = s.enter_context
                bc1 = E(self.sbt("bc1", [128, 2, D], F32))
                self.dma('sp', bc1[:, 0, :], self.rw_k_k.to_broadcast([128, D]), (), ['bc1'])
                self.dma('sp', bc1[:, 1, :], self.rw_k_a.to_broadcast([128, D]), (), ['bc1'])
                kt = [E(self.sbt("ekt%d" % k, [128, D], F32)) for k in range(2)]
                at = [[E(self.sbt("eat%d_%d" % (d, k), [128, D], F32)) for k in range(2)] for d in range(2)]
                kk = [E(self.sbt("ekk%d" % k, [128, D], F32)) for k in range(2)]
                sq = [E(self.sbt("esq%d" % k, [128, D], F32)) for k in range(2)]
                o1 = [[E(self.sbt("eo1%d_%d" % (d, k), [128, D], F32)) for k in range(2)] for d in range(2)]
                o2 = [[E(self.sbt("eo2%d_%d" % (d, k), [128, D], F32)) for k in range(2)] for d in range(2)]
                ss = [E(self.sbt("ess%d" % k, [128, 16], F32)) for k in range(2)]
                for t in range(NT // 128):
                    p = t % 2
                    row = t * 128
                    RR_ = lambda nm: (nm, p)
                    self.dma('sp', kt[p][:], self.RK[row:row + 128, :], [('RK', t)], [RR_('ekt')])
                    for d in range(2):
                        self.dma('act', at[d][p][:], self.RAI[d][row:row + 128, :], [(self.RAI[d].tensor.name, t)], [RR_('eat%d' % d)])
                    self.tt('dve', kk[p][:], kt[p][:], bc1[:, 0, :], ALU.mult, [RR_('ekt'), 'bc1'], [RR_('ekk')])
                    self.act(sq[p][:], kk[p][:], AF.Square, [RR_('ekk')], [RR_('esq')])
                    self.em.op('dve', lambda e, p=p: e.tensor_reduce(out=ss[p][:], in_=R3(sq[p][:]), axis=AX.X, op=ALU.add), [RR_('esq')], [RR_('ess')])
                    self.ts('dve', ss[p][:], ss[p][:], 1e-24, None, ALU.max, None, [RR_('ess')], [RR_('ess')])
                    self.act(ss[p][:], ss[p][:], AF.Ln, [RR_('ess')], [RR_('ess')])
                    self.act(ss[p][:], ss[p][:], AF.Exp, [RR_('ess')], [RR_('ess')], scale=-0.5)
                    self.tt('dve', R3(kk[p][:]), R3(kk[p][:]), bc64(ss[p][:]), ALU.mult, [RR_('ekk'), RR_('ess')], [RR_('ekk')])
                    self.dma('sp', self.KKN[row:row + 128, :], kk[p][:], [RR_('ekk')], [('KKN', t)])
                    for d in range(2):
                        a_ = at[d][p]
                        self.stt('dve', o1[d][p][:], a_[:], -1.0, bc1[:, 1, :], ALU.add, ALU.mult, [RR_('eat%d' % d), 'bc1'], [RR_('eo1%d' % d)])
                        self.stt('dve', o1[d][p][:], o1[d][p][:], 1.0, kt[p][:], ALU.add, ALU.mult, [RR_('eo1%d' % d), RR_('ekt')], [RR_('eo1%d' % d)])
                        self.dma('sp', self.KD[d][row:row + 128, :], o1[d][p][:], [RR_('eo1%d' % d)], [(self.KD[d].tensor.name, t)])
                        self.tt('pool', o2[d][p][:], kk[p][:], a_[:], ALU.mult, [RR_('ekk'), RR_('eat%d' % d)], [RR_('eo2%d' % d)])
                        self.dma('sp', self.BV[d][row:row + 128, :], o2[d][p][:], [RR_('eo2%d' % d)], [(self.BV[d].tensor.name, t)])
                em.flush()
            with ExitStack() as s:
                E = s.enter_context
                msk = E(self.sbt("msk", [128, 4, 128], F32))
                bdm = E(self.sbt("bdm", [128, 128], F32))
                self.dma('sp', msk[:], self.masks.rearrange("m p t -> p m t"), (), ['msk'])
                self.dma('sp', bdm[:], self.bdmask, (), ['bdm'])
                MK4 = E(self.sbt("MK4", [128, 4, 128], F32))
                ST2 = E(self.sbt("ST2", [128, 8, 128], F32))
                ld2 = [{nm: E(self.sbt("ld%d_" % k + nm, [128, D], F32)) for nm in ('lw', 'kd', 'bv', 'kkn', 'v', 'r')} for k in range(2)]
                nchunk = [0]
                cum = E(self.sbt("cum", [128, D], F32))
                tA = [E(self.sbt("tA%d" % k, [128, D], F32)) for k in range(2)]
                kb = E(self.sbt("kbar", [128, D], F32))
                bb = E(self.sbt("bbar", [128, D], F32))
                QT2 = E(self.sbt("QT2", [128, 8, 2, 128], F32))
                khT = E(self.sbt("khT", [128, 8, 128], F32))
                bhT = E(self.sbt("bhT", [128, 8, 128], F32))
                AT3 = E(self.sbt("AT3", [128, 16, 3, 128], F32))
                Lb = [E(self.sbt("Lb%d" % k, [128, 16, 128], F32)) for k in range(2)]
                Ub = [E(self.sbt("Ub%d" % k, [128, 16, 128], F32)) for k in range(2)]
                XT = E(self.sbt("XT", [128, 16, 128], F32))
                rhs_sb = E(self.sbt("rhs_sb", [128, D], F32))
                sa_sb = E(self.sbt("sa_sb", [128, D], F32))
                y_sb = E(self.sbt("y_sb", [128, D], F32))
                gC = E(self.sbt("gC", [128, 8], F32))
                sttmp = E(self.sbt("sttmp", [128, 4, 128], F32))
                for d in range(2):
                    mS, mI, mL = (0, 1, 2) if d == 0 else (2, 3, 0)
                    for q_, mi_ in enumerate((mS, mI, mS, mI)):
                        self.cp('dve', MK4[:, q_, :], msk[:, mi_, :], ['msk'], ['MK4'])
                    self.ms('dve', ST2[:], 0.0, ['ST2'])
                    order = [16, 17] + list(range(16)) if d == 0 else [17, 16] + list(range(15, -1, -1))
                    for ci in order:
                        row = ci * 128
                        emit = ci < 16
                        names = ['lw', 'kd', 'bv', 'kkn', 'v'] + (['r'] if emit else [])
                        srcs = {'lw': self.RLW[d], 'kd': self.KD[d], 'bv': self.BV[d], 'kkn': self.KKN, 'v': self.RV, 'r': self.RR}
                        pp = nchunk[0] % 2
                        nchunk[0] += 1
                        ld = ld2[pp]
                        for qi, nm in enumerate(names):
                            self.dma('sp', ld[nm][:], srcs[nm][row:row + 128, :], [(srcs[nm].tensor.name, ci)], [('ld', nm, pp)])
                        lw = ld['lw']
                        bks = [self.bank(), self.bank()]
                        for nh in range(2):
                            self.mm(PS[bks[nh]][:, :], msk[:, mI, :], lw[:, nh * 512:(nh + 1) * 512], True, True, ['msk', ('ld', 'lw', pp)], [('ps', bks[nh])])
                            self.cp('act' if nh == 0 else 'dve', cum[:, nh * 512:(nh + 1) * 512], PS[bks[nh]][:, :], [('ps', bks[nh])], ['cum'])
                        bt = [self.bank(), self.bank()]
                        for nh in range(2):
                            self.mm(PS[bt[nh]][:, :], self.ones[:], lw[:, nh * 512:(nh + 1) * 512], True, True, ['ones', ('ld', 'lw', pp)], [('ps', bt[nh])])
                        bg = self.bank()
                        for hp in range(8):
                            self.mm(PS[bg][:, hp:hp + 1], lw[:, hp * 128:(hp + 1) * 128], self.ones[:, 0:1], True, True, ['ones', ('ld', 'lw', pp)], [('ps', bg)])
                        self.act(gC[:], PS[bg][:, 0:8], AF.Exp, [('ps', bg)], ['gC'])
                        for nh in range(2):
                            sl = slice(nh * 512, (nh + 1) * 512)
                            self.tt('dve', tA[0][:, sl], PS[bt[nh]][:, :], cum[:, sl], ALU.subtract, [('ps', bt[nh]), 'cum'], [('tA', 0)])
                        self.act(tA[0][:], tA[0][:], AF.Exp, [('tA', 0)], [('tA', 0)])
                        self.tt('dve', kb[:], ld['kd'][:], tA[0][:], ALU.mult, [('ld', 'kd', pp), ('tA', 0)], ['kbar'])
                        self.tt('pool', bb[:], ld['bv'][:], tA[0][:], ALU.mult, [('ld', 'bv', pp), ('tA', 0)], ['bbar'])

                        def transp(src, srcr, dst_fn, dstr):
                            for g4 in range(2):
                                bk = self.bank()
                                for q4 in range(4):
                                    hp = g4 * 4 + q4
                                    self.tr(PS[bk][:, q4 * 128:(q4 + 1) * 128], src[:, hp * 128:(hp + 1) * 128], [srcr], [('ps', bk)])
                                dst_fn(g4, PS[bk][:, :].rearrange("p (q t) -> p q t", t=128), bk, dstr)
                        self.tt('dve', tA[0][:], cum[:], lw[:], ALU.subtract, ['cum', ('ld', 'lw', pp)], [('tA', 0)])
                        self.act(tA[0][:], tA[0][:], AF.Exp, [('tA', 0)], [('tA', 0)])
                        self.stt('dve', tA[0][:], ld['kkn'][:], -1.0, tA[0][:], ALU.mult, ALU.mult, [('ld', 'kkn', pp), ('tA', 0)], [('tA', 0)])
                        transp(tA[0], ('tA', 0), lambda g4, pv, bk, dr: self.cp('act', QT2[:, g4 * 4:(g4 + 1) * 4, 0, :], pv, [('ps', bk)], [dr]), 'QT2')
                        if emit:
                            self.act(tA[1][:], cum[:], AF.Exp, ['cum'], [('tA', 1)])
                            self.tt('dve', tA[1][:], tA[1][:], ld['r'][:], ALU.mult, [('tA', 1), ('ld', 'r', pp)], [('tA', 1)])
                            transp(tA[1], ('tA', 1), lambda g4, pv, bk, dr: self.cp('dve', QT2[:, g4 * 4:(g4 + 1) * 4, 1, :], pv, [('ps', bk)], [dr]), 'QT2')
                        self.act(tA[0][:], cum[:], AF.Exp, ['cum'], [('tA', 0)], scale=-1.0)
                        self.tt('dve', tA[1][:], tA[0][:], ld['kd'][:], ALU.mult, [('tA', 0), ('ld', 'kd', pp)], [('tA', 1)])
                        transp(tA[1], ('tA', 1), lambda g4, pv, bk, dr: self.cp('act', khT[:, g4 * 4:(g4 + 1) * 4, :], pv, [('ps', bk)], [dr]), 'khT')
                        self.tt('pool', tA[0][:], tA[0][:], ld['bv'][:], ALU.mult, [('tA', 0), ('ld', 'bv', pp)], [('tA', 0)])
                        transp(tA[0], ('tA', 0), lambda g4, pv, bk, dr: self.cp('dve', bhT[:, g4 * 4:(g4 + 1) * 4, :], pv, [('ps', bk)], [dr]), 'bhT')
                        NQ = 256 if emit else 128
                        for h in range(16):
                            hp, hh = h // 2, h % 2
                            pr = slice(hh * 64, (hh + 1) * 64)
                            bk = self.bank()
                            q2 = QT2[pr, hp, :, :].rearrange("p a t -> p (a t)")
                            self.mm(PS[bk][:, 0:NQ], khT[pr, hp, :], q2[:, 0:NQ], True, True, ['khT', 'QT2'], [('ps', bk)])
                            self.mm(PS[bk][:, 256:256 + NQ], bhT[pr, hp, :], q2[:, 0:NQ], True, True, ['bhT', 'QT2'], [('ps', bk)])
                            pv = PS[bk][:, :].rearrange("p (q t) -> p q t", t=128)
                            eng = 'dve'
                            if emit:
                                self.tt(eng, AT3[:, h, 0:2, :], pv[:, 0:2, :], MK4[:, 0:2, :], ALU.mult, [('ps', bk), 'MK4'], [('AT3', h)])
                                self.tt(eng, Ub[0][:, h, :], pv[:, 2, :], MK4[:, 2, :], ALU.mult, [('ps', bk), 'MK4'], [('Ub', 0, h // 4)])
                                self.tt(eng, AT3[:, h, 2, :], pv[:, 3, :], MK4[:, 3, :], ALU.mult, [('ps', bk), 'MK4'], [('AT3', h)])
                            else:
                                self.tt(eng, AT3[:, h, 0, :], pv[:, 0, :], MK4[:, 0, :], ALU.mult, [('ps', bk), 'MK4'], [('AT3', h)])
                                self.tt(eng, Ub[0][:, h, :], pv[:, 2, :], MK4[:, 2, :], ALU.mult, [('ps', bk), 'MK4'], [('Ub', 0, h // 4)])
                        for g4 in range(4):
                            bk = self.bank()
                            for q4 in range(4):
                                h = g4 * 4 + q4
                                hp, hh = h // 2, h % 2
                                pr = slice(hh * 64, (hh + 1) * 64)
                                self.mm(PS[bk][:, q4 * 128:(q4 + 1) * 128], QT2[pr, hp, 0, :], bhT[pr, hp, :], True, True, ['QT2', 'bhT'], [('ps', bk)])
                            pv = PS[bk][:, :].rearrange("p (q t) -> p q t", t=128)
                            self.tt('dve', Lb[0][:, g4 * 4:(g4 + 1) * 4, :], pv, msk[:, mL, :].unsqueeze(1).to_broadcast([128, 4, 128]), ALU.mult,
                                    [('ps', bk), 'msk'], [('Lb', 0, g4)])
                            self.tt('pool', XT[:, g4 * 4:(g4 + 1) * 4, :], Ub[0][:, g4 * 4:(g4 + 1) * 4, :],
                                    self.ident[:, :].unsqueeze(1).to_broadcast([128, 4, 128]), ALU.add, [('Ub', 0, g4), 'ident'], [('XT', g4)])
                        cur = 0
                        for lev in range(6):
                            nxt = 1 - cur
                            last = lev == 5
                            bL = [self.bank() for _ in range(4)]
                            for q4 in range(4):
                                for g4 in range(4):
                                    h = g4 * 4 + q4
                                    self.mm(PS[bL[g4]][:, q4 * 128:(q4 + 1) * 128], Ub[cur][:, h, :], Lb[cur][:, h, :], True, True,
                                            [('Ub', cur, g4), ('Lb', cur, g4)], [('ps', bL[g4])])
                            for g4 in range(4):
                                self.cp('act', Lb[nxt][:, g4 * 4:(g4 + 1) * 4, :], PS[bL[g4]][:, :].rearrange("p (q t) -> p q t", t=128),
                                        [('ps', bL[g4])], [('Lb', nxt, g4)])
                            if not last:
                                bU = [self.bank() for _ in range(4)]
                                for q4 in range(4):
                                    for g4 in range(4):
                                        h = g4 * 4 + q4
                                        self.mm(PS[bU[g4]][:, q4 * 128:(q4 + 1) * 128], Lb[cur][:, h, :], Ub[cur][:, h, :], True, True,
                                                [('Ub', cur, g4), ('Lb', cur, g4)], [('ps', bU[g4])])
                                for g4 in range(4):
                                    self.cp('dve', Ub[nxt][:, g4 * 4:(g4 + 1) * 4, :], PS[bU[g4]][:, :].rearrange("p (q t) -> p q t", t=128),
                                            [('ps', bU[g4])], [('Ub', nxt, g4)])
                            bX = [self.bank() for _ in range(4)]
                            for q4 in range(4):
                                for g4 in range(4):
                                    h = g4 * 4 + q4
                                    self.mm(PS[bX[g4]][:, q4 * 128:(q4 + 1) * 128], Lb[nxt][:, h, :], XT[:, h, :], True, True,
                                            [('Lb', nxt, g4), ('XT', g4)], [('ps', bX[g4])])
                            for g4 in range(4):
                                self.tt('dve', XT[:, g4 * 4:(g4 + 1) * 4, :], XT[:, g4 * 4:(g4 + 1) * 4, :],
                                        PS[bX[g4]][:, :].rearrange("p (q t) -> p q t", t=128), ALU.add,
                                        [('ps', bX[g4]), ('XT', g4)], [('XT', g4)])
                            cur = nxt
                        v_ = ld['v']
                        br = [self.bank(), self.bank()]
                        for hp in range(8):
                            bk = br[hp // 4]
                            c0 = (hp % 4) * 128
                            self.mm(PS[bk][:, c0:c0 + 128], QT2[:, hp, 0, :], ST2[:, hp, :], True, False, ['QT2', 'ST2'], [('ps', bk)])
                            for hh in range(2):
                                h = hp * 2 + hh
                                self.mm(PS[bk][:, c0 + hh * 64:c0 + (hh + 1) * 64], AT3[:, h, 0, :], v_[:, h * 64:(h + 1) * 64], False, hh == 1,
                                        [('AT3', h), ('ld', 'v', pp)], [('ps', bk)])
                        self.cp('act', rhs_sb[:, 0:512], PS[br[0]][:, :], [('ps', br[0])], ['rhs_sb'])
                        self.cp('dve', rhs_sb[:, 512:1024], PS[br[1]][:, :], [('ps', br[1])], ['rhs_sb'])
                        bs_ = [self.bank(), self.bank()]
                        for h in range(16):
                            bk = bs_[h // 8]
                            c0 = (h % 8) * 64
                            self.mm(PS[bk][:, c0:c0 + 64], XT[:, h, :], rhs_sb[:, h * 64:(h + 1) * 64], True, True, [('XT', h // 4), 'rhs_sb'], [('ps', bk)])
                        self.cp('act', sa_sb[:, 0:512], PS[bs_[0]][:, :], [('ps', bs_[0])], ['sa_sb'])
                        self.cp('dve', sa_sb[:, 512:1024], PS[bs_[1]][:, :], [('ps', bs_[1])], ['sa_sb'])
                        if emit:
                            by = [self.bank(), self.bank()]
                            for hp in range(8):
                                bk = by[hp // 4]
                                c0 = (hp % 4) * 128
                                self.mm(PS[bk][:, c0:c0 + 128], QT2[:, hp, 1, :], ST2[:, hp, :], True, False, ['QT2', 'ST2'], [('ps', bk)])
                                for hh in range(2):
                                    h = hp * 2 + hh
                                    self.mm(PS[bk][:, c0 + hh * 64:c0 + (hh + 1) * 64], AT3[:, h, 1, :], v_[:, h * 64:(h + 1) * 64], False, False,
                                            [('AT3', h), ('ld', 'v', pp)], [('ps', bk)])
                                    self.mm(PS[bk][:, c0 + hh * 64:c0 + (hh + 1) * 64], AT3[:, h, 2, :], sa_sb[:, h * 64:(h + 1) * 64], False, hh == 1,
                                            [('AT3', h), 'sa_sb'], [('ps', bk)])
                            self.cp('act', y_sb[:, 0:512], PS[by[0]][:, :], [('ps', by[0])], ['y_sb'])
                            self.cp('dve', y_sb[:, 512:1024], PS[by[1]][:, :], [('ps', by[1])], ['y_sb'])
                            self.dma('sp', self.YS[d][row:row + 128, :], y_sb[:], ['y_sb'], [(self.YS[d].tensor.name, ci)])
                        bst = [self.bank(), self.bank()]
                        for hp in range(8):
                            bk = bst[hp // 4]
                            c0 = (hp % 4) * 128
                            self.mm(PS[bk][:, c0:c0 + 128], kb[:, hp * 128:(hp + 1) * 128], v_[:, hp * 128:(hp + 1) * 128], True, False,
                                    ['kbar', ('ld', 'v', pp)], [('ps', bk)])
                            self.mm(PS[bk][:, c0:c0 + 128], bb[:, hp * 128:(hp + 1) * 128], sa_sb[:, hp * 128:(hp + 1) * 128], False, True,
                                    ['bbar', 'sa_sb'], [('ps', bk)])
                        for g4 in range(2):
                            self.tt('dve', sttmp[:], PS[bst[g4]][:, :].rearrange("p (q t) -> p q t", t=128),
                                    bdm[:, :].unsqueeze(1).to_broadcast([128, 4, 128]), ALU.mult, [('ps', bst[g4]), 'bdm'], ['sttmp'])
                            self.tt('pool', ST2[:, g4 * 4:(g4 + 1) * 4, :], ST2[:, g4 * 4:(g4 + 1) * 4, :],
                                    gC[:, g4 * 4:(g4 + 1) * 4].unsqueeze(2).to_broadcast([128, 4, 128]), ALU.mult, ['ST2', 'gC'], ['ST2'])
                            self.tt('dve', ST2[:, g4 * 4:(g4 + 1) * 4, :], ST2[:, g4 * 4:(g4 + 1) * 4, :], sttmp[:], ALU.add, ['ST2', 'sttmp'], ['ST2'])
                em.flush()
            with ExitStack() as s:
                E = s.enter_context
                w_o = E(self.sbt("rw_wo", [128, KC, D], BF16))
                bc2 = E(self.sbt("bc2", [128, 3, D], F32))
                self.dma('sp', bc2[:, 0, :], self.rw_r_k.to_broadcast([128, D]), (), ['bc2'])
                self.dma('sp', bc2[:, 1, :], self.rw_gn_g.to_broadcast([128, D]), (), ['bc2'])
                self.dma('sp', bc2[:, 2, :], self.rw_gn_b.to_broadcast([128, D]), (), ['bc2'])
                src = self.rw_w_o.rearrange("(k p) n -> p k n", p=128)
                for hh in range(2):
                    self.dma('pool', w_o[:, :, hh * 512:(hh + 1) * 512], src[:, :, hh * 512:(hh + 1) * 512], (), [('w_o', hh)])
                tl = self.alloc_pn_tiles(E, True)
                L_ = {nm: E(self.sbt("d_" + nm, [128, D], F32)) for nm in ('y0', 'y1', 'r', 'kd0', 'kd1', 'v', 'g')}
                acc = E(self.sbt("d_acc", [128, D], F32))
                tq = E(self.sbt("d_tq", [128, D], F32))
                st16 = E(self.sbt("d_st", [128, 4, 16], F32))
                zT = E(self.sbt("d_zT", [128, KC, 128], BF16))
                for t in range(T // 128):
                    row = t * 128
                    srcs = {'y0': self.YS[0], 'y1': self.YS[1], 'r': self.RR, 'kd0': self.KD[0], 'kd1': self.KD[1], 'v': self.RV, 'g': self.RG}
                    for qi, nm in enumerate(L_):
                        self.dma('sp' if qi % 2 == 0 else 'act', L_[nm][:], srcs[nm][row:row + 128, :], [(srcs[nm].tensor.name, t)], [('dl', nm)])
                    for d in range(2):
                        y = L_['y%d' % d]
                        yr = ('dl', 'y%d' % d)
                        self.em.op('dve', lambda e, y=y: e.tensor_reduce(out=st16[:, 0, :], in_=R3(y[:]), axis=AX.X, op=ALU.add), [yr], ['st16'])
                        self.act(tq[:], y[:], AF.Square, [yr], ['tq'])
                        self.em.op('dve', lambda e: e.tensor_reduce(out=st16[:, 1, :], in_=R3(tq[:]), axis=AX.X, op=ALU.add), ['tq'], ['st16'])
                        self.ts('dve', st16[:, 0, :], st16[:, 0, :], 1.0 / 64, None, ALU.mult, None, ['st16'], ['st16'])
                        self.tt('dve', st16[:, 2, :], st16[:, 0, :], st16[:, 0, :], ALU.mult, ['st16'], ['st16'])
                        self.stt('dve', st16[:, 1, :], st16[:, 1, :], 1.0 / 64, st16[:, 2, :], ALU.mult, ALU.subtract, ['st16'], ['st16'])
                        self.act(st16[:, 1, :], st16[:, 1, :], AF.Ln, ['st16', 'epsc'], ['st16'], bias=self.epsc[:, 1:2])
                        self.act(st16[:, 1, :], st16[:, 1, :], AF.Exp, ['st16'], ['st16'], scale=-0.5)
                        self.tt('dve', R3(y[:]), R3(y[:]), bc64(st16[:, 0, :]), ALU.subtract, [yr, 'st16'], [yr])
                        self.tt('dve', R3(y[:]), R3(y[:]), bc64(st16[:, 1, :]), ALU.mult, [yr, 'st16'], [yr])
                        self.tt('pool', y[:], y[:], bc2[:, 1, :], ALU.mult, [yr, 'bc2'], [yr])
                        if d == 0:
                            self.tt('pool', acc[:], y[:], bc2[:, 2, :], ALU.add, [yr, 'bc2'], ['acc'])
                        else:
                            self.tt('pool', y[:], y[:], bc2[:, 2, :], ALU.add, [yr, 'bc2'], [yr])
                            self.tt('pool', acc[:], acc[:], y[:], ALU.add, [yr, 'acc'], ['acc'])
                        kd_ = L_['kd%d' % d]
                        kr_ = ('dl', 'kd%d' % d)
                        self.tt('dve', tq[:], L_['r'][:], kd_[:], ALU.mult, [('dl', 'r'), kr_], ['tq'])
                        self.tt('dve', tq[:], tq[:], bc2[:, 0, :], ALU.mult, ['tq', 'bc2'], ['tq'])
                        self.em.op('dve', lambda e: e.tensor_reduce(out=st16[:, 3, :], in_=R3(tq[:]), axis=AX.X, op=ALU.add), ['tq'], ['st16'])
                        self.tt('dve', R3(tq[:]), R3(L_['v'][:]), bc64(st16[:, 3, :]), ALU.mult, [('dl', 'v'), 'st16'], ['tq'])
                        self.tt('dve', acc[:], acc[:], tq[:], ALU.add, ['acc', 'tq'], ['acc'])
                    self.tt('dve', acc[:], acc[:], L_['g'][:], ALU.mult, ['acc', ('dl', 'g')], ['acc'])
                    for half in range(2):
                        bk = self.bank()
                        for kk_ in range(4):
                            k = half * 4 + kk_
                            self.tr(PS[bk][:, kk_ * 128:(kk_ + 1) * 128], acc[:, k * 128:(k + 1) * 128], ['acc'], [('ps', bk)])
                        self.cp('act' if half == 0 else 'dve', zT[:, half * 4:(half + 1) * 4, :], PS[bk][:, :].rearrange("p (q t) -> p q t", t=128),
                                [('ps', bk)], ['zT'])
                    bs = []
                    for nh in range(2):
                        bk = self.bank()
                        bs.append(bk)
                        for k in range(KC):
                            self.mm(PS[bk][:, :], zT[:, k, :], w_o[:, k, nh * 512:(nh + 1) * 512], k == 0, k == KC - 1, ['zT', ('w_o', nh)], [('ps', bk)])
                    self.postnorm_tile(tl, b * T + row, b, 0, hsrc, [PS[bs[0]][:, :], PS[bs[1]][:, :]], [[('ps', bs[0])], [('ps', bs[1])]],
                                       self.H, 'H', bias_bc=None, router=True)
                em.flush()


def tr128(v, nchunk):
    return np.ascontiguousarray(v.reshape(nchunk, 128).T)


def make_bm_tab(rpb):
    NEG = np.float32(-30000.0)
    out = np.full((16, 5, 2, 64, 10, 64), NEG, np.float32)
    rep = {0: 0, 1: 1, 2: 2, 3: 14, 4: 15}
    c = np.arange(64)
    kc = np.arange(64)
    win_c0 = np.clip(c - 8, 0, 48)
    col_ok = (kc[None, :] >= win_c0[:, None]) & (kc[None, :] < win_c0[:, None] + 16)
    cidx = np.clip(kc[None, :] - c[:, None] + 15, 0, 30)
    for cls, i_ in rep.items():
        ks = min(max(2 * i_ - 4, 0), 22)
        for rr in range(2):
            r = 2 * i_ + rr
            row0 = min(max(r - 4, 0), 24)
            for krel in range(10):
                kr = ks + krel
                if not (row0 <= kr < row0 + 8):
                    continue
                ridx = kr - r + 7
                vals = rpb[:, ridx, :][:, cidx]
                out[:, cls, rr, :, krel, :] = np.where(col_ok[None], vals, NEG)
    return np.ascontiguousarray(out.reshape(16, 5, 128, 640))


def prep_shared(inp):
    f = lambda a: np.ascontiguousarray(np.asarray(a, dtype=np.float32))
    sh = {}
    sh['ident'] = np.eye(128, dtype=np.float32)
    sh['ada_w'] = f(inp['ada_w'])
    sh['ada_b'] = f(inp['ada_b'])
    sh['ada_bT'] = np.stack([tr128(f(inp['ada_b'])[i], 48) for i in range(DEPTH)])
    sh['ln_g'] = f(inp['ln_g'])
    sh['ln_b'] = f(inp['ln_b'])
    sh['conv_w_in'] = f(inp['conv_w_in'])
    sh['conv_b_inT'] = np.stack([tr128(f(inp['conv_b_in'])[i], 16) for i in range(2)])
    wdw = f(inp['conv_w_dw'])
    sh['conv_w_dwT'] = np.ascontiguousarray(wdw.reshape(2, CONVW, KC, 128).transpose(0, 3, 2, 1))
    sh['conv_b_dwT'] = np.stack([tr128(f(inp['conv_b_dw'])[i], KC) for i in range(2)])
    sh['conv_ln_gT'] = np.stack([tr128(f(inp['conv_ln_g'])[i], KC) for i in range(2)])
    sh['conv_ln_bT'] = np.stack([tr128(f(inp['conv_ln_b'])[i], KC) for i in range(2)])
    sh['conv_w_out'] = f(inp['conv_w_out'])
    sh['conv_b_out'] = f(inp['conv_b_out'])
    sh['na_w_qkv'] = f(inp['na_w_qkv'])
    sh['na_w_o'] = f(inp['na_w_o'])
    sh['bm_tab'] = make_bm_tab(f(inp['na_rpb'])[0])
    mp = f(inp['rw_mu_prev'])[0]
    mn = f(inp['rw_mu_next'])[0]
    sh['rw_mu_prevT'] = np.ascontiguousarray(mp.reshape(6, KC, 128).transpose(2, 0, 1))
    sh['rw_mu_nextT'] = np.ascontiguousarray(mn.reshape(6, KC, 128).transpose(2, 0, 1))
    for nm in ("rw_w_r", "rw_w_k", "rw_w_v", "rw_w_o", "rw_w0", "rw_a0", "rw_w1", "rw_a1", "rw_w2", "rw_a2", "rw_g1", "rw_g2"):
        sh[nm] = np.ascontiguousarray(f(inp[nm])[0])
    for nm in ("rw_k_k", "rw_k_a", "rw_r_k", "rw_gn_g", "rw_gn_b"):
        sh[nm] = np.ascontiguousarray(f(inp[nm])[0].reshape(1, D))
    tri = np.triu(np.ones((128, 128), np.float32))
    sh['masks'] = np.stack([np.triu(tri, 1), tri, np.tril(tri.T, -1), tri.T]).astype(np.float32)
    bd = np.zeros((128, 128), np.float32)
    bd[:64, :64] = 1
    bd[64:, 64:] = 1
    sh['bdmask'] = bd
    sh['moe_router'] = f(inp['moe_router'])
    sh['moe_w_gate'] = f(inp['moe_w_gate'])
    sh['moe_w_up'] = f(inp['moe_w_up'])
    sh['moe_w_down'] = f(inp['moe_w_down'])
    return sh


def prep_core(inp, core):
    f = lambda a: np.asarray(a, dtype=np.float32)
    b0 = core * NB
    x = f(inp['x'])[b0:b0 + NB].reshape(NB * T, D)
    cx = f(inp['ctx'])[b0:b0 + NB].reshape(NB * L, D)
    hin = np.ascontiguousarray(np.concatenate([x, cx], axis=0))
    conds = np.stack([f(inp['c'])[b0], f(inp['c'])[b0 + 1], f(inp['c_ctx'])], axis=1)
    cT = np.ascontiguousarray(conds.reshape(KC, 128, 3).transpose(1, 0, 2))
    return {'hin': hin, 'cT': cT}


_NC_CACHE = {}


def kernel(**inputs):
    kb = KB()
    nc = kb.build()
    sh = prep_shared(inputs)
    in_maps = []
    for c in range(NCORES):
        m = dict(sh)
        m.update(prep_core(inputs, c))
        in_maps.append(m)
    res = run_bass_kernel_spmd(nc, in_maps, core_ids=list(range(NCORES)))
    outs = [r["out"].reshape(NB, T, D) for r in res.results]
    return np.concatenate(outs, axis=0).astype(np.float32)
```

```python
import numpy as np
import concourse.bass as bass
import concourse.mybir as mybir
from concourse.bass_utils import run_bass_kernel_spmd
from contextlib import ExitStack

F32 = mybir.dt.float32
BF16 = mybir.dt.bfloat16
U32 = mybir.dt.uint32
I32 = mybir.dt.int32
ALU = mybir.AluOpType
AF = mybir.ActivationFunctionType
AX = mybir.AxisListType

ENGS = ('pe', 'act', 'dve', 'pool', 'sp')
DMAQ = ('sp', 'act', 'pool')


class Em:
    def __init__(self, nc, ndma=8):
        self.nc = nc
        self.K = ndma
        self.stack = ExitStack()
        ent = self.stack.enter_context
        self.csem = {e: ent(nc.semaphore("c_" + e)) for e in ENGS}
        self.dsem = {q: [ent(nc.semaphore("d_%s%d" % (q, i))) for i in range(ndma)] for q in DMAQ}
        self.ccnt = {e: 0 for e in ENGS}
        self.dcnt = {q: 0 for q in DMAQ}
        self.seen = {e: {} for e in ENGS}
        self.lastw = {}
        self.rd = {}
        self.prog = {e: [] for e in ENGS}
        self.ninstr = 0
        self.sigcnt = {}
        self.pe_sync_toks = set()

    def sem(self, key):
        if key[0] == 'c':
            return self.csem[key[1]]
        return self.dsem[key[1]][key[2]]

    def _deps(self, eng, reads, writes, pe_sync=False):
        need = {}
        pe_keep = pe_sync
        for r in reads:
            t = self.lastw.get(r)
            if t is not None and need.get(t[0], 0) < t[1]:
                need[t[0]] = t[1]
        for w in writes:
            t = self.lastw.get(w)
            if t is not None:
                if t in self.pe_sync_toks:
                    pe_keep = True
                if need.get(t[0], 0) < t[1]:
                    need[t[0]] = t[1]
            for k, v in self.rd.get(w, {}).items():
                if need.get(k, 0) < v:
                    need[k] = v
        waits = []
        seen = self.seen[eng]
        for k, v in need.items():
            if eng == 'pe' and k == ('c', 'pe') and not pe_keep:
                continue
            if seen.get(k, 0) < v:
                seen[k] = v
                waits.append((k, v))
        return waits

    def _reg(self, tok, reads, writes):
        for r in reads:
            self.rd.setdefault(r, {})[tok[0]] = tok[1]
        for w in writes:
            self.lastw[w] = tok
            self.rd[w] = {}

    def op(self, eng, fn, reads=(), writes=(), pe_sync=False):
        waits = self._deps(eng, reads, writes, pe_sync)
        self.ccnt[eng] += 1
        tok = (('c', eng), self.ccnt[eng])
        if pe_sync:
            self.pe_sync_toks.add(tok)
        self.prog[eng].append((waits, fn, tok, 1))
        self._reg(tok, reads, writes)
        self.ninstr += 1

    def dma(self, q, fn, reads=(), writes=()):
        waits = self._deps(q, reads, writes)
        n = self.dcnt[q]
        self.dcnt[q] += 1
        slot = n % self.K
        val = 16 * (n // self.K + 1)
        key = ('d', q, slot)
        if val > 16 and self.seen[q].get(key, 0) < val - 16:
            self.seen[q][key] = val - 16
            waits.append((key, val - 16))
        tok = (key, val)
        self.prog[q].append((waits, fn, tok, 16))
        self._reg(tok, reads, writes)
        self.ninstr += 1

    def barrier(self):
        for e in ENGS:
            waits = []
            seen = self.seen[e]
            for e2 in ENGS:
                k = ('c', e2)
                v = self.ccnt[e2]
                if e2 != e and v > 0 and seen.get(k, 0) < v:
                    seen[k] = v
                    waits.append((k, v))
            for q in DMAQ:
                n = self.dcnt[q]
                for slot in range(self.K):
                    cnt = (n - slot + self.K - 1) // self.K if n > slot else 0
                    if cnt > 0:
                        k = ('d', q, slot)
                        v = 16 * cnt
                        if seen.get(k, 0) < v:
                            seen[k] = v
                            waits.append((k, v))
            if waits:
                self.prog[e].append((waits, None, None, 0))
        self.lastw = {}
        self.rd = {}

    def flush(self):
        self.barrier()
        nc = self.nc
        waited = {e: set() for e in ENGS}
        for e in ENGS:
            for waits, fn, tok, inc in self.prog[e]:
                for k, v in waits:
                    if k[0] == 'c':
                        waited[k[1]].add(v)
        newval = {e: {} for e in ENGS}
        for e in ENGS:
            sc = self.sigcnt.get(e, 0)
            for waits, fn, tok, inc in self.prog[e]:
                if fn is not None and tok[0][0] == 'c' and tok[1] in waited[e]:
                    sc += 1
                    newval[e][tok[1]] = sc
            self.sigcnt[e] = sc

        def tr_(k, v):
            if k[0] == 'c':
                return newval[k[1]][v]
            return v
        with nc.Block() as block:
            for e, meth in (('pe', block.tensor), ('act', block.scalar), ('dve', block.vector),
                            ('pool', block.gpsimd), ('sp', block.sync)):
                prog = self.prog[e]
                self.prog[e] = []

                def body(eng, prog=prog, e=e):
                    for waits, fn, tok, inc in prog:
                        if fn is None:
                            for k, v in waits:
                                eng.wait_ge(self.sem(k), tr_(k, v))
                            continue
                        for k, v in waits[:-1]:
                            eng.wait_ge(self.sem(k), tr_(k, v))
                        ins = fn(eng)
                        if waits:
                            k, v = waits[-1]
                            ins._wait_ge(self.sem(k), tr_(k, v))
                        if tok[0][0] == 'c':
                            if tok[1] in newval[e]:
                                ins.then_inc(self.sem(tok[0]), 1)
                        else:
                            ins.then_inc(self.sem(tok[0]), inc)
                meth(body)

    def close(self):
        self.stack.close()


D = 1024
T = 2048
L = 256
NB = 2
KC = 8
NE = 16
CAP = 256
CAPC = 32
DE = 2048
DEPTH = 4
NROW = NB * T + NB * L
CTX0 = NB * T
ALPHA = float(8 ** 0.25)
LN_EPS = 1e-5
GRID_W = 64
CONVW = 31
NCORES = 8


class KB:
    def __init__(self, layers=(0, 1, 2, 3), n_exp=NE, debug=False):
        self.layers = list(layers)
        self.n_exp = n_exp
        self.debug = debug
        nc = bass.Bass("TRN2", target_bir_lowering=False)
        self.nc = nc
        self.em = Em(nc)
        self.dr = {}
        self._psi = 0

    def din(self, name, shape, dt=F32):
        t = self.nc.dram_tensor(name, list(shape), dt, kind="ExternalInput").ap()
        self.dr[name] = t
        return t

    def dout(self, name, shape, dt=F32):
        t = self.nc.dram_tensor(name, list(shape), dt, kind="ExternalOutput").ap()
        self.dr[name] = t
        return t

    def dint(self, name, shape, dt=F32):
        t = self.nc.dram_tensor(name, list(shape), dt, kind="ExternalOutput" if self.debug else "Internal").ap()
        self.dr[name] = t
        return t

    def sbt(self, name, shape, dt):
        self._uid = getattr(self, '_uid', 0) + 1
        return self.nc.sbuf_tensor("%s_%d" % (name, self._uid), shape, dt)

    def bank(self):
        i = self._psi
        self._psi = (i + 1) % 8
        return i

    def mm(self, out, lhsT, rhs, start, stop, r, w):
        self.em.op('pe', lambda e: e.matmul(out, lhsT=lhsT, rhs=rhs, start=start, stop=stop), r, w, pe_sync=(lhsT.dtype == F32))

    def tr(self, out, in_, r, w, ident=None):
        idn = self.ident if ident is None else ident
        np_ = in_.shape[0]
        self.em.op('pe', lambda e: e.transpose(out=out, in_=in_, identity=idn[0:np_, 0:np_]), list(r) + ['ident'], w)

    def act(self, out, in_, func, r, w, bias=None, scale=None, accum_out=None):
        kw = {}
        if bias is not None:
            kw['bias'] = bias
        if scale is not None:
            kw['scale'] = scale
        if accum_out is not None:
            kw['accum_out'] = accum_out
        self.em.op('act', lambda e: e.activation(out=out, in_=in_, func=func, **kw), r, w)

    def tt(self, eng, out, in0, in1, op, r, w):
        self.em.op(eng, lambda e: e.tensor_tensor(out=out, in0=in0, in1=in1, op=op), r, w)

    def ts(self, eng, out, in0, s1, s2, op0, op1, r, w):
        if op1 is None:
            self.em.op(eng, lambda e: e.tensor_scalar(out=out, in0=in0, scalar1=s1, scalar2=None, op0=op0), r, w)
        else:
            self.em.op(eng, lambda e: e.tensor_scalar(out=out, in0=in0, scalar1=s1, scalar2=s2, op0=op0, op1=op1), r, w)

    def stt(self, eng, out, in0, scalar, in1, op0, op1, r, w):
        self.em.op(eng, lambda e: e.scalar_tensor_tensor(out=out, in0=in0, scalar=scalar, in1=in1, op0=op0, op1=op1), r, w)

    def cp(self, eng, out, in_, r, w):
        if eng == 'act':
            self.em.op('act', lambda e: e.copy(out=out, in_=in_), r, w)
        else:
            self.em.op(eng, lambda e: e.tensor_copy(out=out, in_=in_), r, w)

    def ms(self, eng, ap, val, w):
        self.em.op(eng, lambda e: e.memset(ap, val), (), w)

    def dma(self, q, out, in_, r, w, **kw):
        self.em.dma(q, lambda e: e.dma_start(out=out, in_=in_, **kw), r, w)

    def build(self):
        nc = self.nc
        em = self.em
        st = ExitStack()
        self.st = st
        E = st.enter_context
        self.hin = self.din("hin", [NROW, D])
        self.cT = self.din("cT", [128, KC, 3])
        self.ident_d = self.din("ident", [128, 128])
        self.ada_w = self.din("ada_w", [DEPTH, D, 6 * D])
        self.ada_bT = self.din("ada_bT", [DEPTH, 128, 48])
        self.ada_b = self.din("ada_b", [DEPTH, 6 * D])
        self.ln_g = self.din("ln_g", [DEPTH, 2, D])
        self.ln_b = self.din("ln_b", [DEPTH, 2, D])
        self.conv_w_in = self.din("conv_w_in", [2, D, 2 * D])
        self.conv_b_inT = self.din("conv_b_inT", [2, 128, 16])
        self.conv_w_dwT = self.din("conv_w_dwT", [2, 128, KC, CONVW])
        self.conv_b_dwT = self.din("conv_b_dwT", [2, 128, KC])
        self.conv_ln_gT = self.din("conv_ln_gT", [2, 128, KC])
        self.conv_ln_bT = self.din("conv_ln_bT", [2, 128, KC])
        self.conv_w_out = self.din("conv_w_out", [2, D, D])
        self.conv_b_out = self.din("conv_b_out", [2, D])
        self.moe_router = self.din("moe_router", [DEPTH, D, NE])
        self.moe_w_gate = self.din("moe_w_gate", [DEPTH, NE, D, DE])
        self.moe_w_up = self.din("moe_w_up", [DEPTH, NE, D, DE])
        self.moe_w_down = self.din("moe_w_down", [DEPTH, NE, DE, D])
        self.out = self.dout("out", [NB * T, D])
        self.H = self.dint("H", [NROW, D])
        self.Y = self.dint("Y", [NROW, D])
        self.extra_io()

        self.ident = E(self.sbt("ident_s", [128, 128], F32))
        self.ones = E(self.sbt("ones_s", [128, 128], F32))
        self.scT = E(self.sbt("scT", [128, KC, 3], BF16))
        self.modT = E(self.sbt("modT", [128, 32, 3], F32))
        self.ga_bc = E(self.sbt("ga_bc", [128, 2, 3, D], BF16))
        self.lnp = E(self.sbt("lnp", [128, 2, D], F32))
        self.affT = E(self.sbt("affT", [48, T], F32))
        self.affTc = E(self.sbt("affTc", [16, NB * L], F32))
        self.wr = E(self.sbt("wr", [128, KC, NE], F32))
        self.PS = [E(nc.psum_tensor("ps%d" % i, [128, 512], F32)) for i in range(8)]

        self.dma('sp', self.ident[:], self.ident_d, (), ['ident'])
        self.ms('dve', self.ones[:], 1.0, ['ones'])
        self.epsc = E(self.sbt("epsc", [128, 2], F32))
        self.ms('dve', self.epsc[:, 0:1], LN_EPS, ['epsc'])
        self.ms('dve', self.epsc[:, 1:2], 64e-5, ['epsc'])
        self.ms('dve', self.affT[:], 0.0, ['affT'])
        self.ms('dve', self.affTc[:], 0.0, ['affTc'])
        with ExitStack() as s2:
            cTs = s2.enter_context(self.sbt("cTs", [128, KC, 3], F32))
            self.dma('sp', cTs[:], self.cT, (), ['cTs'])
            self.act(self.scT[:], cTs[:], AF.Silu, ['cTs'], ['scT'])
            em.flush()

        for i in self.layers:
            kind, jj = i % 3, i // 3
            ctx_read = i <= 2
            ctx_live = i < 2
            self.ada_phase(i)
            if kind == 0:
                self.conv_phase(i, jj, ctx_live)
            elif kind == 1:
                self.na_phase(i, jj, ctx_live)
            else:
                self.rwkv_phase(i, jj)
            self.moe_phase(i, ctx_live, last=(i == self.layers[-1]))
        em.flush()
        st.close()
        em.close()
        return nc

    def extra_io(self):
        self.na_w_qkv = self.din("na_w_qkv", [1, D, 3 * D])
        self.na_w_o = self.din("na_w_o", [1, D, D])
        self.bm_tab = self.din("bm_tab", [16, 5, 128, 640])
        self.V_d = self.dint("V_d", [T + L, D], BF16)
        NT = T + L
        self.rw_mu_prevT = self.din("rw_mu_prevT", [128, 6, KC])
        self.rw_mu_nextT = self.din("rw_mu_nextT", [128, 6, KC])
        for nm in ("rw_w_r", "rw_w_k", "rw_w_v", "rw_w_o"):
            setattr(self, nm, self.din(nm, [D, D]))
        self.rw_w0 = self.din("rw_w0", [2, D])
        self.rw_a0 = self.din("rw_a0", [2, D])
        self.rw_w1 = self.din("rw_w1", [2, D, 64])
        self.rw_a1 = self.din("rw_a1", [2, D, 64])
        self.rw_w2 = self.din("rw_w2", [2, 64, D])
        self.rw_a2 = self.din("rw_a2", [2, 64, D])
        self.rw_g1 = self.din("rw_g1", [D, 128])
        self.rw_g2 = self.din("rw_g2", [128, D])
        for nm in ("rw_k_k", "rw_k_a", "rw_r_k", "rw_gn_g", "rw_gn_b"):
            setattr(self, nm, self.din(nm, [1, D]))
        self.masks = self.din("masks", [4, 128, 128])
        self.bdmask = self.din("bdmask", [128, 128])
        self.RK = self.dint("RK", [NT, D])
        self.RV = self.dint("RV", [NT, D])
        self.RR = self.dint("RR", [NT, D])
        self.RG = self.dint("RG", [NT, D])
        self.KKN = self.dint("KKN", [NT, D])
        self.RLW = [self.dint("RLW%d" % d, [NT, D]) for d in range(2)]
        self.RAI = [self.dint("RAI%d" % d, [NT, D]) for d in range(2)]
        self.KD = [self.dint("KD%d" % d, [NT, D]) for d in range(2)]
        self.BV = [self.dint("BV%d" % d, [NT, D]) for d in range(2)]
        self.YS = [self.dint("YS%d" % d, [T, D]) for d in range(2)]

    def load_lnp(self, i, g):
        for r_, src in ((0, self.ln_g), (1, self.ln_b)):
            self.dma('sp', self.lnp[:, r_, :], src[i, g:g + 1, :].to_broadcast([128, D]), (), [('lnp', r_)])

    def hsrc(self, i):
        return self.hin if i == self.layers[0] else self.H

    def ada_phase(self, i):
        nc = self.nc
        PS = self.PS
        with ExitStack() as s:
            E = s.enter_context
            wA = [E(self.sbt("wA%d" % k, [128, KC, D], BF16)) for k in range(2)]
            abT = E(self.sbt("abT", [128, 48], F32))
            gb = E(self.sbt("gb", [128, 2, D], F32))
            self.scR = E(self.sbt("scR", [128, KC, 3, 128], BF16))
            for j in range(3):
                self.cp('dve', self.scR[:, :, j, :], self.scT[:, :, j:j + 1].to_broadcast([128, KC, 128]), ['scT'], ['scR'])
            self.dma('sp', abT[:], self.ada_bT[i], (), ['abT'])
            for g, cb in enumerate((2, 5)):
                self.dma('sp', gb[:, g, :], self.ada_b[i:i + 1, cb * D:(cb + 1) * D].to_broadcast([128, D]), (), [('gb', g)])
            self.load_lnp(i, 0)
            self.dma('sp', self.wr[:], self.moe_router[i].rearrange("(k p) e -> p k e", p=128), (), ['wr'])
            for cb in range(6):
                wb = wA[cb % 2]
                src = self.ada_w[i][:, cb * D:(cb + 1) * D].rearrange("(k p) n -> p k n", p=128)
                for hh in range(2):
                    self.dma('pool', wb[:, :, hh * 512:(hh + 1) * 512], src[:, :, hh * 512:(hh + 1) * 512], (), [('wA', cb % 2, hh)])
                rw = [('wA', cb % 2, 0), ('wA', cb % 2, 1)]
                if cb in (0, 1, 3, 4):
                    v = (0, 1, None, 2, 3)[cb]
                    b = self.bank()
                    for fc in range(KC):
                        for k in range(KC):
                            self.mm(PS[b][:, fc * 3:fc * 3 + 3], wb[:, k, fc * 128:(fc + 1) * 128], self.scT[:, k, :],
                                    k == 0, k == KC - 1, rw + ['scT'], [('ps', b)])
                    pv = PS[b][:, 0:24].rearrange("p (f j) -> p f j", j=3)
                    self.tt('dve', self.modT[:, v * 8:(v + 1) * 8, :], pv,
                            abT[:, cb * 8:(cb + 1) * 8].unsqueeze(2).to_broadcast([128, 8, 3]), ALU.add,
                            [('ps', b), 'abT'], [('modT', v)])
                    if cb in (1, 4):
                        self.ts('dve', self.modT[:, v * 8:(v + 1) * 8, :], self.modT[:, v * 8:(v + 1) * 8, :], 1.0, None, ALU.add, None,
                                [('modT', v)], [('modT', v)])
                else:
                    g = 0 if cb == 2 else 1
                    for j in range(3):
                        for nh in range(2):
                            b = self.bank()
                            for k in range(KC):
                                self.mm(PS[b][:, :], self.scR[:, k, j, :], wb[:, k, nh * 512:(nh + 1) * 512],
                                        k == 0, k == KC - 1, rw + ['scR'], [('ps', b)])
                            self.tt('dve', self.ga_bc[:, g, j, nh * 512:(nh + 1) * 512], PS[b][:, :], gb[:, g, nh * 512:(nh + 1) * 512], ALU.add,
                                    [('ps', b), ('gb', g)], [('ga', g, j)])
            self.em.flush()

    def streams(self, ctx):
        s = [(0, T, 0, 0), (T, T, 1, 1)]
        if ctx:
            s += [(CTX0, L, 2, 2), (CTX0 + L, L, 2, 3)]
        return s

    def load_modT_to(self, i):
        pass

    def postnorm_tile(self, tl, row0, cond, g, hsrc, ysrc, yr, dst, dst_res, bias_bc=None, router=None):
        PS = self.PS
        ht, t1, t2, st6, mv, sm = tl['ht'], tl['t1'], tl['t2'], tl['st6'], tl['mv'], tl['sm']
        n = tl['n']
        tl['n'] += 1
        p = n % 2
        ht, t1 = ht[p], t1[p]
        R = lambda nm: (nm, p)
        self.dma('sp', ht[:], hsrc[row0:row0 + 128, :], [('H', row0 // 128)] if hsrc is self.H else [], [R('ht')])
        for nh in range(2):
            sl = slice(nh * 512, (nh + 1) * 512)
            if bias_bc is not None:
                self.tt('dve', t1[:, sl], ysrc[nh], bias_bc[:, sl], ALU.add, list(yr[nh]) + ['bias_bc'], [R('t1')])
                self.tt('pool', t1[:, sl], t1[:, sl], self.ga_bc[:, g, cond, sl], ALU.mult, [R('t1'), ('ga', g, cond)], [R('t1')])
            else:
                self.tt('dve', t1[:, sl], ysrc[nh], self.ga_bc[:, g, cond, sl], ALU.mult, list(yr[nh]) + [('ga', g, cond)], [R('t1')])
        self.stt('dve', t1[:], ht[:], ALPHA, t1[:], ALU.mult, ALU.add, [R('ht'), R('t1')], [R('t1')])
        for c in range(2):
            self.em.op('dve', lambda e, c=c: e.bn_stats(out=st6[p][:, c, :], in_=t1[:, c * 512:(c + 1) * 512]), [R('t1')], [R('st6')])
        self.em.op('dve', lambda e: e.bn_aggr(out=mv[p][:], in_=st6[p][:]), [R('st6')], [R('mv')])
        self.act(sm[p][:, 0:1], mv[p][:, 1:2], AF.Ln, [R('mv')], [R('sm')], bias=self.epsc[:, 0:1])
        self.act(sm[p][:, 0:1], sm[p][:, 0:1], AF.Exp, [R('sm')], [R('sm')], scale=-0.5)
        self.stt('dve', sm[p][:, 1:2], mv[p][:, 0:1], -1.0, sm[p][:, 0:1], ALU.mult, ALU.mult, [R('mv'), R('sm')], [R('sm')])
        self.act(t1[:], t1[:], AF.Identity, [R('t1'), R('sm')], [R('t1')], bias=sm[p][:, 1:2], scale=sm[p][:, 0:1])
        self.tt('pool', t1[:], t1[:], self.lnp[:, 0, :], ALU.mult, [R('t1'), ('lnp', 0)], [R('t1')])
        self.tt('dve', t1[:], t1[:], self.lnp[:, 1, :], ALU.add, [R('t1'), ('lnp', 1)], [R('t1')])
        self.dma('sp', dst[row0:row0 + 128, :] if dst is not self.out else dst[row0:row0 + 128, :], t1[:], [R('t1')], [(dst_res, row0 // 128)])
        if router is not None:
            self.router_tile(tl, t1, R('t1'), row0, cond, p)

    def router_tile(self, tl, hnew, hres, row0, cond, p):
        PS = self.PS
        hmT = tl['hmT'][p]
        lg = tl['lg'][p]
        R = lambda nm: (nm, p)
        for half in range(2):
            b = self.bank()
            for kk in range(4):
                k = half * 4 + kk
                self.tr(PS[b][:, kk * 128:(kk + 1) * 128], hnew[:, k * 128:(k + 1) * 128], [hres], [('ps', b)])
            for kk in range(4):
                k = half * 4 + kk
                eng = 'act' if kk % 2 == 0 else 'dve'
                if eng == 'act':
                    self.act(hmT[:, k, :], PS[b][:, kk * 128:(kk + 1) * 128], AF.Identity, [('ps', b), ('modT', 2), ('modT', 3)], [R('hmT')],
                             bias=self.modT[:, 16 + k, cond:cond + 1], scale=self.modT[:, 24 + k, cond:cond + 1])
                else:
                    self.ts('dve', hmT[:, k, :], PS[b][:, kk * 128:(kk + 1) * 128], self.modT[:, 24 + k, cond:cond + 1],
                            self.modT[:, 16 + k, cond:cond + 1], ALU.mult, ALU.add, [('ps', b), ('modT', 2), ('modT', 3)], [R('hmT')])
        b = self.bank()
        for k in range(KC):
            self.mm(PS[b][:, 0:NE], hmT[:, k, :], self.wr[:, k, :], k == 0, k == KC - 1, [R('hmT'), 'wr'], [('ps', b)])
        sm = tl['sm2'][p]
        self.em.op('dve', lambda e: e.tensor_reduce(out=sm[:, 0:1], in_=PS[b][:, 0:NE], axis=AX.X, op=ALU.max, negate=True), [('ps', b)], [R('sm2')])
        is_ctx = row0 >= CTX0
        bsel = (row0 - CTX0) // L if is_ctx else row0 // T
        c0 = 32 if (bsel == 1 and not is_ctx) else 0
        self.act(lg[:, c0:c0 + NE], PS[b][:, 0:NE], AF.Exp, [('ps', b), R('sm2')], [R('lg')], bias=sm[:, 0:1], accum_out=sm[:, 1:2])
        self.em.op('dve', lambda e: e.reciprocal(out=sm[:, 2:3], in_=sm[:, 1:2]), [R('sm2')], [R('sm2')])
        self.ts('dve', lg[:, c0:c0 + NE], lg[:, c0:c0 + NE], sm[:, 2:3], None, ALU.mult, None, [R('lg'), R('sm2')], [R('lg')])
        b2 = self.bank()
        npart = c0 + NE
        self.tr(PS[b2][0:npart, 0:128], lg[:, 0:npart], [R('lg')], [('ps', b2)])
        if is_ctx:
            col = (row0 - CTX0)
            self.cp('act', self.affTc[0:16, col:col + 128], PS[b2][0:16, 0:128], [('ps', b2)], ['affTc'])
        else:
            col = row0 - bsel * T
            self.cp('act', self.affT[c0:c0 + 16, col:col + 128], PS[b2][c0:c0 + 16, 0:128], [('ps', b2)], ['affT'])

    def alloc_pn_tiles(self, E, router):
        nc = self.nc
        tl = {'n': 0}
        tl['ht'] = [E(self.sbt("pn_ht%d" % k, [128, D], F32)) for k in range(2)]
        tl['t1'] = [E(self.sbt("pn_t1%d" % k, [128, D], F32)) for k in range(2)]
        tl['t2'] = None
        tl['st6'] = [E(self.sbt("pn_st%d" % k, [128, 2, 6], F32)) for k in range(2)]
        tl['mv'] = [E(self.sbt("pn_mv%d" % k, [128, 2], F32)) for k in range(2)]
        tl['sm'] = [E(self.sbt("pn_sm%d" % k, [128, 2], F32)) for k in range(2)]
        if router:
            tl['hmT'] = [E(self.sbt("pn_hmT%d" % k, [128, KC, 128], F32)) for k in range(2)]
            tl['lg'] = [E(self.sbt("pn_lg%d" % k, [128, 48], F32)) for k in range(2)]
            tl['sm2'] = [E(self.sbt("pn_sm2%d" % k, [128, 4], F32)) for k in range(2)]
            for k in range(2):
                self.ms('dve', tl['lg'][k][:], 0.0, [('lg', k)])
        return tl

    def conv_phase(self, i, jj, ctx_live):
        nc = self.nc
        PS = self.PS
        em = self.em
        hsrc = self.hsrc(i)
        with ExitStack() as so:
            convout = so.enter_context(self.sbt("convout", [128, KC, T], F32))
            vec = so.enter_context(self.sbt("cvec", [128, 16 + KC * CONVW + 3 * KC], F32))
            b_in = vec[:, 0:16]
            w_dw = vec[:, 16:16 + KC * CONVW].rearrange("p (k j) -> p k j", j=CONVW)
            o = 16 + KC * CONVW
            b_dw = vec[:, o:o + KC]
            cg = vec[:, o + KC:o + 2 * KC]
            cb_ = vec[:, o + 2 * KC:o + 3 * KC]
            self.dma('sp', b_in, self.conv_b_inT[jj], (), ['cvec'])
            self.dma('sp', w_dw, self.conv_w_dwT[jj], (), ['cvec'])
            self.dma('sp', b_dw, self.conv_b_dwT[jj], (), ['cvec'])
            self.dma('sp', cg, self.conv_ln_gT[jj], (), ['cvec'])
            self.dma('sp', cb_, self.conv_ln_bT[jj], (), ['cvec'])
            for (row0, nt, cond, sid) in self.streams(ctx_live):
                ntile = nt // 128
                CH = 512 if nt >= 512 else nt
                nch = nt // CH
                with ExitStack() as s:
                    E = s.enter_context
                    uT = E(self.sbt("uT", [128, KC, nt], BF16))
                    w_in = E(self.sbt("w_in", [128, KC, 2 * D], BF16))
                    xpad = [E(self.sbt("xpad%d" % k, [128, nt + 30], F32)) for k in range(2)]
                    sg = [E(self.sbt("sg%d" % k, [128, 512], F32)) for k in range(2)]
                    ht = [E(self.sbt("cht%d" % k, [128, D], F32)) for k in range(2)]
                    src = self.conv_w_in[jj].rearrange("(k p) n -> p k n", p=128)
                    for hh in range(4):
                        self.dma('pool', w_in[:, :, hh * 512:(hh + 1) * 512], src[:, :, hh * 512:(hh + 1) * 512], (), [('w_in', hh)])
                    for k in range(2):
                        self.ms('pool', xpad[k][:, 0:15], 0.0, [('xpad', k)])
                        self.ms('pool', xpad[k][:, nt + 15:nt + 30], 0.0, [('xpad', k)])
                    for tt_ in range(ntile):
                        p = tt_ % 2
                        r0 = row0 + tt_ * 128
                        self.dma('sp', ht[p][:], hsrc[r0:r0 + 128, :], [('H', r0 // 128)] if hsrc is self.H else [], [('cht', p)])
                        for half in range(2):
                            b = self.bank()
                            for kk in range(4):
                                k = half * 4 + kk
                                self.tr(PS[b][:, kk * 128:(kk + 1) * 128], ht[p][:, k * 128:(k + 1) * 128], [('cht', p)], [('ps', b)])
                            for kk in range(4):
                                k = half * 4 + kk
                                if kk % 2 == 0:
                                    self.act(uT[:, k, tt_ * 128:(tt_ + 1) * 128], PS[b][:, kk * 128:(kk + 1) * 128], AF.Identity,
                                             [('ps', b), ('modT', 0), ('modT', 1)], [('uT', tt_ * 128 // CH)],
                                             bias=self.modT[:, k, cond:cond + 1], scale=self.modT[:, 8 + k, cond:cond + 1])
                                else:
                                    self.ts('dve', uT[:, k, tt_ * 128:(tt_ + 1) * 128], PS[b][:, kk * 128:(kk + 1) * 128],
                                            self.modT[:, 8 + k, cond:cond + 1], self.modT[:, k, cond:cond + 1], ALU.mult, ALU.add,
                                            [('ps', b), ('modT', 0), ('modT', 1)], [('uT', tt_ * 128 // CH)])
                    for j in range(KC):
                        xp = xpad[j % 2]
                        for cc in range(nch):
                            tsl = slice(cc * CH, (cc + 1) * CH)
                            bg = self.bank()
                            for k in range(KC):
                                self.mm(PS[bg][:, 0:CH], w_in[:, k, (8 + j) * 128:(9 + j) * 128], uT[:, k, tsl], k == 0, k == KC - 1,
                                        [('w_in', (8 + j) // 4), ('uT', cc)], [('ps', bg)])
                            bv = self.bank()
                            for k in range(KC):
                                self.mm(PS[bv][:, 0:CH], w_in[:, k, j * 128:(j + 1) * 128], uT[:, k, tsl], k == 0, k == KC - 1,
                                        [('w_in', j // 4), ('uT', cc)], [('ps', bv)])
                            q = cc % 2
                            self.act(sg[q][:, 0:CH], PS[bg][:, 0:CH], AF.Sigmoid, [('ps', bg), 'cvec'], [('sg', q)], bias=b_in[:, 8 + j:9 + j])
                            self.stt('dve', xp[:, 15 + cc * CH:15 + (cc + 1) * CH], PS[bv][:, 0:CH], b_in[:, j:j + 1], sg[q][:, 0:CH],
                                     ALU.add, ALU.mult, [('ps', bv), ('sg', q), 'cvec'], [('xpad', j % 2)])
                        acc = convout[:, j, 0:nt]
                        self.ts('dve', acc, xp[:, 0:nt], w_dw[:, j, 0:1], b_dw[:, j:j + 1], ALU.mult, ALU.add,
                                [('xpad', j % 2), 'cvec'], [('convout', j)])
                        for tap in range(1, CONVW):
                            self.stt('dve', acc, xp[:, tap:tap + nt], w_dw[:, j, tap:tap + 1], acc, ALU.mult, ALU.add,
                                     [('xpad', j % 2), ('convout', j), 'cvec'], [('convout', j)])
                    em.flush()
                with ExitStack() as s:
                    E = s.enter_context
                    sT = E(self.sbt("sT", [128, KC, nt], BF16))
                    w_out = E(self.sbt("w_out", [128, KC, D], BF16))
                    bo_bc = E(self.sbt("bo_bc", [128, D], F32))
                    sq = [E(self.sbt("sq%d" % k, [128, 512], F32)) for k in range(2)]
                    mean = E(self.sbt("cmean", [128, 512], F32))
                    rstd = E(self.sbt("crstd", [128, 512], F32))
                    xn = [E(self.sbt("cxn%d" % k, [128, 512], F32)) for k in range(2)]
                    tl = self.alloc_pn_tiles(E, True)
                    src = self.conv_w_out[jj].rearrange("(k p) n -> p k n", p=128)
                    for hh in range(2):
                        self.dma('pool', w_out[:, :, hh * 512:(hh + 1) * 512], src[:, :, hh * 512:(hh + 1) * 512], (), [('w_out', hh)])
                    self.dma('sp', bo_bc[:], self.conv_b_out[jj:jj + 1, :].to_broadcast([128, D]), (), ['bias_bc'])
                    for cc in range(nch):
                        tsl = slice(cc * CH, (cc + 1) * CH)
                        bm = self.bank()
                        for k in range(KC):
                            self.mm(PS[bm][:, 0:CH], self.ones[:], convout[:, k, tsl], k == 0, k == KC - 1, ['ones', ('convout', k)], [('ps', bm)])
                        bq = self.bank()
                        for k in range(KC):
                            q = k % 2
                            self.act(sq[q][:, 0:CH], convout[:, k, tsl], AF.Square, [('convout', k)], [('sq', q)])
                            self.mm(PS[bq][:, 0:CH], self.ones[:], sq[q][:, 0:CH], k == 0, k == KC - 1, ['ones', ('sq', q)], [('ps', bq)])
                        self.act(mean[:, 0:CH], PS[bm][:, 0:CH], AF.Identity, [('ps', bm)], ['cmean'], scale=1.0 / D)
                        self.tt('dve', rstd[:, 0:CH], mean[:, 0:CH], mean[:, 0:CH], ALU.mult, ['cmean'], ['crstd'])
                        self.stt('dve', rstd[:, 0:CH], PS[bq][:, 0:CH], 1.0 / D, rstd[:, 0:CH], ALU.mult, ALU.subtract, [('ps', bq), 'crstd'], ['crstd'])
                        self.act(rstd[:, 0:CH], rstd[:, 0:CH], AF.Ln, ['crstd'], ['crstd'], bias=self.epsc[:, 0:1])
                        self.act(rstd[:, 0:CH], rstd[:, 0:CH], AF.Exp, ['crstd'], ['crstd'], scale=-0.5)
                        for k in range(KC):
                            q = k % 2
                            self.tt('dve', xn[q][:, 0:CH], convout[:, k, tsl], mean[:, 0:CH], ALU.subtract, [('convout', k), 'cmean'], [('cxn', q)])
                            self.tt('pool', xn[q][:, 0:CH], xn[q][:, 0:CH], rstd[:, 0:CH], ALU.mult, [('cxn', q), 'crstd'], [('cxn', q)])
                            self.act(sT[:, k, tsl], xn[q][:, 0:CH], AF.Silu, [('cxn', q), 'cvec'], [('sT', cc)],
                                     bias=cb_[:, k:k + 1], scale=cg[:, k:k + 1])
                    for tt_ in range(ntile):
                        r0 = row0 + tt_ * 128
                        bs = []
                        for nh in range(2):
                            b = self.bank()
                            bs.append(b)
                            for k in range(KC):
                                self.mm(PS[b][:, :], sT[:, k, tt_ * 128:(tt_ + 1) * 128], w_out[:, k, nh * 512:(nh + 1) * 512], k == 0, k == KC - 1,
                                        [('sT', tt_ * 128 // CH), ('w_out', nh)], [('ps', b)])
                        self.postnorm_tile(tl, r0, cond, 0, hsrc, [PS[bs[0]][:, :], PS[bs[1]][:, :]], [[('ps', bs[0])], [('ps', bs[1])]],
                                           self.H, 'H', bias_bc=bo_bc, router=True)
                    em.flush()

    def moe_phase(self, i, ctx_live, last):
        nc = self.nc
        PS = self.PS
        em = self.em
        nexp = self.n_exp
        with ExitStack() as so:
            EO = so.enter_context
            idxT = EO(self.sbt("idxT", [128, 2, 48], I32))
            gateT = EO(self.sbt("gateT", [128, 2, 48], F32))
            idxTc = EO(self.sbt("idxTc", [64, 16], I32))
            gateTc = EO(self.sbt("gateTc", [64, 16], F32))
            self.zero = EO(self.sbt("zero", [128, D], F32))
            self.ms('dve', self.zero[:], 0.0, ['zero'])
            self.load_lnp(i, 1)
            nrows = NROW if ctx_live else NB * T
            for r0 in range(0, nrows, 128):
                self.dma('sp', self.Y[r0:r0 + 128, :], self.zero[:], ['zero'], ['Y'])
            with ExitStack() as s:
                E = s.enter_context
                wk = E(self.sbt("wk", [48, T], F32))
                vals = E(self.sbt("vals", [48, CAP], F32))
                idxu = E(self.sbt("idxu", [48, CAP], U32))
                idxf = E(self.sbt("idxf", [48, CAP], F32))
                self.cp('dve', wk[:], self.affT[:], ['affT'], ['wk'])
                for it in range(CAP // 8):
                    sl = slice(it * 8, (it + 1) * 8)
                    self.em.op('dve', lambda e, sl=sl: e.max(out=vals[:, sl], in_=wk[:]), ['wk'], ['vals'])
                    self.em.op('dve', lambda e, sl=sl: e.max_index(out=idxu[:, sl], in_max=vals[:, sl], in_values=wk[:]), ['wk', 'vals'], ['idxu'])
                    if it < CAP // 8 - 1:
                        self.em.op('dve', lambda e, sl=sl: e.match_replace(out=wk[:], in_to_replace=vals[:, sl], in_values=wk[:], imm_value=-1.0),
                                   ['wk', 'vals'], ['wk'])
                self.cp('dve', idxf[:], idxu[:], ['idxu'], ['idxf'])
                self.ts('dve', idxf[32:48, :], idxf[32:48, :], float(T), None, ALU.add, None, ['idxf'], ['idxf'])
                for j in range(2):
                    b = self.bank()
                    self.tr(PS[b][:, 0:48], idxf[:, j * 128:(j + 1) * 128], ['idxf'], [('ps', b)])
                    self.cp('dve', idxT[:, j, :], PS[b][:, 0:48], [('ps', b)], ['idxT'])
                    b = self.bank()
                    self.tr(PS[b][:, 0:48], vals[:, j * 128:(j + 1) * 128], ['vals'], [('ps', b)])
                    self.cp('act', gateT[:, j, :], PS[b][:, 0:48], [('ps', b)], ['gateT'])
                if ctx_live:
                    wkc = E(self.sbt("wkc", [16, NB * L], F32))
                    valc = E(self.sbt("valc", [16, NB * CAPC], F32))
                    idxcu = E(self.sbt("idxcu", [16, NB * CAPC], U32))
                    idxcf = E(self.sbt("idxcf", [16, NB * CAPC], F32))
                    self.cp('dve', wkc[:], self.affTc[:], ['affTc'], ['wkc'])
                    for bb in range(NB):
                        for it in range(CAPC // 8):
                            sl = slice(bb * CAPC + it * 8, bb * CAPC + (it + 1) * 8)
                            wsl = slice(bb * L, (bb + 1) * L)
                            self.em.op('dve', lambda e, sl=sl, wsl=wsl: e.max(out=valc[:, sl], in_=wkc[:, wsl]), ['wkc'], ['valc'])
                            self.em.op('dve', lambda e, sl=sl, wsl=wsl: e.max_index(out=idxcu[:, sl], in_max=valc[:, sl], in_values=wkc[:, wsl]),
                                       ['wkc', 'valc'], ['idxcu'])
                            if it < CAPC // 8 - 1:
                                self.em.op('dve', lambda e, sl=sl, wsl=wsl: e.match_replace(out=wkc[:, wsl], in_to_replace=valc[:, sl],
                                                                                          in_values=wkc[:, wsl], imm_value=-1.0),
                                           ['wkc', 'valc'], ['wkc'])
                    self.cp('dve', idxcf[:], idxcu[:], ['idxcu'], ['idxcf'])
                    for bb in range(NB):
                        self.ts('dve', idxcf[:, bb * CAPC:(bb + 1) * CAPC], idxcf[:, bb * CAPC:(bb + 1) * CAPC], float(CTX0 + bb * L), None,
                                ALU.add, None, ['idxcf'], ['idxcf'])
                    b = self.bank()
                    self.tr(PS[b][0:64, 0:16], idxcf[:, :], ['idxcf'], [('ps', b)])
                    self.cp('dve', idxTc[:], PS[b][0:64, 0:16], [('ps', b)], ['idxTc'])
                    b = self.bank()
                    self.tr(PS[b][0:64, 0:16], valc[:, :], ['valc'], [('ps', b)])
                    self.cp('act', gateTc[:], PS[b][0:64, 0:16], [('ps', b)], ['gateTc'])
                em.flush()
            NTOK = 4 * 128 + (64 if ctx_live else 0)
            with ExitStack() as s:
                E = s.enter_context
                NS = 3
                G = [E(self.sbt("Gw%d" % k, [128, KC, 512], BF16)) for k in range(NS)]
                U = [E(self.sbt("Uw%d" % k, [128, KC, 512], BF16)) for k in range(NS)]
                Wd = [E(self.sbt("Wd%d" % k, [128, 16, D], BF16)) for k in range(2)]
                xg = [E(self.sbt("xg%d" % k, [128, D], F32)) for k in range(4)]
                xsT = [E(self.sbt("xsT%d" % k, [128, KC, NTOK], BF16)) for k in range(1)]
                hidT = [E(self.sbt("hidT%d" % k, [128, 16, NTOK], BF16)) for k in range(1)]
                sg = [E(self.sbt("msg%d" % k, [128, NTOK], F32)) for k in range(2)]
                ys = [E(self.sbt("ys%d" % k, [128, D], F32)) for k in range(2)]

                def load_gu(sq_):
                    if sq_ >= nexp * 4:
                        return
                    e, c = sq_ // 4, sq_ % 4
                    sl_ = sq_ % NS
                    sgw = self.moe_w_gate[i, e].rearrange("(k p) n -> p k n", p=128)
                    suw = self.moe_w_up[i, e].rearrange("(k p) n -> p k n", p=128)
                    self.dma('pool', G[sl_][:, :, :], sgw[:, :, c * 512:(c + 1) * 512], (), [('G', sl_)])
                    self.dma('pool', U[sl_][:, :, :], suw[:, :, c * 512:(c + 1) * 512], (), [('U', sl_)])

                def load_d(e):
                    sdw = self.moe_w_down[i, e].rearrange("(f p) n -> p f n", p=128)
                    for hh in range(4):
                        self.dma('pool', Wd[e % 2][:, hh * 4:(hh + 1) * 4, :], sdw[:, hh * 4:(hh + 1) * 4, :], (), [('Wd', e % 2, hh)])

                for sq_ in range(NS):
                    load_gu(sq_)
                load_d(0)
                xs = xsT[0]
                hd = hidT[0]
                allH = [('H', r) for r in range(NROW // 128)]

                def gather_tokens(e):
                    for bb in range(NB):
                        for j in range(2):
                            g = xg[bb * 2 + j]
                            col = bb * 32 + e
                            self.em.dma('pool', lambda en, g=g, j=j, col=col: en.indirect_dma_start(
                                out=g[:], out_offset=None, in_=self.H,
                                in_offset=bass.IndirectOffsetOnAxis(ap=idxT[:, j, col:col + 1], axis=0)),
                                ['idxT'] + allH, [('xg', bb * 2 + j)])

                def build_xs(e):
                    for bb in range(NB):
                        bks = [self.bank() for _ in range(4)]
                        for j in range(2):
                            g = xg[bb * 2 + j]
                            gr = ('xg', bb * 2 + j)
                            for k in range(KC):
                                bk = bks[k // 2]
                                off = (k % 2) * 256 + j * 128
                                self.tr(PS[bk][:, off:off + 128], g[:, k * 128:(k + 1) * 128], [gr], [('ps', bk)])
                        if bb == 0 and ctx_live:
                            self.em.dma('pool', lambda en, e=e: en.indirect_dma_start(
                                out=xg[0][0:64, :], out_offset=None, in_=self.H,
                                in_offset=bass.IndirectOffsetOnAxis(ap=idxTc[:, e:e + 1], axis=0)),
                                ['idxTc'] + allH, [('xg', 0)])
                        for k in range(KC):
                            bk = bks[k // 2]
                            off = (k % 2) * 256
                            dst = xs[:, k, bb * 256:(bb + 1) * 256]
                            if k % 2 == 0:
                                self.act(dst, PS[bk][:, off:off + 256], AF.Identity, [('ps', bk), ('modT', 2), ('modT', 3)], [('xsT', 0)],
                                         bias=self.modT[:, 16 + k, bb:bb + 1], scale=self.modT[:, 24 + k, bb:bb + 1])
                            else:
                                self.ts('dve', dst, PS[bk][:, off:off + 256], self.modT[:, 24 + k, bb:bb + 1], self.modT[:, 16 + k, bb:bb + 1],
                                        ALU.mult, ALU.add, [('ps', bk), ('modT', 2), ('modT', 3)], [('xsT', 0)])
                    if ctx_live:
                        g = xg[0]
                        gr = ('xg', 0)
                        bks = [self.bank() for _ in range(1)]
                        for k in range(KC):
                            self.tr(PS[bks[0]][:, k * 64:(k + 1) * 64], g[0:64, k * 128:(k + 1) * 128], [gr], [('ps', bks[0])])
                        for k in range(KC):
                            dst = xs[:, k, 512:576]
                            self.ts('dve', dst, PS[bks[0]][:, k * 64:(k + 1) * 64], self.modT[:, 24 + k, 2:3], self.modT[:, 16 + k, 2:3],
                                    ALU.mult, ALU.add, [('ps', bks[0]), ('modT', 2), ('modT', 3)], [('xsT', 0)])

                gather_tokens(0)
                build_xs(0)
                for e in range(nexp):
                    if e + 1 < nexp:
                        gather_tokens(e + 1)
                        load_d(e + 1)
                    for c in range(4):
                        for fl in range(4):
                            ft = c * 4 + fl
                            bg = self.bank()
                            bu = self.bank()
                            sl_ = (e * 4 + c) % NS
                            for (bk_, W, wr_) in ((bg, G[sl_], ('G', sl_)), (bu, U[sl_], ('U', sl_))):
                                for k in range(KC):
                                    self.mm(PS[bk_][:, 0:512], W[:, k, fl * 128:(fl + 1) * 128], xs[:, k, 0:512], k == 0, k == KC - 1,
                                            [wr_, ('xsT', 0)], [('ps', bk_)])
                            q = ft % 2
                            self.act(sg[q][:, 0:512], PS[bg][:, 0:512], AF.Silu, [('ps', bg)], [('msg', q)])
                            self.tt('dve', hd[:, ft, 0:512], sg[q][:, 0:512], PS[bu][:, 0:512], ALU.mult, [('msg', q), ('ps', bu)], [('hidT', ft)])
                            if ctx_live:
                                bc = self.bank()
                                for gi, (W, wr_) in enumerate(((G[sl_], ('G', sl_)), (U[sl_], ('U', sl_)))):
                                    for k in range(KC):
                                        self.mm(PS[bc][:, gi * 64:(gi + 1) * 64], W[:, k, fl * 128:(fl + 1) * 128], xs[:, k, 512:576], k == 0, k == KC - 1,
                                                [wr_, ('xsT', 0)], [('ps', bc)])
                                self.act(sg[q][:, 512:576], PS[bc][:, 0:64], AF.Silu, [('ps', bc)], [('msg', q)])
                                self.tt('dve', hd[:, ft, 512:576], sg[q][:, 512:576], PS[bc][:, 64:128], ALU.mult, [('msg', q), ('ps', bc)], [('hidT', ft)])
                        load_gu(e * 4 + c + NS)
                    if e + 1 < nexp:
                        build_xs(e + 1)
                    ttiles = [(bb, j, 128) for bb in range(NB) for j in range(2)] + ([(2, 0, 64)] if ctx_live else [])
                    prevY = [('Ys', (e + 1) % 2, t_) for t_ in range(5)]
                    for ti, (bb, j, np_) in enumerate(ttiles):
                        y_ = ys[ti % 2]
                        yr = ('ys', ti % 2)
                        tok0 = ti * 128
                        for nh in range(2):
                            b = self.bank()
                            for ft in range(16):
                                self.mm(PS[b][0:np_, :], hd[:, ft, tok0:tok0 + np_], Wd[e % 2][:, ft, nh * 512:(nh + 1) * 512], ft == 0, ft == 15,
                                        [('hidT', ft), ('Wd', e % 2, ft // 4)], [('ps', b)])
                            gcol = gateTc[:, e:e + 1] if bb == 2 else gateT[:, j, bb * 32 + e:bb * 32 + e + 1]
                            gres = 'gateTc' if bb == 2 else 'gateT'
                            if nh == 0:
                                self.ts('dve', y_[0:np_, 0:512], PS[b][0:np_, :], gcol[0:np_, :], None, ALU.mult, None, [('ps', b), gres], [yr])
                            else:
                                self.act(y_[0:np_, 512:1024], PS[b][0:np_, :], AF.Identity, [('ps', b), gres], [yr], scale=gcol[0:np_, :])
                        if bb == 2:
                            self.em.dma('pool', lambda en, y_=y_, e=e: en.indirect_dma_start(
                                out=self.Y, out_offset=bass.IndirectOffsetOnAxis(ap=idxTc[:, e:e + 1], axis=0),
                                in_=y_[0:64, :], in_offset=None, compute_op=ALU.add), [yr, 'idxTc', 'Y'] + prevY, [('Ys', e % 2, ti)])
                        else:
                            col = bb * 32 + e
                            self.em.dma('pool', lambda en, y_=y_, j=j, col=col: en.indirect_dma_start(
                                out=self.Y, out_offset=bass.IndirectOffsetOnAxis(ap=idxT[:, j, col:col + 1], axis=0),
                                in_=y_[:, :], in_offset=None, compute_op=ALU.add), [yr, 'idxT', 'Y'] + prevY, [('Ys', e % 2, ti)])
                em.flush()
            with ExitStack() as s:
                E = s.enter_context
                tl = self.alloc_pn_tiles(E, False)
                yt = [E(self.sbt("pn_y%d" % k, [128, D], F32)) for k in range(2)]
                for (row0, nt, cond, sid) in self.streams(ctx_live):
                    for tt_ in range(nt // 128):
                        r0 = row0 + tt_ * 128
                        p = tl['n'] % 2
                        self.dma('act', yt[p][:], self.Y[r0:r0 + 128, :], ['Y'], [('pn_y', p)])
                        if last and row0 < CTX0:
                            dst, dres = self.out, 'out'
                        else:
                            dst, dres = self.H, 'H'
                        self.postnorm_tile(tl, r0, cond, 1, self.H, [yt[p][:, 0:512], yt[p][:, 512:1024]], [[('pn_y', p)], [('pn_y', p)]],
                                           dst, dres, bias_bc=None, router=None)
                em.flush()

    def modT_tiles(self, uT, hsrc, ht, row0, nt, cond, c0, CH, res_name='uT'):
        PS = self.PS
        for tt_ in range(nt // 128):
            p = self._mt % 2
            self._mt += 1
            r0 = row0 + tt_ * 128
            col = c0 + tt_ * 128
            self.dma('sp', ht[p][:], hsrc[r0:r0 + 128, :], [('H', r0 // 128)] if hsrc is self.H else [], [('cht', p)])
            for half in range(2):
                b = self.bank()
                for kk in range(4):
                    k = half * 4 + kk
                    self.tr(PS[b][:, kk * 128:(kk + 1) * 128], ht[p][:, k * 128:(k + 1) * 128], [('cht', p)], [('ps', b)])
                for kk in range(4):
                    k = half * 4 + kk
                    if kk % 2 == 0:
                        self.act(uT[:, k, col:col + 128], PS[b][:, kk * 128:(kk + 1) * 128], AF.Identity,
                                 [('ps', b), ('modT', 0), ('modT', 1)], [(res_name, col // CH)],
                                 bias=self.modT[:, k, cond:cond + 1], scale=self.modT[:, 8 + k, cond:cond + 1])
                    else:
                        self.ts('dve', uT[:, k, col:col + 128], PS[b][:, kk * 128:(kk + 1) * 128],
                                self.modT[:, 8 + k, cond:cond + 1], self.modT[:, k, cond:cond + 1], ALU.mult, ALU.add,
                                [('ps', b), ('modT', 0), ('modT', 1)], [(res_name, col // CH)])

    def na_phase(self, i, jj, ctx_live):
        nc = self.nc
        PS = self.PS
        em = self.em
        hsrc = self.hsrc(i)
        NT = T + L
        chunks = [(0, 512), (512, 512), (1024, 512), (1536, 512), (2048, 256)]
        classes = ((0, [0]), (1, [1]), (2, list(range(2, 14))), (3, [14]), (4, [15]))
        with ExitStack() as so:
            EO = so.enter_context
            qT = EO(self.sbt("qT", [128, 8, NT], BF16))
            kT = EO(self.sbt("kT", [128, 8, NT], BF16))
            identb = EO(self.sbt("identb", [128, 128], BF16))
            self.cp('dve', identb[:], self.ident[:], ['ident'], ['identb'])
            for b in range(NB):
                with ExitStack() as s:
                    E = s.enter_context
                    uT = E(self.sbt("uT", [128, KC, NT], BF16))
                    wq = [E(self.sbt("wq%d" % k, [128, KC, D], BF16)) for k in range(2)]
                    vst = [E(self.sbt("vst%d" % k, [128, D], BF16)) for k in range(2)]
                    ht = [E(self.sbt("cht%d" % k, [128, D], F32)) for k in range(2)]

                    def loadw(blk, slot):
                        src = self.na_w_qkv[jj][:, blk * D:(blk + 1) * D].rearrange("(k p) n -> p k n", p=128)
                        for hh in range(2):
                            self.dma('pool', wq[slot][:, :, hh * 512:(hh + 1) * 512], src[:, :, hh * 512:(hh + 1) * 512], (), [('wq', slot, hh)])
                    loadw(0, 0)
                    loadw(1, 1)
                    self._mt = 0
                    self.modT_tiles(uT, hsrc, ht, b * T, T, b, 0, 512)
                    self.modT_tiles(uT, hsrc, ht, CTX0 + b * L, L, 2, T, 512)
                    for blk, dstT, nm in ((0, qT, 'qT'), (1, kT, 'kT')):
                        w = wq[blk]
                        for hp in range(8):
                            for (c0, n) in chunks:
                                bk = self.bank()
                                for k in range(KC):
                                    self.mm(PS[bk][:, 0:n], w[:, k, hp * 128:(hp + 1) * 128], uT[:, k, c0:c0 + n], k == 0, k == KC - 1,
                                            [('wq', blk, hp // 4), ('uT', c0 // 512)], [('ps', bk)])
                                if blk == 0:
                                    self.act(dstT[:, hp, c0:c0 + n], PS[bk][:, 0:n], AF.Identity, [('ps', bk)], [(nm, hp)], scale=0.125)
                                else:
                                    self.cp('dve', dstT[:, hp, c0:c0 + n], PS[bk][:, 0:n], [('ps', bk)], [(nm, hp)])
                    loadw(2, 0)
                    for t in range(NT // 128):
                        p = t % 2
                        for nh in range(2):
                            bk = self.bank()
                            for k in range(KC):
                                self.mm(PS[bk][:, :], uT[:, k, t * 128:(t + 1) * 128], wq[0][:, k, nh * 512:(nh + 1) * 512], k == 0, k == KC - 1,
                                        [('wq', 0, nh), ('uT', t * 128 // 512)], [('ps', bk)])
                            if nh == 0:
                                self.cp('act', vst[p][:, 0:512], PS[bk][:, :], [('ps', bk)], [('vst', p)])
                            else:
                                self.cp('dve', vst[p][:, 512:1024], PS[bk][:, :], [('ps', bk)], [('vst', p)])
                        self.dma('sp', self.V_d[t * 128:(t + 1) * 128, :], vst[p][:], [('vst', p)], [('V_d', t)])
                    em.flush()
                with ExitStack() as sm_:
                    oT = sm_.enter_context(self.sbt("oT", [128, 8, NT], BF16))
                    with ExitStack() as s:
                        E = s.enter_context
                        Vh = [E(self.sbt("Vh%d" % k, [128, NT // 128, 128], BF16)) for k in range(2)]
                        BM = [E(self.sbt("BM%d" % k, [128, 640], F32)) for k in range(2)]
                        S = [E(self.sbt("S%d" % k, [128, 896], F32)) for k in range(2)]
                        Pn = [E(self.sbt("Pn%d" % k, [128, 896], BF16)) for k in range(2)]
                        PT = [E(self.sbt("PT%d" % k, [128, 896], BF16)) for k in range(2)]
                        smx = [E(self.sbt("smx%d" % k, [128, 4], F32)) for k in range(2)]
                        nbm = 0
                        nq = 0
                        vsrc = self.V_d.rearrange("(t p) c -> p t c", p=128)
                        for hp in range(8):
                            vh = Vh[hp % 2]
                            self.dma('sp', vh[:], vsrc[:, :, hp * 128:(hp + 1) * 128], [('V_d', t) for t in range(NT // 128)], [('Vh', hp % 2)])
                            for hh in range(2):
                                h = hp * 2 + hh
                                pr = slice(hh * 64, (hh + 1) * 64)

                                def attend(qcol, segs, bm, nkeys, vts):
                                    nonlocal nq
                                    m = nq % 2
                                    nq += 1
                                    S_, Pn_, PT_, sx = S[m], Pn[m], PT[m], smx[m]
                                    R = lambda nm: (nm, m)
                                    bks = {}
                                    scol = 0
                                    for (bi, pc0, kc0, n, loc) in segs:
                                        if bi not in bks:
                                            bks[bi] = self.bank()
                                        bk = bks[bi]
                                        self.mm(PS[bk][:, pc0:pc0 + n], qT[pr, hp, qcol:qcol + 128], kT[pr, hp, kc0:kc0 + n], True, True,
                                                [('qT', hp), ('kT', hp)], [('ps', bk)])
                                    for (bi, pc0, kc0, n, loc) in segs:
                                        bk = bks[bi]
                                        if loc:
                                            self.tt('dve', S_[:, scol:scol + n], PS[bk][:, pc0:pc0 + n], bm[0][:, scol:scol + n], ALU.add,
                                                    [('ps', bk), bm[1]], [R('S')])
                                        else:
                                            self.cp('act', S_[:, scol:scol + n], PS[bk][:, pc0:pc0 + n], [('ps', bk)], [R('S')])
                                        scol += n
                                    self.em.op('dve', lambda e: e.tensor_reduce(out=sx[:, 0:1], in_=S_[:, 0:nkeys], axis=AX.X, op=ALU.max, negate=True),
                                               [R('S')], [R('smx')])
                                    self.act(S_[:, 0:nkeys], S_[:, 0:nkeys], AF.Exp, [R('S'), R('smx')], [R('S')], bias=sx[:, 0:1], accum_out=sx[:, 1:2])
                                    self.em.op('dve', lambda e: e.reciprocal(out=sx[:, 2:3], in_=sx[:, 1:2]), [R('smx')], [R('smx')])
                                    self.ts('dve', Pn_[:, 0:nkeys], S_[:, 0:nkeys], sx[:, 2:3], None, ALU.mult, None, [R('S'), R('smx')], [R('Pn')])
                                    bT = self.bank()
                                    PTv = PS[bT][:, :].bitcast(BF16)
                                    nch = nkeys // 128
                                    for c in range(nch):
                                        self.tr(PTv[:, c * 128:(c + 1) * 128], Pn_[:, c * 128:(c + 1) * 128], [R('Pn'), 'identb'], [('ps', bT)], ident=identb)
                                    if m == 0:
                                        self.cp('act', PT_[:, 0:nkeys], PTv[:, 0:nkeys], [('ps', bT)], [R('PT')])
                                    else:
                                        self.cp('dve', PT_[:, 0:nkeys], PTv[:, 0:nkeys], [('ps', bT)], [R('PT')])
                                    bO = self.bank()
                                    for c in range(nch):
                                        self.mm(PS[bO][pr, 0:128], vh[:, vts[c], hh * 64:(hh + 1) * 64], PT_[:, c * 128:(c + 1) * 128], c == 0, c == nch - 1,
                                                [('Vh', hp % 2), R('PT')], [('ps', bO)])
                                    if m == 0:
                                        self.cp('dve', oT[pr, hp, qcol:qcol + 128], PS[bO][pr, 0:128], [('ps', bO)], [('oT', qcol // 128)])
                                    else:
                                        self.cp('act', oT[pr, hp, qcol:qcol + 128], PS[bO][pr, 0:128], [('ps', bO)], [('oT', qcol // 128)])

                                for cls, tiles in classes:
                                    bmt = BM[nbm % 2]
                                    bmr = ('BM', nbm % 2)
                                    nbm += 1
                                    self.dma('sp', bmt[:], self.bm_tab[h, cls], (), [bmr])
                                    for i_ in tiles:
                                        ks = min(max(2 * i_ - 4, 0), 22)
                                        k0 = ks * 64
                                        segs = [(0, 0, k0, 512, True), (1, 0, k0 + 512, 128, True), (1, 128, T, 256, False)]
                                        vts = [ks // 2 + c for c in range(5)] + [16, 17]
                                        attend(i_ * 128, segs, (bmt, bmr), 896, vts)
                                for j in range(2):
                                    attend(T + j * 128, [(0, 0, T, 256, False)], None, 256, [16, 17])
                        em.flush()
                    with ExitStack() as s:
                        E = s.enter_context
                        w_o = E(self.sbt("w_o", [128, KC, D], BF16))
                        tl = self.alloc_pn_tiles(E, True)
                        src = self.na_w_o[jj].rearrange("(k p) n -> p k n", p=128)
                        for hh in range(2):
                            self.dma('pool', w_o[:, :, hh * 512:(hh + 1) * 512], src[:, :, hh * 512:(hh + 1) * 512], (), [('w_o', hh)])
                        for t in range(NT // 128):
                            if t < 16:
                                r0, cond = b * T + t * 128, b
                            else:
                                r0, cond = CTX0 + b * L + (t - 16) * 128, 2
                            bs = []
                            for nh in range(2):
                                bk = self.bank()
                                bs.append(bk)
                                for k in range(KC):
                                    self.mm(PS[bk][:, :], oT[:, k, t * 128:(t + 1) * 128], w_o[:, k, nh * 512:(nh + 1) * 512], k == 0, k == KC - 1,
                                            [('oT', t), ('w_o', nh)], [('ps', bk)])
                            self.postnorm_tile(tl, r0, cond, 0, hsrc, [PS[bs[0]][:, :], PS[bs[1]][:, :]], [[('ps', bs[0])], [('ps', bs[1])]],
                                               self.H, 'H', bias_bc=None, router=True)
                        em.flush()


    def rwkv_phase(self, i, jj):
        nc = self.nc
        PS = self.PS
        em = self.em
        hsrc = self.hsrc(i)
        NT = T + L
        C64 = 0.6065306597126334
        R3 = lambda ap, dd=64: ap.rearrange("p (h d) -> p h d", d=dd)

        def bc64(ap16):
            return ap16.unsqueeze(2).to_broadcast([128, 16, 64])

        for b in range(NB):
            with ExitStack() as s:
                E = s.enter_context
                wbuf = [E(self.sbt("rwW%d" % k, [128, KC, D], BF16)) for k in range(1)]
                w1c = E(self.sbt("w1c", [128, KC, 128], BF16))
                a1c = E(self.sbt("a1c", [128, KC, 128], BF16))
                g1s = E(self.sbt("g1s", [128, KC, 128], BF16))
                w2s = E(self.sbt("w2s", [128, D], BF16))
                a2s = E(self.sbt("a2s", [128, D], BF16))
                g2s = E(self.sbt("g2s", [128, D], BF16))
                muT = E(self.sbt("muT", [128, 2, 6, KC], F32))
                bc0 = E(self.sbt("bc0", [128, 2, D], F32))
                u32 = E(self.sbt("u32", [128, KC, T + 2], F32))
                lerp = E(self.sbt("lerp", [128, KC, T], BF16))
                tmp = [E(self.sbt("ltmp%d" % k, [128, T], F32)) for k in range(2)]
                loT = E(self.sbt("loT", [128, T], BF16))
                stg = [E(self.sbt("stg%d" % k, [128, D], F32)) for k in range(2)]
                ht = [E(self.sbt("cht%d" % k, [128, D], F32)) for k in range(2)]
                self.dma('sp', muT[:, 0], self.rw_mu_prevT, (), ['muT'])
                self.dma('sp', muT[:, 1], self.rw_mu_nextT, (), ['muT'])
                for d in range(2):
                    self.dma('pool', w1c[:, :, d * 64:(d + 1) * 64], self.rw_w1[d].rearrange("(k p) n -> p k n", p=128), (), ['w1c'])
                    self.dma('pool', a1c[:, :, d * 64:(d + 1) * 64], self.rw_a1[d].rearrange("(k p) n -> p k n", p=128), (), ['a1c'])
                self.dma('pool', g1s[:], self.rw_g1.rearrange("(k p) n -> p k n", p=128), (), ['g1s'])
                self.dma('pool', w2s[:], self.rw_w2.rearrange("d j n -> (d j) n"), (), ['w2s'])
                self.dma('pool', a2s[:], self.rw_a2.rearrange("d j n -> (d j) n"), (), ['a2s'])
                self.dma('pool', g2s[:], self.rw_g2, (), ['g2s'])
                nst = [0]

                def loadW(src, slot):
                    v = src.rearrange("(k p) n -> p k n", p=128)
                    for hh in range(2):
                        self.dma('pool', wbuf[slot][:, :, hh * 512:(hh + 1) * 512], v[:, :, hh * 512:(hh + 1) * 512], (), [('rwW', slot, hh)])

                def stage_out(dst, row, evs):
                    p = nst[0] % 2
                    nst[0] += 1
                    for nh in range(2):
                        evs[nh](stg[p][:, nh * 512:(nh + 1) * 512], ('stg', p))
                    self.dma('sp', dst[row:row + 128, :], stg[p][:], [('stg', p)], [(dst.tensor.name, row // 128)])

                for (row0, nt, cond, roff, is_lat) in ((b * T, T, b, 0, True), (CTX0 + b * L, L, 2, T, False)):
                    ntile = nt // 128
                    CH = min(512, nt)
                    nch = nt // CH
                    for k in range(KC):
                        self.ms('pool', u32[:, k, 0:1], 0.0, [('u32', 0)])
                        self.ms('pool', u32[:, k, nt + 1:nt + 2], 0.0, [('u32', 0)])
                    self._mt = 0
                    self.modT_tiles(u32, hsrc, ht, row0, nt, cond, 1, 1 << 30, res_name='u32')
                    lerps = (0, 1, 2, 3, 4, 5) if is_lat else (1, 2, 3, 4)
                    wsrc = {0: self.rw_w_r, 2: self.rw_w_k, 3: self.rw_w_v}
                    wslot = {0: 0, 2: 0, 3: 0}
                    for n in lerps:
                        if n in wsrc:
                            loadW(wsrc[n], wslot[n])
                        for k in range(KC):
                            uk = u32[:, k, 1:nt + 1]
                            t0, t1 = tmp[0][:, 0:nt], tmp[1][:, 0:nt]
                            self.tt('dve', t0, u32[:, k, 0:nt], uk, ALU.subtract, [('u32', 0)], [('ltmp', 0)])
                            self.stt('dve', t0, t0, muT[:, 0, n, k:k + 1], uk, ALU.mult, ALU.add, [('ltmp', 0), 'muT', ('u32', 0)], [('ltmp', 0)])
                            self.tt('pool', t1, u32[:, k, 2:nt + 2], uk, ALU.subtract, [('u32', 0)], [('ltmp', 1)])
                            self.stt('dve', lerp[:, k, 0:nt], t1, muT[:, 1, n, k:k + 1], t0, ALU.mult, ALU.add,
                                     [('ltmp', 0), ('ltmp', 1), 'muT'], [('lerp', k)])
                        lr = [('lerp', k) for k in range(KC)]
                        if n in wsrc:
                            dst = {0: self.RR, 2: self.RK, 3: self.RV}[n]
                            W = wbuf[wslot[n]]
                            for t in range(ntile):
                                bks = [self.bank(), self.bank()]
                                for nh in range(2):
                                    for k in range(KC):
                                        self.mm(PS[bks[nh]][:, :], lerp[:, k, t * 128:(t + 1) * 128], W[:, k, nh * 512:(nh + 1) * 512], k == 0, k == KC - 1,
                                                lr + [('rwW', wslot[n], nh)], [('ps', bks[nh])])
                                stage_out(dst, roff + t * 128, [
                                    lambda o, r_, bk=bks[0]: self.cp('act', o, PS[bk][:, :], [('ps', bk)], [r_]),
                                    lambda o, r_, bk=bks[1]: self.cp('dve', o, PS[bk][:, :], [('ps', bk)], [r_])])
                        else:
                            l1 = {1: w1c, 4: a1c, 5: g1s}[n]
                            l1n = {1: 'w1c', 4: 'a1c', 5: 'g1s'}[n]
                            for cc in range(nch):
                                bk = self.bank()
                                for k in range(KC):
                                    self.mm(PS[bk][:, 0:CH], l1[:, k, :], lerp[:, k, cc * CH:(cc + 1) * CH], k == 0, k == KC - 1, lr + [l1n], [('ps', bk)])
                                fn = {1: AF.Tanh, 4: AF.Identity, 5: AF.Sigmoid}[n]
                                self.act(loT[:, cc * CH:(cc + 1) * CH], PS[bk][:, 0:CH], fn, [('ps', bk)], ['loT'])
                            if n == 5:
                                for t in range(ntile):
                                    bks = [self.bank(), self.bank()]
                                    for nh in range(2):
                                        self.mm(PS[bks[nh]][:, :], loT[:, t * 128:(t + 1) * 128], g2s[:, nh * 512:(nh + 1) * 512], True, True, ['loT', 'g2s'], [('ps', bks[nh])])
                                    stage_out(self.RG, roff + t * 128, [
                                        lambda o, r_, bk=bks[0]: self.cp('act', o, PS[bk][:, :], [('ps', bk)], [r_]),
                                        lambda o, r_, bk=bks[1]: self.cp('dve', o, PS[bk][:, :], [('ps', bk)], [r_])])
                            else:
                                l2, l2n = (w2s, 'w2s') if n == 1 else (a2s, 'a2s')
                                for d in range(2):
                                    self.dma('sp', bc0[:, d, :], (self.rw_w0 if n == 1 else self.rw_a0)[d:d + 1, :].to_broadcast([128, D]), (), ['bc0'])
                                for d in range(2):
                                    pr = slice(d * 64, (d + 1) * 64)
                                    dst = (self.RLW if n == 1 else self.RAI)[d]
                                    bci = d
                                    for t in range(ntile):
                                        bks = [self.bank(), self.bank()]
                                        for nh in range(2):
                                            self.mm(PS[bks[nh]][:, :], loT[pr, t * 128:(t + 1) * 128], l2[pr, nh * 512:(nh + 1) * 512], True, True, ['loT', l2n], [('ps', bks[nh])])

                                        def ev(o, r_, bk, nh, n=n, bci=bci):
                                            self.tt('dve', o, PS[bk][:, :], bc0[:, bci, nh * 512:(nh + 1) * 512], ALU.add, [('ps', bk), 'bc0'], [r_])
                                            self.act(o, o, AF.Sigmoid, [r_], [r_])
                                            if n == 1:
                                                self.ts('pool', o, o, -C64, None, ALU.mult, None, [r_], [r_])
                                        stage_out(dst, roff + t * 128, [
                                            lambda o, r_, bk=bks[0]: ev(o, r_, bk, 0),
                                            lambda o, r_, bk=bks[1]: ev(o, r_, bk, 1)])
                em.flush()
            with ExitStack() as s:
                E = s.enter_context
                bc1 = E(self.sbt("bc1", [128, 2, D], F32))
                self.dma('sp', bc1[:, 0, :], self.rw_k_k.to_broadcast([128, D]), (), ['bc1'])
                self.dma('sp', bc1[:, 1, :], self.rw_k_a.to_broadcast([128, D]), (), ['bc1'])
                kt = [E(self.sbt("ekt%d" % k, [128, D], F32)) for k in range(2)]
                at = [[E(self.sbt("eat%d_%d" % (d, k), [128, D], F32)) for k in range(2)] for d in range(2)]
                kk = [E(self.sbt("ekk%d" % k, [128, D], F32)) for k in range(2)]
                sq = [E(self.sbt("esq%d" % k, [128, D], F32)) for k in range(2)]
                o1 = [[E(self.sbt("eo1%d_%d" % (d, k), [128, D], F32)) for k in range(2)] for d in range(2)]
                o2 = [[E(self.sbt("eo2%d_%d" % (d, k), [128, D], F32)) for k in range(2)] for d in range(2)]
                ss = [E(self.sbt("ess%d" % k, [128, 16], F32)) for k in range(2)]
                for t in range(NT // 128):
                    p = t % 2
                    row = t * 128
                    RR_ = lambda nm: (nm, p)
                    self.dma('sp', kt[p][:], self.RK[row:row + 128, :], [('RK', t)], [RR_('ekt')])
                    for d in range(2):
                        self.dma('act', at[d][p][:], self.RAI[d][row:row + 128, :], [(self.RAI[d].tensor.name, t)], [RR_('eat%d' % d)])
                    self.tt('dve', kk[p][:], kt[p][:], bc1[:, 0, :], ALU.mult, [RR_('ekt'), 'bc1'], [RR_('ekk')])
                    self.act(sq[p][:], kk[p][:], AF.Square, [RR_('ekk')], [RR_('esq')])
                    self.em.op('dve', lambda e, p=p: e.tensor_reduce(out=ss[p][:], in_=R3(sq[p][:]), axis=AX.X, op=ALU.add), [RR_('esq')], [RR_('ess')])
                    self.ts('dve', ss[p][:], ss[p][:], 1e-24, None, ALU.max, None, [RR_('ess')], [RR_('ess')])
                    self.act(ss[p][:], ss[p][:], AF.Ln, [RR_('ess')], [RR_('ess')])
                    self.act(ss[p][:], ss[p][:], AF.Exp, [RR_('ess')], [RR_('ess')], scale=-0.5)
                    self.tt('dve', R3(kk[p][:]), R3(kk[p][:]), bc64(ss[p][:]), ALU.mult, [RR_('ekk'), RR_('ess')], [RR_('ekk')])
                    self.dma('sp', self.KKN[row:row + 128, :], kk[p][:], [RR_('ekk')], [('KKN', t)])
                    for d in range(2):
                        a_ = at[d][p]
                        self.stt('dve', o1[d][p][:], a_[:], -1.0, bc1[:, 1, :], ALU.add, ALU.mult, [RR_('eat%d' % d), 'bc1'], [RR_('eo1%d' % d)])
                        self.stt('dve', o1[d][p][:], o1[d][p][:], 1.0, kt[p][:], ALU.add, ALU.mult, [RR_('eo1%d' % d), RR_('ekt')], [RR_('eo1%d' % d)])
                        self.dma('sp', self.KD[d][row:row + 128, :], o1[d][p][:], [RR_('eo1%d' % d)], [(self.KD[d].tensor.name, t)])
                        self.tt('pool', o2[d][p][:], kk[p][:], a_[:], ALU.mult, [RR_('ekk'), RR_('eat%d' % d)], [RR_('eo2%d' % d)])
                        self.dma('sp', self.BV[d][row:row + 128, :], o2[d][p][:], [RR_('eo2%d' % d)], [(self.BV[d].tensor.name, t)])
                em.flush()
            with ExitStack() as s:
                E = s.enter_context
                msk = E(self.sbt("msk", [128, 4, 128], F32))
                bdm = E(self.sbt("bdm", [128, 128], F32))
                self.dma('sp', msk[:], self.masks.rearrange("m p t -> p m t"), (), ['msk'])
                self.dma('sp', bdm[:], self.bdmask, (), ['bdm'])
                MK4 = E(self.sbt("MK4", [128, 4, 128], F32))
                ST2 = E(self.sbt("ST2", [128, 8, 128], F32))
                ld2 = [{nm: E(self.sbt("ld%d_" % k + nm, [128, D], F32)) for nm in ('lw', 'kd', 'bv', 'kkn', 'v', 'r')} for k in range(2)]
                nchunk = [0]
                cum = E(self.sbt("cum", [128, D], F32))
                tA = [E(self.sbt("tA%d" % k, [128, D], F32)) for k in range(2)]
                kb = E(self.sbt("kbar", [128, D], F32))
                bb = E(self.sbt("bbar", [128, D], F32))
                QT2 = E(self.sbt("QT2", [128, 8, 2, 128], F32))
                khT = E(self.sbt("khT", [128, 8, 128], F32))
                bhT = E(self.sbt("bhT", [128, 8, 128], F32))
                AT3 = E(self.sbt("AT3", [128, 16, 3, 128], F32))
                Lb = [E(self.sbt("Lb%d" % k, [128, 16, 128], F32)) for k in range(2)]
                Ub = [E(self.sbt("Ub%d" % k, [128, 16, 128], F32)) for k in range(2)]
                XT = E(self.sbt("XT", [128, 16, 128], F32))
                rhs_sb = E(self.sbt("rhs_sb", [128, D], F32))
                sa_sb = E(self.sbt("sa_sb", [128, D], F32))
                y_sb = E(self.sbt("y_sb", [128, D], F32))
                gC = E(self.sbt("gC", [128, 8], F32))
                sttmp = E(self.sbt("sttmp", [128, 4, 128], F32))
                for d in range(2):
                    mS, mI, mL = (0, 1, 2) if d == 0 else (2, 3, 0)
                    for q_, mi_ in enumerate((mS, mI, mS, mI)):
                        self.cp('dve', MK4[:, q_, :], msk[:, mi_, :], ['msk'], ['MK4'])
                    self.ms('dve', ST2[:], 0.0, ['ST2'])
                    order = [16, 17] + list(range(16)) if d == 0 else [17, 16] + list(range(15, -1, -1))
                    for ci in order:
                        row = ci * 128
                        emit = ci < 16
                        names = ['lw', 'kd', 'bv', 'kkn', 'v'] + (['r'] if emit else [])
                        srcs = {'lw': self.RLW[d], 'kd': self.KD[d], 'bv': self.BV[d], 'kkn': self.KKN, 'v': self.RV, 'r': self.RR}
                        pp = nchunk[0] % 2
                        nchunk[0] += 1
                        ld = ld2[pp]
                        for qi, nm in enumerate(names):
                            self.dma('sp', ld[nm][:], srcs[nm][row:row + 128, :], [(srcs[nm].tensor.name, ci)], [('ld', nm, pp)])
                        lw = ld['lw']
                        bks = [self.bank(), self.bank()]
                        for nh in range(2):
                            self.mm(PS[bks[nh]][:, :], msk[:, mI, :], lw[:, nh * 512:(nh + 1) * 512], True, True, ['msk', ('ld', 'lw', pp)], [('ps', bks[nh])])
                            self.cp('act' if nh == 0 else 'dve', cum[:, nh * 512:(nh + 1) * 512], PS[bks[nh]][:, :], [('ps', bks[nh])], ['cum'])
                        bt = [self.bank(), self.bank()]
                        for nh in range(2):
                            self.mm(PS[bt[nh]][:, :], self.ones[:], lw[:, nh * 512:(nh + 1) * 512], True, True, ['ones', ('ld', 'lw', pp)], [('ps', bt[nh])])
                        bg = self.bank()
                        for hp in range(8):
                            self.mm(PS[bg][:, hp:hp + 1], lw[:, hp * 128:(hp + 1) * 128], self.ones[:, 0:1], True, True, ['ones', ('ld', 'lw', pp)], [('ps', bg)])
                        self.act(gC[:], PS[bg][:, 0:8], AF.Exp, [('ps', bg)], ['gC'])
                        for nh in range(2):
                            sl = slice(nh * 512, (nh + 1) * 512)
                            self.tt('dve', tA[0][:, sl], PS[bt[nh]][:, :], cum[:, sl], ALU.subtract, [('ps', bt[nh]), 'cum'], [('tA', 0)])
                        self.act(tA[0][:], tA[0][:], AF.Exp, [('tA', 0)], [('tA', 0)])
                        self.tt('dve', kb[:], ld['kd'][:], tA[0][:], ALU.mult, [('ld', 'kd', pp), ('tA', 0)], ['kbar'])
                        self.tt('pool', bb[:], ld['bv'][:], tA[0][:], ALU.mult, [('ld', 'bv', pp), ('tA', 0)], ['bbar'])

                        def transp(src, srcr, dst_fn, dstr):
                            for g4 in range(2):
                                bk = self.bank()
                                for q4 in range(4):
                                    hp = g4 * 4 + q4
                                    self.tr(PS[bk][:, q4 * 128:(q4 + 1) * 128], src[:, hp * 128:(hp + 1) * 128], [srcr], [('ps', bk)])
                                dst_fn(g4, PS[bk][:, :].rearrange("p (q t) -> p q t", t=128), bk, dstr)
                        self.tt('dve', tA[0][:], cum[:], lw[:], ALU.subtract, ['cum', ('ld', 'lw', pp)], [('tA', 0)])
                        self.act(tA[0][:], tA[0][:], AF.Exp, [('tA', 0)], [('tA', 0)])
                        self.stt('dve', tA[0][:], ld['kkn'][:], -1.0, tA[0][:], ALU.mult, ALU.mult, [('ld', 'kkn', pp), ('tA', 0)], [('tA', 0)])
                        transp(tA[0], ('tA', 0), lambda g4, pv, bk, dr: self.cp('act', QT2[:, g4 * 4:(g4 + 1) * 4, 0, :], pv, [('ps', bk)], [dr]), 'QT2')
                        if emit:
                            self.act(tA[1][:], cum[:], AF.Exp, ['cum'], [('tA', 1)])
                            self.tt('dve', tA[1][:], tA[1][:], ld['r'][:], ALU.mult, [('tA', 1), ('ld', 'r', pp)], [('tA', 1)])
                            transp(tA[1], ('tA', 1), lambda g4, pv, bk, dr: self.cp('dve', QT2[:, g4 * 4:(g4 + 1) * 4, 1, :], pv, [('ps', bk)], [dr]), 'QT2')
                        self.act(tA[0][:], cum[:], AF.Exp, ['cum'], [('tA', 0)], scale=-1.0)
                        self.tt('dve', tA[1][:], tA[0][:], ld['kd'][:], ALU.mult, [('tA', 0), ('ld', 'kd', pp)], [('tA', 1)])
                        transp(tA[1], ('tA', 1), lambda g4, pv, bk, dr: self.cp('act', khT[:, g4 * 4:(g4 + 1) * 4, :], pv, [('ps', bk)], [dr]), 'khT')
                        self.tt('pool', tA[0][:], tA[0][:], ld['bv'][:], ALU.mult, [('tA', 0), ('ld', 'bv', pp)], [('tA', 0)])
                        transp(tA[0], ('tA', 0), lambda g4, pv, bk, dr: self.cp('dve', bhT[:, g4 * 4:(g4 + 1) * 4, :], pv, [('ps', bk)], [dr]), 'bhT')
                        NQ = 256 if emit else 128
                        for h in range(16):
                            hp, hh = h // 2, h % 2
                            pr = slice(hh * 64, (hh + 1) * 64)
                            bk = self.bank()
                            q2 = QT2[pr, hp, :, :].rearrange("p a t -> p (a t)")
                            self.mm(PS[bk][:, 0:NQ], khT[pr, hp, :], q2[:, 0:NQ], True, True, ['khT', 'QT2'], [('ps', bk)])
                            self.mm(PS[bk][:, 256:256 + NQ], bhT[pr, hp, :], q2[:, 0:NQ], True, True, ['bhT', 'QT2'], [('ps', bk)])
                            pv = PS[bk][:, :].rearrange("p (q t) -> p q t", t=128)
                            eng = 'dve'
                            if emit:
                                self.tt(eng, AT3[:, h, 0:2, :], pv[:, 0:2, :], MK4[:, 0:2, :], ALU.mult, [('ps', bk), 'MK4'], [('AT3', h)])
                                self.tt(eng, Ub[0][:, h, :], pv[:, 2, :], MK4[:, 2, :], ALU.mult, [('ps', bk), 'MK4'], [('Ub', 0, h // 4)])
                                self.tt(eng, AT3[:, h, 2, :], pv[:, 3, :], MK4[:, 3, :], ALU.mult, [('ps', bk), 'MK4'], [('AT3', h)])
                            else:
                                self.tt(eng, AT3[:, h, 0, :], pv[:, 0, :], MK4[:, 0, :], ALU.mult, [('ps', bk), 'MK4'], [('AT3', h)])
                                self.tt(eng, Ub[0][:, h, :], pv[:, 2, :], MK4[:, 2, :], ALU.mult, [('ps', bk), 'MK4'], [('Ub', 0, h // 4)])
                        for g4 in range(4):
                            bk = self.bank()
                            for q4 in range(4):
                                h = g4 * 4 + q4
                                hp, hh = h // 2, h % 2
                                pr = slice(hh * 64, (hh + 1) * 64)
                                self.mm(PS[bk][:, q4 * 128:(q4 + 1) * 128], QT2[pr, hp, 0, :], bhT[pr, hp, :], True, True, ['QT2', 'bhT'], [('ps', bk)])
                            pv = PS[bk][:, :].rearrange("p (q t) -> p q t", t=128)
                            self.tt('dve', Lb[0][:, g4 * 4:(g4 + 1) * 4, :], pv, msk[:, mL, :].unsqueeze(1).to_broadcast([128, 4, 128]), ALU.mult,
                                    [('ps', bk), 'msk'], [('Lb', 0, g4)])
                            self.tt('pool', XT[:, g4 * 4:(g4 + 1) * 4, :], Ub[0][:, g4 * 4:(g4 + 1) * 4, :],
                                    self.ident[:, :].unsqueeze(1).to_broadcast([128, 4, 128]), ALU.add, [('Ub', 0, g4), 'ident'], [('XT', g4)])
                        cur = 0
                        for lev in range(6):
                            nxt = 1 - cur
                            last = lev == 5
                            bL = [self.bank() for _ in range(4)]
                            for q4 in range(4):
                                for g4 in range(4):
                                    h = g4 * 4 + q4
                                    self.mm(PS[bL[g4]][:, q4 * 128:(q4 + 1) * 128], Ub[cur][:, h, :], Lb[cur][:, h, :], True, True,
                                            [('Ub', cur, g4), ('Lb', cur, g4)], [('ps', bL[g4])])
                            for g4 in range(4):
                                self.cp('act', Lb[nxt][:, g4 * 4:(g4 + 1) * 4, :], PS[bL[g4]][:, :].rearrange("p (q t) -> p q t", t=128),
                                        [('ps', bL[g4])], [('Lb', nxt, g4)])
                            if not last:
                                bU = [self.bank() for _ in range(4)]
                                for q4 in range(4):
                                    for g4 in range(4):
                                        h = g4 * 4 + q4
                                        self.mm(PS[bU[g4]][:, q4 * 128:(q4 + 1) * 128], Lb[cur][:, h, :], Ub[cur][:, h, :], True, True,
                                                [('Ub', cur, g4), ('Lb', cur, g4)], [('ps', bU[g4])])
                                for g4 in range(4):
                                    self.cp('dve', Ub[nxt][:, g4 * 4:(g4 + 1) * 4, :], PS[bU[g4]][:, :].rearrange("p (q t) -> p q t", t=128),
                                            [('ps', bU[g4])], [('Ub', nxt, g4)])
                            bX = [self.bank() for _ in range(4)]
                            for q4 in range(4):
                                for g4 in range(4):
                                    h = g4 * 4 + q4
                                    self.mm(PS[bX[g4]][:, q4 * 128:(q4 + 1) * 128], Lb[nxt][:, h, :], XT[:, h, :], True, True,
                                            [('Lb', nxt, g4), ('XT', g4)], [('ps', bX[g4])])
                            for g4 in range(4):
                                self.tt('dve', XT[:, g4 * 4:(g4 + 1) * 4, :], XT[:, g4 * 4:(g4 + 1) * 4, :],
                                        PS[bX[g4]][:, :].rearrange("p (q t) -> p q t", t=128), ALU.add,
                                        [('ps', bX[g4]), ('XT', g4)], [('XT', g4)])
                            cur = nxt
                        v_ = ld['v']
                        br = [self.bank(), self.bank()]
                        for hp in range(8):
                            bk = br[hp // 4]
                            c0 = (hp % 4) * 128
                            self.mm(PS[bk][:, c0:c0 + 128], QT2[:, hp, 0, :], ST2[:, hp, :], True, False, ['QT2', 'ST2'], [('ps', bk)])
                            for hh in range(2):
                                h = hp * 2 + hh
                                self.mm(PS[bk][:, c0 + hh * 64:c0 + (hh + 1) * 64], AT3[:, h, 0, :], v_[:, h * 64:(h + 1) * 64], False, hh == 1,
                                        [('AT3', h), ('ld', 'v', pp)], [('ps', bk)])
                        self.cp('act', rhs_sb[:, 0:512], PS[br[0]][:, :], [('ps', br[0])], ['rhs_sb'])
                        self.cp('dve', rhs_sb[:, 512:1024], PS[br[1]][:, :], [('ps', br[1])], ['rhs_sb'])
                        bs_ = [self.bank(), self.bank()]
                        for h in range(16):
                            bk = bs_[h // 8]
                            c0 = (h % 8) * 64
                            self.mm(PS[bk][:, c0:c0 + 64], XT[:, h, :], rhs_sb[:, h * 64:(h + 1) * 64], True, True, [('XT', h // 4), 'rhs_sb'], [('ps', bk)])
                        self.cp('act', sa_sb[:, 0:512], PS[bs_[0]][:, :], [('ps', bs_[0])], ['sa_sb'])
                        self.cp('dve', sa_sb[:, 512:1024], PS[bs_[1]][:, :], [('ps', bs_[1])], ['sa_sb'])
                        if emit:
                            by = [self.bank(), self.bank()]
                            for hp in range(8):
                                bk = by[hp // 4]
                                c0 = (hp % 4) * 128
                                self.mm(PS[bk][:, c0:c0 + 128], QT2[:, hp, 1, :], ST2[:, hp, :], True, False, ['QT2', 'ST2'], [('ps', bk)])
                                for hh in range(2):
                                    h = hp * 2 + hh
                                    self.mm(PS[bk][:, c0 + hh * 64:c0 + (hh + 1) * 64], AT3[:, h, 1, :], v_[:, h * 64:(h + 1) * 64], False, False,
                                            [('AT3', h), ('ld', 'v', pp)], [('ps', bk)])
                                    self.mm(PS[bk][:, c0 + hh * 64:c0 + (hh + 1) * 64], AT3[:, h, 2, :], sa_sb[:, h * 64:(h + 1) * 64], False, hh == 1,
                                            [('AT3', h), 'sa_sb'], [('ps', bk)])
                            self.cp('act', y_sb[:, 0:512], PS[by[0]][:, :], [('ps', by[0])], ['y_sb'])
                            self.cp('dve', y_sb[:, 512:1024], PS[by[1]][:, :], [('ps', by[1])], ['y_sb'])
                            self.dma('sp', self.YS[d][row:row + 128, :], y_sb[:], ['y_sb'], [(self.YS[d].tensor.name, ci)])
                        bst = [self.bank(), self.bank()]
                        for hp in range(8):
                            bk = bst[hp // 4]
                            c0 = (hp % 4) * 128
                            self.mm(PS[bk][:, c0:c0 + 128], kb[:, hp * 128:(hp + 1) * 128], v_[:, hp * 128:(hp + 1) * 128], True, False,
                                    ['kbar', ('ld', 'v', pp)], [('ps', bk)])
                            self.mm(PS[bk][:, c0:c0 + 128], bb[:, hp * 128:(hp + 1) * 128], sa_sb[:, hp * 128:(hp + 1) * 128], False, True,
                                    ['bbar', 'sa_sb'], [('ps', bk)])
                        for g4 in range(2):
                            self.tt('dve', sttmp[:], PS[bst[g4]][:, :].rearrange("p (q t) -> p q t", t=128),
                                    bdm[:, :].unsqueeze(1).to_broadcast([128, 4, 128]), ALU.mult, [('ps', bst[g4]), 'bdm'], ['sttmp'])
                            self.tt('pool', ST2[:, g4 * 4:(g4 + 1) * 4, :], ST2[:, g4 * 4:(g4 + 1) * 4, :],
                                    gC[:, g4 * 4:(g4 + 1) * 4].unsqueeze(2).to_broadcast([128, 4, 128]), ALU.mult, ['ST2', 'gC'], ['ST2'])
                            self.tt('dve', ST2[:, g4 * 4:(g4 + 1) * 4, :], ST2[:, g4 * 4:(g4 + 1) * 4, :], sttmp[:], ALU.add, ['ST2', 'sttmp'], ['ST2'])
                em.flush()
            with ExitStack() as s:
                E = s.enter_context
                w_o = E(self.sbt("rw_wo", [128, KC, D], BF16))
                bc2 = E(self.sbt("bc2", [128, 3, D], F32))
                self.dma('sp', bc2[:, 0, :], self.rw_r_k.to_broadcast([128, D]), (), ['bc2'])
                self.dma('sp', bc2[:, 1, :], self.rw_gn_g.to_broadcast([128, D]), (), ['bc2'])
                self.dma('sp', bc2[:, 2, :], self.rw_gn_b.to_broadcast([128, D]), (), ['bc2'])
                src = self.rw_w_o.rearrange("(k p) n -> p k n", p=128)
                for hh in range(2):
                    self.dma('pool', w_o[:, :, hh * 512:(hh + 1) * 512], src[:, :, hh * 512:(hh + 1) * 512], (), [('w_o', hh)])
                tl = self.alloc_pn_tiles(E, True)
                L_ = {nm: E(self.sbt("d_" + nm, [128, D], F32)) for nm in ('y0', 'y1', 'r', 'kd0', 'kd1', 'v', 'g')}
                acc = E(self.sbt("d_acc", [128, D], F32))
                tq = E(self.sbt("d_tq", [128, D], F32))
                st16 = E(self.sbt("d_st", [128, 4, 16], F32))
                zT = E(self.sbt("d_zT", [128, KC, 128], BF16))
                for t in range(T // 128):
                    row = t * 128
                    srcs = {'y0': self.YS[0], 'y1': self.YS[1], 'r': self.RR, 'kd0': self.KD[0], 'kd1': self.KD[1], 'v': self.RV, 'g': self.RG}
                    for qi, nm in enumerate(L_):
                        self.dma('sp' if qi % 2 == 0 else 'act', L_[nm][:], srcs[nm][row:row + 128, :], [(srcs[nm].tensor.name, t)], [('dl', nm)])
                    for d in range(2):
                        y = L_['y%d' % d]
                        yr = ('dl', 'y%d' % d)
                        self.em.op('dve', lambda e, y=y: e.tensor_reduce(out=st16[:, 0, :], in_=R3(y[:]), axis=AX.X, op=ALU.add), [yr], ['st16'])
                        self.act(tq[:], y[:], AF.Square, [yr], ['tq'])
                        self.em.op('dve', lambda e: e.tensor_reduce(out=st16[:, 1, :], in_=R3(tq[:]), axis=AX.X, op=ALU.add), ['tq'], ['st16'])
                        self.ts('dve', st16[:, 0, :], st16[:, 0, :], 1.0 / 64, None, ALU.mult, None, ['st16'], ['st16'])
                        self.tt('dve', st16[:, 2, :], st16[:, 0, :], st16[:, 0, :], ALU.mult, ['st16'], ['st16'])
                        self.stt('dve', st16[:, 1, :], st16[:, 1, :], 1.0 / 64, st16[:, 2, :], ALU.mult, ALU.subtract, ['st16'], ['st16'])
                        self.act(st16[:, 1, :], st16[:, 1, :], AF.Ln, ['st16', 'epsc'], ['st16'], bias=self.epsc[:, 1:2])
                        self.act(st16[:, 1, :], st16[:, 1, :], AF.Exp, ['st16'], ['st16'], scale=-0.5)
                        self.tt('dve', R3(y[:]), R3(y[:]), bc64(st16[:, 0, :]), ALU.subtract, [yr, 'st16'], [yr])
                        self.tt('dve', R3(y[:]), R3(y[:]), bc64(st16[:, 1, :]), ALU.mult, [yr, 'st16'], [yr])
                        self.tt('pool', y[:], y[:], bc2[:, 1, :], ALU.mult, [yr, 'bc2'], [yr])
                        if d == 0:
                            self.tt('pool', acc[:], y[:], bc2[:, 2, :], ALU.add, [yr, 'bc2'], ['acc'])
                        else:
                            self.tt('pool', y[:], y[:], bc2[:, 2, :], ALU.add, [yr, 'bc2'], [yr])
                            self.tt('pool', acc[:], acc[:], y[:], ALU.add, [yr, 'acc'], ['acc'])
                        kd_ = L_['kd%d' % d]
                        kr_ = ('dl', 'kd%d' % d)
                        self.tt('dve', tq[:], L_['r'][:], kd_[:], ALU.mult, [('dl', 'r'), kr_], ['tq'])
                        self.tt('dve', tq[:], tq[:], bc2[:, 0, :], ALU.mult, ['tq', 'bc2'], ['tq'])
                        self.em.op('dve', lambda e: e.tensor_reduce(out=st16[:, 3, :], in_=R3(tq[:]), axis=AX.X, op=ALU.add), ['tq'], ['st16'])
                        self.tt('dve', R3(tq[:]), R3(L_['v'][:]), bc64(st16[:, 3, :]), ALU.mult, [('dl', 'v'), 'st16'], ['tq'])
                        self.tt('dve', acc[:], acc[:], tq[:], ALU.add, ['acc', 'tq'], ['acc'])
                    self.tt('dve', acc[:], acc[:], L_['g'][:], ALU.mult, ['acc', ('dl', 'g')], ['acc'])
                    for half in range(2):
                        bk = self.bank()
                        for kk_ in range(4):
                            k = half * 4 + kk_
                            self.tr(PS[bk][:, kk_ * 128:(kk_ + 1) * 128], acc[:, k * 128:(k + 1) * 128], ['acc'], [('ps', bk)])
                        self.cp('act' if half == 0 else 'dve', zT[:, half * 4:(half + 1) * 4, :], PS[bk][:, :].rearrange("p (q t) -> p q t", t=128),
                                [('ps', bk)], ['zT'])
                    bs = []
                    for nh in range(2):
                        bk = self.bank()
                        bs.append(bk)
                        for k in range(KC):
                            self.mm(PS[bk][:, :], zT[:, k, :], w_o[:, k, nh * 512:(nh + 1) * 512], k == 0, k == KC - 1, ['zT', ('w_o', nh)], [('ps', bk)])
                    self.postnorm_tile(tl, b * T + row, b, 0, hsrc, [PS[bs[0]][:, :], PS[bs[1]][:, :]], [[('ps', bs[0])], [('ps', bs[1])]],
                                       self.H, 'H', bias_bc=None, router=True)
                em.flush()


def tr128(v, nchunk):
    return np.ascontiguousarray(v.reshape(nchunk, 128).T)


def make_bm_tab(rpb):
    NEG = np.float32(-30000.0)
    out = np.full((16, 5, 2, 64, 10, 64), NEG, np.float32)
    rep = {0: 0, 1: 1, 2: 2, 3: 14, 4: 15}
    c = np.arange(64)
    kc = np.arange(64)
    win_c0 = np.clip(c - 8, 0, 48)
    col_ok = (kc[None, :] >= win_c0[:, None]) & (kc[None, :] < win_c0[:, None] + 16)
    cidx = np.clip(kc[None, :] - c[:, None] + 15, 0, 30)
    for cls, i_ in rep.items():
        ks = min(max(2 * i_ - 4, 0), 22)
        for rr in range(2):
            r = 2 * i_ + rr
            row0 = min(max(r - 4, 0), 24)
            for krel in range(10):
                kr = ks + krel
                if not (row0 <= kr < row0 + 8):
                    continue
                ridx = kr - r + 7
                vals = rpb[:, ridx, :][:, cidx]
                out[:, cls, rr, :, krel, :] = np.where(col_ok[None], vals, NEG)
    return np.ascontiguousarray(out.reshape(16, 5, 128, 640))


def prep_shared(inp):
    f = lambda a: np.ascontiguousarray(np.asarray(a, dtype=np.float32))
    sh = {}
    sh['ident'] = np.eye(128, dtype=np.float32)
    sh['ada_w'] = f(inp['ada_w'])
    sh['ada_b'] = f(inp['ada_b'])
    sh['ada_bT'] = np.stack([tr128(f(inp['ada_b'])[i], 48) for i in range(DEPTH)])
    sh['ln_g'] = f(inp['ln_g'])
    sh['ln_b'] = f(inp['ln_b'])
    sh['conv_w_in'] = f(inp['conv_w_in'])
    sh['conv_b_inT'] = np.stack([tr128(f(inp['conv_b_in'])[i], 16) for i in range(2)])
    wdw = f(inp['conv_w_dw'])
    sh['conv_w_dwT'] = np.ascontiguousarray(wdw.reshape(2, CONVW, KC, 128).transpose(0, 3, 2, 1))
    sh['conv_b_dwT'] = np.stack([tr128(f(inp['conv_b_dw'])[i], KC) for i in range(2)])
    sh['conv_ln_gT'] = np.stack([tr128(f(inp['conv_ln_g'])[i], KC) for i in range(2)])
    sh['conv_ln_bT'] = np.stack([tr128(f(inp['conv_ln_b'])[i], KC) for i in range(2)])
    sh['conv_w_out'] = f(inp['conv_w_out'])
    sh['conv_b_out'] = f(inp['conv_b_out'])
    sh['na_w_qkv'] = f(inp['na_w_qkv'])
    sh['na_w_o'] = f(inp['na_w_o'])
    sh['bm_tab'] = make_bm_tab(f(inp['na_rpb'])[0])
    mp = f(inp['rw_mu_prev'])[0]
    mn = f(inp['rw_mu_next'])[0]
    sh['rw_mu_prevT'] = np.ascontiguousarray(mp.reshape(6, KC, 128).transpose(2, 0, 1))
    sh['rw_mu_nextT'] = np.ascontiguousarray(mn.reshape(6, KC, 128).transpose(2, 0, 1))
    for nm in ("rw_w_r", "rw_w_k", "rw_w_v", "rw_w_o", "rw_w0", "rw_a0", "rw_w1", "rw_a1", "rw_w2", "rw_a2", "rw_g1", "rw_g2"):
        sh[nm] = np.ascontiguousarray(f(inp[nm])[0])
    for nm in ("rw_k_k", "rw_k_a", "rw_r_k", "rw_gn_g", "rw_gn_b"):
        sh[nm] = np.ascontiguousarray(f(inp[nm])[0].reshape(1, D))
    tri = np.triu(np.ones((128, 128), np.float32))
    sh['masks'] = np.stack([np.triu(tri, 1), tri, np.tril(tri.T, -1), tri.T]).astype(np.float32)
    bd = np.zeros((128, 128), np.float32)
    bd[:64, :64] = 1
    bd[64:, 64:] = 1
    sh['bdmask'] = bd
    sh['moe_router'] = f(inp['moe_router'])
    sh['moe_w_gate'] = f(inp['moe_w_gate'])
    sh['moe_w_up'] = f(inp['moe_w_up'])
    sh['moe_w_down'] = f(inp['moe_w_down'])
    return sh


def prep_core(inp, core):
    f = lambda a: np.asarray(a, dtype=np.float32)
    b0 = core * NB
    x = f(inp['x'])[b0:b0 + NB].reshape(NB * T, D)
    cx = f(inp['ctx'])[b0:b0 + NB].reshape(NB * L, D)
    hin = np.ascontiguousarray(np.concatenate([x, cx], axis=0))
    conds = np.stack([f(inp['c'])[b0], f(inp['c'])[b0 + 1], f(inp['c_ctx'])], axis=1)
    cT = np.ascontiguousarray(conds.reshape(KC, 128, 3).transpose(1, 0, 2))
    return {'hin': hin, 'cT': cT}


_NC_CACHE = {}


def kernel(**inputs):
    kb = KB()
    nc = kb.build()
    sh = prep_shared(inputs)
    in_maps = []
    for c in range(NCORES):
        m = dict(sh)
        m.update(prep_core(inputs, c))
        in_maps.append(m)
    res = run_bass_kernel_spmd(nc, in_maps, core_ids=list(range(NCORES)))
    outs = [r["out"].reshape(NB, T, D) for r in res.results]
    return np.concatenate(outs, axis=0).astype(np.float32)
```

```python
import numpy as np
import concourse.bass as bass
import concourse.mybir as mybir
from concourse.bass_utils import run_bass_kernel_spmd
from contextlib import ExitStack

F32 = mybir.dt.float32
F32R = mybir.dt.float32r
BF16 = mybir.dt.bfloat16
U32 = mybir.dt.uint32
I32 = mybir.dt.int32
ALU = mybir.AluOpType
AF = mybir.ActivationFunctionType
AX = mybir.AxisListType

ENGS = ('pe', 'act', 'dve', 'pool', 'sp')
DMAQ = ('sp', 'act', 'pool')


class Em:
    def __init__(self, nc, ndma=8):
        self.nc = nc
        self.K = ndma
        self.stack = ExitStack()
        ent = self.stack.enter_context
        self.csem = {e: ent(nc.semaphore("c_" + e)) for e in ENGS}
        self.dsem = {q: [ent(nc.semaphore("d_%s%d" % (q, i))) for i in range(ndma)] for q in DMAQ}
        self.ccnt = {e: 0 for e in ENGS}
        self.dcnt = {q: 0 for q in DMAQ}
        self.seen = {e: {} for e in ENGS}
        self.lastw = {}
        self.rd = {}
        self.prog = {e: [] for e in ENGS}
        self.ninstr = 0
        self.sigcnt = {}
        self.pe_sync_toks = set()

    def sem(self, key):
        if key[0] == 'c':
            return self.csem[key[1]]
        return self.dsem[key[1]][key[2]]

    def _deps(self, eng, reads, writes, pe_sync=False):
        need = {}
        pe_keep = pe_sync
        for r in reads:
            t = self.lastw.get(r)
            if t is not None and need.get(t[0], 0) < t[1]:
                need[t[0]] = t[1]
        for w in writes:
            t = self.lastw.get(w)
            if t is not None:
                if t in self.pe_sync_toks:
                    pe_keep = True
                if need.get(t[0], 0) < t[1]:
                    need[t[0]] = t[1]
            for k, v in self.rd.get(w, {}).items():
                if need.get(k, 0) < v:
                    need[k] = v
        waits = []
        seen = self.seen[eng]
        for k, v in need.items():
            if eng == 'pe' and k == ('c', 'pe') and not pe_keep:
                continue
            if seen.get(k, 0) < v:
                seen[k] = v
                waits.append((k, v))
        return waits

    def _reg(self, tok, reads, writes):
        for r in reads:
            self.rd.setdefault(r, {})[tok[0]] = tok[1]
        for w in writes:
            self.lastw[w] = tok
            self.rd[w] = {}

    def op(self, eng, fn, reads=(), writes=(), pe_sync=False):
        waits = self._deps(eng, reads, writes, pe_sync)
        self.ccnt[eng] += 1
        tok = (('c', eng), self.ccnt[eng])
        if pe_sync:
            self.pe_sync_toks.add(tok)
        self.prog[eng].append((waits, fn, tok, 1))
        self._reg(tok, reads, writes)
        self.ninstr += 1

    def dma(self, q, fn, reads=(), writes=()):
        waits = self._deps(q, reads, writes)
        n = self.dcnt[q]
        self.dcnt[q] += 1
        slot = n % self.K
        val = 16 * (n // self.K + 1)
        key = ('d', q, slot)
        if val > 16 and self.seen[q].get(key, 0) < val - 16:
            self.seen[q][key] = val - 16
            waits.append((key, val - 16))
        tok = (key, val)
        self.prog[q].append((waits, fn, tok, 16))
        self._reg(tok, reads, writes)
        self.ninstr += 1

    def barrier(self):
        for e in ENGS:
            waits = []
            seen = self.seen[e]
            for e2 in ENGS:
                k = ('c', e2)
                v = self.ccnt[e2]
                if e2 != e and v > 0 and seen.get(k, 0) < v:
                    seen[k] = v
                    waits.append((k, v))
            for q in DMAQ:
                n = self.dcnt[q]
                for slot in range(self.K):
                    cnt = (n - slot + self.K - 1) // self.K if n > slot else 0
                    if cnt > 0:
                        k = ('d', q, slot)
                        v = 16 * cnt
                        if seen.get(k, 0) < v:
                            seen[k] = v
                            waits.append((k, v))
            if waits:
                self.prog[e].append((waits, None, None, 0))
        self.lastw = {}
        self.rd = {}

    def flush(self):
        self.barrier()
        nc = self.nc
        waited = {e: set() for e in ENGS}
        for e in ENGS:
            for waits, fn, tok, inc in self.prog[e]:
                for k, v in waits:
                    if k[0] == 'c':
                        waited[k[1]].add(v)
        newval = {e: {} for e in ENGS}
        for e in ENGS:
            sc = self.sigcnt.get(e, 0)
            for waits, fn, tok, inc in self.prog[e]:
                if fn is not None and tok[0][0] == 'c' and tok[1] in waited[e]:
                    sc += 1
                    newval[e][tok[1]] = sc
            self.sigcnt[e] = sc

        def tr_(k, v):
            if k[0] == 'c':
                return newval[k[1]][v]
            return v
        with nc.Block() as block:
            for e, meth in (('pe', block.tensor), ('act', block.scalar), ('dve', block.vector),
                            ('pool', block.gpsimd), ('sp', block.sync)):
                prog = self.prog[e]
                self.prog[e] = []

                def body(eng, prog=prog, e=e):
                    for waits, fn, tok, inc in prog:
                        if fn is None:
                            for k, v in waits:
                                eng.wait_ge(self.sem(k), tr_(k, v))
                            continue
                        for k, v in waits[:-1]:
                            eng.wait_ge(self.sem(k), tr_(k, v))
                        ins = fn(eng)
                        if waits:
                            k, v = waits[-1]
                            ins._wait_ge(self.sem(k), tr_(k, v))
                        if tok[0][0] == 'c':
                            if tok[1] in newval[e]:
                                ins.then_inc(self.sem(tok[0]), 1)
                        else:
                            ins.then_inc(self.sem(tok[0]), inc)
                meth(body)

    def close(self):
        self.stack.close()


D = 1024
T = 2048
L = 256
NB = 2
KC = 8
NE = 16
CAP = 256
CAPC = 32
DE = 2048
DEPTH = 4
NROW = NB * T + NB * L
CTX0 = NB * T
ALPHA = float(8 ** 0.25)
LN_EPS = 1e-5
GRID_W = 64
CONVW = 31
NCORES = 8


class KB:
    def __init__(self, layers=(0, 1, 2, 3), n_exp=NE, debug=False):
        self.layers = list(layers)
        self.n_exp = n_exp
        self.debug = debug
        nc = bass.Bass("TRN2", target_bir_lowering=False)
        self.nc = nc
        self.em = Em(nc)
        self.dr = {}
        self._psi = 0

    def din(self, name, shape, dt=F32):
        t = self.nc.dram_tensor(name, list(shape), dt, kind="ExternalInput").ap()
        self.dr[name] = t
        return t

    def dout(self, name, shape, dt=F32):
        t = self.nc.dram_tensor(name, list(shape), dt, kind="ExternalOutput").ap()
        self.dr[name] = t
        return t

    def dint(self, name, shape, dt=F32):
        t = self.nc.dram_tensor(name, list(shape), dt, kind="ExternalOutput" if self.debug else "Internal").ap()
        self.dr[name] = t
        return t

    def sbt(self, name, shape, dt):
        self._uid = getattr(self, '_uid', 0) + 1
        return self.nc.sbuf_tensor("%s_%d" % (name, self._uid), shape, dt)

    def bank(self):
        i = self._psi
        self._psi = (i + 1) % 8
        return i

    def mm(self, out, lhsT, rhs, start, stop, r, w, sync=None):
        ps = (lhsT.dtype == F32) if sync is None else sync
        self.em.op('pe', lambda e: e.matmul(out, lhsT=lhsT, rhs=rhs, start=start, stop=stop), r, w, pe_sync=ps)

    def tr(self, out, in_, r, w, ident=None):
        idn = self.ident if ident is None else ident
        np_ = in_.shape[0]
        self.em.op('pe', lambda e: e.transpose(out=out, in_=in_, identity=idn[0:np_, 0:np_]), list(r) + ['ident'], w)

    def act(self, out, in_, func, r, w, bias=None, scale=None, accum_out=None):
        kw = {}
        if bias is not None:
            kw['bias'] = bias
        if scale is not None:
            kw['scale'] = scale
        if accum_out is not None:
            kw['accum_out'] = accum_out
        self.em.op('act', lambda e: e.activation(out=out, in_=in_, func=func, **kw), r, w)

    def tt(self, eng, out, in0, in1, op, r, w):
        self.em.op(eng, lambda e: e.tensor_tensor(out=out, in0=in0, in1=in1, op=op), r, w)

    def ts(self, eng, out, in0, s1, s2, op0, op1, r, w):
        if op1 is None:
            self.em.op(eng, lambda e: e.tensor_scalar(out=out, in0=in0, scalar1=s1, scalar2=None, op0=op0), r, w)
        else:
            self.em.op(eng, lambda e: e.tensor_scalar(out=out, in0=in0, scalar1=s1, scalar2=s2, op0=op0, op1=op1), r, w)

    def stt(self, eng, out, in0, scalar, in1, op0, op1, r, w):
        self.em.op(eng, lambda e: e.scalar_tensor_tensor(out=out, in0=in0, scalar=scalar, in1=in1, op0=op0, op1=op1), r, w)

    def cp(self, eng, out, in_, r, w):
        if eng == 'act':
            self.em.op('act', lambda e: e.copy(out=out, in_=in_), r, w)
        else:
            self.em.op(eng, lambda e: e.tensor_copy(out=out, in_=in_), r, w)

    def ms(self, eng, ap, val, w):
        self.em.op(eng, lambda e: e.memset(ap, val), (), w)

    def dma(self, q, out, in_, r, w, **kw):
        self.em.dma(q, lambda e: e.dma_start(out=out, in_=in_, **kw), r, w)

    def build(self):
        nc = self.nc
        em = self.em
        st = ExitStack()
        self.st = st
        E = st.enter_context
        self.hin = self.din("hin", [NROW, D])
        self.cT = self.din("cT", [128, KC, 3])
        self.ident_d = self.din("ident", [128, 128])
        self.ada_w = self.din("ada_w", [DEPTH, D, 6 * D])
        self.ada_bT = self.din("ada_bT", [DEPTH, 128, 48])
        self.ada_b = self.din("ada_b", [DEPTH, 6 * D])
        self.ln_g = self.din("ln_g", [DEPTH, 2, D])
        self.ln_b = self.din("ln_b", [DEPTH, 2, D])
        self.conv_w_in = self.din("conv_w_in", [2, D, 2 * D])
        self.conv_b_inT = self.din("conv_b_inT", [2, 128, 16])
        self.conv_w_dwT = self.din("conv_w_dwT", [2, 128, KC, CONVW])
        self.conv_b_dwT = self.din("conv_b_dwT", [2, 128, KC])
        self.conv_ln_gT = self.din("conv_ln_gT", [2, 128, KC])
        self.conv_ln_bT = self.din("conv_ln_bT", [2, 128, KC])
        self.conv_w_out = self.din("conv_w_out", [2, D, D])
        self.conv_b_out = self.din("conv_b_out", [2, D])
        self.moe_router = self.din("moe_router", [DEPTH, D, NE])
        self.moe_w_gate = self.din("moe_w_gate", [DEPTH, NE, D, DE])
        self.moe_w_up = self.din("moe_w_up", [DEPTH, NE, D, DE])
        self.moe_w_down = self.din("moe_w_down", [DEPTH, NE, DE, D])
        self.out = self.dout("out", [NB * T, D])
        self.H = self.dint("H", [NROW, D])
        self.Y = self.dint("Y", [NROW, D])
        self.extra_io()

        self.ident = E(self.sbt("ident_s", [128, 128], F32))
        self.ones = E(self.sbt("ones_s", [128, 128], F32))
        self.scT = E(self.sbt("scT", [128, KC, 3], BF16))
        self.modT = E(self.sbt("modT", [128, 32, 3], F32))
        self.ga_bc = E(self.sbt("ga_bc", [128, 2, 3, D], BF16))
        self.lnp = E(self.sbt("lnp", [128, 2, D], F32))
        self.affT = E(self.sbt("affT", [48, T], F32))
        self.affTc = E(self.sbt("affTc", [16, NB * L], F32))
        self.wr = E(self.sbt("wr", [128, KC, NE], F32))
        self.PS = [E(nc.psum_tensor("ps%d" % i, [128, 512], F32)) for i in range(8)]

        self.dma('sp', self.ident[:], self.ident_d, (), ['ident'])
        self.ms('dve', self.ones[:], 1.0, ['ones'])
        self.epsc = E(self.sbt("epsc", [128, 2], F32))
        self.ms('dve', self.epsc[:, 0:1], LN_EPS, ['epsc'])
        self.ms('dve', self.epsc[:, 1:2], 64e-5, ['epsc'])
        self.ms('dve', self.affT[:], 0.0, ['affT'])
        self.ms('dve', self.affTc[:], 0.0, ['affTc'])
        with ExitStack() as s2:
            cTs = s2.enter_context(self.sbt("cTs", [128, KC, 3], F32))
            self.dma('sp', cTs[:], self.cT, (), ['cTs'])
            self.act(self.scT[:], cTs[:], AF.Silu, ['cTs'], ['scT'])
            em.flush()

        for i in self.layers:
            kind, jj = i % 3, i // 3
            ctx_read = i <= 2
            ctx_live = i < 2
            self.ada_phase(i)
            if kind == 0:
                self.conv_phase(i, jj, ctx_live)
            elif kind == 1:
                self.na_phase(i, jj, ctx_live)
            else:
                self.rwkv_phase(i, jj)
            self.moe_phase(i, ctx_live, last=(i == self.layers[-1]))
        em.flush()
        st.close()
        em.close()
        return nc

    def extra_io(self):
        self.na_w_qkv = self.din("na_w_qkv", [1, D, 3 * D])
        self.na_w_o = self.din("na_w_o", [1, D, D])
        self.bm_tab = self.din("bm_tab", [16, 5, 128, 640])
        self.V_d = self.dint("V_d", [T + L, D], BF16)
        NT = T + L
        self.rw_mu_prevT = self.din("rw_mu_prevT", [128, 6, KC])
        self.rw_mu_nextT = self.din("rw_mu_nextT", [128, 6, KC])
        for nm in ("rw_w_r", "rw_w_k", "rw_w_v", "rw_w_o"):
            setattr(self, nm, self.din(nm, [D, D]))
        self.rw_w0 = self.din("rw_w0", [2, D])
        self.rw_a0 = self.din("rw_a0", [2, D])
        self.rw_w1 = self.din("rw_w1", [2, D, 64])
        self.rw_a1 = self.din("rw_a1", [2, D, 64])
        self.rw_w2 = self.din("rw_w2", [2, 64, D])
        self.rw_a2 = self.din("rw_a2", [2, 64, D])
        self.rw_g1 = self.din("rw_g1", [D, 128])
        self.rw_g2 = self.din("rw_g2", [128, D])
        for nm in ("rw_k_k", "rw_k_a", "rw_r_k", "rw_gn_g", "rw_gn_b"):
            setattr(self, nm, self.din(nm, [1, D]))
        self.masks = self.din("masks", [4, 128, 128])
        self.bdmask = self.din("bdmask", [128, 128])
        self.RK = self.dint("RK", [NT, D])
        self.RV = self.dint("RV", [NT, D])
        self.RR = self.dint("RR", [NT, D])
        self.RG = self.dint("RG", [NT, D])
        self.KKN = self.dint("KKN", [NT, D])
        self.RLW = [self.dint("RLW%d" % d, [NT, D]) for d in range(2)]
        self.RAI = [self.dint("RAI%d" % d, [NT, D]) for d in range(2)]
        self.KD = [self.dint("KD%d" % d, [NT, D]) for d in range(2)]
        self.BV = [self.dint("BV%d" % d, [NT, D]) for d in range(2)]
        self.YS = [self.dint("YS%d" % d, [T, D]) for d in range(2)]

    def load_lnp(self, i, g):
        for r_, src in ((0, self.ln_g), (1, self.ln_b)):
            self.dma('sp', self.lnp[:, r_, :], src[i, g:g + 1, :].to_broadcast([128, D]), (), [('lnp', r_)])

    def hsrc(self, i):
        return self.hin if i == self.layers[0] else self.H

    def ada_phase(self, i):
        nc = self.nc
        PS = self.PS
        with ExitStack() as s:
            E = s.enter_context
            wA = [E(self.sbt("wA%d" % k, [128, KC, D], BF16)) for k in range(2)]
            abT = E(self.sbt("abT", [128, 48], F32))
            gb = E(self.sbt("gb", [128, 2, D], F32))
            self.scR = E(self.sbt("scR", [128, KC, 3, 128], BF16))
            for j in range(3):
                self.cp('dve', self.scR[:, :, j, :], self.scT[:, :, j:j + 1].to_broadcast([128, KC, 128]), ['scT'], ['scR'])
            self.dma('sp', abT[:], self.ada_bT[i], (), ['abT'])
            for g, cb in enumerate((2, 5)):
                self.dma('sp', gb[:, g, :], self.ada_b[i:i + 1, cb * D:(cb + 1) * D].to_broadcast([128, D]), (), [('gb', g)])
            self.load_lnp(i, 0)
            self.dma('sp', self.wr[:], self.moe_router[i].rearrange("(k p) e -> p k e", p=128), (), ['wr'])
            for cb in range(6):
                wb = wA[cb % 2]
                src = self.ada_w[i][:, cb * D:(cb + 1) * D].rearrange("(k p) n -> p k n", p=128)
                for hh in range(2):
                    self.dma('pool', wb[:, :, hh * 512:(hh + 1) * 512], src[:, :, hh * 512:(hh + 1) * 512], (), [('wA', cb % 2, hh)])
                rw = [('wA', cb % 2, 0), ('wA', cb % 2, 1)]
                if cb in (0, 1, 3, 4):
                    v = (0, 1, None, 2, 3)[cb]
                    b = self.bank()
                    for fc in range(KC):
                        for k in range(KC):
                            self.mm(PS[b][:, fc * 3:fc * 3 + 3], wb[:, k, fc * 128:(fc + 1) * 128], self.scT[:, k, :],
                                    k == 0, k == KC - 1, rw + ['scT'], [('ps', b)])
                    pv = PS[b][:, 0:24].rearrange("p (f j) -> p f j", j=3)
                    self.tt('dve', self.modT[:, v * 8:(v + 1) * 8, :], pv,
                            abT[:, cb * 8:(cb + 1) * 8].unsqueeze(2).to_broadcast([128, 8, 3]), ALU.add,
                            [('ps', b), 'abT'], [('modT', v)])
                    if cb in (1, 4):
                        self.ts('dve', self.modT[:, v * 8:(v + 1) * 8, :], self.modT[:, v * 8:(v + 1) * 8, :], 1.0, None, ALU.add, None,
                                [('modT', v)], [('modT', v)])
                else:
                    g = 0 if cb == 2 else 1
                    for j in range(3):
                        for nh in range(2):
                            b = self.bank()
                            for k in range(KC):
                                self.mm(PS[b][:, :], self.scR[:, k, j, :], wb[:, k, nh * 512:(nh + 1) * 512],
                                        k == 0, k == KC - 1, rw + ['scR'], [('ps', b)])
                            self.tt('dve', self.ga_bc[:, g, j, nh * 512:(nh + 1) * 512], PS[b][:, :], gb[:, g, nh * 512:(nh + 1) * 512], ALU.add,
                                    [('ps', b), ('gb', g)], [('ga', g, j)])
            self.em.flush()

    def streams(self, ctx):
        s = [(0, T, 0, 0), (T, T, 1, 1)]
        if ctx:
            s += [(CTX0, L, 2, 2), (CTX0 + L, L, 2, 3)]
        return s

    def load_modT_to(self, i):
        pass

    def postnorm_tile(self, tl, row0, cond, g, hsrc, ysrc, yr, dst, dst_res, bias_bc=None, router=None):
        PS = self.PS
        ht, t1, t2, st6, mv, sm = tl['ht'], tl['t1'], tl['t2'], tl['st6'], tl['mv'], tl['sm']
        n = tl['n']
        tl['n'] += 1
        p = n % 2
        ht, t1 = ht[p], t1[p]
        R = lambda nm: (nm, p)
        self.dma('sp', ht[:], hsrc[row0:row0 + 128, :], [('H', row0 // 128)] if hsrc is self.H else [], [R('ht')])
        for nh in range(2):
            sl = slice(nh * 512, (nh + 1) * 512)
            if bias_bc is not None:
                self.tt('dve', t1[:, sl], ysrc[nh], bias_bc[:, sl], ALU.add, list(yr[nh]) + ['bias_bc'], [R('t1')])
                self.tt('pool', t1[:, sl], t1[:, sl], self.ga_bc[:, g, cond, sl], ALU.mult, [R('t1'), ('ga', g, cond)], [R('t1')])
            else:
                self.tt('dve', t1[:, sl], ysrc[nh], self.ga_bc[:, g, cond, sl], ALU.mult, list(yr[nh]) + [('ga', g, cond)], [R('t1')])
        self.stt('dve', t1[:], ht[:], ALPHA, t1[:], ALU.mult, ALU.add, [R('ht'), R('t1')], [R('t1')])
        for c in range(2):
            self.em.op('dve', lambda e, c=c: e.bn_stats(out=st6[p][:, c, :], in_=t1[:, c * 512:(c + 1) * 512]), [R('t1')], [R('st6')])
        self.em.op('dve', lambda e: e.bn_aggr(out=mv[p][:], in_=st6[p][:]), [R('st6')], [R('mv')])
        self.act(sm[p][:, 0:1], mv[p][:, 1:2], AF.Ln, [R('mv')], [R('sm')], bias=self.epsc[:, 0:1])
        self.act(sm[p][:, 0:1], sm[p][:, 0:1], AF.Exp, [R('sm')], [R('sm')], scale=-0.5)
        self.stt('dve', sm[p][:, 1:2], mv[p][:, 0:1], -1.0, sm[p][:, 0:1], ALU.mult, ALU.mult, [R('mv'), R('sm')], [R('sm')])
        self.act(t1[:], t1[:], AF.Identity, [R('t1'), R('sm')], [R('t1')], bias=sm[p][:, 1:2], scale=sm[p][:, 0:1])
        self.tt('pool', t1[:], t1[:], self.lnp[:, 0, :], ALU.mult, [R('t1'), ('lnp', 0)], [R('t1')])
        self.tt('dve', t1[:], t1[:], self.lnp[:, 1, :], ALU.add, [R('t1'), ('lnp', 1)], [R('t1')])
        self.dma('sp', dst[row0:row0 + 128, :] if dst is not self.out else dst[row0:row0 + 128, :], t1[:], [R('t1')], [(dst_res, row0 // 128)])
        if router is not None:
            self.router_tile(tl, t1, R('t1'), row0, cond, p)

    def router_tile(self, tl, hnew, hres, row0, cond, p):
        PS = self.PS
        hmT = tl['hmT'][p]
        lg = tl['lg'][p]
        R = lambda nm: (nm, p)
        for half in range(2):
            b = self.bank()
            for kk in range(4):
                k = half * 4 + kk
                self.tr(PS[b][:, kk * 128:(kk + 1) * 128], hnew[:, k * 128:(k + 1) * 128], [hres], [('ps', b)])
            for kk in range(4):
                k = half * 4 + kk
                eng = 'act' if kk % 2 == 0 else 'dve'
                if eng == 'act':
                    self.act(hmT[:, k, :], PS[b][:, kk * 128:(kk + 1) * 128], AF.Identity, [('ps', b), ('modT', 2), ('modT', 3)], [R('hmT')],
                             bias=self.modT[:, 16 + k, cond:cond + 1], scale=self.modT[:, 24 + k, cond:cond + 1])
                else:
                    self.ts('dve', hmT[:, k, :], PS[b][:, kk * 128:(kk + 1) * 128], self.modT[:, 24 + k, cond:cond + 1],
                            self.modT[:, 16 + k, cond:cond + 1], ALU.mult, ALU.add, [('ps', b), ('modT', 2), ('modT', 3)], [R('hmT')])
        b = self.bank()
        for k in range(KC):
            self.mm(PS[b][:, 0:NE], hmT[:, k, :], self.wr[:, k, :], k == 0, k == KC - 1, [R('hmT'), 'wr'], [('ps', b)])
        sm = tl['sm2'][p]
        self.em.op('dve', lambda e: e.tensor_reduce(out=sm[:, 0:1], in_=PS[b][:, 0:NE], axis=AX.X, op=ALU.max, negate=True), [('ps', b)], [R('sm2')])
        is_ctx = row0 >= CTX0
        bsel = (row0 - CTX0) // L if is_ctx else row0 // T
        c0 = 32 if (bsel == 1 and not is_ctx) else 0
        self.act(lg[:, c0:c0 + NE], PS[b][:, 0:NE], AF.Exp, [('ps', b), R('sm2')], [R('lg')], bias=sm[:, 0:1], accum_out=sm[:, 1:2])
        self.em.op('dve', lambda e: e.reciprocal(out=sm[:, 2:3], in_=sm[:, 1:2]), [R('sm2')], [R('sm2')])
        self.ts('dve', lg[:, c0:c0 + NE], lg[:, c0:c0 + NE], sm[:, 2:3], None, ALU.mult, None, [R('lg'), R('sm2')], [R('lg')])
        b2 = self.bank()
        npart = c0 + NE
        self.tr(PS[b2][0:npart, 0:128], lg[:, 0:npart], [R('lg')], [('ps', b2)])
        if is_ctx:
            col = (row0 - CTX0)
            self.cp('act', self.affTc[0:16, col:col + 128], PS[b2][0:16, 0:128], [('ps', b2)], ['affTc'])
        else:
            col = row0 - bsel * T
            self.cp('act', self.affT[c0:c0 + 16, col:col + 128], PS[b2][c0:c0 + 16, 0:128], [('ps', b2)], ['affT'])

    def alloc_pn_tiles(self, E, router):
        nc = self.nc
        tl = {'n': 0}
        tl['ht'] = [E(self.sbt("pn_ht%d" % k, [128, D], F32)) for k in range(2)]
        tl['t1'] = [E(self.sbt("pn_t1%d" % k, [128, D], F32)) for k in range(2)]
        tl['t2'] = None
        tl['st6'] = [E(self.sbt("pn_st%d" % k, [128, 2, 6], F32)) for k in range(2)]
        tl['mv'] = [E(self.sbt("pn_mv%d" % k, [128, 2], F32)) for k in range(2)]
        tl['sm'] = [E(self.sbt("pn_sm%d" % k, [128, 2], F32)) for k in range(2)]
        if router:
            tl['hmT'] = [E(self.sbt("pn_hmT%d" % k, [128, KC, 128], F32)) for k in range(2)]
            tl['lg'] = [E(self.sbt("pn_lg%d" % k, [128, 48], F32)) for k in range(2)]
            tl['sm2'] = [E(self.sbt("pn_sm2%d" % k, [128, 4], F32)) for k in range(2)]
            for k in range(2):
                self.ms('dve', tl['lg'][k][:], 0.0, [('lg', k)])
        return tl

    def conv_phase(self, i, jj, ctx_live):
        nc = self.nc
        PS = self.PS
        em = self.em
        hsrc = self.hsrc(i)
        with ExitStack() as so:
            convout = so.enter_context(self.sbt("convout", [128, KC, T], F32))
            vec = so.enter_context(self.sbt("cvec", [128, 16 + KC * CONVW + 3 * KC], F32))
            b_in = vec[:, 0:16]
            w_dw = vec[:, 16:16 + KC * CONVW].rearrange("p (k j) -> p k j", j=CONVW)
            o = 16 + KC * CONVW
            b_dw = vec[:, o:o + KC]
            cg = vec[:, o + KC:o + 2 * KC]
            cb_ = vec[:, o + 2 * KC:o + 3 * KC]
            self.dma('sp', b_in, self.conv_b_inT[jj], (), ['cvec'])
            self.dma('sp', w_dw, self.conv_w_dwT[jj], (), ['cvec'])
            self.dma('sp', b_dw, self.conv_b_dwT[jj], (), ['cvec'])
            self.dma('sp', cg, self.conv_ln_gT[jj], (), ['cvec'])
            self.dma('sp', cb_, self.conv_ln_bT[jj], (), ['cvec'])
            for (row0, nt, cond, sid) in self.streams(ctx_live):
                ntile = nt // 128
                CH = 512 if nt >= 512 else nt
                nch = nt // CH
                with ExitStack() as s:
                    E = s.enter_context
                    uT = E(self.sbt("uT", [128, KC, nt], BF16))
                    w_in = E(self.sbt("w_in", [128, KC, 2 * D], BF16))
                    xpad = [E(self.sbt("xpad%d" % k, [128, nt + 30], F32)) for k in range(2)]
                    sg = [E(self.sbt("sg%d" % k, [128, 512], F32)) for k in range(2)]
                    ht = [E(self.sbt("cht%d" % k, [128, D], F32)) for k in range(2)]
                    src = self.conv_w_in[jj].rearrange("(k p) n -> p k n", p=128)
                    for hh in range(4):
                        self.dma('pool', w_in[:, :, hh * 512:(hh + 1) * 512], src[:, :, hh * 512:(hh + 1) * 512], (), [('w_in', hh)])
                    for k in range(2):
                        self.ms('pool', xpad[k][:, 0:15], 0.0, [('xpad', k)])
                        self.ms('pool', xpad[k][:, nt + 15:nt + 30], 0.0, [('xpad', k)])
                    for tt_ in range(ntile):
                        p = tt_ % 2
                        r0 = row0 + tt_ * 128
                        self.dma('sp', ht[p][:], hsrc[r0:r0 + 128, :], [('H', r0 // 128)] if hsrc is self.H else [], [('cht', p)])
                        for half in range(2):
                            b = self.bank()
                            for kk in range(4):
                                k = half * 4 + kk
                                self.tr(PS[b][:, kk * 128:(kk + 1) * 128], ht[p][:, k * 128:(k + 1) * 128], [('cht', p)], [('ps', b)])
                            for kk in range(4):
                                k = half * 4 + kk
                                if kk % 2 == 0:
                                    self.act(uT[:, k, tt_ * 128:(tt_ + 1) * 128], PS[b][:, kk * 128:(kk + 1) * 128], AF.Identity,
                                             [('ps', b), ('modT', 0), ('modT', 1)], [('uT', tt_ * 128 // CH)],
                                             bias=self.modT[:, k, cond:cond + 1], scale=self.modT[:, 8 + k, cond:cond + 1])
                                else:
                                    self.ts('dve', uT[:, k, tt_ * 128:(tt_ + 1) * 128], PS[b][:, kk * 128:(kk + 1) * 128],
                                            self.modT[:, 8 + k, cond:cond + 1], self.modT[:, k, cond:cond + 1], ALU.mult, ALU.add,
                                            [('ps', b), ('modT', 0), ('modT', 1)], [('uT', tt_ * 128 // CH)])
                    for j in range(KC):
                        xp = xpad[j % 2]
                        for cc in range(nch):
                            tsl = slice(cc * CH, (cc + 1) * CH)
                            bg = self.bank()
                            for k in range(KC):
                                self.mm(PS[bg][:, 0:CH], w_in[:, k, (8 + j) * 128:(9 + j) * 128], uT[:, k, tsl], k == 0, k == KC - 1,
                                        [('w_in', (8 + j) // 4), ('uT', cc)], [('ps', bg)])
                            bv = self.bank()
                            for k in range(KC):
                                self.mm(PS[bv][:, 0:CH], w_in[:, k, j * 128:(j + 1) * 128], uT[:, k, tsl], k == 0, k == KC - 1,
                                        [('w_in', j // 4), ('uT', cc)], [('ps', bv)])
                            q = cc % 2
                            self.act(sg[q][:, 0:CH], PS[bg][:, 0:CH], AF.Sigmoid, [('ps', bg), 'cvec'], [('sg', q)], bias=b_in[:, 8 + j:9 + j])
                            self.stt('dve', xp[:, 15 + cc * CH:15 + (cc + 1) * CH], PS[bv][:, 0:CH], b_in[:, j:j + 1], sg[q][:, 0:CH],
                                     ALU.add, ALU.mult, [('ps', bv), ('sg', q), 'cvec'], [('xpad', j % 2)])
                        acc = convout[:, j, 0:nt]
                        self.ts('dve', acc, xp[:, 0:nt], w_dw[:, j, 0:1], b_dw[:, j:j + 1], ALU.mult, ALU.add,
                                [('xpad', j % 2), 'cvec'], [('convout', j)])
                        for tap in range(1, CONVW):
                            self.stt('dve', acc, xp[:, tap:tap + nt], w_dw[:, j, tap:tap + 1], acc, ALU.mult, ALU.add,
                                     [('xpad', j % 2), ('convout', j), 'cvec'], [('convout', j)])
                    em.flush()
                with ExitStack() as s:
                    E = s.enter_context
                    sT = E(self.sbt("sT", [128, KC, nt], BF16))
                    w_out = E(self.sbt("w_out", [128, KC, D], BF16))
                    bo_bc = E(self.sbt("bo_bc", [128, D], F32))
                    sq = [E(self.sbt("sq%d" % k, [128, 512], F32)) for k in range(2)]
                    mean = E(self.sbt("cmean", [128, 512], F32))
                    rstd = E(self.sbt("crstd", [128, 512], F32))
                    xn = [E(self.sbt("cxn%d" % k, [128, 512], F32)) for k in range(2)]
                    tl = self.alloc_pn_tiles(E, True)
                    src = self.conv_w_out[jj].rearrange("(k p) n -> p k n", p=128)
                    for hh in range(2):
                        self.dma('pool', w_out[:, :, hh * 512:(hh + 1) * 512], src[:, :, hh * 512:(hh + 1) * 512], (), [('w_out', hh)])
                    self.dma('sp', bo_bc[:], self.conv_b_out[jj:jj + 1, :].to_broadcast([128, D]), (), ['bias_bc'])
                    for cc in range(nch):
                        tsl = slice(cc * CH, (cc + 1) * CH)
                        bm = self.bank()
                        for k in range(KC):
                            self.mm(PS[bm][:, 0:CH], self.ones[:], convout[:, k, tsl], k == 0, k == KC - 1, ['ones', ('convout', k)], [('ps', bm)])
                        bq = self.bank()
                        for k in range(KC):
                            q = k % 2
                            self.act(sq[q][:, 0:CH], convout[:, k, tsl], AF.Square, [('convout', k)], [('sq', q)])
                            self.mm(PS[bq][:, 0:CH], self.ones[:], sq[q][:, 0:CH], k == 0, k == KC - 1, ['ones', ('sq', q)], [('ps', bq)])
                        self.act(mean[:, 0:CH], PS[bm][:, 0:CH], AF.Identity, [('ps', bm)], ['cmean'], scale=1.0 / D)
                        self.tt('dve', rstd[:, 0:CH], mean[:, 0:CH], mean[:, 0:CH], ALU.mult, ['cmean'], ['crstd'])
                        self.stt('dve', rstd[:, 0:CH], PS[bq][:, 0:CH], 1.0 / D, rstd[:, 0:CH], ALU.mult, ALU.subtract, [('ps', bq), 'crstd'], ['crstd'])
                        self.act(rstd[:, 0:CH], rstd[:, 0:CH], AF.Ln, ['crstd'], ['crstd'], bias=self.epsc[:, 0:1])
                        self.act(rstd[:, 0:CH], rstd[:, 0:CH], AF.Exp, ['crstd'], ['crstd'], scale=-0.5)
                        for k in range(KC):
                            q = k % 2
                            self.tt('dve', xn[q][:, 0:CH], convout[:, k, tsl], mean[:, 0:CH], ALU.subtract, [('convout', k), 'cmean'], [('cxn', q)])
                            self.tt('pool', xn[q][:, 0:CH], xn[q][:, 0:CH], rstd[:, 0:CH], ALU.mult, [('cxn', q), 'crstd'], [('cxn', q)])
                            self.act(sT[:, k, tsl], xn[q][:, 0:CH], AF.Silu, [('cxn', q), 'cvec'], [('sT', cc)],
                                     bias=cb_[:, k:k + 1], scale=cg[:, k:k + 1])
                    for tt_ in range(ntile):
                        r0 = row0 + tt_ * 128
                        bs = []
                        for nh in range(2):
                            b = self.bank()
                            bs.append(b)
                            for k in range(KC):
                                self.mm(PS[b][:, :], sT[:, k, tt_ * 128:(tt_ + 1) * 128], w_out[:, k, nh * 512:(nh + 1) * 512], k == 0, k == KC - 1,
                                        [('sT', tt_ * 128 // CH), ('w_out', nh)], [('ps', b)])
                        self.postnorm_tile(tl, r0, cond, 0, hsrc, [PS[bs[0]][:, :], PS[bs[1]][:, :]], [[('ps', bs[0])], [('ps', bs[1])]],
                                           self.H, 'H', bias_bc=bo_bc, router=True)
                    em.flush()

    def moe_phase(self, i, ctx_live, last):
        nc = self.nc
        PS = self.PS
        em = self.em
        nexp = self.n_exp
        with ExitStack() as so:
            EO = so.enter_context
            idxT = EO(self.sbt("idxT", [128, 2, 48], I32))
            gateT = EO(self.sbt("gateT", [128, 2, 48], F32))
            idxTc = EO(self.sbt("idxTc", [64, 16], I32))
            gateTc = EO(self.sbt("gateTc", [64, 16], F32))
            self.zero = EO(self.sbt("zero", [128, D], F32))
            self.ms('dve', self.zero[:], 0.0, ['zero'])
            self.load_lnp(i, 1)
            nrows = NROW if ctx_live else NB * T
            for r0 in range(0, nrows, 128):
                self.dma('sp', self.Y[r0:r0 + 128, :], self.zero[:], ['zero'], ['Y'])
            with ExitStack() as s:
                E = s.enter_context
                wk = E(self.sbt("wk", [48, T], F32))
                vals = E(self.sbt("vals", [48, CAP], F32))
                idxu = E(self.sbt("idxu", [48, CAP], U32))
                idxf = E(self.sbt("idxf", [48, CAP], F32))
                self.cp('dve', wk[:], self.affT[:], ['affT'], ['wk'])
                for it in range(CAP // 8):
                    sl = slice(it * 8, (it + 1) * 8)
                    self.em.op('dve', lambda e, sl=sl: e.max(out=vals[:, sl], in_=wk[:]), ['wk'], ['vals'])
                    self.em.op('dve', lambda e, sl=sl: e.max_index(out=idxu[:, sl], in_max=vals[:, sl], in_values=wk[:]), ['wk', 'vals'], ['idxu'])
                    if it < CAP // 8 - 1:
                        self.em.op('dve', lambda e, sl=sl: e.match_replace(out=wk[:], in_to_replace=vals[:, sl], in_values=wk[:], imm_value=-1.0),
                                   ['wk', 'vals'], ['wk'])
                self.cp('dve', idxf[:], idxu[:], ['idxu'], ['idxf'])
                self.ts('dve', idxf[32:48, :], idxf[32:48, :], float(T), None, ALU.add, None, ['idxf'], ['idxf'])
                for j in range(2):
                    b = self.bank()
                    self.tr(PS[b][:, 0:48], idxf[:, j * 128:(j + 1) * 128], ['idxf'], [('ps', b)])
                    self.cp('dve', idxT[:, j, :], PS[b][:, 0:48], [('ps', b)], ['idxT'])
                    b = self.bank()
                    self.tr(PS[b][:, 0:48], vals[:, j * 128:(j + 1) * 128], ['vals'], [('ps', b)])
                    self.cp('act', gateT[:, j, :], PS[b][:, 0:48], [('ps', b)], ['gateT'])
                if ctx_live:
                    wkc = E(self.sbt("wkc", [16, NB * L], F32))
                    valc = E(self.sbt("valc", [16, NB * CAPC], F32))
                    idxcu = E(self.sbt("idxcu", [16, NB * CAPC], U32))
                    idxcf = E(self.sbt("idxcf", [16, NB * CAPC], F32))
                    self.cp('dve', wkc[:], self.affTc[:], ['affTc'], ['wkc'])
                    for bb in range(NB):
                        for it in range(CAPC // 8):
                            sl = slice(bb * CAPC + it * 8, bb * CAPC + (it + 1) * 8)
                            wsl = slice(bb * L, (bb + 1) * L)
                            self.em.op('dve', lambda e, sl=sl, wsl=wsl: e.max(out=valc[:, sl], in_=wkc[:, wsl]), ['wkc'], ['valc'])
                            self.em.op('dve', lambda e, sl=sl, wsl=wsl: e.max_index(out=idxcu[:, sl], in_max=valc[:, sl], in_values=wkc[:, wsl]),
                                       ['wkc', 'valc'], ['idxcu'])
                            if it < CAPC // 8 - 1:
                                self.em.op('dve', lambda e, sl=sl, wsl=wsl: e.match_replace(out=wkc[:, wsl], in_to_replace=valc[:, sl],
                                                                                          in_values=wkc[:, wsl], imm_value=-1.0),
                                           ['wkc', 'valc'], ['wkc'])
                    self.cp('dve', idxcf[:], idxcu[:], ['idxcu'], ['idxcf'])
                    for bb in range(NB):
                        self.ts('dve', idxcf[:, bb * CAPC:(bb + 1) * CAPC], idxcf[:, bb * CAPC:(bb + 1) * CAPC], float(CTX0 + bb * L), None,
                                ALU.add, None, ['idxcf'], ['idxcf'])
                    b = self.bank()
                    self.tr(PS[b][0:64, 0:16], idxcf[:, :], ['idxcf'], [('ps', b)])
                    self.cp('dve', idxTc[:], PS[b][0:64, 0:16], [('ps', b)], ['idxTc'])
                    b = self.bank()
                    self.tr(PS[b][0:64, 0:16], valc[:, :], ['valc'], [('ps', b)])
                    self.cp('act', gateTc[:], PS[b][0:64, 0:16], [('ps', b)], ['gateTc'])
                em.flush()
            NTOK = 4 * 128 + (64 if ctx_live else 0)
            with ExitStack() as s:
                E = s.enter_context
                NS = 3
                G = [E(self.sbt("Gw%d" % k, [128, KC, 512], BF16)) for k in range(NS)]
                U = [E(self.sbt("Uw%d" % k, [128, KC, 512], BF16)) for k in range(NS)]
                Wd = [E(self.sbt("Wd%d" % k, [128, 16, D], BF16)) for k in range(2)]
                xg = [E(self.sbt("xg%d" % k, [128, D], F32)) for k in range(4)]
                xsT = [E(self.sbt("xsT%d" % k, [128, KC, NTOK], BF16)) for k in range(1)]
                hidT = [E(self.sbt("hidT%d" % k, [128, 16, NTOK], BF16)) for k in range(1)]
                sg = [E(self.sbt("msg%d" % k, [128, NTOK], F32)) for k in range(2)]
                ys = [E(self.sbt("ys%d" % k, [128, D], F32)) for k in range(2)]

                def load_gu(sq_):
                    if sq_ >= nexp * 4:
                        return
                    e, c = sq_ // 4, sq_ % 4
                    sl_ = sq_ % NS
                    sgw = self.moe_w_gate[i, e].rearrange("(k p) n -> p k n", p=128)
                    suw = self.moe_w_up[i, e].rearrange("(k p) n -> p k n", p=128)
                    self.dma('pool', G[sl_][:, :, :], sgw[:, :, c * 512:(c + 1) * 512], (), [('G', sl_)])
                    self.dma('pool', U[sl_][:, :, :], suw[:, :, c * 512:(c + 1) * 512], (), [('U', sl_)])

                def load_d(e):
                    sdw = self.moe_w_down[i, e].rearrange("(f p) n -> p f n", p=128)
                    for hh in range(4):
                        self.dma('pool', Wd[e % 2][:, hh * 4:(hh + 1) * 4, :], sdw[:, hh * 4:(hh + 1) * 4, :], (), [('Wd', e % 2, hh)])

                for sq_ in range(NS):
                    load_gu(sq_)
                load_d(0)
                xs = xsT[0]
                hd = hidT[0]
                allH = [('H', r) for r in range(NROW // 128)]

                def gather_tokens(e):
                    for bb in range(NB):
                        for j in range(2):
                            g = xg[bb * 2 + j]
                            col = bb * 32 + e
                            self.em.dma('pool', lambda en, g=g, j=j, col=col: en.indirect_dma_start(
                                out=g[:], out_offset=None, in_=self.H,
                                in_offset=bass.IndirectOffsetOnAxis(ap=idxT[:, j, col:col + 1], axis=0)),
                                ['idxT'] + allH, [('xg', bb * 2 + j)])

                def build_xs(e):
                    for bb in range(NB):
                        bks = [self.bank() for _ in range(4)]
                        for j in range(2):
                            g = xg[bb * 2 + j]
                            gr = ('xg', bb * 2 + j)
                            for k in range(KC):
                                bk = bks[k // 2]
                                off = (k % 2) * 256 + j * 128
                                self.tr(PS[bk][:, off:off + 128], g[:, k * 128:(k + 1) * 128], [gr], [('ps', bk)])
                        if bb == 0 and ctx_live:
                            self.em.dma('pool', lambda en, e=e: en.indirect_dma_start(
                                out=xg[0][0:64, :], out_offset=None, in_=self.H,
                                in_offset=bass.IndirectOffsetOnAxis(ap=idxTc[:, e:e + 1], axis=0)),
                                ['idxTc'] + allH, [('xg', 0)])
                        for k in range(KC):
                            bk = bks[k // 2]
                            off = (k % 2) * 256
                            dst = xs[:, k, bb * 256:(bb + 1) * 256]
                            if k % 2 == 0:
                                self.act(dst, PS[bk][:, off:off + 256], AF.Identity, [('ps', bk), ('modT', 2), ('modT', 3)], [('xsT', 0)],
                                         bias=self.modT[:, 16 + k, bb:bb + 1], scale=self.modT[:, 24 + k, bb:bb + 1])
                            else:
                                self.ts('dve', dst, PS[bk][:, off:off + 256], self.modT[:, 24 + k, bb:bb + 1], self.modT[:, 16 + k, bb:bb + 1],
                                        ALU.mult, ALU.add, [('ps', bk), ('modT', 2), ('modT', 3)], [('xsT', 0)])
                    if ctx_live:
                        g = xg[0]
                        gr = ('xg', 0)
                        bks = [self.bank() for _ in range(1)]
                        for k in range(KC):
                            self.tr(PS[bks[0]][:, k * 64:(k + 1) * 64], g[0:64, k * 128:(k + 1) * 128], [gr], [('ps', bks[0])])
                        for k in range(KC):
                            dst = xs[:, k, 512:576]
                            self.ts('dve', dst, PS[bks[0]][:, k * 64:(k + 1) * 64], self.modT[:, 24 + k, 2:3], self.modT[:, 16 + k, 2:3],
                                    ALU.mult, ALU.add, [('ps', bks[0]), ('modT', 2), ('modT', 3)], [('xsT', 0)])

                gather_tokens(0)
                build_xs(0)
                for e in range(nexp):
                    if e + 1 < nexp:
                        gather_tokens(e + 1)
                        load_d(e + 1)
                    for c in range(4):
                        for fl in range(4):
                            ft = c * 4 + fl
                            bg = self.bank()
                            bu = self.bank()
                            sl_ = (e * 4 + c) % NS
                            for (bk_, W, wr_) in ((bg, G[sl_], ('G', sl_)), (bu, U[sl_], ('U', sl_))):
                                for k in range(KC):
                                    self.mm(PS[bk_][:, 0:512], W[:, k, fl * 128:(fl + 1) * 128], xs[:, k, 0:512], k == 0, k == KC - 1,
                                            [wr_, ('xsT', 0)], [('ps', bk_)])
                            q = ft % 2
                            self.act(sg[q][:, 0:512], PS[bg][:, 0:512], AF.Silu, [('ps', bg)], [('msg', q)])
                            self.tt('dve', hd[:, ft, 0:512], sg[q][:, 0:512], PS[bu][:, 0:512], ALU.mult, [('msg', q), ('ps', bu)], [('hidT', ft)])
                            if ctx_live:
                                bc = self.bank()
                                for gi, (W, wr_) in enumerate(((G[sl_], ('G', sl_)), (U[sl_], ('U', sl_)))):
                                    for k in range(KC):
                                        self.mm(PS[bc][:, gi * 64:(gi + 1) * 64], W[:, k, fl * 128:(fl + 1) * 128], xs[:, k, 512:576], k == 0, k == KC - 1,
                                                [wr_, ('xsT', 0)], [('ps', bc)])
                                self.act(sg[q][:, 512:576], PS[bc][:, 0:64], AF.Silu, [('ps', bc)], [('msg', q)])
                                self.tt('dve', hd[:, ft, 512:576], sg[q][:, 512:576], PS[bc][:, 64:128], ALU.mult, [('msg', q), ('ps', bc)], [('hidT', ft)])
                        load_gu(e * 4 + c + NS)
                    if e + 1 < nexp:
                        build_xs(e + 1)
                    ttiles = [(bb, j, 128) for bb in range(NB) for j in range(2)] + ([(2, 0, 64)] if ctx_live else [])
                    prevY = [('Ys', (e + 1) % 2, t_) for t_ in range(5)]
                    for ti, (bb, j, np_) in enumerate(ttiles):
                        y_ = ys[ti % 2]
                        yr = ('ys', ti % 2)
                        tok0 = ti * 128
                        for nh in range(2):
                            b = self.bank()
                            for ft in range(16):
                                self.mm(PS[b][0:np_, :], hd[:, ft, tok0:tok0 + np_], Wd[e % 2][:, ft, nh * 512:(nh + 1) * 512], ft == 0, ft == 15,
                                        [('hidT', ft), ('Wd', e % 2, ft // 4)], [('ps', b)])
                            gcol = gateTc[:, e:e + 1] if bb == 2 else gateT[:, j, bb * 32 + e:bb * 32 + e + 1]
                            gres = 'gateTc' if bb == 2 else 'gateT'
                            if nh == 0:
                                self.ts('dve', y_[0:np_, 0:512], PS[b][0:np_, :], gcol[0:np_, :], None, ALU.mult, None, [('ps', b), gres], [yr])
                            else:
                                self.act(y_[0:np_, 512:1024], PS[b][0:np_, :], AF.Identity, [('ps', b), gres], [yr], scale=gcol[0:np_, :])
                        if bb == 2:
                            self.em.dma('pool', lambda en, y_=y_, e=e: en.indirect_dma_start(
                                out=self.Y, out_offset=bass.IndirectOffsetOnAxis(ap=idxTc[:, e:e + 1], axis=0),
                                in_=y_[0:64, :], in_offset=None, compute_op=ALU.add), [yr, 'idxTc', 'Y'] + prevY, [('Ys', e % 2, ti)])
                        else:
                            col = bb * 32 + e
                            self.em.dma('pool', lambda en, y_=y_, j=j, col=col: en.indirect_dma_start(
                                out=self.Y, out_offset=bass.IndirectOffsetOnAxis(ap=idxT[:, j, col:col + 1], axis=0),
                                in_=y_[:, :], in_offset=None, compute_op=ALU.add), [yr, 'idxT', 'Y'] + prevY, [('Ys', e % 2, ti)])
                em.flush()
            with ExitStack() as s:
                E = s.enter_context
                tl = self.alloc_pn_tiles(E, False)
                yt = [E(self.sbt("pn_y%d" % k, [128, D], F32)) for k in range(2)]
                for (row0, nt, cond, sid) in self.streams(ctx_live):
                    for tt_ in range(nt // 128):
                        r0 = row0 + tt_ * 128
                        p = tl['n'] % 2
                        self.dma('act', yt[p][:], self.Y[r0:r0 + 128, :], ['Y'], [('pn_y', p)])
                        if last and row0 < CTX0:
                            dst, dres = self.out, 'out'
                        else:
                            dst, dres = self.H, 'H'
                        self.postnorm_tile(tl, r0, cond, 1, self.H, [yt[p][:, 0:512], yt[p][:, 512:1024]], [[('pn_y', p)], [('pn_y', p)]],
                                           dst, dres, bias_bc=None, router=None)
                em.flush()

    def modT_tiles(self, uT, hsrc, ht, row0, nt, cond, c0, CH, res_name='uT'):
        PS = self.PS
        for tt_ in range(nt // 128):
            p = self._mt % 2
            self._mt += 1
            r0 = row0 + tt_ * 128
            col = c0 + tt_ * 128
            self.dma('sp', ht[p][:], hsrc[r0:r0 + 128, :], [('H', r0 // 128)] if hsrc is self.H else [], [('cht', p)])
            for half in range(2):
                b = self.bank()
                for kk in range(4):
                    k = half * 4 + kk
                    self.tr(PS[b][:, kk * 128:(kk + 1) * 128], ht[p][:, k * 128:(k + 1) * 128], [('cht', p)], [('ps', b)])
                for kk in range(4):
                    k = half * 4 + kk
                    if kk % 2 == 0:
                        self.act(uT[:, k, col:col + 128], PS[b][:, kk * 128:(kk + 1) * 128], AF.Identity,
                                 [('ps', b), ('modT', 0), ('modT', 1)], [(res_name, col // CH)],
                                 bias=self.modT[:, k, cond:cond + 1], scale=self.modT[:, 8 + k, cond:cond + 1])
                    else:
                        self.ts('dve', uT[:, k, col:col + 128], PS[b][:, kk * 128:(kk + 1) * 128],
                                self.modT[:, 8 + k, cond:cond + 1], self.modT[:, k, cond:cond + 1], ALU.mult, ALU.add,
                                [('ps', b), ('modT', 0), ('modT', 1)], [(res_name, col // CH)])

    def na_phase(self, i, jj, ctx_live):
        nc = self.nc
        PS = self.PS
        em = self.em
        hsrc = self.hsrc(i)
        NT = T + L
        chunks = [(0, 512), (512, 512), (1024, 512), (1536, 512), (2048, 256)]
        classes = ((0, [0]), (1, [1]), (2, list(range(2, 14))), (3, [14]), (4, [15]))
        with ExitStack() as so:
            EO = so.enter_context
            qT = EO(self.sbt("qT", [128, 8, NT], BF16))
            kT = EO(self.sbt("kT", [128, 8, NT], BF16))
            identb = EO(self.sbt("identb", [128, 128], BF16))
            self.cp('dve', identb[:], self.ident[:], ['ident'], ['identb'])
            for b in range(NB):
                with ExitStack() as s:
                    E = s.enter_context
                    uT = E(self.sbt("uT", [128, KC, NT], BF16))
                    wq = [E(self.sbt("wq%d" % k, [128, KC, D], BF16)) for k in range(2)]
                    vst = [E(self.sbt("vst%d" % k, [128, D], BF16)) for k in range(2)]
                    ht = [E(self.sbt("cht%d" % k, [128, D], F32)) for k in range(2)]

                    def loadw(blk, slot):
                        src = self.na_w_qkv[jj][:, blk * D:(blk + 1) * D].rearrange("(k p) n -> p k n", p=128)
                        for hh in range(2):
                            self.dma('pool', wq[slot][:, :, hh * 512:(hh + 1) * 512], src[:, :, hh * 512:(hh + 1) * 512], (), [('wq', slot, hh)])
                    loadw(0, 0)
                    loadw(1, 1)
                    self._mt = 0
                    self.modT_tiles(uT, hsrc, ht, b * T, T, b, 0, 512)
                    self.modT_tiles(uT, hsrc, ht, CTX0 + b * L, L, 2, T, 512)
                    for blk, dstT, nm in ((0, qT, 'qT'), (1, kT, 'kT')):
                        w = wq[blk]
                        for hp in range(8):
                            for (c0, n) in chunks:
                                bk = self.bank()
                                for k in range(KC):
                                    self.mm(PS[bk][:, 0:n], w[:, k, hp * 128:(hp + 1) * 128], uT[:, k, c0:c0 + n], k == 0, k == KC - 1,
                                            [('wq', blk, hp // 4), ('uT', c0 // 512)], [('ps', bk)])
                                if blk == 0:
                                    self.act(dstT[:, hp, c0:c0 + n], PS[bk][:, 0:n], AF.Identity, [('ps', bk)], [(nm, hp)], scale=0.125)
                                else:
                                    self.cp('dve', dstT[:, hp, c0:c0 + n], PS[bk][:, 0:n], [('ps', bk)], [(nm, hp)])
                    loadw(2, 0)
                    for t in range(NT // 128):
                        p = t % 2
                        for nh in range(2):
                            bk = self.bank()
                            for k in range(KC):
                                self.mm(PS[bk][:, :], uT[:, k, t * 128:(t + 1) * 128], wq[0][:, k, nh * 512:(nh + 1) * 512], k == 0, k == KC - 1,
                                        [('wq', 0, nh), ('uT', t * 128 // 512)], [('ps', bk)])
                            if nh == 0:
                                self.cp('act', vst[p][:, 0:512], PS[bk][:, :], [('ps', bk)], [('vst', p)])
                            else:
                                self.cp('dve', vst[p][:, 512:1024], PS[bk][:, :], [('ps', bk)], [('vst', p)])
                        self.dma('sp', self.V_d[t * 128:(t + 1) * 128, :], vst[p][:], [('vst', p)], [('V_d', t)])
                    em.flush()
                with ExitStack() as sm_:
                    oT = sm_.enter_context(self.sbt("oT", [128, 8, NT], BF16))
                    with ExitStack() as s:
                        E = s.enter_context
                        Vh = [E(self.sbt("Vh%d" % k, [128, NT // 128, 128], BF16)) for k in range(2)]
                        BM = [E(self.sbt("BM%d" % k, [128, 640], F32)) for k in range(2)]
                        S = [E(self.sbt("S%d" % k, [128, 896], F32)) for k in range(2)]
                        Pn = [E(self.sbt("Pn%d" % k, [128, 896], BF16)) for k in range(2)]
                        PT = [E(self.sbt("PT%d" % k, [128, 896], BF16)) for k in range(2)]
                        smx = [E(self.sbt("smx%d" % k, [128, 4], F32)) for k in range(2)]
                        nbm = 0
                        nq = 0
                        vsrc = self.V_d.rearrange("(t p) c -> p t c", p=128)
                        for hp in range(8):
                            vh = Vh[hp % 2]
                            self.dma('sp', vh[:], vsrc[:, :, hp * 128:(hp + 1) * 128], [('V_d', t) for t in range(NT // 128)], [('Vh', hp % 2)])
                            for hh in range(2):
                                h = hp * 2 + hh
                                pr = slice(hh * 64, (hh + 1) * 64)

                                def attend(qcol, segs, bm, nkeys, vts):
                                    nonlocal nq
                                    m = nq % 2
                                    nq += 1
                                    S_, Pn_, PT_, sx = S[m], Pn[m], PT[m], smx[m]
                                    R = lambda nm: (nm, m)
                                    bks = {}
                                    scol = 0
                                    for (bi, pc0, kc0, n, loc) in segs:
                                        if bi not in bks:
                                            bks[bi] = self.bank()
                                        bk = bks[bi]
                                        self.mm(PS[bk][:, pc0:pc0 + n], qT[pr, hp, qcol:qcol + 128], kT[pr, hp, kc0:kc0 + n], True, True,
                                                [('qT', hp), ('kT', hp)], [('ps', bk)])
                                    for (bi, pc0, kc0, n, loc) in segs:
                                        bk = bks[bi]
                                        if loc:
                                            self.tt('dve', S_[:, scol:scol + n], PS[bk][:, pc0:pc0 + n], bm[0][:, scol:scol + n], ALU.add,
                                                    [('ps', bk), bm[1]], [R('S')])
                                        else:
                                            self.cp('act', S_[:, scol:scol + n], PS[bk][:, pc0:pc0 + n], [('ps', bk)], [R('S')])
                                        scol += n
                                    self.em.op('dve', lambda e: e.tensor_reduce(out=sx[:, 0:1], in_=S_[:, 0:nkeys], axis=AX.X, op=ALU.max, negate=True),
                                               [R('S')], [R('smx')])
                                    self.act(S_[:, 0:nkeys], S_[:, 0:nkeys], AF.Exp, [R('S'), R('smx')], [R('S')], bias=sx[:, 0:1], accum_out=sx[:, 1:2])
                                    self.em.op('dve', lambda e: e.reciprocal(out=sx[:, 2:3], in_=sx[:, 1:2]), [R('smx')], [R('smx')])
                                    self.ts('dve', Pn_[:, 0:nkeys], S_[:, 0:nkeys], sx[:, 2:3], None, ALU.mult, None, [R('S'), R('smx')], [R('Pn')])
                                    bT = self.bank()
                                    PTv = PS[bT][:, :].bitcast(BF16)
                                    nch = nkeys // 128
                                    for c in range(nch):
                                        self.tr(PTv[:, c * 128:(c + 1) * 128], Pn_[:, c * 128:(c + 1) * 128], [R('Pn'), 'identb'], [('ps', bT)], ident=identb)
                                    if m == 0:
                                        self.cp('act', PT_[:, 0:nkeys], PTv[:, 0:nkeys], [('ps', bT)], [R('PT')])
                                    else:
                                        self.cp('dve', PT_[:, 0:nkeys], PTv[:, 0:nkeys], [('ps', bT)], [R('PT')])
                                    bO = self.bank()
                                    for c in range(nch):
                                        self.mm(PS[bO][pr, 0:128], vh[:, vts[c], hh * 64:(hh + 1) * 64], PT_[:, c * 128:(c + 1) * 128], c == 0, c == nch - 1,
                                                [('Vh', hp % 2), R('PT')], [('ps', bO)])
                                    if m == 0:
                                        self.cp('dve', oT[pr, hp, qcol:qcol + 128], PS[bO][pr, 0:128], [('ps', bO)], [('oT', qcol // 128)])
                                    else:
                                        self.cp('act', oT[pr, hp, qcol:qcol + 128], PS[bO][pr, 0:128], [('ps', bO)], [('oT', qcol // 128)])

                                for cls, tiles in classes:
                                    bmt = BM[nbm % 2]
                                    bmr = ('BM', nbm % 2)
                                    nbm += 1
                                    self.dma('sp', bmt[:], self.bm_tab[h, cls], (), [bmr])
                                    for i_ in tiles:
                                        ks = min(max(2 * i_ - 4, 0), 22)
                                        k0 = ks * 64
                                        segs = [(0, 0, k0, 512, True), (1, 0, k0 + 512, 128, True), (1, 128, T, 256, False)]
                                        vts = [ks // 2 + c for c in range(5)] + [16, 17]
                                        attend(i_ * 128, segs, (bmt, bmr), 896, vts)
                                for j in range(2):
                                    attend(T + j * 128, [(0, 0, T, 256, False)], None, 256, [16, 17])
                        em.flush()
                    with ExitStack() as s:
                        E = s.enter_context
                        w_o = E(self.sbt("w_o", [128, KC, D], BF16))
                        tl = self.alloc_pn_tiles(E, True)
                        src = self.na_w_o[jj].rearrange("(k p) n -> p k n", p=128)
                        for hh in range(2):
                            self.dma('pool', w_o[:, :, hh * 512:(hh + 1) * 512], src[:, :, hh * 512:(hh + 1) * 512], (), [('w_o', hh)])
                        for t in range(NT // 128):
                            if t < 16:
                                r0, cond = b * T + t * 128, b
                            else:
                                r0, cond = CTX0 + b * L + (t - 16) * 128, 2
                            bs = []
                            for nh in range(2):
                                bk = self.bank()
                                bs.append(bk)
                                for k in range(KC):
                                    self.mm(PS[bk][:, :], oT[:, k, t * 128:(t + 1) * 128], w_o[:, k, nh * 512:(nh + 1) * 512], k == 0, k == KC - 1,
                                            [('oT', t), ('w_o', nh)], [('ps', bk)])
                            self.postnorm_tile(tl, r0, cond, 0, hsrc, [PS[bs[0]][:, :], PS[bs[1]][:, :]], [[('ps', bs[0])], [('ps', bs[1])]],
                                               self.H, 'H', bias_bc=None, router=True)
                        em.flush()


    def rwkv_phase(self, i, jj):
        nc = self.nc
        PS = self.PS
        em = self.em
        hsrc = self.hsrc(i)
        NT = T + L
        C64 = 0.6065306597126334
        R3 = lambda ap, dd=64: ap.rearrange("p (h d) -> p h d", d=dd)

        def bc64(ap16):
            return ap16.unsqueeze(2).to_broadcast([128, 16, 64])

        for b in range(NB):
            with ExitStack() as s:
                E = s.enter_context
                wbuf = [E(self.sbt("rwW%d" % k, [128, KC, D], BF16)) for k in range(1)]
                w1c = E(self.sbt("w1c", [128, KC, 128], BF16))
                a1c = E(self.sbt("a1c", [128, KC, 128], BF16))
                g1s = E(self.sbt("g1s", [128, KC, 128], BF16))
                w2s = E(self.sbt("w2s", [128, D], BF16))
                a2s = E(self.sbt("a2s", [128, D], BF16))
                g2s = E(self.sbt("g2s", [128, D], BF16))
                muT = E(self.sbt("muT", [128, 2, 6, KC], F32))
                bc0 = E(self.sbt("bc0", [128, 2, D], F32))
                u32 = E(self.sbt("u32", [128, KC, T + 2], F32))
                lerp = E(self.sbt("lerp", [128, KC, T], BF16))
                tmp = [E(self.sbt("ltmp%d" % k, [128, T], F32)) for k in range(2)]
                loT = E(self.sbt("loT", [128, T], BF16))
                stg = [E(self.sbt("stg%d" % k, [128, D], F32)) for k in range(2)]
                ht = [E(self.sbt("cht%d" % k, [128, D], F32)) for k in range(2)]
                self.dma('sp', muT[:, 0], self.rw_mu_prevT, (), ['muT'])
                self.dma('sp', muT[:, 1], self.rw_mu_nextT, (), ['muT'])
                for d in range(2):
                    self.dma('pool', w1c[:, :, d * 64:(d + 1) * 64], self.rw_w1[d].rearrange("(k p) n -> p k n", p=128), (), ['w1c'])
                    self.dma('pool', a1c[:, :, d * 64:(d + 1) * 64], self.rw_a1[d].rearrange("(k p) n -> p k n", p=128), (), ['a1c'])
                self.dma('pool', g1s[:], self.rw_g1.rearrange("(k p) n -> p k n", p=128), (), ['g1s'])
                self.dma('pool', w2s[:], self.rw_w2.rearrange("d j n -> (d j) n"), (), ['w2s'])
                self.dma('pool', a2s[:], self.rw_a2.rearrange("d j n -> (d j) n"), (), ['a2s'])
                self.dma('pool', g2s[:], self.rw_g2, (), ['g2s'])
                nst = [0]

                def loadW(src, slot):
                    v = src.rearrange("(k p) n -> p k n", p=128)
                    for hh in range(2):
                        self.dma('pool', wbuf[slot][:, :, hh * 512:(hh + 1) * 512], v[:, :, hh * 512:(hh + 1) * 512], (), [('rwW', slot, hh)])

                def stage_out(dst, row, evs):
                    p = nst[0] % 2
                    nst[0] += 1
                    for nh in range(2):
                        evs[nh](stg[p][:, nh * 512:(nh + 1) * 512], ('stg', p))
                    self.dma('sp', dst[row:row + 128, :], stg[p][:], [('stg', p)], [(dst.tensor.name, row // 128)])

                for (row0, nt, cond, roff, is_lat) in ((b * T, T, b, 0, True), (CTX0 + b * L, L, 2, T, False)):
                    ntile = nt // 128
                    CH = min(512, nt)
                    nch = nt // CH
                    for k in range(KC):
                        self.ms('pool', u32[:, k, 0:1], 0.0, [('u32', 0)])
                        self.ms('pool', u32[:, k, nt + 1:nt + 2], 0.0, [('u32', 0)])
                    self._mt = 0
                    self.modT_tiles(u32, hsrc, ht, row0, nt, cond, 1, 1 << 30, res_name='u32')
                    lerps = (0, 1, 2, 3, 4, 5) if is_lat else (1, 2, 3, 4)
                    wsrc = {0: self.rw_w_r, 2: self.rw_w_k, 3: self.rw_w_v}
                    wslot = {0: 0, 2: 0, 3: 0}
                    for n in lerps:
                        if n in wsrc:
                            loadW(wsrc[n], wslot[n])
                        for k in range(KC):
                            uk = u32[:, k, 1:nt + 1]
                            t0, t1 = tmp[0][:, 0:nt], tmp[1][:, 0:nt]
                            self.tt('dve', t0, u32[:, k, 0:nt], uk, ALU.subtract, [('u32', 0)], [('ltmp', 0)])
                            self.stt('dve', t0, t0, muT[:, 0, n, k:k + 1], uk, ALU.mult, ALU.add, [('ltmp', 0), 'muT', ('u32', 0)], [('ltmp', 0)])
                            self.tt('pool', t1, u32[:, k, 2:nt + 2], uk, ALU.subtract, [('u32', 0)], [('ltmp', 1)])
                            self.stt('dve', lerp[:, k, 0:nt], t1, muT[:, 1, n, k:k + 1], t0, ALU.mult, ALU.add,
                                     [('ltmp', 0), ('ltmp', 1), 'muT'], [('lerp', k)])
                        lr = [('lerp', k) for k in range(KC)]
                        if n in wsrc:
                            dst = {0: self.RR, 2: self.RK, 3: self.RV}[n]
                            W = wbuf[wslot[n]]
                            for t in range(ntile):
                                bks = [self.bank(), self.bank()]
                                for nh in range(2):
                                    for k in range(KC):
                                        self.mm(PS[bks[nh]][:, :], lerp[:, k, t * 128:(t + 1) * 128], W[:, k, nh * 512:(nh + 1) * 512], k == 0, k == KC - 1,
                                                lr + [('rwW', wslot[n], nh)], [('ps', bks[nh])])
                                stage_out(dst, roff + t * 128, [
                                    lambda o, r_, bk=bks[0]: self.cp('act', o, PS[bk][:, :], [('ps', bk)], [r_]),
                                    lambda o, r_, bk=bks[1]: self.cp('dve', o, PS[bk][:, :], [('ps', bk)], [r_])])
                        else:
                            l1 = {1: w1c, 4: a1c, 5: g1s}[n]
                            l1n = {1: 'w1c', 4: 'a1c', 5: 'g1s'}[n]
                            for cc in range(nch):
                                bk = self.bank()
                                for k in range(KC):
                                    self.mm(PS[bk][:, 0:CH], l1[:, k, :], lerp[:, k, cc * CH:(cc + 1) * CH], k == 0, k == KC - 1, lr + [l1n], [('ps', bk)])
                                fn = {1: AF.Tanh, 4: AF.Identity, 5: AF.Sigmoid}[n]
                                self.act(loT[:, cc * CH:(cc + 1) * CH], PS[bk][:, 0:CH], fn, [('ps', bk)], ['loT'])
                            if n == 5:
                                for t in range(ntile):
                                    bks = [self.bank(), self.bank()]
                                    for nh in range(2):
                                        self.mm(PS[bks[nh]][:, :], loT[:, t * 128:(t + 1) * 128], g2s[:, nh * 512:(nh + 1) * 512], True, True, ['loT', 'g2s'], [('ps', bks[nh])])
                                    stage_out(self.RG, roff + t * 128, [
                                        lambda o, r_, bk=bks[0]: self.cp('act', o, PS[bk][:, :], [('ps', bk)], [r_]),
                                        lambda o, r_, bk=bks[1]: self.cp('dve', o, PS[bk][:, :], [('ps', bk)], [r_])])
                            else:
                                l2, l2n = (w2s, 'w2s') if n == 1 else (a2s, 'a2s')
                                for d in range(2):
                                    self.dma('sp', bc0[:, d, :], (self.rw_w0 if n == 1 else self.rw_a0)[d:d + 1, :].to_broadcast([128, D]), (), ['bc0'])
                                for d in range(2):
                                    pr = slice(d * 64, (d + 1) * 64)
                                    dst = (self.RLW if n == 1 else self.RAI)[d]
                                    bci = d
                                    for t in range(ntile):
                                        bks = [self.bank(), self.bank()]
                                        for nh in range(2):
                                            self.mm(PS[bks[nh]][:, :], loT[pr, t * 128:(t + 1) * 128], l2[pr, nh * 512:(nh + 1) * 512], True, True, ['loT', l2n], [('ps', bks[nh])])

                                        def ev(o, r_, bk, nh, n=n, bci=bci):
                                            self.tt('dve', o, PS[bk][:, :], bc0[:, bci, nh * 512:(nh + 1) * 512], ALU.add, [('ps', bk), 'bc0'], [r_])
                                            self.act(o, o, AF.Sigmoid, [r_], [r_])
                                            if n == 1:
                                                self.ts('pool', o, o, -C64, None, ALU.mult, None, [r_], [r_])
                                        stage_out(dst, roff + t * 128, [
                                            lambda o, r_, bk=bks[0]: ev(o, r_, bk, 0),
                                            lambda o, r_, bk=bks[1]: ev(o, r_, bk, 1)])
                em.flush()
            with ExitStack() as s:
                E = s.enter_context
                bc1 = E(self.sbt("bc1", [128, 2, D], F32))
                self.dma('sp', bc1[:, 0, :], self.rw_k_k.to_broadcast([128, D]), (), ['bc1'])
                self.dma('sp', bc1[:, 1, :], self.rw_k_a.to_broadcast([128, D]), (), ['bc1'])
                kt = [E(self.sbt("ekt%d" % k, [128, D], F32)) for k in range(2)]
                at = [[E(self.sbt("eat%d_%d" % (d, k), [128, D], F32)) for k in range(2)] for d in range(2)]
                kk = [E(self.sbt("ekk%d" % k, [128, D], F32)) for k in range(2)]
                sq = [E(self.sbt("esq%d" % k, [128, D], F32)) for k in range(2)]
                o1 = [[E(self.sbt("eo1%d_%d" % (d, k), [128, D], F32)) for k in range(2)] for d in range(2)]
                o2 = [[E(self.sbt("eo2%d_%d" % (d, k), [128, D], F32)) for k in range(2)] for d in range(2)]
                ss = [E(self.sbt("ess%d" % k, [128, 16], F32)) for k in range(2)]
                for t in range(NT // 128):
                    p = t % 2
                    row = t * 128
                    RR_ = lambda nm: (nm, p)
                    self.dma('sp', kt[p][:], self.RK[row:row + 128, :], [('RK', t)], [RR_('ekt')])
                    for d in range(2):
                        self.dma('act', at[d][p][:], self.RAI[d][row:row + 128, :], [(self.RAI[d].tensor.name, t)], [RR_('eat%d' % d)])
                    self.tt('dve', kk[p][:], kt[p][:], bc1[:, 0, :], ALU.mult, [RR_('ekt'), 'bc1'], [RR_('ekk')])
                    self.act(sq[p][:], kk[p][:], AF.Square, [RR_('ekk')], [RR_('esq')])
                    self.em.op('dve', lambda e, p=p: e.tensor_reduce(out=ss[p][:], in_=R3(sq[p][:]), axis=AX.X, op=ALU.add), [RR_('esq')], [RR_('ess')])
                    self.ts('dve', ss[p][:], ss[p][:], 1e-24, None, ALU.max, None, [RR_('ess')], [RR_('ess')])
                    self.act(ss[p][:], ss[p][:], AF.Ln, [RR_('ess')], [RR_('ess')])
                    self.act(ss[p][:], ss[p][:], AF.Exp, [RR_('ess')], [RR_('ess')], scale=-0.5)
                    self.tt('dve', R3(kk[p][:]), R3(kk[p][:]), bc64(ss[p][:]), ALU.mult, [RR_('ekk'), RR_('ess')], [RR_('ekk')])
                    self.dma('sp', self.KKN[row:row + 128, :], kk[p][:], [RR_('ekk')], [('KKN', t)])
                    for d in range(2):
                        a_ = at[d][p]
                        self.stt('dve', o1[d][p][:], a_[:], -1.0, bc1[:, 1, :], ALU.add, ALU.mult, [RR_('eat%d' % d), 'bc1'], [RR_('eo1%d' % d)])
                        self.stt('dve', o1[d][p][:], o1[d][p][:], 1.0, kt[p][:], ALU.add, ALU.mult, [RR_('eo1%d' % d), RR_('ekt')], [RR_('eo1%d' % d)])
                        self.dma('sp', self.KD[d][row:row + 128, :], o1[d][p][:], [RR_('eo1%d' % d)], [(self.KD[d].tensor.name, t)])
                        self.tt('pool', o2[d][p][:], kk[p][:], a_[:], ALU.mult, [RR_('ekk'), RR_('eat%d' % d)], [RR_('eo2%d' % d)])
                        self.dma('sp', self.BV[d][row:row + 128, :], o2[d][p][:], [RR_('eo2%d' % d)], [(self.BV[d].tensor.name, t)])
                em.flush()
            with ExitStack() as s:
                E = s.enter_context
                msk = E(self.sbt("msk", [128, 4, 128], F32))
                bdm = E(self.sbt("bdm", [128, 128], F32))
                self.dma('sp', msk[:], self.masks.rearrange("m p t -> p m t"), (), ['msk'])
                self.dma('sp', bdm[:], self.bdmask, (), ['bdm'])
                MK4 = E(self.sbt("MK4", [128, 4, 128], F32))
                ST2 = E(self.sbt("ST2", [128, 8, 128], F32R))
                ld2 = [{nm: E(self.sbt("ld%d_" % k + nm, [128, D], F32)) for nm in ('lw', 'kd', 'bv', 'kkn', 'v', 'r')} for k in range(2)]
                nchunk = [0]
                cum = E(self.sbt("cum", [128, D], F32))
                tA = [E(self.sbt("tA%d" % k, [128, D], F32)) for k in range(2)]
                kb = E(self.sbt("kbar", [128, D], F32R))
                bb = E(self.sbt("bbar", [128, D], F32R))
                QT2 = E(self.sbt("QT2", [128, 8, 2, 128], F32R))
                khT = E(self.sbt("khT", [128, 8, 128], F32R))
                bhT = E(self.sbt("bhT", [128, 8, 128], F32R))
                AT3 = E(self.sbt("AT3", [128, 16, 3, 128], F32R))
                Lb = [E(self.sbt("Lb%d" % k, [128, 16, 128], F32R)) for k in range(2)]
                Ub = [E(self.sbt("Ub%d" % k, [128, 16, 128], F32R)) for k in range(2)]
                XT = E(self.sbt("XT", [128, 16, 128], F32R))
                rhs_sb = E(self.sbt("rhs_sb", [128, D], F32R))
                sa_sb = E(self.sbt("sa_sb", [128, D], F32R))
                vr = E(self.sbt("v_r", [128, D], F32R))
                y_sb = E(self.sbt("y_sb", [128, D], F32))
                gC = E(self.sbt("gC", [128, 8], F32))
                sttmp = E(self.sbt("sttmp", [128, 4, 128], F32))
                for d in range(2):
                    mS, mI, mL = (0, 1, 2) if d == 0 else (2, 3, 0)
                    for q_, mi_ in enumerate((mS, mI, mS, mI)):
                        self.cp('dve', MK4[:, q_, :], msk[:, mi_, :], ['msk'], ['MK4'])
                    self.ts('dve', ST2[:], self.ones[:, :].unsqueeze(1).to_broadcast([128, 8, 128]), 0.0, None, ALU.mult, None, ['ones'], ['ST2'])
                    order = [16, 17] + list(range(16)) if d == 0 else [17, 16] + list(range(15, -1, -1))
                    for ci in order:
                        row = ci * 128
                        emit = ci < 16
                        names = ['lw', 'kd', 'bv', 'kkn', 'v'] + (['r'] if emit else [])
                        srcs = {'lw': self.RLW[d], 'kd': self.KD[d], 'bv': self.BV[d], 'kkn': self.KKN, 'v': self.RV, 'r': self.RR}
                        pp = nchunk[0] % 2
                        nchunk[0] += 1
                        ld = ld2[pp]
                        for qi, nm in enumerate(names):
                            self.dma('sp', ld[nm][:], srcs[nm][row:row + 128, :], [(srcs[nm].tensor.name, ci)], [('ld', nm, pp)])
                        lw = ld['lw']
                        bks = [self.bank(), self.bank()]
                        for nh in range(2):
                            self.mm(PS[bks[nh]][:, :], msk[:, mI, :], lw[:, nh * 512:(nh + 1) * 512], True, True, ['msk', ('ld', 'lw', pp)], [('ps', bks[nh])])
                            self.cp('act' if nh == 0 else 'dve', cum[:, nh * 512:(nh + 1) * 512], PS[bks[nh]][:, :], [('ps', bks[nh])], ['cum'])
                        bt = [self.bank(), self.bank()]
                        for nh in range(2):
                            self.mm(PS[bt[nh]][:, :], self.ones[:], lw[:, nh * 512:(nh + 1) * 512], True, True, ['ones', ('ld', 'lw', pp)], [('ps', bt[nh])])
                        bg = self.bank()
                        for hp in range(8):
                            self.mm(PS[bg][:, hp:hp + 1], lw[:, hp * 128:(hp + 1) * 128], self.ones[:, 0:1], True, True, ['ones', ('ld', 'lw', pp)], [('ps', bg)])
                        self.act(gC[:], PS[bg][:, 0:8], AF.Exp, [('ps', bg)], ['gC'])
                        for nh in range(2):
                            sl = slice(nh * 512, (nh + 1) * 512)
                            self.tt('dve', tA[0][:, sl], PS[bt[nh]][:, :], cum[:, sl], ALU.subtract, [('ps', bt[nh]), 'cum'], [('tA', 0)])
                        self.act(tA[0][:], tA[0][:], AF.Exp, [('tA', 0)], [('tA', 0)])
                        self.tt('dve', kb[:], ld['kd'][:], tA[0][:], ALU.mult, [('ld', 'kd', pp), ('tA', 0)], ['kbar'])
                        self.tt('pool', bb[:], ld['bv'][:], tA[0][:], ALU.mult, [('ld', 'bv', pp), ('tA', 0)], ['bbar'])

                        def transp(src, srcr, dst_fn, dstr):
                            for g4 in range(2):
                                bk = self.bank()
                                for q4 in range(4):
                                    hp = g4 * 4 + q4
                                    self.tr(PS[bk][:, q4 * 128:(q4 + 1) * 128], src[:, hp * 128:(hp + 1) * 128], [srcr], [('ps', bk)])
                                dst_fn(g4, PS[bk][:, :].rearrange("p (q t) -> p q t", t=128), bk, dstr)
                        self.tt('dve', tA[0][:], cum[:], lw[:], ALU.subtract, ['cum', ('ld', 'lw', pp)], [('tA', 0)])
                        self.act(tA[0][:], tA[0][:], AF.Exp, [('tA', 0)], [('tA', 0)])
                        self.stt('dve', tA[0][:], ld['kkn'][:], -1.0, tA[0][:], ALU.mult, ALU.mult, [('ld', 'kkn', pp), ('tA', 0)], [('tA', 0)])
                        transp(tA[0], ('tA', 0), lambda g4, pv, bk, dr: self.cp('act', QT2[:, g4 * 4:(g4 + 1) * 4, 0, :], pv, [('ps', bk)], [dr]), 'QT2')
                        if emit:
                            self.act(tA[1][:], cum[:], AF.Exp, ['cum'], [('tA', 1)])
                            self.tt('dve', tA[1][:], tA[1][:], ld['r'][:], ALU.mult, [('tA', 1), ('ld', 'r', pp)], [('tA', 1)])
                            transp(tA[1], ('tA', 1), lambda g4, pv, bk, dr: self.cp('dve', QT2[:, g4 * 4:(g4 + 1) * 4, 1, :], pv, [('ps', bk)], [dr]), 'QT2')
                        self.act(tA[0][:], cum[:], AF.Exp, ['cum'], [('tA', 0)], scale=-1.0)
                        self.tt('dve', tA[1][:], tA[0][:], ld['kd'][:], ALU.mult, [('tA', 0), ('ld', 'kd', pp)], [('tA', 1)])
                        transp(tA[1], ('tA', 1), lambda g4, pv, bk, dr: self.cp('act', khT[:, g4 * 4:(g4 + 1) * 4, :], pv, [('ps', bk)], [dr]), 'khT')
                        self.tt('pool', tA[0][:], tA[0][:], ld['bv'][:], ALU.mult, [('tA', 0), ('ld', 'bv', pp)], [('tA', 0)])
                        transp(tA[0], ('tA', 0), lambda g4, pv, bk, dr: self.cp('dve', bhT[:, g4 * 4:(g4 + 1) * 4, :], pv, [('ps', bk)], [dr]), 'bhT')
                        NQ = 256 if emit else 128
                        for h in range(16):
                            hp, hh = h // 2, h % 2
                            pr = slice(hh * 64, (hh + 1) * 64)
                            bk = self.bank()
                            q2 = QT2[pr, hp, :, :].rearrange("p a t -> p (a t)")
                            self.mm(PS[bk][:, 0:NQ], khT[pr, hp, :], q2[:, 0:NQ], True, True, ['khT', 'QT2'], [('ps', bk)], sync=True)
                            self.mm(PS[bk][:, 256:256 + NQ], bhT[pr, hp, :], q2[:, 0:NQ], True, True, ['bhT', 'QT2'], [('ps', bk)], sync=True)
                            pv = PS[bk][:, :].rearrange("p (q t) -> p q t", t=128)
                            eng = 'dve'
                            if emit:
                                self.tt(eng, AT3[:, h, 0:2, :], pv[:, 0:2, :], MK4[:, 0:2, :], ALU.mult, [('ps', bk), 'MK4'], [('AT3', h)])
                                self.tt(eng, Ub[0][:, h, :], pv[:, 2, :], MK4[:, 2, :], ALU.mult, [('ps', bk), 'MK4'], [('Ub', 0, h // 4)])
                                self.tt(eng, AT3[:, h, 2, :], pv[:, 3, :], MK4[:, 3, :], ALU.mult, [('ps', bk), 'MK4'], [('AT3', h)])
                            else:
                                self.tt(eng, AT3[:, h, 0, :], pv[:, 0, :], MK4[:, 0, :], ALU.mult, [('ps', bk), 'MK4'], [('AT3', h)])
                                self.tt(eng, Ub[0][:, h, :], pv[:, 2, :], MK4[:, 2, :], ALU.mult, [('ps', bk), 'MK4'], [('Ub', 0, h // 4)])
                        for g4 in range(4):
                            bk = self.bank()
                            for q4 in range(4):
                                h = g4 * 4 + q4
                                hp, hh = h // 2, h % 2
                                pr = slice(hh * 64, (hh + 1) * 64)
                                self.mm(PS[bk][:, q4 * 128:(q4 + 1) * 128], QT2[pr, hp, 0, :], bhT[pr, hp, :], True, True, ['QT2', 'bhT'], [('ps', bk)], sync=True)
                            pv = PS[bk][:, :].rearrange("p (q t) -> p q t", t=128)
                            self.tt('dve', Lb[0][:, g4 * 4:(g4 + 1) * 4, :], pv, msk[:, mL, :].unsqueeze(1).to_broadcast([128, 4, 128]), ALU.mult,
                                    [('ps', bk), 'msk'], [('Lb', 0, g4)])
                            self.tt('pool', XT[:, g4 * 4:(g4 + 1) * 4, :], Ub[0][:, g4 * 4:(g4 + 1) * 4, :],
                                    self.ident[:, :].unsqueeze(1).to_broadcast([128, 4, 128]), ALU.add, [('Ub', 0, g4), 'ident'], [('XT', g4)])
                        cur = 0
                        for lev in range(6):
                            nxt = 1 - cur
                            last = lev == 5
                            bL = [self.bank() for _ in range(4)]
                            for q4 in range(4):
                                for g4 in range(4):
                                    h = g4 * 4 + q4
                                    self.mm(PS[bL[g4]][:, q4 * 128:(q4 + 1) * 128], Ub[cur][:, h, :], Lb[cur][:, h, :], True, True,
                                            [('Ub', cur, g4), ('Lb', cur, g4)], [('ps', bL[g4])], sync=True)
                            for g4 in range(4):
                                self.cp('act', Lb[nxt][:, g4 * 4:(g4 + 1) * 4, :], PS[bL[g4]][:, :].rearrange("p (q t) -> p q t", t=128),
                                        [('ps', bL[g4])], [('Lb', nxt, g4)])
                            if not last:
                                bU = [self.bank() for _ in range(4)]
                                for q4 in range(4):
                                    for g4 in range(4):
                                        h = g4 * 4 + q4
                                        self.mm(PS[bU[g4]][:, q4 * 128:(q4 + 1) * 128], Lb[cur][:, h, :], Ub[cur][:, h, :], True, True,
                                                [('Ub', cur, g4), ('Lb', cur, g4)], [('ps', bU[g4])], sync=True)
                                for g4 in range(4):
                                    self.cp('dve', Ub[nxt][:, g4 * 4:(g4 + 1) * 4, :], PS[bU[g4]][:, :].rearrange("p (q t) -> p q t", t=128),
                                            [('ps', bU[g4])], [('Ub', nxt, g4)])
                            bX = [self.bank() for _ in range(4)]
                            for q4 in range(4):
                                for g4 in range(4):
                                    h = g4 * 4 + q4
                                    self.mm(PS[bX[g4]][:, q4 * 128:(q4 + 1) * 128], Lb[nxt][:, h, :], XT[:, h, :], True, True,
                                            [('Lb', nxt, g4), ('XT', g4)], [('ps', bX[g4])], sync=True)
                            for g4 in range(4):
                                self.tt('dve', XT[:, g4 * 4:(g4 + 1) * 4, :], XT[:, g4 * 4:(g4 + 1) * 4, :],
                                        PS[bX[g4]][:, :].rearrange("p (q t) -> p q t", t=128), ALU.add,
                                        [('ps', bX[g4]), ('XT', g4)], [('XT', g4)])
                            cur = nxt
                        v_ = vr
                        self.cp('pool', vr[:], ld['v'][:], [('ld', 'v', pp)], ['vr'])
                        br = [self.bank(), self.bank()]
                        for hp in range(8):
                            bk = br[hp // 4]
                            c0 = (hp % 4) * 128
                            self.mm(PS[bk][:, c0:c0 + 128], QT2[:, hp, 0, :], ST2[:, hp, :], True, False, ['QT2', 'ST2'], [('ps', bk)], sync=True)
                            for hh in range(2):
                                h = hp * 2 + hh
                                self.mm(PS[bk][:, c0 + hh * 64:c0 + (hh + 1) * 64], AT3[:, h, 0, :], v_[:, h * 64:(h + 1) * 64], False, hh == 1,
                                        [('AT3', h), 'vr'], [('ps', bk)], sync=True)
                        self.cp('act', rhs_sb[:, 0:512], PS[br[0]][:, :], [('ps', br[0])], ['rhs_sb'])
                        self.cp('dve', rhs_sb[:, 512:1024], PS[br[1]][:, :], [('ps', br[1])], ['rhs_sb'])
                        bs_ = [self.bank(), self.bank()]
                        for h in range(16):
                            bk = bs_[h // 8]
                            c0 = (h % 8) * 64
                            self.mm(PS[bk][:, c0:c0 + 64], XT[:, h, :], rhs_sb[:, h * 64:(h + 1) * 64], True, True, [('XT', h // 4), 'rhs_sb'], [('ps', bk)], sync=True)
                        self.cp('act', sa_sb[:, 0:512], PS[bs_[0]][:, :], [('ps', bs_[0])], ['sa_sb'])
                        self.cp('dve', sa_sb[:, 512:1024], PS[bs_[1]][:, :], [('ps', bs_[1])], ['sa_sb'])
                        if emit:
                            by = [self.bank(), self.bank()]
                            for hp in range(8):
                                bk = by[hp // 4]
                                c0 = (hp % 4) * 128
                                self.mm(PS[bk][:, c0:c0 + 128], QT2[:, hp, 1, :], ST2[:, hp, :], True, False, ['QT2', 'ST2'], [('ps', bk)], sync=True)
                                for hh in range(2):
                                    h = hp * 2 + hh
                                    self.mm(PS[bk][:, c0 + hh * 64:c0 + (hh + 1) * 64], AT3[:, h, 1, :], v_[:, h * 64:(h + 1) * 64], False, False,
                                            [('AT3', h), 'vr'], [('ps', bk)], sync=True)
                                    self.mm(PS[bk][:, c0 + hh * 64:c0 + (hh + 1) * 64], AT3[:, h, 2, :], sa_sb[:, h * 64:(h + 1) * 64], False, hh == 1,
                                            [('AT3', h), 'sa_sb'], [('ps', bk)], sync=True)
                            self.cp('act', y_sb[:, 0:512], PS[by[0]][:, :], [('ps', by[0])], ['y_sb'])
                            self.cp('dve', y_sb[:, 512:1024], PS[by[1]][:, :], [('ps', by[1])], ['y_sb'])
                            self.dma('sp', self.YS[d][row:row + 128, :], y_sb[:], ['y_sb'], [(self.YS[d].tensor.name, ci)])
                        bst = [self.bank(), self.bank()]
                        for hp in range(8):
                            bk = bst[hp // 4]
                            c0 = (hp % 4) * 128
                            self.mm(PS[bk][:, c0:c0 + 128], kb[:, hp * 128:(hp + 1) * 128], v_[:, hp * 128:(hp + 1) * 128], True, False,
                                    ['kbar', 'vr'], [('ps', bk)], sync=True)
                            self.mm(PS[bk][:, c0:c0 + 128], bb[:, hp * 128:(hp + 1) * 128], sa_sb[:, hp * 128:(hp + 1) * 128], False, True,
                                    ['bbar', 'sa_sb'], [('ps', bk)], sync=True)
                        for g4 in range(2):
                            self.tt('dve', sttmp[:], PS[bst[g4]][:, :].rearrange("p (q t) -> p q t", t=128),
                                    bdm[:, :].unsqueeze(1).to_broadcast([128, 4, 128]), ALU.mult, [('ps', bst[g4]), 'bdm'], ['sttmp'])
                            self.tt('pool', ST2[:, g4 * 4:(g4 + 1) * 4, :], ST2[:, g4 * 4:(g4 + 1) * 4, :],
                                    gC[:, g4 * 4:(g4 + 1) * 4].unsqueeze(2).to_broadcast([128, 4, 128]), ALU.mult, ['ST2', 'gC'], ['ST2'])
                            self.tt('dve', ST2[:, g4 * 4:(g4 + 1) * 4, :], ST2[:, g4 * 4:(g4 + 1) * 4, :], sttmp[:], ALU.add, ['ST2', 'sttmp'], ['ST2'])
                em.flush()
            with ExitStack() as s:
                E = s.enter_context
                w_o = E(self.sbt("rw_wo", [128, KC, D], BF16))
                bc2 = E(self.sbt("bc2", [128, 3, D], F32))
                self.dma('sp', bc2[:, 0, :], self.rw_r_k.to_broadcast([128, D]), (), ['bc2'])
                self.dma('sp', bc2[:, 1, :], self.rw_gn_g.to_broadcast([128, D]), (), ['bc2'])
                self.dma('sp', bc2[:, 2, :], self.rw_gn_b.to_broadcast([128, D]), (), ['bc2'])
                src = self.rw_w_o.rearrange("(k p) n -> p k n", p=128)
                for hh in range(2):
                    self.dma('pool', w_o[:, :, hh * 512:(hh + 1) * 512], src[:, :, hh * 512:(hh + 1) * 512], (), [('w_o', hh)])
                tl = self.alloc_pn_tiles(E, True)
                L_ = {nm: E(self.sbt("d_" + nm, [128, D], F32)) for nm in ('y0', 'y1', 'r', 'kd0', 'kd1', 'v', 'g')}
                acc = E(self.sbt("d_acc", [128, D], F32))
                tq = E(self.sbt("d_tq", [128, D], F32))
                st16 = E(self.sbt("d_st", [128, 4, 16], F32))
                zT = E(self.sbt("d_zT", [128, KC, 128], BF16))
                for t in range(T // 128):
                    row = t * 128
                    srcs = {'y0': self.YS[0], 'y1': self.YS[1], 'r': self.RR, 'kd0': self.KD[0], 'kd1': self.KD[1], 'v': self.RV, 'g': self.RG}
                    for qi, nm in enumerate(L_):
                        self.dma('sp' if qi % 2 == 0 else 'act', L_[nm][:], srcs[nm][row:row + 128, :], [(srcs[nm].tensor.name, t)], [('dl', nm)])
                    for d in range(2):
                        y = L_['y%d' % d]
                        yr = ('dl', 'y%d' % d)
                        self.em.op('dve', lambda e, y=y: e.tensor_reduce(out=st16[:, 0, :], in_=R3(y[:]), axis=AX.X, op=ALU.add), [yr], ['st16'])
                        self.act(tq[:], y[:], AF.Square, [yr], ['tq'])
                        self.em.op('dve', lambda e: e.tensor_reduce(out=st16[:, 1, :], in_=R3(tq[:]), axis=AX.X, op=ALU.add), ['tq'], ['st16'])
                        self.ts('dve', st16[:, 0, :], st16[:, 0, :], 1.0 / 64, None, ALU.mult, None, ['st16'], ['st16'])
                        self.tt('dve', st16[:, 2, :], st16[:, 0, :], st16[:, 0, :], ALU.mult, ['st16'], ['st16'])
                        self.stt('dve', st16[:, 1, :], st16[:, 1, :], 1.0 / 64, st16[:, 2, :], ALU.mult, ALU.subtract, ['st16'], ['st16'])
                        self.act(st16[:, 1, :], st16[:, 1, :], AF.Ln, ['st16', 'epsc'], ['st16'], bias=self.epsc[:, 1:2])
                        self.act(st16[:, 1, :], st16[:, 1, :], AF.Exp, ['st16'], ['st16'], scale=-0.5)
                        self.tt('dve', R3(y[:]), R3(y[:]), bc64(st16[:, 0, :]), ALU.subtract, [yr, 'st16'], [yr])
                        self.tt('dve', R3(y[:]), R3(y[:]), bc64(st16[:, 1, :]), ALU.mult, [yr, 'st16'], [yr])
                        self.tt('pool', y[:], y[:], bc2[:, 1, :], ALU.mult, [yr, 'bc2'], [yr])
                        if d == 0:
                            self.tt('pool', acc[:], y[:], bc2[:, 2, :], ALU.add, [yr, 'bc2'], ['acc'])
                        else:
                            self.tt('pool', y[:], y[:], bc2[:, 2, :], ALU.add, [yr, 'bc2'], [yr])
                            self.tt('pool', acc[:], acc[:], y[:], ALU.add, [yr, 'acc'], ['acc'])
                        kd_ = L_['kd%d' % d]
                        kr_ = ('dl', 'kd%d' % d)
                        self.tt('dve', tq[:], L_['r'][:], kd_[:], ALU.mult, [('dl', 'r'), kr_], ['tq'])
                        self.tt('dve', tq[:], tq[:], bc2[:, 0, :], ALU.mult, ['tq', 'bc2'], ['tq'])
                        self.em.op('dve', lambda e: e.tensor_reduce(out=st16[:, 3, :], in_=R3(tq[:]), axis=AX.X, op=ALU.add), ['tq'], ['st16'])
                        self.tt('dve', R3(tq[:]), R3(L_['v'][:]), bc64(st16[:, 3, :]), ALU.mult, [('dl', 'v'), 'st16'], ['tq'])
                        self.tt('dve', acc[:], acc[:], tq[:], ALU.add, ['acc', 'tq'], ['acc'])
                    self.tt('dve', acc[:], acc[:], L_['g'][:], ALU.mult, ['acc', ('dl', 'g')], ['acc'])
                    for half in range(2):
                        bk = self.bank()
                        for kk_ in range(4):
                            k = half * 4 + kk_
                            self.tr(PS[bk][:, kk_ * 128:(kk_ + 1) * 128], acc[:, k * 128:(k + 1) * 128], ['acc'], [('ps', bk)])
                        self.cp('act' if half == 0 else 'dve', zT[:, half * 4:(half + 1) * 4, :], PS[bk][:, :].rearrange("p (q t) -> p q t", t=128),
                                [('ps', bk)], ['zT'])
                    bs = []
                    for nh in range(2):
                        bk = self.bank()
                        bs.append(bk)
                        for k in range(KC):
                            self.mm(PS[bk][:, :], zT[:, k, :], w_o[:, k, nh * 512:(nh + 1) * 512], k == 0, k == KC - 1, ['zT', ('w_o', nh)], [('ps', bk)])
                    self.postnorm_tile(tl, b * T + row, b, 0, hsrc, [PS[bs[0]][:, :], PS[bs[1]][:, :]], [[('ps', bs[0])], [('ps', bs[1])]],
                                       self.H, 'H', bias_bc=None, router=True)
                em.flush()


def tr128(v, nchunk):
    return np.ascontiguousarray(v.reshape(nchunk, 128).T)


def make_bm_tab(rpb):
    NEG = np.float32(-30000.0)
    out = np.full((16, 5, 2, 64, 10, 64), NEG, np.float32)
    rep = {0: 0, 1: 1, 2: 2, 3: 14, 4: 15}
    c = np.arange(64)
    kc = np.arange(64)
    win_c0 = np.clip(c - 8, 0, 48)
    col_ok = (kc[None, :] >= win_c0[:, None]) & (kc[None, :] < win_c0[:, None] + 16)
    cidx = np.clip(kc[None, :] - c[:, None] + 15, 0, 30)
    for cls, i_ in rep.items():
        ks = min(max(2 * i_ - 4, 0), 22)
        for rr in range(2):
            r = 2 * i_ + rr
            row0 = min(max(r - 4, 0), 24)
            for krel in range(10):
                kr = ks + krel
                if not (row0 <= kr < row0 + 8):
                    continue
                ridx = kr - r + 7
                vals = rpb[:, ridx, :][:, cidx]
                out[:, cls, rr, :, krel, :] = np.where(col_ok[None], vals, NEG)
    return np.ascontiguousarray(out.reshape(16, 5, 128, 640))


def prep_shared(inp):
    f = lambda a: np.ascontiguousarray(np.asarray(a, dtype=np.float32))
    sh = {}
    sh['ident'] = np.eye(128, dtype=np.float32)
    sh['ada_w'] = f(inp['ada_w'])
    sh['ada_b'] = f(inp['ada_b'])
    sh['ada_bT'] = np.stack([tr128(f(inp['ada_b'])[i], 48) for i in range(DEPTH)])
    sh['ln_g'] = f(inp['ln_g'])
    sh['ln_b'] = f(inp['ln_b'])
    sh['conv_w_in'] = f(inp['conv_w_in'])
    sh['conv_b_inT'] = np.stack([tr128(f(inp['conv_b_in'])[i], 16) for i in range(2)])
    wdw = f(inp['conv_w_dw'])
    sh['conv_w_dwT'] = np.ascontiguousarray(wdw.reshape(2, CONVW, KC, 128).transpose(0, 3, 2, 1))
    sh['conv_b_dwT'] = np.stack([tr128(f(inp['conv_b_dw'])[i], KC) for i in range(2)])
    sh['conv_ln_gT'] = np.stack([tr128(f(inp['conv_ln_g'])[i], KC) for i in range(2)])
    sh['conv_ln_bT'] = np.stack([tr128(f(inp['conv_ln_b'])[i], KC) for i in range(2)])
    sh['conv_w_out'] = f(inp['conv_w_out'])
    sh['conv_b_out'] = f(inp['conv_b_out'])
    sh['na_w_qkv'] = f(inp['na_w_qkv'])
    sh['na_w_o'] = f(inp['na_w_o'])
    sh['bm_tab'] = make_bm_tab(f(inp['na_rpb'])[0])
    mp = f(inp['rw_mu_prev'])[0]
    mn = f(inp['rw_mu_next'])[0]
    sh['rw_mu_prevT'] = np.ascontiguousarray(mp.reshape(6, KC, 128).transpose(2, 0, 1))
    sh['rw_mu_nextT'] = np.ascontiguousarray(mn.reshape(6, KC, 128).transpose(2, 0, 1))
    for nm in ("rw_w_r", "rw_w_k", "rw_w_v", "rw_w_o", "rw_w0", "rw_a0", "rw_w1", "rw_a1", "rw_w2", "rw_a2", "rw_g1", "rw_g2"):
        sh[nm] = np.ascontiguousarray(f(inp[nm])[0])
    for nm in ("rw_k_k", "rw_k_a", "rw_r_k", "rw_gn_g", "rw_gn_b"):
        sh[nm] = np.ascontiguousarray(f(inp[nm])[0].reshape(1, D))
    tri = np.triu(np.ones((128, 128), np.float32))
    sh['masks'] = np.stack([np.triu(tri, 1), tri, np.tril(tri.T, -1), tri.T]).astype(np.float32)
    bd = np.zeros((128, 128), np.float32)
    bd[:64, :64] = 1
    bd[64:, 64:] = 1
    sh['bdmask'] = bd
    sh['moe_router'] = f(inp['moe_router'])
    sh['moe_w_gate'] = f(inp['moe_w_gate'])
    sh['moe_w_up'] = f(inp['moe_w_up'])
    sh['moe_w_down'] = f(inp['moe_w_down'])
    return sh


def prep_core(inp, core):
    f = lambda a: np.asarray(a, dtype=np.float32)
    b0 = core * NB
    x = f(inp['x'])[b0:b0 + NB].reshape(NB * T, D)
    cx = f(inp['ctx'])[b0:b0 + NB].reshape(NB * L, D)
    hin = np.ascontiguousarray(np.concatenate([x, cx], axis=0))
    conds = np.stack([f(inp['c'])[b0], f(inp['c'])[b0 + 1], f(inp['c_ctx'])], axis=1)
    cT = np.ascontiguousarray(conds.reshape(KC, 128, 3).transpose(1, 0, 2))
    return {'hin': hin, 'cT': cT}


_NC_CACHE = {}


def kernel(**inputs):
    kb = KB()
    nc = kb.build()
    sh = prep_shared(inputs)
    in_maps = []
    for c in range(NCORES):
        m = dict(sh)
        m.update(prep_core(inputs, c))
        in_maps.append(m)
    res = run_bass_kernel_spmd(nc, in_maps, core_ids=list(range(NCORES)))
    outs = [r["out"].reshape(NB, T, D) for r in res.results]
    return np.concatenate(outs, axis=0).astype(np.float32)
```

```python
import numpy as np
import concourse.bass as bass
import concourse.mybir as mybir
from concourse.bass_utils import run_bass_kernel_spmd
from contextlib import ExitStack

F32 = mybir.dt.float32
F32R = mybir.dt.float32r
BF16 = mybir.dt.bfloat16
U32 = mybir.dt.uint32
I32 = mybir.dt.int32
ALU = mybir.AluOpType
AF = mybir.ActivationFunctionType
AX = mybir.AxisListType

ENGS = ('pe', 'act', 'dve', 'pool', 'sp')
DMAQ = ('sp', 'act', 'pool')


class Em:
    def __init__(self, nc, ndma=8):
        self.nc = nc
        self.K = ndma
        self.stack = ExitStack()
        ent = self.stack.enter_context
        self.csem = {e: ent(nc.semaphore("c_" + e)) for e in ENGS}
        self.dsem = {q: [ent(nc.semaphore("d_%s%d" % (q, i))) for i in range(ndma)] for q in DMAQ}
        self.ccnt = {e: 0 for e in ENGS}
        self.dcnt = {q: 0 for q in DMAQ}
        self.seen = {e: {} for e in ENGS}
        self.lastw = {}
        self.rd = {}
        self.prog = {e: [] for e in ENGS}
        self.ninstr = 0
        self.sigcnt = {}
        self.pe_sync_toks = set()

    def sem(self, key):
        if key[0] == 'c':
            return self.csem[key[1]]
        return self.dsem[key[1]][key[2]]

    def _deps(self, eng, reads, writes, pe_sync=False):
        need = {}
        pe_keep = pe_sync
        for r in reads:
            t = self.lastw.get(r)
            if t is not None and need.get(t[0], 0) < t[1]:
                need[t[0]] = t[1]
        for w in writes:
            t = self.lastw.get(w)
            if t is not None:
                if t in self.pe_sync_toks:
                    pe_keep = True
                if need.get(t[0], 0) < t[1]:
                    need[t[0]] = t[1]
            for k, v in self.rd.get(w, {}).items():
                if need.get(k, 0) < v:
                    need[k] = v
        waits = []
        seen = self.seen[eng]
        for k, v in need.items():
            if eng == 'pe' and k == ('c', 'pe') and not pe_keep:
                continue
            if seen.get(k, 0) < v:
                seen[k] = v
                waits.append((k, v))
        return waits

    def _reg(self, tok, reads, writes):
        for r in reads:
            self.rd.setdefault(r, {})[tok[0]] = tok[1]
        for w in writes:
            self.lastw[w] = tok
            self.rd[w] = {}

    def op(self, eng, fn, reads=(), writes=(), pe_sync=False):
        waits = self._deps(eng, reads, writes, pe_sync)
        self.ccnt[eng] += 1
        tok = (('c', eng), self.ccnt[eng])
        if pe_sync:
            self.pe_sync_toks.add(tok)
        self.prog[eng].append((waits, fn, tok, 1))
        self._reg(tok, reads, writes)
        self.ninstr += 1

    def dma(self, q, fn, reads=(), writes=()):
        waits = self._deps(q, reads, writes)
        n = self.dcnt[q]
        self.dcnt[q] += 1
        slot = n % self.K
        val = 16 * (n // self.K + 1)
        key = ('d', q, slot)
        if val > 16 and self.seen[q].get(key, 0) < val - 16:
            self.seen[q][key] = val - 16
            waits.append((key, val - 16))
        tok = (key, val)
        self.prog[q].append((waits, fn, tok, 16))
        self._reg(tok, reads, writes)
        self.ninstr += 1

    def barrier(self):
        for e in ENGS:
            waits = []
            seen = self.seen[e]
            for e2 in ENGS:
                k = ('c', e2)
                v = self.ccnt[e2]
                if e2 != e and v > 0 and seen.get(k, 0) < v:
                    seen[k] = v
                    waits.append((k, v))
            for q in DMAQ:
                n = self.dcnt[q]
                for slot in range(self.K):
                    cnt = (n - slot + self.K - 1) // self.K if n > slot else 0
                    if cnt > 0:
                        k = ('d', q, slot)
                        v = 16 * cnt
                        if seen.get(k, 0) < v:
                            seen[k] = v
                            waits.append((k, v))
            if waits:
                self.prog[e].append((waits, None, None, 0))
        self.lastw = {}
        self.rd = {}

    def flush(self):
        self.barrier()
        nc = self.nc
        waited = {e: set() for e in ENGS}
        for e in ENGS:
            for waits, fn, tok, inc in self.prog[e]:
                for k, v in waits:
                    if k[0] == 'c':
                        waited[k[1]].add(v)
        newval = {e: {} for e in ENGS}
        for e in ENGS:
            sc = self.sigcnt.get(e, 0)
            for waits, fn, tok, inc in self.prog[e]:
                if fn is not None and tok[0][0] == 'c' and tok[1] in waited[e]:
                    sc += 1
                    newval[e][tok[1]] = sc
            self.sigcnt[e] = sc

        def tr_(k, v):
            if k[0] == 'c':
                return newval[k[1]][v]
            return v
        with nc.Block() as block:
            for e, meth in (('pe', block.tensor), ('act', block.scalar), ('dve', block.vector),
                            ('pool', block.gpsimd), ('sp', block.sync)):
                prog = self.prog[e]
                self.prog[e] = []

                def body(eng, prog=prog, e=e):
                    for waits, fn, tok, inc in prog:
                        if fn is None:
                            for k, v in waits:
                                eng.wait_ge(self.sem(k), tr_(k, v))
                            continue
                        for k, v in waits[:-1]:
                            eng.wait_ge(self.sem(k), tr_(k, v))
                        ins = fn(eng)
                        if waits:
                            k, v = waits[-1]
                            ins._wait_ge(self.sem(k), tr_(k, v))
                        if tok[0][0] == 'c':
                            if tok[1] in newval[e]:
                                ins.then_inc(self.sem(tok[0]), 1)
                        else:
                            ins.then_inc(self.sem(tok[0]), inc)
                meth(body)

    def close(self):
        self.stack.close()


D = 1024
T = 2048
L = 256
NB = 2
KC = 8
NE = 16
CAP = 256
CAPC = 32
DE = 2048
DEPTH = 4
NROW = NB * T + NB * L
CTX0 = NB * T
ALPHA = float(8 ** 0.25)
LN_EPS = 1e-5
GRID_W = 64
CONVW = 31
NCORES = 8


class KB:
    def __init__(self, layers=(0, 1, 2, 3), n_exp=NE, debug=False):
        self.layers = list(layers)
        self.n_exp = n_exp
        self.debug = debug
        nc = bass.Bass("TRN2", target_bir_lowering=False)
        self.nc = nc
        self.em = Em(nc)
        self.dr = {}
        self._psi = 0

    def din(self, name, shape, dt=F32):
        t = self.nc.dram_tensor(name, list(shape), dt, kind="ExternalInput").ap()
        self.dr[name] = t
        return t

    def dout(self, name, shape, dt=F32):
        t = self.nc.dram_tensor(name, list(shape), dt, kind="ExternalOutput").ap()
        self.dr[name] = t
        return t

    def dint(self, name, shape, dt=F32):
        t = self.nc.dram_tensor(name, list(shape), dt, kind="ExternalOutput" if self.debug else "Internal").ap()
        self.dr[name] = t
        return t

    def sbt(self, name, shape, dt):
        self._uid = getattr(self, '_uid', 0) + 1
        return self.nc.sbuf_tensor("%s_%d" % (name, self._uid), shape, dt)

    def bank(self):
        i = self._psi
        self._psi = (i + 1) % 8
        return i

    def mm(self, out, lhsT, rhs, start, stop, r, w, sync=None):
        ps = (lhsT.dtype == F32) if sync is None else sync
        self.em.op('pe', lambda e: e.matmul(out, lhsT=lhsT, rhs=rhs, start=start, stop=stop), r, w, pe_sync=ps)

    def tr(self, out, in_, r, w, ident=None):
        idn = self.ident if ident is None else ident
        np_ = in_.shape[0]
        self.em.op('pe', lambda e: e.transpose(out=out, in_=in_, identity=idn[0:np_, 0:np_]), list(r) + ['ident'], w)

    def act(self, out, in_, func, r, w, bias=None, scale=None, accum_out=None):
        kw = {}
        if bias is not None:
            kw['bias'] = bias
        if scale is not None:
            kw['scale'] = scale
        if accum_out is not None:
            kw['accum_out'] = accum_out
        self.em.op('act', lambda e: e.activation(out=out, in_=in_, func=func, **kw), r, w)

    def tt(self, eng, out, in0, in1, op, r, w):
        self.em.op(eng, lambda e: e.tensor_tensor(out=out, in0=in0, in1=in1, op=op), r, w)

    def ts(self, eng, out, in0, s1, s2, op0, op1, r, w):
        if op1 is None:
            self.em.op(eng, lambda e: e.tensor_scalar(out=out, in0=in0, scalar1=s1, scalar2=None, op0=op0), r, w)
        else:
            self.em.op(eng, lambda e: e.tensor_scalar(out=out, in0=in0, scalar1=s1, scalar2=s2, op0=op0, op1=op1), r, w)

    def stt(self, eng, out, in0, scalar, in1, op0, op1, r, w):
        self.em.op(eng, lambda e: e.scalar_tensor_tensor(out=out, in0=in0, scalar=scalar, in1=in1, op0=op0, op1=op1), r, w)

    def cp(self, eng, out, in_, r, w):
        if eng == 'act':
            self.em.op('act', lambda e: e.copy(out=out, in_=in_), r, w)
        else:
            self.em.op(eng, lambda e: e.tensor_copy(out=out, in_=in_), r, w)

    def ms(self, eng, ap, val, w):
        self.em.op(eng, lambda e: e.memset(ap, val), (), w)

    def dma(self, q, out, in_, r, w, **kw):
        self.em.dma(q, lambda e: e.dma_start(out=out, in_=in_, **kw), r, w)

    def build(self):
        nc = self.nc
        em = self.em
        st = ExitStack()
        self.st = st
        E = st.enter_context
        self.hin = self.din("hin", [NROW, D])
        self.cT = self.din("cT", [128, KC, 3])
        self.ident_d = self.din("ident", [128, 128])
        self.ada_w = self.din("ada_w", [DEPTH, D, 6 * D])
        self.ada_bT = self.din("ada_bT", [DEPTH, 128, 48])
        self.ada_b = self.din("ada_b", [DEPTH, 6 * D])
        self.ln_g = self.din("ln_g", [DEPTH, 2, D])
        self.ln_b = self.din("ln_b", [DEPTH, 2, D])
        self.conv_w_in = self.din("conv_w_in", [2, D, 2 * D])
        self.conv_b_inT = self.din("conv_b_inT", [2, 128, 16])
        self.conv_w_dwT = self.din("conv_w_dwT", [2, 128, KC, CONVW])
        self.conv_b_dwT = self.din("conv_b_dwT", [2, 128, KC])
        self.conv_ln_gT = self.din("conv_ln_gT", [2, 128, KC])
        self.conv_ln_bT = self.din("conv_ln_bT", [2, 128, KC])
        self.conv_w_out = self.din("conv_w_out", [2, D, D])
        self.conv_b_out = self.din("conv_b_out", [2, D])
        self.moe_router = self.din("moe_router", [DEPTH, D, NE])
        self.moe_w_gate = self.din("moe_w_gate", [DEPTH, NE, D, DE])
        self.moe_w_up = self.din("moe_w_up", [DEPTH, NE, D, DE])
        self.moe_w_down = self.din("moe_w_down", [DEPTH, NE, DE, D])
        self.out = self.dout("out", [NB * T, D])
        self.H = self.dint("H", [NROW, D])
        self.Y = self.dint("Y", [NROW, D])
        self.extra_io()

        self.ident = E(self.sbt("ident_s", [128, 128], F32))
        self.ones = E(self.sbt("ones_s", [128, 128], F32))
        self.scT = E(self.sbt("scT", [128, KC, 3], BF16))
        self.modT = E(self.sbt("modT", [128, 32, 3], F32))
        self.ga_bc = E(self.sbt("ga_bc", [128, 2, 3, D], BF16))
        self.lnp = E(self.sbt("lnp", [128, 2, D], F32))
        self.affT = E(self.sbt("affT", [48, T], F32))
        self.affTc = E(self.sbt("affTc", [16, NB * L], F32))
        self.wr = E(self.sbt("wr", [128, KC, NE], F32))
        self.PS = [E(nc.psum_tensor("ps%d" % i, [128, 512], F32)) for i in range(8)]

        self.dma('sp', self.ident[:], self.ident_d, (), ['ident'])
        self.ms('dve', self.ones[:], 1.0, ['ones'])
        self.epsc = E(self.sbt("epsc", [128, 2], F32))
        self.ms('dve', self.epsc[:, 0:1], LN_EPS, ['epsc'])
        self.ms('dve', self.epsc[:, 1:2], 64e-5, ['epsc'])
        self.ms('dve', self.affT[:], 0.0, ['affT'])
        self.ms('dve', self.affTc[:], 0.0, ['affTc'])
        with ExitStack() as s2:
            cTs = s2.enter_context(self.sbt("cTs", [128, KC, 3], F32))
            self.dma('sp', cTs[:], self.cT, (), ['cTs'])
            self.act(self.scT[:], cTs[:], AF.Silu, ['cTs'], ['scT'])
            em.flush()

        for i in self.layers:
            kind, jj = i % 3, i // 3
            ctx_read = i <= 2
            ctx_live = i < 2
            self.ada_phase(i)
            if kind == 0:
                self.conv_phase(i, jj, ctx_live)
            elif kind == 1:
                self.na_phase(i, jj, ctx_live)
            else:
                self.rwkv_phase(i, jj)
            self.moe_phase(i, ctx_live, last=(i == self.layers[-1]))
        em.flush()
        st.close()
        em.close()
        return nc

    def extra_io(self):
        self.na_w_qkv = self.din("na_w_qkv", [1, D, 3 * D])
        self.na_w_o = self.din("na_w_o", [1, D, D])
        self.bm_tab = self.din("bm_tab", [16, 5, 128, 640])
        self.V_d = self.dint("V_d", [T + L, D], BF16)
        NT = T + L
        self.rw_mu_prevT = self.din("rw_mu_prevT", [128, 6, KC])
        self.rw_mu_nextT = self.din("rw_mu_nextT", [128, 6, KC])
        for nm in ("rw_w_r", "rw_w_k", "rw_w_v", "rw_w_o"):
            setattr(self, nm, self.din(nm, [D, D]))
        self.rw_w0 = self.din("rw_w0", [2, D])
        self.rw_a0 = self.din("rw_a0", [2, D])
        self.rw_w1 = self.din("rw_w1", [2, D, 64])
        self.rw_a1 = self.din("rw_a1", [2, D, 64])
        self.rw_w2 = self.din("rw_w2", [2, 64, D])
        self.rw_a2 = self.din("rw_a2", [2, 64, D])
        self.rw_g1 = self.din("rw_g1", [D, 128])
        self.rw_g2 = self.din("rw_g2", [128, D])
        for nm in ("rw_k_k", "rw_k_a", "rw_r_k", "rw_gn_g", "rw_gn_b"):
            setattr(self, nm, self.din(nm, [1, D]))
        self.masks = self.din("masks", [4, 128, 128])
        self.bdmask = self.din("bdmask", [128, 128])
        self.RK = self.dint("RK", [NT, D])
        self.RV = self.dint("RV", [NT, D])
        self.RR = self.dint("RR", [NT, D])
        self.RG = self.dint("RG", [NT, D])
        self.KKN = self.dint("KKN", [NT, D])
        self.RLW = [self.dint("RLW%d" % d, [NT, D]) for d in range(2)]
        self.RAI = [self.dint("RAI%d" % d, [NT, D]) for d in range(2)]
        self.KD = [self.dint("KD%d" % d, [NT, D]) for d in range(2)]
        self.BV = [self.dint("BV%d" % d, [NT, D]) for d in range(2)]
        self.YS = [self.dint("YS%d" % d, [T, D]) for d in range(2)]

    def load_lnp(self, i, g):
        for r_, src in ((0, self.ln_g), (1, self.ln_b)):
            self.dma('sp', self.lnp[:, r_, :], src[i, g:g + 1, :].to_broadcast([128, D]), (), [('lnp', r_)])

    def hsrc(self, i):
        return self.hin if i == self.layers[0] else self.H

    def ada_phase(self, i):
        nc = self.nc
        PS = self.PS
        with ExitStack() as s:
            E = s.enter_context
            wA = [E(self.sbt("wA%d" % k, [128, KC, D], BF16)) for k in range(2)]
            abT = E(self.sbt("abT", [128, 48], F32))
            gb = E(self.sbt("gb", [128, 2, D], F32))
            self.scR = E(self.sbt("scR", [128, KC, 3, 128], BF16))
            for j in range(3):
                self.cp('dve', self.scR[:, :, j, :], self.scT[:, :, j:j + 1].to_broadcast([128, KC, 128]), ['scT'], ['scR'])
            self.dma('sp', abT[:], self.ada_bT[i], (), ['abT'])
            for g, cb in enumerate((2, 5)):
                self.dma('sp', gb[:, g, :], self.ada_b[i:i + 1, cb * D:(cb + 1) * D].to_broadcast([128, D]), (), [('gb', g)])
            self.load_lnp(i, 0)
            self.dma('sp', self.wr[:], self.moe_router[i].rearrange("(k p) e -> p k e", p=128), (), ['wr'])
            for cb in range(6):
                wb = wA[cb % 2]
                src = self.ada_w[i][:, cb * D:(cb + 1) * D].rearrange("(k p) n -> p k n", p=128)
                for hh in range(2):
                    self.dma('pool', wb[:, :, hh * 512:(hh + 1) * 512], src[:, :, hh * 512:(hh + 1) * 512], (), [('wA', cb % 2, hh)])
                rw = [('wA', cb % 2, 0), ('wA', cb % 2, 1)]
                if cb in (0, 1, 3, 4):
                    v = (0, 1, None, 2, 3)[cb]
                    b = self.bank()
                    for fc in range(KC):
                        for k in range(KC):
                            self.mm(PS[b][:, fc * 3:fc * 3 + 3], wb[:, k, fc * 128:(fc + 1) * 128], self.scT[:, k, :],
                                    k == 0, k == KC - 1, rw + ['scT'], [('ps', b)])
                    pv = PS[b][:, 0:24].rearrange("p (f j) -> p f j", j=3)
                    self.tt('dve', self.modT[:, v * 8:(v + 1) * 8, :], pv,
                            abT[:, cb * 8:(cb + 1) * 8].unsqueeze(2).to_broadcast([128, 8, 3]), ALU.add,
                            [('ps', b), 'abT'], [('modT', v)])
                    if cb in (1, 4):
                        self.ts('dve', self.modT[:, v * 8:(v + 1) * 8, :], self.modT[:, v * 8:(v + 1) * 8, :], 1.0, None, ALU.add, None,
                                [('modT', v)], [('modT', v)])
                else:
                    g = 0 if cb == 2 else 1
                    for j in range(3):
                        for nh in range(2):
                            b = self.bank()
                            for k in range(KC):
                                self.mm(PS[b][:, :], self.scR[:, k, j, :], wb[:, k, nh * 512:(nh + 1) * 512],
                                        k == 0, k == KC - 1, rw + ['scR'], [('ps', b)])
                            self.tt('dve', self.ga_bc[:, g, j, nh * 512:(nh + 1) * 512], PS[b][:, :], gb[:, g, nh * 512:(nh + 1) * 512], ALU.add,
                                    [('ps', b), ('gb', g)], [('ga', g, j)])
            self.em.flush()

    def streams(self, ctx):
        s = [(0, T, 0, 0), (T, T, 1, 1)]
        if ctx:
            s += [(CTX0, L, 2, 2), (CTX0 + L, L, 2, 3)]
        return s

    def load_modT_to(self, i):
        pass

    def postnorm_tile(self, tl, row0, cond, g, hsrc, ysrc, yr, dst, dst_res, bias_bc=None, router=None):
        PS = self.PS
        ht, t1, t2, st6, mv, sm = tl['ht'], tl['t1'], tl['t2'], tl['st6'], tl['mv'], tl['sm']
        n = tl['n']
        tl['n'] += 1
        p = n % 2
        ht, t1 = ht[p], t1[p]
        R = lambda nm: (nm, p)
        self.dma('sp', ht[:], hsrc[row0:row0 + 128, :], [('H', row0 // 128)] if hsrc is self.H else [], [R('ht')])
        for nh in range(2):
            sl = slice(nh * 512, (nh + 1) * 512)
            if bias_bc is not None:
                self.tt('dve', t1[:, sl], ysrc[nh], bias_bc[:, sl], ALU.add, list(yr[nh]) + ['bias_bc'], [R('t1')])
                self.tt('pool', t1[:, sl], t1[:, sl], self.ga_bc[:, g, cond, sl], ALU.mult, [R('t1'), ('ga', g, cond)], [R('t1')])
            else:
                self.tt('dve', t1[:, sl], ysrc[nh], self.ga_bc[:, g, cond, sl], ALU.mult, list(yr[nh]) + [('ga', g, cond)], [R('t1')])
        self.stt('dve', t1[:], ht[:], ALPHA, t1[:], ALU.mult, ALU.add, [R('ht'), R('t1')], [R('t1')])
        for c in range(2):
            self.em.op('dve', lambda e, c=c: e.bn_stats(out=st6[p][:, c, :], in_=t1[:, c * 512:(c + 1) * 512]), [R('t1')], [R('st6')])
        self.em.op('dve', lambda e: e.bn_aggr(out=mv[p][:], in_=st6[p][:]), [R('st6')], [R('mv')])
        self.act(sm[p][:, 0:1], mv[p][:, 1:2], AF.Ln, [R('mv')], [R('sm')], bias=self.epsc[:, 0:1])
        self.act(sm[p][:, 0:1], sm[p][:, 0:1], AF.Exp, [R('sm')], [R('sm')], scale=-0.5)
        self.stt('dve', sm[p][:, 1:2], mv[p][:, 0:1], -1.0, sm[p][:, 0:1], ALU.mult, ALU.mult, [R('mv'), R('sm')], [R('sm')])
        self.act(t1[:], t1[:], AF.Identity, [R('t1'), R('sm')], [R('t1')], bias=sm[p][:, 1:2], scale=sm[p][:, 0:1])
        self.tt('pool', t1[:], t1[:], self.lnp[:, 0, :], ALU.mult, [R('t1'), ('lnp', 0)], [R('t1')])
        self.tt('dve', t1[:], t1[:], self.lnp[:, 1, :], ALU.add, [R('t1'), ('lnp', 1)], [R('t1')])
        self.dma('sp', dst[row0:row0 + 128, :] if dst is not self.out else dst[row0:row0 + 128, :], t1[:], [R('t1')], [(dst_res, row0 // 128)])
        if router is not None:
            self.router_tile(tl, t1, R('t1'), row0, cond, p)

    def router_tile(self, tl, hnew, hres, row0, cond, p):
        PS = self.PS
        hmT = tl['hmT'][p]
        lg = tl['lg'][p]
        R = lambda nm: (nm, p)
        for half in range(2):
            b = self.bank()
            for kk in range(4):
                k = half * 4 + kk
                self.tr(PS[b][:, kk * 128:(kk + 1) * 128], hnew[:, k * 128:(k + 1) * 128], [hres], [('ps', b)])
            for kk in range(4):
                k = half * 4 + kk
                eng = 'act' if kk % 2 == 0 else 'dve'
                if eng == 'act':
                    self.act(hmT[:, k, :], PS[b][:, kk * 128:(kk + 1) * 128], AF.Identity, [('ps', b), ('modT', 2), ('modT', 3)], [R('hmT')],
                             bias=self.modT[:, 16 + k, cond:cond + 1], scale=self.modT[:, 24 + k, cond:cond + 1])
                else:
                    self.ts('dve', hmT[:, k, :], PS[b][:, kk * 128:(kk + 1) * 128], self.modT[:, 24 + k, cond:cond + 1],
                            self.modT[:, 16 + k, cond:cond + 1], ALU.mult, ALU.add, [('ps', b), ('modT', 2), ('modT', 3)], [R('hmT')])
        b = self.bank()
        for k in range(KC):
            self.mm(PS[b][:, 0:NE], hmT[:, k, :], self.wr[:, k, :], k == 0, k == KC - 1, [R('hmT'), 'wr'], [('ps', b)])
        sm = tl['sm2'][p]
        self.em.op('dve', lambda e: e.tensor_reduce(out=sm[:, 0:1], in_=PS[b][:, 0:NE], axis=AX.X, op=ALU.max, negate=True), [('ps', b)], [R('sm2')])
        is_ctx = row0 >= CTX0
        bsel = (row0 - CTX0) // L if is_ctx else row0 // T
        c0 = 32 if (bsel == 1 and not is_ctx) else 0
        self.act(lg[:, c0:c0 + NE], PS[b][:, 0:NE], AF.Exp, [('ps', b), R('sm2')], [R('lg')], bias=sm[:, 0:1], accum_out=sm[:, 1:2])
        self.em.op('dve', lambda e: e.reciprocal(out=sm[:, 2:3], in_=sm[:, 1:2]), [R('sm2')], [R('sm2')])
        self.ts('dve', lg[:, c0:c0 + NE], lg[:, c0:c0 + NE], sm[:, 2:3], None, ALU.mult, None, [R('lg'), R('sm2')], [R('lg')])
        b2 = self.bank()
        npart = c0 + NE
        self.tr(PS[b2][0:npart, 0:128], lg[:, 0:npart], [R('lg')], [('ps', b2)])
        if is_ctx:
            col = (row0 - CTX0)
            self.cp('act', self.affTc[0:16, col:col + 128], PS[b2][0:16, 0:128], [('ps', b2)], ['affTc'])
        else:
            col = row0 - bsel * T
            self.cp('act', self.affT[c0:c0 + 16, col:col + 128], PS[b2][c0:c0 + 16, 0:128], [('ps', b2)], ['affT'])

    def alloc_pn_tiles(self, E, router):
        nc = self.nc
        tl = {'n': 0}
        tl['ht'] = [E(self.sbt("pn_ht%d" % k, [128, D], F32)) for k in range(2)]
        tl['t1'] = [E(self.sbt("pn_t1%d" % k, [128, D], F32)) for k in range(2)]
        tl['t2'] = None
        tl['st6'] = [E(self.sbt("pn_st%d" % k, [128, 2, 6], F32)) for k in range(2)]
        tl['mv'] = [E(self.sbt("pn_mv%d" % k, [128, 2], F32)) for k in range(2)]
        tl['sm'] = [E(self.sbt("pn_sm%d" % k, [128, 2], F32)) for k in range(2)]
        if router:
            tl['hmT'] = [E(self.sbt("pn_hmT%d" % k, [128, KC, 128], F32)) for k in range(2)]
            tl['lg'] = [E(self.sbt("pn_lg%d" % k, [128, 48], F32)) for k in range(2)]
            tl['sm2'] = [E(self.sbt("pn_sm2%d" % k, [128, 4], F32)) for k in range(2)]
            for k in range(2):
                self.ms('dve', tl['lg'][k][:], 0.0, [('lg', k)])
        return tl

    def conv_phase(self, i, jj, ctx_live):
        nc = self.nc
        PS = self.PS
        em = self.em
        hsrc = self.hsrc(i)
        with ExitStack() as so:
            convout = so.enter_context(self.sbt("convout", [128, KC, T], F32))
            vec = so.enter_context(self.sbt("cvec", [128, 16 + KC * CONVW + 3 * KC], F32))
            b_in = vec[:, 0:16]
            w_dw = vec[:, 16:16 + KC * CONVW].rearrange("p (k j) -> p k j", j=CONVW)
            o = 16 + KC * CONVW
            b_dw = vec[:, o:o + KC]
            cg = vec[:, o + KC:o + 2 * KC]
            cb_ = vec[:, o + 2 * KC:o + 3 * KC]
            self.dma('sp', b_in, self.conv_b_inT[jj], (), ['cvec'])
            self.dma('sp', w_dw, self.conv_w_dwT[jj], (), ['cvec'])
            self.dma('sp', b_dw, self.conv_b_dwT[jj], (), ['cvec'])
            self.dma('sp', cg, self.conv_ln_gT[jj], (), ['cvec'])
            self.dma('sp', cb_, self.conv_ln_bT[jj], (), ['cvec'])
            for (row0, nt, cond, sid) in self.streams(ctx_live):
                ntile = nt // 128
                CH = 512 if nt >= 512 else nt
                nch = nt // CH
                with ExitStack() as s:
                    E = s.enter_context
                    uT = E(self.sbt("uT", [128, KC, nt], BF16))
                    w_in = E(self.sbt("w_in", [128, KC, 2 * D], BF16))
                    xpad = [E(self.sbt("xpad%d" % k, [128, nt + 30], F32)) for k in range(2)]
                    sg = [E(self.sbt("sg%d" % k, [128, 512], F32)) for k in range(2)]
                    ht = [E(self.sbt("cht%d" % k, [128, D], F32)) for k in range(2)]
                    src = self.conv_w_in[jj].rearrange("(k p) n -> p k n", p=128)
                    for hh in range(4):
                        self.dma('pool', w_in[:, :, hh * 512:(hh + 1) * 512], src[:, :, hh * 512:(hh + 1) * 512], (), [('w_in', hh)])
                    for k in range(2):
                        self.ms('pool', xpad[k][:, 0:15], 0.0, [('xpad', k)])
                        self.ms('pool', xpad[k][:, nt + 15:nt + 30], 0.0, [('xpad', k)])
                    for tt_ in range(ntile):
                        p = tt_ % 2
                        r0 = row0 + tt_ * 128
                        self.dma('sp', ht[p][:], hsrc[r0:r0 + 128, :], [('H', r0 // 128)] if hsrc is self.H else [], [('cht', p)])
                        for half in range(2):
                            b = self.bank()
                            for kk in range(4):
                                k = half * 4 + kk
                                self.tr(PS[b][:, kk * 128:(kk + 1) * 128], ht[p][:, k * 128:(k + 1) * 128], [('cht', p)], [('ps', b)])
                            for kk in range(4):
                                k = half * 4 + kk
                                if kk % 2 == 0:
                                    self.act(uT[:, k, tt_ * 128:(tt_ + 1) * 128], PS[b][:, kk * 128:(kk + 1) * 128], AF.Identity,
                                             [('ps', b), ('modT', 0), ('modT', 1)], [('uT', tt_ * 128 // CH)],
                                             bias=self.modT[:, k, cond:cond + 1], scale=self.modT[:, 8 + k, cond:cond + 1])
                                else:
                                    self.ts('dve', uT[:, k, tt_ * 128:(tt_ + 1) * 128], PS[b][:, kk * 128:(kk + 1) * 128],
                                            self.modT[:, 8 + k, cond:cond + 1], self.modT[:, k, cond:cond + 1], ALU.mult, ALU.add,
                                            [('ps', b), ('modT', 0), ('modT', 1)], [('uT', tt_ * 128 // CH)])
                    for j in range(KC):
                        xp = xpad[j % 2]
                        for cc in range(nch):
                            tsl = slice(cc * CH, (cc + 1) * CH)
                            bg = self.bank()
                            for k in range(KC):
                                self.mm(PS[bg][:, 0:CH], w_in[:, k, (8 + j) * 128:(9 + j) * 128], uT[:, k, tsl], k == 0, k == KC - 1,
                                        [('w_in', (8 + j) // 4), ('uT', cc)], [('ps', bg)])
                            bv = self.bank()
                            for k in range(KC):
                                self.mm(PS[bv][:, 0:CH], w_in[:, k, j * 128:(j + 1) * 128], uT[:, k, tsl], k == 0, k == KC - 1,
                                        [('w_in', j // 4), ('uT', cc)], [('ps', bv)])
                            q = cc % 2
                            self.act(sg[q][:, 0:CH], PS[bg][:, 0:CH], AF.Sigmoid, [('ps', bg), 'cvec'], [('sg', q)], bias=b_in[:, 8 + j:9 + j])
                            self.stt('dve', xp[:, 15 + cc * CH:15 + (cc + 1) * CH], PS[bv][:, 0:CH], b_in[:, j:j + 1], sg[q][:, 0:CH],
                                     ALU.add, ALU.mult, [('ps', bv), ('sg', q), 'cvec'], [('xpad', j % 2)])
                        acc = convout[:, j, 0:nt]
                        self.ts('dve', acc, xp[:, 0:nt], w_dw[:, j, 0:1], b_dw[:, j:j + 1], ALU.mult, ALU.add,
                                [('xpad', j % 2), 'cvec'], [('convout', j)])
                        for tap in range(1, CONVW):
                            self.stt('dve', acc, xp[:, tap:tap + nt], w_dw[:, j, tap:tap + 1], acc, ALU.mult, ALU.add,
                                     [('xpad', j % 2), ('convout', j), 'cvec'], [('convout', j)])
                    em.flush()
                with ExitStack() as s:
                    E = s.enter_context
                    sT = E(self.sbt("sT", [128, KC, nt], BF16))
                    w_out = E(self.sbt("w_out", [128, KC, D], BF16))
                    bo_bc = E(self.sbt("bo_bc", [128, D], F32))
                    sq = [E(self.sbt("sq%d" % k, [128, 512], F32)) for k in range(2)]
                    mean = E(self.sbt("cmean", [128, 512], F32))
                    rstd = E(self.sbt("crstd", [128, 512], F32))
                    xn = [E(self.sbt("cxn%d" % k, [128, 512], F32)) for k in range(2)]
                    tl = self.alloc_pn_tiles(E, True)
                    src = self.conv_w_out[jj].rearrange("(k p) n -> p k n", p=128)
                    for hh in range(2):
                        self.dma('pool', w_out[:, :, hh * 512:(hh + 1) * 512], src[:, :, hh * 512:(hh + 1) * 512], (), [('w_out', hh)])
                    self.dma('sp', bo_bc[:], self.conv_b_out[jj:jj + 1, :].to_broadcast([128, D]), (), ['bias_bc'])
                    for cc in range(nch):
                        tsl = slice(cc * CH, (cc + 1) * CH)
                        bm = self.bank()
                        for k in range(KC):
                            self.mm(PS[bm][:, 0:CH], self.ones[:], convout[:, k, tsl], k == 0, k == KC - 1, ['ones', ('convout', k)], [('ps', bm)])
                        bq = self.bank()
                        for k in range(KC):
                            q = k % 2
                            self.act(sq[q][:, 0:CH], convout[:, k, tsl], AF.Square, [('convout', k)], [('sq', q)])
                            self.mm(PS[bq][:, 0:CH], self.ones[:], sq[q][:, 0:CH], k == 0, k == KC - 1, ['ones', ('sq', q)], [('ps', bq)])
                        self.act(mean[:, 0:CH], PS[bm][:, 0:CH], AF.Identity, [('ps', bm)], ['cmean'], scale=1.0 / D)
                        self.tt('dve', rstd[:, 0:CH], mean[:, 0:CH], mean[:, 0:CH], ALU.mult, ['cmean'], ['crstd'])
                        self.stt('dve', rstd[:, 0:CH], PS[bq][:, 0:CH], 1.0 / D, rstd[:, 0:CH], ALU.mult, ALU.subtract, [('ps', bq), 'crstd'], ['crstd'])
                        self.act(rstd[:, 0:CH], rstd[:, 0:CH], AF.Ln, ['crstd'], ['crstd'], bias=self.epsc[:, 0:1])
                        self.act(rstd[:, 0:CH], rstd[:, 0:CH], AF.Exp, ['crstd'], ['crstd'], scale=-0.5)
                        for k in range(KC):
                            q = k % 2
                            self.tt('dve', xn[q][:, 0:CH], convout[:, k, tsl], mean[:, 0:CH], ALU.subtract, [('convout', k), 'cmean'], [('cxn', q)])
                            self.tt('pool', xn[q][:, 0:CH], xn[q][:, 0:CH], rstd[:, 0:CH], ALU.mult, [('cxn', q), 'crstd'], [('cxn', q)])
                            self.act(sT[:, k, tsl], xn[q][:, 0:CH], AF.Silu, [('cxn', q), 'cvec'], [('sT', cc)],
                                     bias=cb_[:, k:k + 1], scale=cg[:, k:k + 1])
                    for tt_ in range(ntile):
                        r0 = row0 + tt_ * 128
                        bs = []
                        for nh in range(2):
                            b = self.bank()
                            bs.append(b)
                            for k in range(KC):
                                self.mm(PS[b][:, :], sT[:, k, tt_ * 128:(tt_ + 1) * 128], w_out[:, k, nh * 512:(nh + 1) * 512], k == 0, k == KC - 1,
                                        [('sT', tt_ * 128 // CH), ('w_out', nh)], [('ps', b)])
                        self.postnorm_tile(tl, r0, cond, 0, hsrc, [PS[bs[0]][:, :], PS[bs[1]][:, :]], [[('ps', bs[0])], [('ps', bs[1])]],
                                           self.H, 'H', bias_bc=bo_bc, router=True)
                    em.flush()

    def moe_phase(self, i, ctx_live, last):
        nc = self.nc
        PS = self.PS
        em = self.em
        nexp = self.n_exp
        with ExitStack() as so:
            EO = so.enter_context
            idxT = EO(self.sbt("idxT", [128, 2, 48], I32))
            gateT = EO(self.sbt("gateT", [128, 2, 48], F32))
            idxTc = EO(self.sbt("idxTc", [64, 16], I32))
            gateTc = EO(self.sbt("gateTc", [64, 16], F32))
            self.zero = EO(self.sbt("zero", [128, D], F32))
            self.ms('dve', self.zero[:], 0.0, ['zero'])
            self.load_lnp(i, 1)
            nrows = NROW if ctx_live else NB * T
            for r0 in range(0, nrows, 128):
                self.dma('sp', self.Y[r0:r0 + 128, :], self.zero[:], ['zero'], ['Y'])
            with ExitStack() as s:
                E = s.enter_context
                wk = E(self.sbt("wk", [48, T], F32))
                vals = E(self.sbt("vals", [48, CAP], F32))
                idxu = E(self.sbt("idxu", [48, CAP], U32))
                idxf = E(self.sbt("idxf", [48, CAP], F32))
                self.cp('dve', wk[:], self.affT[:], ['affT'], ['wk'])
                for it in range(CAP // 8):
                    sl = slice(it * 8, (it + 1) * 8)
                    self.em.op('dve', lambda e, sl=sl: e.max(out=vals[:, sl], in_=wk[:]), ['wk'], ['vals'])
                    self.em.op('dve', lambda e, sl=sl: e.max_index(out=idxu[:, sl], in_max=vals[:, sl], in_values=wk[:]), ['wk', 'vals'], ['idxu'])
                    if it < CAP // 8 - 1:
                        self.em.op('dve', lambda e, sl=sl: e.match_replace(out=wk[:], in_to_replace=vals[:, sl], in_values=wk[:], imm_value=-1.0),
                                   ['wk', 'vals'], ['wk'])
                self.cp('dve', idxf[:], idxu[:], ['idxu'], ['idxf'])
                self.ts('dve', idxf[32:48, :], idxf[32:48, :], float(T), None, ALU.add, None, ['idxf'], ['idxf'])
                for j in range(2):
                    b = self.bank()
                    self.tr(PS[b][:, 0:48], idxf[:, j * 128:(j + 1) * 128], ['idxf'], [('ps', b)])
                    self.cp('dve', idxT[:, j, :], PS[b][:, 0:48], [('ps', b)], ['idxT'])
                    b = self.bank()
                    self.tr(PS[b][:, 0:48], vals[:, j * 128:(j + 1) * 128], ['vals'], [('ps', b)])
                    self.cp('act', gateT[:, j, :], PS[b][:, 0:48], [('ps', b)], ['gateT'])
                if ctx_live:
                    wkc = E(self.sbt("wkc", [16, NB * L], F32))
                    valc = E(self.sbt("valc", [16, NB * CAPC], F32))
                    idxcu = E(self.sbt("idxcu", [16, NB * CAPC], U32))
                    idxcf = E(self.sbt("idxcf", [16, NB * CAPC], F32))
                    self.cp('dve', wkc[:], self.affTc[:], ['affTc'], ['wkc'])
                    for bb in range(NB):
                        for it in range(CAPC // 8):
                            sl = slice(bb * CAPC + it * 8, bb * CAPC + (it + 1) * 8)
                            wsl = slice(bb * L, (bb + 1) * L)
                            self.em.op('dve', lambda e, sl=sl, wsl=wsl: e.max(out=valc[:, sl], in_=wkc[:, wsl]), ['wkc'], ['valc'])
                            self.em.op('dve', lambda e, sl=sl, wsl=wsl: e.max_index(out=idxcu[:, sl], in_max=valc[:, sl], in_values=wkc[:, wsl]),
                                       ['wkc', 'valc'], ['idxcu'])
                            if it < CAPC // 8 - 1:
                                self.em.op('dve', lambda e, sl=sl, wsl=wsl: e.match_replace(out=wkc[:, wsl], in_to_replace=valc[:, sl],
                                                                                          in_values=wkc[:, wsl], imm_value=-1.0),
                                           ['wkc', 'valc'], ['wkc'])
                    self.cp('dve', idxcf[:], idxcu[:], ['idxcu'], ['idxcf'])
                    for bb in range(NB):
                        self.ts('dve', idxcf[:, bb * CAPC:(bb + 1) * CAPC], idxcf[:, bb * CAPC:(bb + 1) * CAPC], float(CTX0 + bb * L), None,
                                ALU.add, None, ['idxcf'], ['idxcf'])
                    b = self.bank()
                    self.tr(PS[b][0:64, 0:16], idxcf[:, :], ['idxcf'], [('ps', b)])
                    self.cp('dve', idxTc[:], PS[b][0:64, 0:16], [('ps', b)], ['idxTc'])
                    b = self.bank()
                    self.tr(PS[b][0:64, 0:16], valc[:, :], ['valc'], [('ps', b)])
                    self.cp('act', gateTc[:], PS[b][0:64, 0:16], [('ps', b)], ['gateTc'])
                em.flush()
            NTOK = 4 * 128 + (64 if ctx_live else 0)
            with ExitStack() as s:
                E = s.enter_context
                NS = 3
                G = [E(self.sbt("Gw%d" % k, [128, KC, 512], BF16)) for k in range(NS)]
                U = [E(self.sbt("Uw%d" % k, [128, KC, 512], BF16)) for k in range(NS)]
                Wd = [E(self.sbt("Wd%d" % k, [128, 16, D], BF16)) for k in range(2)]
                xg = [E(self.sbt("xg%d" % k, [128, D], F32)) for k in range(4)]
                xsT = [E(self.sbt("xsT%d" % k, [128, KC, NTOK], BF16)) for k in range(1)]
                hidT = [E(self.sbt("hidT%d" % k, [128, 16, NTOK], BF16)) for k in range(1)]
                sg = [E(self.sbt("msg%d" % k, [128, NTOK], F32)) for k in range(2)]
                ys = [E(self.sbt("ys%d" % k, [128, D], F32)) for k in range(2)]

                def load_gu(sq_):
                    if sq_ >= nexp * 4:
                        return
                    e, c = sq_ // 4, sq_ % 4
                    sl_ = sq_ % NS
                    sgw = self.moe_w_gate[i, e].rearrange("(k p) n -> p k n", p=128)
                    suw = self.moe_w_up[i, e].rearrange("(k p) n -> p k n", p=128)
                    self.dma('pool', G[sl_][:, :, :], sgw[:, :, c * 512:(c + 1) * 512], (), [('G', sl_)])
                    self.dma('pool', U[sl_][:, :, :], suw[:, :, c * 512:(c + 1) * 512], (), [('U', sl_)])

                def load_d(e):
                    sdw = self.moe_w_down[i, e].rearrange("(f p) n -> p f n", p=128)
                    for hh in range(4):
                        self.dma('pool', Wd[e % 2][:, hh * 4:(hh + 1) * 4, :], sdw[:, hh * 4:(hh + 1) * 4, :], (), [('Wd', e % 2, hh)])

                for sq_ in range(NS):
                    load_gu(sq_)
                load_d(0)
                xs = xsT[0]
                hd = hidT[0]
                allH = [('H', r) for r in range(NROW // 128)]

                def gather_tokens(e):
                    for bb in range(NB):
                        for j in range(2):
                            g = xg[bb * 2 + j]
                            col = bb * 32 + e
                            self.em.dma('pool', lambda en, g=g, j=j, col=col: en.indirect_dma_start(
                                out=g[:], out_offset=None, in_=self.H,
                                in_offset=bass.IndirectOffsetOnAxis(ap=idxT[:, j, col:col + 1], axis=0)),
                                ['idxT'] + allH, [('xg', bb * 2 + j)])

                def build_xs(e):
                    for bb in range(NB):
                        bks = [self.bank() for _ in range(4)]
                        for j in range(2):
                            g = xg[bb * 2 + j]
                            gr = ('xg', bb * 2 + j)
                            for k in range(KC):
                                bk = bks[k // 2]
                                off = (k % 2) * 256 + j * 128
                                self.tr(PS[bk][:, off:off + 128], g[:, k * 128:(k + 1) * 128], [gr], [('ps', bk)])
                        if bb == 0 and ctx_live:
                            self.em.dma('pool', lambda en, e=e: en.indirect_dma_start(
                                out=xg[0][0:64, :], out_offset=None, in_=self.H,
                                in_offset=bass.IndirectOffsetOnAxis(ap=idxTc[:, e:e + 1], axis=0)),
                                ['idxTc'] + allH, [('xg', 0)])
                        for k in range(KC):
                            bk = bks[k // 2]
                            off = (k % 2) * 256
                            dst = xs[:, k, bb * 256:(bb + 1) * 256]
                            if k % 2 == 0:
                                self.act(dst, PS[bk][:, off:off + 256], AF.Identity, [('ps', bk), ('modT', 2), ('modT', 3)], [('xsT', 0)],
                                         bias=self.modT[:, 16 + k, bb:bb + 1], scale=self.modT[:, 24 + k, bb:bb + 1])
                            else:
                                self.ts('dve', dst, PS[bk][:, off:off + 256], self.modT[:, 24 + k, bb:bb + 1], self.modT[:, 16 + k, bb:bb + 1],
                                        ALU.mult, ALU.add, [('ps', bk), ('modT', 2), ('modT', 3)], [('xsT', 0)])
                    if ctx_live:
                        g = xg[0]
                        gr = ('xg', 0)
                        bks = [self.bank() for _ in range(1)]
                        for k in range(KC):
                            self.tr(PS[bks[0]][:, k * 64:(k + 1) * 64], g[0:64, k * 128:(k + 1) * 128], [gr], [('ps', bks[0])])
                        for k in range(KC):
                            dst = xs[:, k, 512:576]
                            self.ts('dve', dst, PS[bks[0]][:, k * 64:(k + 1) * 64], self.modT[:, 24 + k, 2:3], self.modT[:, 16 + k, 2:3],
                                    ALU.mult, ALU.add, [('ps', bks[0]), ('modT', 2), ('modT', 3)], [('xsT', 0)])

                gather_tokens(0)
                build_xs(0)
                for e in range(nexp):
                    if e + 1 < nexp:
                        gather_tokens(e + 1)
                        load_d(e + 1)
                    for c in range(4):
                        for fl in range(4):
                            ft = c * 4 + fl
                            bg = self.bank()
                            bu = self.bank()
                            sl_ = (e * 4 + c) % NS
                            for (bk_, W, wr_) in ((bg, G[sl_], ('G', sl_)), (bu, U[sl_], ('U', sl_))):
                                for k in range(KC):
                                    self.mm(PS[bk_][:, 0:512], W[:, k, fl * 128:(fl + 1) * 128], xs[:, k, 0:512], k == 0, k == KC - 1,
                                            [wr_, ('xsT', 0)], [('ps', bk_)])
                            q = ft % 2
                            self.act(sg[q][:, 0:512], PS[bg][:, 0:512], AF.Silu, [('ps', bg)], [('msg', q)])
                            self.tt('dve', hd[:, ft, 0:512], sg[q][:, 0:512], PS[bu][:, 0:512], ALU.mult, [('msg', q), ('ps', bu)], [('hidT', ft)])
                            if ctx_live:
                                bc = self.bank()
                                for gi, (W, wr_) in enumerate(((G[sl_], ('G', sl_)), (U[sl_], ('U', sl_)))):
                                    for k in range(KC):
                                        self.mm(PS[bc][:, gi * 64:(gi + 1) * 64], W[:, k, fl * 128:(fl + 1) * 128], xs[:, k, 512:576], k == 0, k == KC - 1,
                                                [wr_, ('xsT', 0)], [('ps', bc)])
                                self.act(sg[q][:, 512:576], PS[bc][:, 0:64], AF.Silu, [('ps', bc)], [('msg', q)])
                                self.tt('dve', hd[:, ft, 512:576], sg[q][:, 512:576], PS[bc][:, 64:128], ALU.mult, [('msg', q), ('ps', bc)], [('hidT', ft)])
                        load_gu(e * 4 + c + NS)
                    if e + 1 < nexp:
                        build_xs(e + 1)
                    ttiles = [(bb, j, 128) for bb in range(NB) for j in range(2)] + ([(2, 0, 64)] if ctx_live else [])
                    prevY = [('Ys', (e + 1) % 2, t_) for t_ in range(5)]
                    for ti, (bb, j, np_) in enumerate(ttiles):
                        y_ = ys[ti % 2]
                        yr = ('ys', ti % 2)
                        tok0 = ti * 128
                        for nh in range(2):
                            b = self.bank()
                            for ft in range(16):
                                self.mm(PS[b][0:np_, :], hd[:, ft, tok0:tok0 + np_], Wd[e % 2][:, ft, nh * 512:(nh + 1) * 512], ft == 0, ft == 15,
                                        [('hidT', ft), ('Wd', e % 2, ft // 4)], [('ps', b)])
                            gcol = gateTc[:, e:e + 1] if bb == 2 else gateT[:, j, bb * 32 + e:bb * 32 + e + 1]
                            gres = 'gateTc' if bb == 2 else 'gateT'
                            if nh == 0:
                                self.ts('dve', y_[0:np_, 0:512], PS[b][0:np_, :], gcol[0:np_, :], None, ALU.mult, None, [('ps', b), gres], [yr])
                            else:
                                self.act(y_[0:np_, 512:1024], PS[b][0:np_, :], AF.Identity, [('ps', b), gres], [yr], scale=gcol[0:np_, :])
                        if bb == 2:
                            self.em.dma('pool', lambda en, y_=y_, e=e: en.indirect_dma_start(
                                out=self.Y, out_offset=bass.IndirectOffsetOnAxis(ap=idxTc[:, e:e + 1], axis=0),
                                in_=y_[0:64, :], in_offset=None, compute_op=ALU.add), [yr, 'idxTc', 'Y'] + prevY, [('Ys', e % 2, ti)])
                        else:
                            col = bb * 32 + e
                            self.em.dma('pool', lambda en, y_=y_, j=j, col=col: en.indirect_dma_start(
                                out=self.Y, out_offset=bass.IndirectOffsetOnAxis(ap=idxT[:, j, col:col + 1], axis=0),
                                in_=y_[:, :], in_offset=None, compute_op=ALU.add), [yr, 'idxT', 'Y'] + prevY, [('Ys', e % 2, ti)])
                em.flush()
            with ExitStack() as s:
                E = s.enter_context
                tl = self.alloc_pn_tiles(E, False)
                yt = [E(self.sbt("pn_y%d" % k, [128, D], F32)) for k in range(2)]
                for (row0, nt, cond, sid) in self.streams(ctx_live):
                    for tt_ in range(nt // 128):
                        r0 = row0 + tt_ * 128
                        p = tl['n'] % 2
                        self.dma('act', yt[p][:], self.Y[r0:r0 + 128, :], ['Y'], [('pn_y', p)])
                        if last and row0 < CTX0:
                            dst, dres = self.out, 'out'
                        else:
                            dst, dres = self.H, 'H'
                        self.postnorm_tile(tl, r0, cond, 1, self.H, [yt[p][:, 0:512], yt[p][:, 512:1024]], [[('pn_y', p)], [('pn_y', p)]],
                                           dst, dres, bias_bc=None, router=None)
                em.flush()

    def modT_tiles(self, uT, hsrc, ht, row0, nt, cond, c0, CH, res_name='uT'):
        PS = self.PS
        for tt_ in range(nt // 128):
            p = self._mt % 2
            self._mt += 1
            r0 = row0 + tt_ * 128
            col = c0 + tt_ * 128
            self.dma('sp', ht[p][:], hsrc[r0:r0 + 128, :], [('H', r0 // 128)] if hsrc is self.H else [], [('cht', p)])
            for half in range(2):
                b = self.bank()
                for kk in range(4):
                    k = half * 4 + kk
                    self.tr(PS[b][:, kk * 128:(kk + 1) * 128], ht[p][:, k * 128:(k + 1) * 128], [('cht', p)], [('ps', b)])
                for kk in range(4):
                    k = half * 4 + kk
                    if kk % 2 == 0:
                        self.act(uT[:, k, col:col + 128], PS[b][:, kk * 128:(kk + 1) * 128], AF.Identity,
                                 [('ps', b), ('modT', 0), ('modT', 1)], [(res_name, col // CH)],
                                 bias=self.modT[:, k, cond:cond + 1], scale=self.modT[:, 8 + k, cond:cond + 1])
                    else:
                        self.ts('dve', uT[:, k, col:col + 128], PS[b][:, kk * 128:(kk + 1) * 128],
                                self.modT[:, 8 + k, cond:cond + 1], self.modT[:, k, cond:cond + 1], ALU.mult, ALU.add,
                                [('ps', b), ('modT', 0), ('modT', 1)], [(res_name, col // CH)])

    def na_phase(self, i, jj, ctx_live):
        nc = self.nc
        PS = self.PS
        em = self.em
        hsrc = self.hsrc(i)
        NT = T + L
        chunks = [(0, 512), (512, 512), (1024, 512), (1536, 512), (2048, 256)]
        classes = ((0, [0]), (1, [1]), (2, list(range(2, 14))), (3, [14]), (4, [15]))
        with ExitStack() as so:
            EO = so.enter_context
            qT = EO(self.sbt("qT", [128, 8, NT], BF16))
            kT = EO(self.sbt("kT", [128, 8, NT], BF16))
            identb = EO(self.sbt("identb", [128, 128], BF16))
            self.cp('dve', identb[:], self.ident[:], ['ident'], ['identb'])
            for b in range(NB):
                with ExitStack() as s:
                    E = s.enter_context
                    uT = E(self.sbt("uT", [128, KC, NT], BF16))
                    wq = [E(self.sbt("wq%d" % k, [128, KC, D], BF16)) for k in range(2)]
                    vst = [E(self.sbt("vst%d" % k, [128, D], BF16)) for k in range(2)]
                    ht = [E(self.sbt("cht%d" % k, [128, D], F32)) for k in range(2)]

                    def loadw(blk, slot):
                        src = self.na_w_qkv[jj][:, blk * D:(blk + 1) * D].rearrange("(k p) n -> p k n", p=128)
                        for hh in range(2):
                            self.dma('pool', wq[slot][:, :, hh * 512:(hh + 1) * 512], src[:, :, hh * 512:(hh + 1) * 512], (), [('wq', slot, hh)])
                    loadw(0, 0)
                    loadw(1, 1)
                    self._mt = 0
                    self.modT_tiles(uT, hsrc, ht, b * T, T, b, 0, 512)
                    self.modT_tiles(uT, hsrc, ht, CTX0 + b * L, L, 2, T, 512)
                    for blk, dstT, nm in ((0, qT, 'qT'), (1, kT, 'kT')):
                        w = wq[blk]
                        for hp in range(8):
                            for (c0, n) in chunks:
                                bk = self.bank()
                                for k in range(KC):
                                    self.mm(PS[bk][:, 0:n], w[:, k, hp * 128:(hp + 1) * 128], uT[:, k, c0:c0 + n], k == 0, k == KC - 1,
                                            [('wq', blk, hp // 4), ('uT', c0 // 512)], [('ps', bk)])
                                if blk == 0:
                                    self.act(dstT[:, hp, c0:c0 + n], PS[bk][:, 0:n], AF.Identity, [('ps', bk)], [(nm, hp)], scale=0.125)
                                else:
                                    self.cp('dve', dstT[:, hp, c0:c0 + n], PS[bk][:, 0:n], [('ps', bk)], [(nm, hp)])
                    loadw(2, 0)
                    for t in range(NT // 128):
                        p = t % 2
                        for nh in range(2):
                            bk = self.bank()
                            for k in range(KC):
                                self.mm(PS[bk][:, :], uT[:, k, t * 128:(t + 1) * 128], wq[0][:, k, nh * 512:(nh + 1) * 512], k == 0, k == KC - 1,
                                        [('wq', 0, nh), ('uT', t * 128 // 512)], [('ps', bk)])
                            if nh == 0:
                                self.cp('act', vst[p][:, 0:512], PS[bk][:, :], [('ps', bk)], [('vst', p)])
                            else:
                                self.cp('dve', vst[p][:, 512:1024], PS[bk][:, :], [('ps', bk)], [('vst', p)])
                        self.dma('sp', self.V_d[t * 128:(t + 1) * 128, :], vst[p][:], [('vst', p)], [('V_d', t)])
                    em.flush()
                with ExitStack() as sm_:
                    oT = sm_.enter_context(self.sbt("oT", [128, 8, NT], BF16))
                    with ExitStack() as s:
                        E = s.enter_context
                        Vh = [E(self.sbt("Vh%d" % k, [128, NT // 128, 128], BF16)) for k in range(2)]
                        BM = [E(self.sbt("BM%d" % k, [128, 640], F32)) for k in range(2)]
                        S = [E(self.sbt("S%d" % k, [128, 896], F32)) for k in range(3)]
                        Pn = [E(self.sbt("Pn%d" % k, [128, 896], BF16)) for k in range(3)]
                        PT = [E(self.sbt("PT%d" % k, [128, 896], BF16)) for k in range(3)]
                        smx = [E(self.sbt("smx%d" % k, [128, 4], F32)) for k in range(3)]
                        nbm = 0
                        nq = 0
                        pend = [None]

                        def run_att(g):
                            next(g)
                            if pend[0] is not None:
                                next(pend[0], None)
                            pend[0] = g

                        vsrc = self.V_d.rearrange("(t p) c -> p t c", p=128)
                        for hp in range(8):
                            vh = Vh[hp % 2]
                            self.dma('sp', vh[:], vsrc[:, :, hp * 128:(hp + 1) * 128], [('V_d', t) for t in range(NT // 128)], [('Vh', hp % 2)])
                            for hh in range(2):
                                h = hp * 2 + hh
                                pr = slice(hh * 64, (hh + 1) * 64)

                                def attend(qcol, segs, bm, nkeys, vts, pr_a=pr, hp_a=hp, hh_a=hh, vh_a=vh):
                                    nonlocal nq
                                    m = nq % 3
                                    nq += 1
                                    pr, hp, hh, vh = pr_a, hp_a, hh_a, vh_a
                                    S_, Pn_, PT_, sx = S[m], Pn[m], PT[m], smx[m]
                                    R = lambda nm: (nm, m)
                                    bks = {}
                                    scol = 0
                                    for (bi, pc0, kc0, n, loc) in segs:
                                        if bi not in bks:
                                            bks[bi] = self.bank()
                                        bk = bks[bi]
                                        self.mm(PS[bk][:, pc0:pc0 + n], qT[pr, hp, qcol:qcol + 128], kT[pr, hp, kc0:kc0 + n], True, True,
                                                [('qT', hp), ('kT', hp)], [('ps', bk)])
                                    for (bi, pc0, kc0, n, loc) in segs:
                                        bk = bks[bi]
                                        if loc:
                                            self.tt('dve', S_[:, scol:scol + n], PS[bk][:, pc0:pc0 + n], bm[0][:, scol:scol + n], ALU.add,
                                                    [('ps', bk), bm[1]], [R('S')])
                                        else:
                                            self.cp('act', S_[:, scol:scol + n], PS[bk][:, pc0:pc0 + n], [('ps', bk)], [R('S')])
                                        scol += n
                                    self.em.op('dve', lambda e: e.tensor_reduce(out=sx[:, 0:1], in_=S_[:, 0:nkeys], axis=AX.X, op=ALU.max, negate=True),
                                               [R('S')], [R('smx')])
                                    self.act(S_[:, 0:nkeys], S_[:, 0:nkeys], AF.Exp, [R('S'), R('smx')], [R('S')], bias=sx[:, 0:1], accum_out=sx[:, 1:2])
                                    self.em.op('dve', lambda e: e.reciprocal(out=sx[:, 2:3], in_=sx[:, 1:2]), [R('smx')], [R('smx')])
                                    self.ts('dve', Pn_[:, 0:nkeys], S_[:, 0:nkeys], sx[:, 2:3], None, ALU.mult, None, [R('S'), R('smx')], [R('Pn')])
                                    yield
                                    bT = self.bank()
                                    PTv = PS[bT][:, :].bitcast(BF16)
                                    nch = nkeys // 128
                                    for c in range(nch):
                                        self.tr(PTv[:, c * 128:(c + 1) * 128], Pn_[:, c * 128:(c + 1) * 128], [R('Pn'), 'identb'], [('ps', bT)], ident=identb)
                                    if m != 1:
                                        self.cp('act', PT_[:, 0:nkeys], PTv[:, 0:nkeys], [('ps', bT)], [R('PT')])
                                    else:
                                        self.cp('dve', PT_[:, 0:nkeys], PTv[:, 0:nkeys], [('ps', bT)], [R('PT')])
                                    bO = bks[max(bks)]
                                    for c in range(nch):
                                        self.mm(PS[bO][pr, 384:512], vh[:, vts[c], hh * 64:(hh + 1) * 64], PT_[:, c * 128:(c + 1) * 128], c == 0, c == nch - 1,
                                                [('Vh', hp % 2), R('PT')], [('ps', bO)])
                                    if m == 0:
                                        self.cp('dve', oT[pr, hp, qcol:qcol + 128], PS[bO][pr, 384:512], [('ps', bO)], [('oT', qcol // 128)])
                                    else:
                                        self.cp('act', oT[pr, hp, qcol:qcol + 128], PS[bO][pr, 384:512], [('ps', bO)], [('oT', qcol // 128)])

                                for cls, tiles in classes:
                                    bmt = BM[nbm % 2]
                                    bmr = ('BM', nbm % 2)
                                    nbm += 1
                                    self.dma('sp', bmt[:], self.bm_tab[h, cls], (), [bmr])
                                    for i_ in tiles:
                                        ks = min(max(2 * i_ - 4, 0), 22)
                                        k0 = ks * 64
                                        segs = [(0, 0, k0, 512, True), (1, 0, k0 + 512, 128, True), (1, 128, T, 256, False)]
                                        vts = [ks // 2 + c for c in range(5)] + [16, 17]
                                        run_att(attend(i_ * 128, segs, (bmt, bmr), 896, vts))
                                for j in range(2):
                                    run_att(attend(T + j * 128, [(0, 0, T, 256, False)], None, 256, [16, 17]))
                        if pend[0] is not None:
                            next(pend[0], None)
                            pend[0] = None
                        em.flush()
                    with ExitStack() as s:
                        E = s.enter_context
                        w_o = E(self.sbt("w_o", [128, KC, D], BF16))
                        tl = self.alloc_pn_tiles(E, True)
                        src = self.na_w_o[jj].rearrange("(k p) n -> p k n", p=128)
                        for hh in range(2):
                            self.dma('pool', w_o[:, :, hh * 512:(hh + 1) * 512], src[:, :, hh * 512:(hh + 1) * 512], (), [('w_o', hh)])
                        for t in range(NT // 128):
                            if t < 16:
                                r0, cond = b * T + t * 128, b
                            else:
                                r0, cond = CTX0 + b * L + (t - 16) * 128, 2
                            bs = []
                            for nh in range(2):
                                bk = self.bank()
                                bs.append(bk)
                                for k in range(KC):
                                    self.mm(PS[bk][:, :], oT[:, k, t * 128:(t + 1) * 128], w_o[:, k, nh * 512:(nh + 1) * 512], k == 0, k == KC - 1,
                                            [('oT', t), ('w_o', nh)], [('ps', bk)])
                            self.postnorm_tile(tl, r0, cond, 0, hsrc, [PS[bs[0]][:, :], PS[bs[1]][:, :]], [[('ps', bs[0])], [('ps', bs[1])]],
                                               self.H, 'H', bias_bc=None, router=True)
                        em.flush()


    def rwkv_phase(self, i, jj):
        nc = self.nc
        PS = self.PS
        em = self.em
        hsrc = self.hsrc(i)
        NT = T + L
        C64 = 0.6065306597126334
        R3 = lambda ap, dd=64: ap.rearrange("p (h d) -> p h d", d=dd)

        def bc64(ap16):
            return ap16.unsqueeze(2).to_broadcast([128, 16, 64])

        for b in range(NB):
            with ExitStack() as s:
                E = s.enter_context
                wbuf = [E(self.sbt("rwW%d" % k, [128, KC, D], BF16)) for k in range(1)]
                w1c = E(self.sbt("w1c", [128, KC, 128], BF16))
                a1c = E(self.sbt("a1c", [128, KC, 128], BF16))
                g1s = E(self.sbt("g1s", [128, KC, 128], BF16))
                w2s = E(self.sbt("w2s", [128, D], BF16))
                a2s = E(self.sbt("a2s", [128, D], BF16))
                g2s = E(self.sbt("g2s", [128, D], BF16))
                muT = E(self.sbt("muT", [128, 2, 6, KC], F32))
                bc0 = E(self.sbt("bc0", [128, 2, D], F32))
                u32 = E(self.sbt("u32", [128, KC, T + 2], F32))
                lerp = E(self.sbt("lerp", [128, KC, T], BF16))
                tmp = [E(self.sbt("ltmp%d" % k, [128, T], F32)) for k in range(2)]
                loT = E(self.sbt("loT", [128, T], BF16))
                stg = [E(self.sbt("stg%d" % k, [128, D], F32)) for k in range(2)]
                ht = [E(self.sbt("cht%d" % k, [128, D], F32)) for k in range(2)]
                self.dma('sp', muT[:, 0], self.rw_mu_prevT, (), ['muT'])
                self.dma('sp', muT[:, 1], self.rw_mu_nextT, (), ['muT'])
                for d in range(2):
                    self.dma('pool', w1c[:, :, d * 64:(d + 1) * 64], self.rw_w1[d].rearrange("(k p) n -> p k n", p=128), (), ['w1c'])
                    self.dma('pool', a1c[:, :, d * 64:(d + 1) * 64], self.rw_a1[d].rearrange("(k p) n -> p k n", p=128), (), ['a1c'])
                self.dma('pool', g1s[:], self.rw_g1.rearrange("(k p) n -> p k n", p=128), (), ['g1s'])
                self.dma('pool', w2s[:], self.rw_w2.rearrange("d j n -> (d j) n"), (), ['w2s'])
                self.dma('pool', a2s[:], self.rw_a2.rearrange("d j n -> (d j) n"), (), ['a2s'])
                self.dma('pool', g2s[:], self.rw_g2, (), ['g2s'])
                nst = [0]

                def loadW(src, slot):
                    v = src.rearrange("(k p) n -> p k n", p=128)
                    for hh in range(2):
                        self.dma('pool', wbuf[slot][:, :, hh * 512:(hh + 1) * 512], v[:, :, hh * 512:(hh + 1) * 512], (), [('rwW', slot, hh)])

                def stage_out(dst, row, evs):
                    p = nst[0] % 2
                    nst[0] += 1
                    for nh in range(2):
                        evs[nh](stg[p][:, nh * 512:(nh + 1) * 512], ('stg', p))
                    self.dma('sp', dst[row:row + 128, :], stg[p][:], [('stg', p)], [(dst.tensor.name, row // 128)])

                for (row0, nt, cond, roff, is_lat) in ((b * T, T, b, 0, True), (CTX0 + b * L, L, 2, T, False)):
                    ntile = nt // 128
                    CH = min(512, nt)
                    nch = nt // CH
                    for k in range(KC):
                        self.ms('pool', u32[:, k, 0:1], 0.0, [('u32', 0)])
                        self.ms('pool', u32[:, k, nt + 1:nt + 2], 0.0, [('u32', 0)])
                    self._mt = 0
                    self.modT_tiles(u32, hsrc, ht, row0, nt, cond, 1, 1 << 30, res_name='u32')
                    lerps = (0, 1, 2, 3, 4, 5) if is_lat else (1, 2, 3, 4)
                    wsrc = {0: self.rw_w_r, 2: self.rw_w_k, 3: self.rw_w_v}
                    wslot = {0: 0, 2: 0, 3: 0}
                    for n in lerps:
                        if n in wsrc:
                            loadW(wsrc[n], wslot[n])
                        for k in range(KC):
                            uk = u32[:, k, 1:nt + 1]
                            t0, t1 = tmp[0][:, 0:nt], tmp[1][:, 0:nt]
                            self.tt('dve', t0, u32[:, k, 0:nt], uk, ALU.subtract, [('u32', 0)], [('ltmp', 0)])
                            self.stt('dve', t0, t0, muT[:, 0, n, k:k + 1], uk, ALU.mult, ALU.add, [('ltmp', 0), 'muT', ('u32', 0)], [('ltmp', 0)])
                            self.tt('pool', t1, u32[:, k, 2:nt + 2], uk, ALU.subtract, [('u32', 0)], [('ltmp', 1)])
                            self.stt('dve', lerp[:, k, 0:nt], t1, muT[:, 1, n, k:k + 1], t0, ALU.mult, ALU.add,
                                     [('ltmp', 0), ('ltmp', 1), 'muT'], [('lerp', k)])
                        lr = [('lerp', k) for k in range(KC)]
                        if n in wsrc:
                            dst = {0: self.RR, 2: self.RK, 3: self.RV}[n]
                            W = wbuf[wslot[n]]
                            for t in range(ntile):
                                bks = [self.bank(), self.bank()]
                                for nh in range(2):
                                    for k in range(KC):
                                        self.mm(PS[bks[nh]][:, :], lerp[:, k, t * 128:(t + 1) * 128], W[:, k, nh * 512:(nh + 1) * 512], k == 0, k == KC - 1,
                                                lr + [('rwW', wslot[n], nh)], [('ps', bks[nh])])
                                stage_out(dst, roff + t * 128, [
                                    lambda o, r_, bk=bks[0]: self.cp('act', o, PS[bk][:, :], [('ps', bk)], [r_]),
                                    lambda o, r_, bk=bks[1]: self.cp('dve', o, PS[bk][:, :], [('ps', bk)], [r_])])
                        else:
                            l1 = {1: w1c, 4: a1c, 5: g1s}[n]
                            l1n = {1: 'w1c', 4: 'a1c', 5: 'g1s'}[n]
                            for cc in range(nch):
                                bk = self.bank()
                                for k in range(KC):
                                    self.mm(PS[bk][:, 0:CH], l1[:, k, :], lerp[:, k, cc * CH:(cc + 1) * CH], k == 0, k == KC - 1, lr + [l1n], [('ps', bk)])
                                fn = {1: AF.Tanh, 4: AF.Identity, 5: AF.Sigmoid}[n]
                                self.act(loT[:, cc * CH:(cc + 1) * CH], PS[bk][:, 0:CH], fn, [('ps', bk)], ['loT'])
                            if n == 5:
                                for t in range(ntile):
                                    bks = [self.bank(), self.bank()]
                                    for nh in range(2):
                                        self.mm(PS[bks[nh]][:, :], loT[:, t * 128:(t + 1) * 128], g2s[:, nh * 512:(nh + 1) * 512], True, True, ['loT', 'g2s'], [('ps', bks[nh])])
                                    stage_out(self.RG, roff + t * 128, [
                                        lambda o, r_, bk=bks[0]: self.cp('act', o, PS[bk][:, :], [('ps', bk)], [r_]),
                                        lambda o, r_, bk=bks[1]: self.cp('dve', o, PS[bk][:, :], [('ps', bk)], [r_])])
                            else:
                                l2, l2n = (w2s, 'w2s') if n == 1 else (a2s, 'a2s')
                                for d in range(2):
                                    self.dma('sp', bc0[:, d, :], (self.rw_w0 if n == 1 else self.rw_a0)[d:d + 1, :].to_broadcast([128, D]), (), ['bc0'])
                                for d in range(2):
                                    pr = slice(d * 64, (d + 1) * 64)
                                    dst = (self.RLW if n == 1 else self.RAI)[d]
                                    bci = d
                                    for t in range(ntile):
                                        bks = [self.bank(), self.bank()]
                                        for nh in range(2):
                                            self.mm(PS[bks[nh]][:, :], loT[pr, t * 128:(t + 1) * 128], l2[pr, nh * 512:(nh + 1) * 512], True, True, ['loT', l2n], [('ps', bks[nh])])

                                        def ev(o, r_, bk, nh, n=n, bci=bci):
                                            self.tt('dve', o, PS[bk][:, :], bc0[:, bci, nh * 512:(nh + 1) * 512], ALU.add, [('ps', bk), 'bc0'], [r_])
                                            self.act(o, o, AF.Sigmoid, [r_], [r_])
                                            if n == 1:
                                                self.ts('pool', o, o, -C64, None, ALU.mult, None, [r_], [r_])
                                        stage_out(dst, roff + t * 128, [
                                            lambda o, r_, bk=bks[0]: ev(o, r_, bk, 0),
                                            lambda o, r_, bk=bks[1]: ev(o, r_, bk, 1)])
                em.flush()
            with ExitStack() as s:
                E = s.enter_context
                bc1 = E(self.sbt("bc1", [128, 2, D], F32))
                self.dma('sp', bc1[:, 0, :], self.rw_k_k.to_broadcast([128, D]), (), ['bc1'])
                self.dma('sp', bc1[:, 1, :], self.rw_k_a.to_broadcast([128, D]), (), ['bc1'])
                kt = [E(self.sbt("ekt%d" % k, [128, D], F32)) for k in range(2)]
                at = [[E(self.sbt("eat%d_%d" % (d, k), [128, D], F32)) for k in range(2)] for d in range(2)]
                kk = [E(self.sbt("ekk%d" % k, [128, D], F32)) for k in range(2)]
                sq = [E(self.sbt("esq%d" % k, [128, D], F32)) for k in range(2)]
                o1 = [[E(self.sbt("eo1%d_%d" % (d, k), [128, D], F32)) for k in range(2)] for d in range(2)]
                o2 = [[E(self.sbt("eo2%d_%d" % (d, k), [128, D], F32)) for k in range(2)] for d in range(2)]
                ss = [E(self.sbt("ess%d" % k, [128, 16], F32)) for k in range(2)]
                for t in range(NT // 128):
                    p = t % 2
                    row = t * 128
                    RR_ = lambda nm: (nm, p)
                    self.dma('sp', kt[p][:], self.RK[row:row + 128, :], [('RK', t)], [RR_('ekt')])
                    for d in range(2):
                        self.dma('act', at[d][p][:], self.RAI[d][row:row + 128, :], [(self.RAI[d].tensor.name, t)], [RR_('eat%d' % d)])
                    self.tt('dve', kk[p][:], kt[p][:], bc1[:, 0, :], ALU.mult, [RR_('ekt'), 'bc1'], [RR_('ekk')])
                    self.act(sq[p][:], kk[p][:], AF.Square, [RR_('ekk')], [RR_('esq')])
                    self.em.op('dve', lambda e, p=p: e.tensor_reduce(out=ss[p][:], in_=R3(sq[p][:]), axis=AX.X, op=ALU.add), [RR_('esq')], [RR_('ess')])
                    self.ts('dve', ss[p][:], ss[p][:], 1e-24, None, ALU.max, None, [RR_('ess')], [RR_('ess')])
                    self.act(ss[p][:], ss[p][:], AF.Ln, [RR_('ess')], [RR_('ess')])
                    self.act(ss[p][:], ss[p][:], AF.Exp, [RR_('ess')], [RR_('ess')], scale=-0.5)
                    self.tt('dve', R3(kk[p][:]), R3(kk[p][:]), bc64(ss[p][:]), ALU.mult, [RR_('ekk'), RR_('ess')], [RR_('ekk')])
                    self.dma('sp', self.KKN[row:row + 128, :], kk[p][:], [RR_('ekk')], [('KKN', t)])
                    for d in range(2):
                        a_ = at[d][p]
                        self.stt('dve', o1[d][p][:], a_[:], -1.0, bc1[:, 1, :], ALU.add, ALU.mult, [RR_('eat%d' % d), 'bc1'], [RR_('eo1%d' % d)])
                        self.stt('dve', o1[d][p][:], o1[d][p][:], 1.0, kt[p][:], ALU.add, ALU.mult, [RR_('eo1%d' % d), RR_('ekt')], [RR_('eo1%d' % d)])
                        self.dma('sp', self.KD[d][row:row + 128, :], o1[d][p][:], [RR_('eo1%d' % d)], [(self.KD[d].tensor.name, t)])
                        self.tt('pool', o2[d][p][:], kk[p][:], a_[:], ALU.mult, [RR_('ekk'), RR_('eat%d' % d)], [RR_('eo2%d' % d)])
                        self.dma('sp', self.BV[d][row:row + 128, :], o2[d][p][:], [RR_('eo2%d' % d)], [(self.BV[d].tensor.name, t)])
                em.flush()
            with ExitStack() as s:
                E = s.enter_context
                msk = E(self.sbt("msk", [128, 4, 128], F32))
                bdm = E(self.sbt("bdm", [128, 128], F32))
                self.dma('sp', msk[:], self.masks.rearrange("m p t -> p m t"), (), ['msk'])
                self.dma('sp', bdm[:], self.bdmask, (), ['bdm'])
                MK4 = E(self.sbt("MK4", [128, 4, 128], F32))
                ST2 = E(self.sbt("ST2", [128, 8, 128], F32R))
                ld2 = [{nm: E(self.sbt("ld%d_" % k + nm, [128, D], F32)) for nm in ('lw', 'kd', 'bv', 'kkn', 'v', 'r')} for k in range(2)]
                nchunk = [0]
                cum = E(self.sbt("cum", [128, D], F32))
                tA = [E(self.sbt("tA%d" % k, [128, D], F32)) for k in range(2)]
                kb = E(self.sbt("kbar", [128, D], F32R))
                bb = E(self.sbt("bbar", [128, D], F32R))
                QT2 = E(self.sbt("QT2", [128, 8, 2, 128], F32R))
                khT = E(self.sbt("khT", [128, 8, 128], F32R))
                bhT = E(self.sbt("bhT", [128, 8, 128], F32R))
                AT3 = E(self.sbt("AT3", [128, 16, 3, 128], F32R))
                Lb = [E(self.sbt("Lb%d" % k, [128, 16, 128], F32R)) for k in range(2)]
                Ub = [E(self.sbt("Ub%d" % k, [128, 16, 128], F32R)) for k in range(2)]
                XT = E(self.sbt("XT", [128, 16, 128], F32R))
                rhs_sb = E(self.sbt("rhs_sb", [128, D], F32R))
                sa_sb = E(self.sbt("sa_sb", [128, D], F32R))
                vr = E(self.sbt("v_r", [128, D], F32R))
                y_sb = E(self.sbt("y_sb", [128, D], F32))
                gC = E(self.sbt("gC", [128, 8], F32))
                sttmp = E(self.sbt("sttmp", [128, 4, 128], F32))
                for d in range(2):
                    mS, mI, mL = (0, 1, 2) if d == 0 else (2, 3, 0)
                    for q_, mi_ in enumerate((mS, mI, mS, mI)):
                        self.cp('dve', MK4[:, q_, :], msk[:, mi_, :], ['msk'], ['MK4'])
                    self.ts('dve', ST2[:], self.ones[:, :].unsqueeze(1).to_broadcast([128, 8, 128]), 0.0, None, ALU.mult, None, ['ones'], ['ST2'])
                    order = [16, 17] + list(range(16)) if d == 0 else [17, 16] + list(range(15, -1, -1))
                    for ci in order:
                        row = ci * 128
                        emit = ci < 16
                        names = ['lw', 'kd', 'bv', 'kkn', 'v'] + (['r'] if emit else [])
                        srcs = {'lw': self.RLW[d], 'kd': self.KD[d], 'bv': self.BV[d], 'kkn': self.KKN, 'v': self.RV, 'r': self.RR}
                        pp = nchunk[0] % 2
                        nchunk[0] += 1
                        ld = ld2[pp]
                        for qi, nm in enumerate(names):
                            self.dma('sp', ld[nm][:], srcs[nm][row:row + 128, :], [(srcs[nm].tensor.name, ci)], [('ld', nm, pp)])
                        lw = ld['lw']
                        bks = [self.bank(), self.bank()]
                        for nh in range(2):
                            self.mm(PS[bks[nh]][:, :], msk[:, mI, :], lw[:, nh * 512:(nh + 1) * 512], True, True, ['msk', ('ld', 'lw', pp)], [('ps', bks[nh])])
                            self.cp('act' if nh == 0 else 'dve', cum[:, nh * 512:(nh + 1) * 512], PS[bks[nh]][:, :], [('ps', bks[nh])], ['cum'])
                        bt = [self.bank(), self.bank()]
                        for nh in range(2):
                            self.mm(PS[bt[nh]][:, :], self.ones[:], lw[:, nh * 512:(nh + 1) * 512], True, True, ['ones', ('ld', 'lw', pp)], [('ps', bt[nh])])
                        bg = self.bank()
                        for hp in range(8):
                            self.mm(PS[bg][:, hp:hp + 1], lw[:, hp * 128:(hp + 1) * 128], self.ones[:, 0:1], True, True, ['ones', ('ld', 'lw', pp)], [('ps', bg)])
                        self.act(gC[:], PS[bg][:, 0:8], AF.Exp, [('ps', bg)], ['gC'])
                        for nh in range(2):
                            sl = slice(nh * 512, (nh + 1) * 512)
                            self.tt('dve', tA[0][:, sl], PS[bt[nh]][:, :], cum[:, sl], ALU.subtract, [('ps', bt[nh]), 'cum'], [('tA', 0)])
                        self.act(tA[0][:], tA[0][:], AF.Exp, [('tA', 0)], [('tA', 0)])
                        self.tt('dve', kb[:], ld['kd'][:], tA[0][:], ALU.mult, [('ld', 'kd', pp), ('tA', 0)], ['kbar'])
                        self.tt('pool', bb[:], ld['bv'][:], tA[0][:], ALU.mult, [('ld', 'bv', pp), ('tA', 0)], ['bbar'])

                        def transp(src, srcr, dst_fn, dstr):
                            for g4 in range(2):
                                bk = self.bank()
                                for q4 in range(4):
                                    hp = g4 * 4 + q4
                                    self.tr(PS[bk][:, q4 * 128:(q4 + 1) * 128], src[:, hp * 128:(hp + 1) * 128], [srcr], [('ps', bk)])
                                dst_fn(g4, PS[bk][:, :].rearrange("p (q t) -> p q t", t=128), bk, dstr)
                        self.tt('dve', tA[0][:], cum[:], lw[:], ALU.subtract, ['cum', ('ld', 'lw', pp)], [('tA', 0)])
                        self.act(tA[0][:], tA[0][:], AF.Exp, [('tA', 0)], [('tA', 0)])
                        self.stt('dve', tA[0][:], ld['kkn'][:], -1.0, tA[0][:], ALU.mult, ALU.mult, [('ld', 'kkn', pp), ('tA', 0)], [('tA', 0)])
                        transp(tA[0], ('tA', 0), lambda g4, pv, bk, dr: self.cp('act', QT2[:, g4 * 4:(g4 + 1) * 4, 0, :], pv, [('ps', bk)], [dr]), 'QT2')
                        if emit:
                            self.act(tA[1][:], cum[:], AF.Exp, ['cum'], [('tA', 1)])
                            self.tt('dve', tA[1][:], tA[1][:], ld['r'][:], ALU.mult, [('tA', 1), ('ld', 'r', pp)], [('tA', 1)])
                            transp(tA[1], ('tA', 1), lambda g4, pv, bk, dr: self.cp('dve', QT2[:, g4 * 4:(g4 + 1) * 4, 1, :], pv, [('ps', bk)], [dr]), 'QT2')
                        self.act(tA[0][:], cum[:], AF.Exp, ['cum'], [('tA', 0)], scale=-1.0)
                        self.tt('dve', tA[1][:], tA[0][:], ld['kd'][:], ALU.mult, [('tA', 0), ('ld', 'kd', pp)], [('tA', 1)])
                        transp(tA[1], ('tA', 1), lambda g4, pv, bk, dr: self.cp('act', khT[:, g4 * 4:(g4 + 1) * 4, :], pv, [('ps', bk)], [dr]), 'khT')
                        self.tt('pool', tA[0][:], tA[0][:], ld['bv'][:], ALU.mult, [('tA', 0), ('ld', 'bv', pp)], [('tA', 0)])
                        transp(tA[0], ('tA', 0), lambda g4, pv, bk, dr: self.cp('dve', bhT[:, g4 * 4:(g4 + 1) * 4, :], pv, [('ps', bk)], [dr]), 'bhT')
                        NQ = 256 if emit else 128
                        for h in range(16):
                            hp, hh = h // 2, h % 2
                            pr = slice(hh * 64, (hh + 1) * 64)
                            bk = self.bank()
                            q2 = QT2[pr, hp, :, :].rearrange("p a t -> p (a t)")
                            self.mm(PS[bk][:, 0:NQ], khT[pr, hp, :], q2[:, 0:NQ], True, True, ['khT', 'QT2'], [('ps', bk)], sync=True)
                            self.mm(PS[bk][:, 256:256 + NQ], bhT[pr, hp, :], q2[:, 0:NQ], True, True, ['bhT', 'QT2'], [('ps', bk)], sync=True)
                            pv = PS[bk][:, :].rearrange("p (q t) -> p q t", t=128)
                            eng = 'dve'
                            if emit:
                                self.tt(eng, AT3[:, h, 0:2, :], pv[:, 0:2, :], MK4[:, 0:2, :], ALU.mult, [('ps', bk), 'MK4'], [('AT3', h)])
                                self.tt(eng, Ub[0][:, h, :], pv[:, 2, :], MK4[:, 2, :], ALU.mult, [('ps', bk), 'MK4'], [('Ub', 0, h // 4)])
                                self.tt(eng, AT3[:, h, 2, :], pv[:, 3, :], MK4[:, 3, :], ALU.mult, [('ps', bk), 'MK4'], [('AT3', h)])
                            else:
                                self.tt(eng, AT3[:, h, 0, :], pv[:, 0, :], MK4[:, 0, :], ALU.mult, [('ps', bk), 'MK4'], [('AT3', h)])
                                self.tt(eng, Ub[0][:, h, :], pv[:, 2, :], MK4[:, 2, :], ALU.mult, [('ps', bk), 'MK4'], [('Ub', 0, h // 4)])
                        for g4 in range(4):
                            bk = self.bank()
                            for q4 in range(4):
                                h = g4 * 4 + q4
                                hp, hh = h // 2, h % 2
                                pr = slice(hh * 64, (hh + 1) * 64)
                                self.mm(PS[bk][:, q4 * 128:(q4 + 1) * 128], QT2[pr, hp, 0, :], bhT[pr, hp, :], True, True, ['QT2', 'bhT'], [('ps', bk)], sync=True)
                            pv = PS[bk][:, :].rearrange("p (q t) -> p q t", t=128)
                            self.tt('dve', Lb[0][:, g4 * 4:(g4 + 1) * 4, :], pv, msk[:, mL, :].unsqueeze(1).to_broadcast([128, 4, 128]), ALU.mult,
                                    [('ps', bk), 'msk'], [('Lb', 0, g4)])
                            self.tt('pool', XT[:, g4 * 4:(g4 + 1) * 4, :], Ub[0][:, g4 * 4:(g4 + 1) * 4, :],
                                    self.ident[:, :].unsqueeze(1).to_broadcast([128, 4, 128]), ALU.add, [('Ub', 0, g4), 'ident'], [('XT', g4)])
                        cur = 0
                        for lev in range(6):
                            nxt = 1 - cur
                            last = lev == 5
                            bL = [self.bank() for _ in range(4)]
                            for q4 in range(4):
                                for g4 in range(4):
                                    h = g4 * 4 + q4
                                    self.mm(PS[bL[g4]][:, q4 * 128:(q4 + 1) * 128], Ub[cur][:, h, :], Lb[cur][:, h, :], True, True,
                                            [('Ub', cur, g4), ('Lb', cur, g4)], [('ps', bL[g4])], sync=True)
                            for g4 in range(4):
                                self.cp('act', Lb[nxt][:, g4 * 4:(g4 + 1) * 4, :], PS[bL[g4]][:, :].rearrange("p (q t) -> p q t", t=128),
                                        [('ps', bL[g4])], [('Lb', nxt, g4)])
                            if not last:
                                bU = [self.bank() for _ in range(4)]
                                for q4 in range(4):
                                    for g4 in range(4):
                                        h = g4 * 4 + q4
                                        self.mm(PS[bU[g4]][:, q4 * 128:(q4 + 1) * 128], Lb[cur][:, h, :], Ub[cur][:, h, :], True, True,
                                                [('Ub', cur, g4), ('Lb', cur, g4)], [('ps', bU[g4])], sync=True)
                                for g4 in range(4):
                                    self.cp('dve', Ub[nxt][:, g4 * 4:(g4 + 1) * 4, :], PS[bU[g4]][:, :].rearrange("p (q t) -> p q t", t=128),
                                            [('ps', bU[g4])], [('Ub', nxt, g4)])
                            bX = [self.bank() for _ in range(4)]
                            for q4 in range(4):
                                for g4 in range(4):
                                    h = g4 * 4 + q4
                                    self.mm(PS[bX[g4]][:, q4 * 128:(q4 + 1) * 128], Lb[nxt][:, h, :], XT[:, h, :], True, True,
                                            [('Lb', nxt, g4), ('XT', g4)], [('ps', bX[g4])], sync=True)
                            for g4 in range(4):
                                self.tt('dve', XT[:, g4 * 4:(g4 + 1) * 4, :], XT[:, g4 * 4:(g4 + 1) * 4, :],
                                        PS[bX[g4]][:, :].rearrange("p (q t) -> p q t", t=128), ALU.add,
                                        [('ps', bX[g4]), ('XT', g4)], [('XT', g4)])
                            cur = nxt
                        v_ = vr
                        self.cp('pool', vr[:], ld['v'][:], [('ld', 'v', pp)], ['vr'])
                        br = [self.bank(), self.bank()]
                        for hp in range(8):
                            bk = br[hp // 4]
                            c0 = (hp % 4) * 128
                            self.mm(PS[bk][:, c0:c0 + 128], QT2[:, hp, 0, :], ST2[:, hp, :], True, False, ['QT2', 'ST2'], [('ps', bk)], sync=True)
                            for hh in range(2):
                                h = hp * 2 + hh
                                self.mm(PS[bk][:, c0 + hh * 64:c0 + (hh + 1) * 64], AT3[:, h, 0, :], v_[:, h * 64:(h + 1) * 64], False, hh == 1,
                                        [('AT3', h), 'vr'], [('ps', bk)], sync=True)
                        self.cp('act', rhs_sb[:, 0:512], PS[br[0]][:, :], [('ps', br[0])], ['rhs_sb'])
                        self.cp('dve', rhs_sb[:, 512:1024], PS[br[1]][:, :], [('ps', br[1])], ['rhs_sb'])
                        bs_ = [self.bank(), self.bank()]
                        for h in range(16):
                            bk = bs_[h // 8]
                            c0 = (h % 8) * 64
                            self.mm(PS[bk][:, c0:c0 + 64], XT[:, h, :], rhs_sb[:, h * 64:(h + 1) * 64], True, True, [('XT', h // 4), 'rhs_sb'], [('ps', bk)], sync=True)
                        self.cp('act', sa_sb[:, 0:512], PS[bs_[0]][:, :], [('ps', bs_[0])], ['sa_sb'])
                        self.cp('dve', sa_sb[:, 512:1024], PS[bs_[1]][:, :], [('ps', bs_[1])], ['sa_sb'])
                        if emit:
                            by = [self.bank(), self.bank()]
                            for hp in range(8):
                                bk = by[hp // 4]
                                c0 = (hp % 4) * 128
                                self.mm(PS[bk][:, c0:c0 + 128], QT2[:, hp, 1, :], ST2[:, hp, :], True, False, ['QT2', 'ST2'], [('ps', bk)], sync=True)
                                for hh in range(2):
                                    h = hp * 2 + hh
                                    self.mm(PS[bk][:, c0 + hh * 64:c0 + (hh + 1) * 64], AT3[:, h, 1, :], v_[:, h * 64:(h + 1) * 64], False, False,
                                            [('AT3', h), 'vr'], [('ps', bk)], sync=True)
                                    self.mm(PS[bk][:, c0 + hh * 64:c0 + (hh + 1) * 64], AT3[:, h, 2, :], sa_sb[:, h * 64:(h + 1) * 64], False, hh == 1,
                                            [('AT3', h), 'sa_sb'], [('ps', bk)], sync=True)
                            self.cp('act', y_sb[:, 0:512], PS[by[0]][:, :], [('ps', by[0])], ['y_sb'])
                            self.cp('dve', y_sb[:, 512:1024], PS[by[1]][:, :], [('ps', by[1])], ['y_sb'])
                            self.dma('sp', self.YS[d][row:row + 128, :], y_sb[:], ['y_sb'], [(self.YS[d].tensor.name, ci)])
                        bst = [self.bank(), self.bank()]
                        for hp in range(8):
                            bk = bst[hp // 4]
                            c0 = (hp % 4) * 128
                            self.mm(PS[bk][:, c0:c0 + 128], kb[:, hp * 128:(hp + 1) * 128], v_[:, hp * 128:(hp + 1) * 128], True, False,
                                    ['kbar', 'vr'], [('ps', bk)], sync=True)
                            self.mm(PS[bk][:, c0:c0 + 128], bb[:, hp * 128:(hp + 1) * 128], sa_sb[:, hp * 128:(hp + 1) * 128], False, True,
                                    ['bbar', 'sa_sb'], [('ps', bk)], sync=True)
                        for g4 in range(2):
                            self.tt('dve', sttmp[:], PS[bst[g4]][:, :].rearrange("p (q t) -> p q t", t=128),
                                    bdm[:, :].unsqueeze(1).to_broadcast([128, 4, 128]), ALU.mult, [('ps', bst[g4]), 'bdm'], ['sttmp'])
                            self.tt('pool', ST2[:, g4 * 4:(g4 + 1) * 4, :], ST2[:, g4 * 4:(g4 + 1) * 4, :],
                                    gC[:, g4 * 4:(g4 + 1) * 4].unsqueeze(2).to_broadcast([128, 4, 128]), ALU.mult, ['ST2', 'gC'], ['ST2'])
                            self.tt('dve', ST2[:, g4 * 4:(g4 + 1) * 4, :], ST2[:, g4 * 4:(g4 + 1) * 4, :], sttmp[:], ALU.add, ['ST2', 'sttmp'], ['ST2'])
                em.flush()
            with ExitStack() as s:
                E = s.enter_context
                w_o = E(self.sbt("rw_wo", [128, KC, D], BF16))
                bc2 = E(self.sbt("bc2", [128, 3, D], F32))
                self.dma('sp', bc2[:, 0, :], self.rw_r_k.to_broadcast([128, D]), (), ['bc2'])
                self.dma('sp', bc2[:, 1, :], self.rw_gn_g.to_broadcast([128, D]), (), ['bc2'])
                self.dma('sp', bc2[:, 2, :], self.rw_gn_b.to_broadcast([128, D]), (), ['bc2'])
                src = self.rw_w_o.rearrange("(k p) n -> p k n", p=128)
                for hh in range(2):
                    self.dma('pool', w_o[:, :, hh * 512:(hh + 1) * 512], src[:, :, hh * 512:(hh + 1) * 512], (), [('w_o', hh)])
                tl = self.alloc_pn_tiles(E, True)
                L_ = {nm: E(self.sbt("d_" + nm, [128, D], F32)) for nm in ('y0', 'y1', 'r', 'kd0', 'kd1', 'v', 'g')}
                acc = E(self.sbt("d_acc", [128, D], F32))
                tq = E(self.sbt("d_tq", [128, D], F32))
                st16 = E(self.sbt("d_st", [128, 4, 16], F32))
                zT = E(self.sbt("d_zT", [128, KC, 128], BF16))
                for t in range(T // 128):
                    row = t * 128
                    srcs = {'y0': self.YS[0], 'y1': self.YS[1], 'r': self.RR, 'kd0': self.KD[0], 'kd1': self.KD[1], 'v': self.RV, 'g': self.RG}
                    for qi, nm in enumerate(L_):
                        self.dma('sp' if qi % 2 == 0 else 'act', L_[nm][:], srcs[nm][row:row + 128, :], [(srcs[nm].tensor.name, t)], [('dl', nm)])
                    for d in range(2):
                        y = L_['y%d' % d]
                        yr = ('dl', 'y%d' % d)
                        self.em.op('dve', lambda e, y=y: e.tensor_reduce(out=st16[:, 0, :], in_=R3(y[:]), axis=AX.X, op=ALU.add), [yr], ['st16'])
                        self.act(tq[:], y[:], AF.Square, [yr], ['tq'])
                        self.em.op('dve', lambda e: e.tensor_reduce(out=st16[:, 1, :], in_=R3(tq[:]), axis=AX.X, op=ALU.add), ['tq'], ['st16'])
                        self.ts('dve', st16[:, 0, :], st16[:, 0, :], 1.0 / 64, None, ALU.mult, None, ['st16'], ['st16'])
                        self.tt('dve', st16[:, 2, :], st16[:, 0, :], st16[:, 0, :], ALU.mult, ['st16'], ['st16'])
                        self.stt('dve', st16[:, 1, :], st16[:, 1, :], 1.0 / 64, st16[:, 2, :], ALU.mult, ALU.subtract, ['st16'], ['st16'])
                        self.act(st16[:, 1, :], st16[:, 1, :], AF.Ln, ['st16', 'epsc'], ['st16'], bias=self.epsc[:, 1:2])
                        self.act(st16[:, 1, :], st16[:, 1, :], AF.Exp, ['st16'], ['st16'], scale=-0.5)
                        self.tt('dve', R3(y[:]), R3(y[:]), bc64(st16[:, 0, :]), ALU.subtract, [yr, 'st16'], [yr])
                        self.tt('dve', R3(y[:]), R3(y[:]), bc64(st16[:, 1, :]), ALU.mult, [yr, 'st16'], [yr])
                        self.tt('pool', y[:], y[:], bc2[:, 1, :], ALU.mult, [yr, 'bc2'], [yr])
                        if d == 0:
                            self.tt('pool', acc[:], y[:], bc2[:, 2, :], ALU.add, [yr, 'bc2'], ['acc'])
                        else:
                            self.tt('pool', y[:], y[:], bc2[:, 2, :], ALU.add, [yr, 'bc2'], [yr])
                            self.tt('pool', acc[:], acc[:], y[:], ALU.add, [yr, 'acc'], ['acc'])
                        kd_ = L_['kd%d' % d]
                        kr_ = ('dl', 'kd%d' % d)
                        self.tt('dve', tq[:], L_['r'][:], kd_[:], ALU.mult, [('dl', 'r'), kr_], ['tq'])
                        self.tt('dve', tq[:], tq[:], bc2[:, 0, :], ALU.mult, ['tq', 'bc2'], ['tq'])
                        self.em.op('dve', lambda e: e.tensor_reduce(out=st16[:, 3, :], in_=R3(tq[:]), axis=AX.X, op=ALU.add), ['tq'], ['st16'])
                        self.tt('dve', R3(tq[:]), R3(L_['v'][:]), bc64(st16[:, 3, :]), ALU.mult, [('dl', 'v'), 'st16'], ['tq'])
                        self.tt('dve', acc[:], acc[:], tq[:], ALU.add, ['acc', 'tq'], ['acc'])
                    self.tt('dve', acc[:], acc[:], L_['g'][:], ALU.mult, ['acc', ('dl', 'g')], ['acc'])
                    for half in range(2):
                        bk = self.bank()
                        for kk_ in range(4):
                            k = half * 4 + kk_
                            self.tr(PS[bk][:, kk_ * 128:(kk_ + 1) * 128], acc[:, k * 128:(k + 1) * 128], ['acc'], [('ps', bk)])
                        self.cp('act' if half == 0 else 'dve', zT[:, half * 4:(half + 1) * 4, :], PS[bk][:, :].rearrange("p (q t) -> p q t", t=128),
                                [('ps', bk)], ['zT'])
                    bs = []
                    for nh in range(2):
                        bk = self.bank()
                        bs.append(bk)
                        for k in range(KC):
                            self.mm(PS[bk][:, :], zT[:, k, :], w_o[:, k, nh * 512:(nh + 1) * 512], k == 0, k == KC - 1, ['zT', ('w_o', nh)], [('ps', bk)])
                    self.postnorm_tile(tl, b * T + row, b, 0, hsrc, [PS[bs[0]][:, :], PS[bs[1]][:, :]], [[('ps', bs[0])], [('ps', bs[1])]],
                                       self.H, 'H', bias_bc=None, router=True)
                em.flush()


def tr128(v, nchunk):
    return np.ascontiguousarray(v.reshape(nchunk, 128).T)


def make_bm_tab(rpb):
    NEG = np.float32(-30000.0)
    out = np.full((16, 5, 2, 64, 10, 64), NEG, np.float32)
    rep = {0: 0, 1: 1, 2: 2, 3: 14, 4: 15}
    c = np.arange(64)
    kc = np.arange(64)
    win_c0 = np.clip(c - 8, 0, 48)
    col_ok = (kc[None, :] >= win_c0[:, None]) & (kc[None, :] < win_c0[:, None] + 16)
    cidx = np.clip(kc[None, :] - c[:, None] + 15, 0, 30)
    for cls, i_ in rep.items():
        ks = min(max(2 * i_ - 4, 0), 22)
        for rr in range(2):
            r = 2 * i_ + rr
            row0 = min(max(r - 4, 0), 24)
            for krel in range(10):
                kr = ks + krel
                if not (row0 <= kr < row0 + 8):
                    continue
                ridx = kr - r + 7
                vals = rpb[:, ridx, :][:, cidx]
                out[:, cls, rr, :, krel, :] = np.where(col_ok[None], vals, NEG)
    return np.ascontiguousarray(out.reshape(16, 5, 128, 640))


def prep_shared(inp):
    f = lambda a: np.ascontiguousarray(np.asarray(a, dtype=np.float32))
    sh = {}
    sh['ident'] = np.eye(128, dtype=np.float32)
    sh['ada_w'] = f(inp['ada_w'])
    sh['ada_b'] = f(inp['ada_b'])
    sh['ada_bT'] = np.stack([tr128(f(inp['ada_b'])[i], 48) for i in range(DEPTH)])
    sh['ln_g'] = f(inp['ln_g'])
    sh['ln_b'] = f(inp['ln_b'])
    sh['conv_w_in'] = f(inp['conv_w_in'])
    sh['conv_b_inT'] = np.stack([tr128(f(inp['conv_b_in'])[i], 16) for i in range(2)])
    wdw = f(inp['conv_w_dw'])
    sh['conv_w_dwT'] = np.ascontiguousarray(wdw.reshape(2, CONVW, KC, 128).transpose(0, 3, 2, 1))
    sh['conv_b_dwT'] = np.stack([tr128(f(inp['conv_b_dw'])[i], KC) for i in range(2)])
    sh['conv_ln_gT'] = np.stack([tr128(f(inp['conv_ln_g'])[i], KC) for i in range(2)])
    sh['conv_ln_bT'] = np.stack([tr128(f(inp['conv_ln_b'])[i], KC) for i in range(2)])
    sh['conv_w_out'] = f(inp['conv_w_out'])
    sh['conv_b_out'] = f(inp['conv_b_out'])
    sh['na_w_qkv'] = f(inp['na_w_qkv'])
    sh['na_w_o'] = f(inp['na_w_o'])
    sh['bm_tab'] = make_bm_tab(f(inp['na_rpb'])[0])
    mp = f(inp['rw_mu_prev'])[0]
    mn = f(inp['rw_mu_next'])[0]
    sh['rw_mu_prevT'] = np.ascontiguousarray(mp.reshape(6, KC, 128).transpose(2, 0, 1))
    sh['rw_mu_nextT'] = np.ascontiguousarray(mn.reshape(6, KC, 128).transpose(2, 0, 1))
    for nm in ("rw_w_r", "rw_w_k", "rw_w_v", "rw_w_o", "rw_w0", "rw_a0", "rw_w1", "rw_a1", "rw_w2", "rw_a2", "rw_g1", "rw_g2"):
        sh[nm] = np.ascontiguousarray(f(inp[nm])[0])
    for nm in ("rw_k_k", "rw_k_a", "rw_r_k", "rw_gn_g", "rw_gn_b"):
        sh[nm] = np.ascontiguousarray(f(inp[nm])[0].reshape(1, D))
    tri = np.triu(np.ones((128, 128), np.float32))
    sh['masks'] = np.stack([np.triu(tri, 1), tri, np.tril(tri.T, -1), tri.T]).astype(np.float32)
    bd = np.zeros((128, 128), np.float32)
    bd[:64, :64] = 1
    bd[64:, 64:] = 1
    sh['bdmask'] = bd
    sh['moe_router'] = f(inp['moe_router'])
    sh['moe_w_gate'] = f(inp['moe_w_gate'])
    sh['moe_w_up'] = f(inp['moe_w_up'])
    sh['moe_w_down'] = f(inp['moe_w_down'])
    return sh


def prep_core(inp, core):
    f = lambda a: np.asarray(a, dtype=np.float32)
    b0 = core * NB
    x = f(inp['x'])[b0:b0 + NB].reshape(NB * T, D)
    cx = f(inp['ctx'])[b0:b0 + NB].reshape(NB * L, D)
    hin = np.ascontiguousarray(np.concatenate([x, cx], axis=0))
    conds = np.stack([f(inp['c'])[b0], f(inp['c'])[b0 + 1], f(inp['c_ctx'])], axis=1)
    cT = np.ascontiguousarray(conds.reshape(KC, 128, 3).transpose(1, 0, 2))
    return {'hin': hin, 'cT': cT}


_NC_CACHE = {}


def kernel(**inputs):
    kb = KB()
    nc = kb.build()
    sh = prep_shared(inputs)
    in_maps = []
    for c in range(NCORES):
        m = dict(sh)
        m.update(prep_core(inputs, c))
        in_maps.append(m)
    res = run_bass_kernel_spmd(nc, in_maps, core_ids=list(range(NCORES)))
    outs = [r["out"].reshape(NB, T, D) for r in res.results]
    return np.concatenate(outs, axis=0).astype(np.float32)
```

```python
import numpy as np
import concourse.bass as bass
import concourse.mybir as mybir
from concourse.bass_utils import run_bass_kernel_spmd
from contextlib import ExitStack

F32 = mybir.dt.float32
F32R = mybir.dt.float32r
BF16 = mybir.dt.bfloat16
U32 = mybir.dt.uint32
I32 = mybir.dt.int32
ALU = mybir.AluOpType
AF = mybir.ActivationFunctionType
AX = mybir.AxisListType

ENGS = ('pe', 'act', 'dve', 'pool', 'sp')
DMAQ = ('sp', 'act', 'pool')


class Em:
    def __init__(self, nc, ndma=8):
        self.nc = nc
        self.K = ndma
        self.stack = ExitStack()
        ent = self.stack.enter_context
        self.csem = {e: ent(nc.semaphore("c_" + e)) for e in ENGS}
        self.dsem = {q: [ent(nc.semaphore("d_%s%d" % (q, i))) for i in range(ndma)] for q in DMAQ}
        self.ccnt = {e: 0 for e in ENGS}
        self.dcnt = {q: 0 for q in DMAQ}
        self.seen = {e: {} for e in ENGS}
        self.lastw = {}
        self.rd = {}
        self.prog = {e: [] for e in ENGS}
        self.ninstr = 0
        self.sigcnt = {}
        self.pe_sync_toks = set()

    def sem(self, key):
        if key[0] == 'c':
            return self.csem[key[1]]
        return self.dsem[key[1]][key[2]]

    def _deps(self, eng, reads, writes, pe_sync=False):
        need = {}
        pe_keep = pe_sync
        for r in reads:
            t = self.lastw.get(r)
            if t is not None and need.get(t[0], 0) < t[1]:
                need[t[0]] = t[1]
        for w in writes:
            t = self.lastw.get(w)
            if t is not None:
                if t in self.pe_sync_toks:
                    pe_keep = True
                if need.get(t[0], 0) < t[1]:
                    need[t[0]] = t[1]
            for k, v in self.rd.get(w, {}).items():
                if need.get(k, 0) < v:
                    need[k] = v
        waits = []
        seen = self.seen[eng]
        for k, v in need.items():
            if eng == 'pe' and k == ('c', 'pe') and not pe_keep:
                continue
            if seen.get(k, 0) < v:
                seen[k] = v
                waits.append((k, v))
        return waits

    def _reg(self, tok, reads, writes):
        for r in reads:
            self.rd.setdefault(r, {})[tok[0]] = tok[1]
        for w in writes:
            self.lastw[w] = tok
            self.rd[w] = {}

    def op(self, eng, fn, reads=(), writes=(), pe_sync=False):
        waits = self._deps(eng, reads, writes, pe_sync)
        self.ccnt[eng] += 1
        tok = (('c', eng), self.ccnt[eng])
        if pe_sync:
            self.pe_sync_toks.add(tok)
        self.prog[eng].append((waits, fn, tok, 1))
        self._reg(tok, reads, writes)
        self.ninstr += 1

    def dma(self, q, fn, reads=(), writes=()):
        waits = self._deps(q, reads, writes)
        n = self.dcnt[q]
        self.dcnt[q] += 1
        slot = n % self.K
        val = 16 * (n // self.K + 1)
        key = ('d', q, slot)
        if val > 16 and self.seen[q].get(key, 0) < val - 16:
            self.seen[q][key] = val - 16
            waits.append((key, val - 16))
        tok = (key, val)
        self.prog[q].append((waits, fn, tok, 16))
        self._reg(tok, reads, writes)
        self.ninstr += 1

    def barrier(self):
        for e in ENGS:
            waits = []
            seen = self.seen[e]
            for e2 in ENGS:
                k = ('c', e2)
                v = self.ccnt[e2]
                if e2 != e and v > 0 and seen.get(k, 0) < v:
                    seen[k] = v
                    waits.append((k, v))
            for q in DMAQ:
                n = self.dcnt[q]
                for slot in range(self.K):
                    cnt = (n - slot + self.K - 1) // self.K if n > slot else 0
                    if cnt > 0:
                        k = ('d', q, slot)
                        v = 16 * cnt
                        if seen.get(k, 0) < v:
                            seen[k] = v
                            waits.append((k, v))
            if waits:
                self.prog[e].append((waits, None, None, 0))
        self.lastw = {}
        self.rd = {}

    def flush(self):
        self.barrier()
        nc = self.nc
        waited = {e: set() for e in ENGS}
        for e in ENGS:
            for waits, fn, tok, inc in self.prog[e]:
                for k, v in waits:
                    if k[0] == 'c':
                        waited[k[1]].add(v)
        newval = {e: {} for e in ENGS}
        for e in ENGS:
            sc = self.sigcnt.get(e, 0)
            for waits, fn, tok, inc in self.prog[e]:
                if fn is not None and tok[0][0] == 'c' and tok[1] in waited[e]:
                    sc += 1
                    newval[e][tok[1]] = sc
            self.sigcnt[e] = sc

        def tr_(k, v):
            if k[0] == 'c':
                return newval[k[1]][v]
            return v
        with nc.Block() as block:
            for e, meth in (('pe', block.tensor), ('act', block.scalar), ('dve', block.vector),
                            ('pool', block.gpsimd), ('sp', block.sync)):
                prog = self.prog[e]
                self.prog[e] = []

                def body(eng, prog=prog, e=e):
                    for waits, fn, tok, inc in prog:
                        if fn is None:
                            for k, v in waits:
                                eng.wait_ge(self.sem(k), tr_(k, v))
                            continue
                        for k, v in waits[:-1]:
                            eng.wait_ge(self.sem(k), tr_(k, v))
                        ins = fn(eng)
                        if waits:
                            k, v = waits[-1]
                            ins._wait_ge(self.sem(k), tr_(k, v))
                        if tok[0][0] == 'c':
                            if tok[1] in newval[e]:
                                ins.then_inc(self.sem(tok[0]), 1)
                        else:
                            ins.then_inc(self.sem(tok[0]), inc)
                meth(body)

    def close(self):
        self.stack.close()


D = 1024
T = 2048
L = 256
NB = 2
KC = 8
NE = 16
CAP = 256
CAPC = 32
DE = 2048
DEPTH = 4
NROW = NB * T + NB * L
CTX0 = NB * T
ALPHA = float(8 ** 0.25)
LN_EPS = 1e-5
GRID_W = 64
CONVW = 31
NCORES = 8


class KB:
    def __init__(self, layers=(0, 1, 2, 3), n_exp=NE, debug=False):
        self.layers = list(layers)
        self.n_exp = n_exp
        self.debug = debug
        nc = bass.Bass("TRN2", target_bir_lowering=False)
        self.nc = nc
        self.em = Em(nc)
        self._pend_router = None
        _orig_flush = self.em.flush

        def _flush():
            if self._pend_router is not None:
                f = self._pend_router
                self._pend_router = None
                f()
            _orig_flush()
        self.em.flush = _flush
        self.dr = {}
        self._psi = 0

    def din(self, name, shape, dt=F32):
        t = self.nc.dram_tensor(name, list(shape), dt, kind="ExternalInput").ap()
        self.dr[name] = t
        return t

    def dout(self, name, shape, dt=F32):
        t = self.nc.dram_tensor(name, list(shape), dt, kind="ExternalOutput").ap()
        self.dr[name] = t
        return t

    def dint(self, name, shape, dt=F32):
        t = self.nc.dram_tensor(name, list(shape), dt, kind="ExternalOutput" if self.debug else "Internal").ap()
        self.dr[name] = t
        return t

    def sbt(self, name, shape, dt):
        self._uid = getattr(self, '_uid', 0) + 1
        return self.nc.sbuf_tensor("%s_%d" % (name, self._uid), shape, dt)

    def bank(self):
        i = self._psi
        self._psi = (i + 1) % 8
        return i

    def mm(self, out, lhsT, rhs, start, stop, r, w, sync=None):
        ps = (lhsT.dtype == F32) if sync is None else sync
        self.em.op('pe', lambda e: e.matmul(out, lhsT=lhsT, rhs=rhs, start=start, stop=stop), r, w, pe_sync=ps)

    def tr(self, out, in_, r, w, ident=None):
        idn = self.ident if ident is None else ident
        np_ = in_.shape[0]
        self.em.op('pe', lambda e: e.transpose(out=out, in_=in_, identity=idn[0:np_, 0:np_]), list(r) + ['ident'], w)

    def act(self, out, in_, func, r, w, bias=None, scale=None, accum_out=None):
        kw = {}
        if bias is not None:
            kw['bias'] = bias
        if scale is not None:
            kw['scale'] = scale
        if accum_out is not None:
            kw['accum_out'] = accum_out
        self.em.op('act', lambda e: e.activation(out=out, in_=in_, func=func, **kw), r, w)

    def tt(self, eng, out, in0, in1, op, r, w):
        self.em.op(eng, lambda e: e.tensor_tensor(out=out, in0=in0, in1=in1, op=op), r, w)

    def ts(self, eng, out, in0, s1, s2, op0, op1, r, w):
        if op1 is None:
            self.em.op(eng, lambda e: e.tensor_scalar(out=out, in0=in0, scalar1=s1, scalar2=None, op0=op0), r, w)
        else:
            self.em.op(eng, lambda e: e.tensor_scalar(out=out, in0=in0, scalar1=s1, scalar2=s2, op0=op0, op1=op1), r, w)

    def stt(self, eng, out, in0, scalar, in1, op0, op1, r, w):
        self.em.op(eng, lambda e: e.scalar_tensor_tensor(out=out, in0=in0, scalar=scalar, in1=in1, op0=op0, op1=op1), r, w)

    def cp(self, eng, out, in_, r, w):
        if eng == 'act':
            self.em.op('act', lambda e: e.copy(out=out, in_=in_), r, w)
        else:
            self.em.op(eng, lambda e: e.tensor_copy(out=out, in_=in_), r, w)

    def ms(self, eng, ap, val, w):
        self.em.op(eng, lambda e: e.memset(ap, val), (), w)

    def dma(self, q, out, in_, r, w, **kw):
        self.em.dma(q, lambda e: e.dma_start(out=out, in_=in_, **kw), r, w)

    def build(self):
        nc = self.nc
        em = self.em
        st = ExitStack()
        self.st = st
        E = st.enter_context
        self.hin = self.din("hin", [NROW, D])
        self.cT = self.din("cT", [128, KC, 3])
        self.ident_d = self.din("ident", [128, 128])
        self.ada_w = self.din("ada_w", [DEPTH, D, 6 * D])
        self.ada_bT = self.din("ada_bT", [DEPTH, 128, 48])
        self.ada_b = self.din("ada_b", [DEPTH, 6 * D])
        self.ln_g = self.din("ln_g", [DEPTH, 2, D])
        self.ln_b = self.din("ln_b", [DEPTH, 2, D])
        self.conv_w_in = self.din("conv_w_in", [2, D, 2 * D])
        self.conv_b_inT = self.din("conv_b_inT", [2, 128, 16])
        self.conv_w_dwT = self.din("conv_w_dwT", [2, 128, KC, CONVW])
        self.conv_b_dwT = self.din("conv_b_dwT", [2, 128, KC])
        self.conv_ln_gT = self.din("conv_ln_gT", [2, 128, KC])
        self.conv_ln_bT = self.din("conv_ln_bT", [2, 128, KC])
        self.conv_w_out = self.din("conv_w_out", [2, D, D])
        self.conv_b_out = self.din("conv_b_out", [2, D])
        self.moe_router = self.din("moe_router", [DEPTH, D, NE])
        self.moe_w_gate = self.din("moe_w_gate", [DEPTH, NE, D, DE])
        self.moe_w_up = self.din("moe_w_up", [DEPTH, NE, D, DE])
        self.moe_w_down = self.din("moe_w_down", [DEPTH, NE, DE, D])
        self.out = self.dout("out", [NB * T, D])
        self.H = self.dint("H", [NROW, D])
        self.Y = self.dint("Y", [NROW, D])
        self.extra_io()

        self.ident = E(self.sbt("ident_s", [128, 128], F32))
        self.ones = E(self.sbt("ones_s", [128, 128], F32))
        self.scT = E(self.sbt("scT", [128, KC, 3], BF16))
        self.modT = E(self.sbt("modT", [128, 32, 3], F32))
        self.ga_bc = E(self.sbt("ga_bc", [128, 2, 3, D], BF16))
        self.lnp = E(self.sbt("lnp", [128, 2, D], F32))
        self.affT = E(self.sbt("affT", [48, T], F32))
        self.affTc = E(self.sbt("affTc", [16, NB * L], F32))
        self.wr = E(self.sbt("wr", [128, KC, NE], F32))
        self.PS = [E(nc.psum_tensor("ps%d" % i, [128, 512], F32)) for i in range(8)]

        self.dma('sp', self.ident[:], self.ident_d, (), ['ident'])
        self.ms('dve', self.ones[:], 1.0, ['ones'])
        self.epsc = E(self.sbt("epsc", [128, 2], F32))
        self.ms('dve', self.epsc[:, 0:1], LN_EPS, ['epsc'])
        self.ms('dve', self.epsc[:, 1:2], 64e-5, ['epsc'])
        self.ms('dve', self.affT[:], 0.0, ['affT'])
        self.ms('dve', self.affTc[:], 0.0, ['affTc'])
        with ExitStack() as s2:
            cTs = s2.enter_context(self.sbt("cTs", [128, KC, 3], F32))
            self.dma('sp', cTs[:], self.cT, (), ['cTs'])
            self.act(self.scT[:], cTs[:], AF.Silu, ['cTs'], ['scT'])
            em.flush()

        for i in self.layers:
            kind, jj = i % 3, i // 3
            ctx_read = i <= 2
            ctx_live = i < 2
            self.ada_phase(i)
            if kind == 0:
                self.conv_phase(i, jj, ctx_live)
            elif kind == 1:
                self.na_phase(i, jj, ctx_live)
            else:
                self.rwkv_phase(i, jj)
            self.moe_phase(i, ctx_live, last=(i == self.layers[-1]))
        em.flush()
        st.close()
        em.close()
        return nc

    def extra_io(self):
        self.na_w_qkv = self.din("na_w_qkv", [1, D, 3 * D])
        self.na_w_o = self.din("na_w_o", [1, D, D])
        self.bm_tab = self.din("bm_tab", [16, 5, 128, 640])
        self.V_d = self.dint("V_d", [T + L, D], BF16)
        NT = T + L
        self.rw_mu_prevT = self.din("rw_mu_prevT", [128, 6, KC])
        self.rw_mu_nextT = self.din("rw_mu_nextT", [128, 6, KC])
        for nm in ("rw_w_r", "rw_w_k", "rw_w_v", "rw_w_o"):
            setattr(self, nm, self.din(nm, [D, D]))
        self.rw_w0 = self.din("rw_w0", [2, D])
        self.rw_a0 = self.din("rw_a0", [2, D])
        self.rw_w1 = self.din("rw_w1", [2, D, 64])
        self.rw_a1 = self.din("rw_a1", [2, D, 64])
        self.rw_w2 = self.din("rw_w2", [2, 64, D])
        self.rw_a2 = self.din("rw_a2", [2, 64, D])
        self.rw_g1 = self.din("rw_g1", [D, 128])
        self.rw_g2 = self.din("rw_g2", [128, D])
        for nm in ("rw_k_k", "rw_k_a", "rw_r_k", "rw_gn_g", "rw_gn_b"):
            setattr(self, nm, self.din(nm, [1, D]))
        self.masks = self.din("masks", [4, 128, 128])
        self.bdmask = self.din("bdmask", [128, 128])
        self.RK = self.dint("RK", [NT, D])
        self.RV = self.dint("RV", [NT, D])
        self.RR = self.dint("RR", [NT, D])
        self.RG = self.dint("RG", [NT, D])
        self.KKN = self.dint("KKN", [NT, D])
        self.RLW = [self.dint("RLW%d" % d, [NT, D]) for d in range(2)]
        self.RAI = [self.dint("RAI%d" % d, [NT, D]) for d in range(2)]
        self.KD = [self.dint("KD%d" % d, [NT, D]) for d in range(2)]
        self.BV = [self.dint("BV%d" % d, [NT, D]) for d in range(2)]
        self.YS = [self.dint("YS%d" % d, [T, D]) for d in range(2)]

    def load_lnp(self, i, g):
        for r_, src in ((0, self.ln_g), (1, self.ln_b)):
            self.dma('sp', self.lnp[:, r_, :], src[i, g:g + 1, :].to_broadcast([128, D]), (), [('lnp', r_)])

    def hsrc(self, i):
        return self.hin if i == self.layers[0] else self.H

    def ada_phase(self, i):
        nc = self.nc
        PS = self.PS
        with ExitStack() as s:
            E = s.enter_context
            wA = [E(self.sbt("wA%d" % k, [128, KC, D], BF16)) for k in range(2)]
            abT = E(self.sbt("abT", [128, 48], F32))
            gb = E(self.sbt("gb", [128, 2, D], F32))
            self.scR = E(self.sbt("scR", [128, KC, 3, 128], BF16))
            for j in range(3):
                self.cp('dve', self.scR[:, :, j, :], self.scT[:, :, j:j + 1].to_broadcast([128, KC, 128]), ['scT'], ['scR'])
            self.dma('sp', abT[:], self.ada_bT[i], (), ['abT'])
            for g, cb in enumerate((2, 5)):
                self.dma('sp', gb[:, g, :], self.ada_b[i:i + 1, cb * D:(cb + 1) * D].to_broadcast([128, D]), (), [('gb', g)])
            self.load_lnp(i, 0)
            self.dma('sp', self.wr[:], self.moe_router[i].rearrange("(k p) e -> p k e", p=128), (), ['wr'])
            for cb in range(6):
                wb = wA[cb % 2]
                src = self.ada_w[i][:, cb * D:(cb + 1) * D].rearrange("(k p) n -> p k n", p=128)
                for hh in range(2):
                    self.dma('pool', wb[:, :, hh * 512:(hh + 1) * 512], src[:, :, hh * 512:(hh + 1) * 512], (), [('wA', cb % 2, hh)])
                rw = [('wA', cb % 2, 0), ('wA', cb % 2, 1)]
                if cb in (0, 1, 3, 4):
                    v = (0, 1, None, 2, 3)[cb]
                    b = self.bank()
                    for fc in range(KC):
                        for k in range(KC):
                            self.mm(PS[b][:, fc * 3:fc * 3 + 3], wb[:, k, fc * 128:(fc + 1) * 128], self.scT[:, k, :],
                                    k == 0, k == KC - 1, rw + ['scT'], [('ps', b)])
                    pv = PS[b][:, 0:24].rearrange("p (f j) -> p f j", j=3)
                    self.tt('dve', self.modT[:, v * 8:(v + 1) * 8, :], pv,
                            abT[:, cb * 8:(cb + 1) * 8].unsqueeze(2).to_broadcast([128, 8, 3]), ALU.add,
                            [('ps', b), 'abT'], [('modT', v)])
                    if cb in (1, 4):
                        self.ts('dve', self.modT[:, v * 8:(v + 1) * 8, :], self.modT[:, v * 8:(v + 1) * 8, :], 1.0, None, ALU.add, None,
                                [('modT', v)], [('modT', v)])
                else:
                    g = 0 if cb == 2 else 1
                    for j in range(3):
                        for nh in range(2):
                            b = self.bank()
                            for k in range(KC):
                                self.mm(PS[b][:, :], self.scR[:, k, j, :], wb[:, k, nh * 512:(nh + 1) * 512],
                                        k == 0, k == KC - 1, rw + ['scR'], [('ps', b)])
                            self.tt('dve', self.ga_bc[:, g, j, nh * 512:(nh + 1) * 512], PS[b][:, :], gb[:, g, nh * 512:(nh + 1) * 512], ALU.add,
                                    [('ps', b), ('gb', g)], [('ga', g, j)])
            self.em.flush()

    def streams(self, ctx):
        s = [(0, T, 0, 0), (T, T, 1, 1)]
        if ctx:
            s += [(CTX0, L, 2, 2), (CTX0 + L, L, 2, 3)]
        return s

    def load_modT_to(self, i):
        pass

    def postnorm_tile(self, tl, row0, cond, g, hsrc, ysrc, yr, dst, dst_res, bias_bc=None, router=None):
        PS = self.PS
        ht, t1, t2, st6, mv, sm = tl['ht'], tl['t1'], tl['t2'], tl['st6'], tl['mv'], tl['sm']
        n = tl['n']
        tl['n'] += 1
        p = n % 2
        ht, t1 = ht[p], t1[p]
        R = lambda nm: (nm, p)
        self.dma('sp', ht[:], hsrc[row0:row0 + 128, :], [('H', row0 // 128)] if hsrc is self.H else [], [R('ht')])
        for nh in range(2):
            sl = slice(nh * 512, (nh + 1) * 512)
            if bias_bc is not None:
                self.tt('dve', t1[:, sl], ysrc[nh], bias_bc[:, sl], ALU.add, list(yr[nh]) + ['bias_bc'], [R('t1')])
                self.tt('pool', t1[:, sl], t1[:, sl], self.ga_bc[:, g, cond, sl], ALU.mult, [R('t1'), ('ga', g, cond)], [R('t1')])
            else:
                self.tt('dve', t1[:, sl], ysrc[nh], self.ga_bc[:, g, cond, sl], ALU.mult, list(yr[nh]) + [('ga', g, cond)], [R('t1')])
        self.stt('dve', t1[:], ht[:], ALPHA, t1[:], ALU.mult, ALU.add, [R('ht'), R('t1')], [R('t1')])
        for c in range(2):
            self.em.op('dve', lambda e, c=c: e.bn_stats(out=st6[p][:, c, :], in_=t1[:, c * 512:(c + 1) * 512]), [R('t1')], [R('st6')])
        self.em.op('dve', lambda e: e.bn_aggr(out=mv[p][:], in_=st6[p][:]), [R('st6')], [R('mv')])
        self.act(sm[p][:, 0:1], mv[p][:, 1:2], AF.Ln, [R('mv')], [R('sm')], bias=self.epsc[:, 0:1])
        self.act(sm[p][:, 0:1], sm[p][:, 0:1], AF.Exp, [R('sm')], [R('sm')], scale=-0.5)
        self.stt('dve', sm[p][:, 1:2], mv[p][:, 0:1], -1.0, sm[p][:, 0:1], ALU.mult, ALU.mult, [R('mv'), R('sm')], [R('sm')])
        self.act(t1[:], t1[:], AF.Identity, [R('t1'), R('sm')], [R('t1')], bias=sm[p][:, 1:2], scale=sm[p][:, 0:1])
        self.tt('pool', t1[:], t1[:], self.lnp[:, 0, :], ALU.mult, [R('t1'), ('lnp', 0)], [R('t1')])
        self.tt('dve', t1[:], t1[:], self.lnp[:, 1, :], ALU.add, [R('t1'), ('lnp', 1)], [R('t1')])
        self.dma('sp', dst[row0:row0 + 128, :] if dst is not self.out else dst[row0:row0 + 128, :], t1[:], [R('t1')], [(dst_res, row0 // 128)])
        prev = self._pend_router
        self._pend_router = None
        if prev is not None:
            prev()
        if router is not None:
            self._pend_router = (lambda t1=t1, rk=R('t1'), row0=row0, cond=cond, p=p: self.router_tile(tl, t1, rk, row0, cond, p))

    def router_tile(self, tl, hnew, hres, row0, cond, p):
        PS = self.PS
        hmT = tl['hmT'][p]
        lg = tl['lg'][p]
        R = lambda nm: (nm, p)
        for half in range(2):
            b = self.bank()
            for kk in range(4):
                k = half * 4 + kk
                self.tr(PS[b][:, kk * 128:(kk + 1) * 128], hnew[:, k * 128:(k + 1) * 128], [hres], [('ps', b)])
            for kk in range(4):
                k = half * 4 + kk
                eng = 'act' if kk % 2 == 0 else 'dve'
                if eng == 'act':
                    self.act(hmT[:, k, :], PS[b][:, kk * 128:(kk + 1) * 128], AF.Identity, [('ps', b), ('modT', 2), ('modT', 3)], [R('hmT')],
                             bias=self.modT[:, 16 + k, cond:cond + 1], scale=self.modT[:, 24 + k, cond:cond + 1])
                else:
                    self.ts('dve', hmT[:, k, :], PS[b][:, kk * 128:(kk + 1) * 128], self.modT[:, 24 + k, cond:cond + 1],
                            self.modT[:, 16 + k, cond:cond + 1], ALU.mult, ALU.add, [('ps', b), ('modT', 2), ('modT', 3)], [R('hmT')])
        b = self.bank()
        for k in range(KC):
            self.mm(PS[b][:, 0:NE], hmT[:, k, :], self.wr[:, k, :], k == 0, k == KC - 1, [R('hmT'), 'wr'], [('ps', b)])
        sm = tl['sm2'][p]
        self.em.op('dve', lambda e: e.tensor_reduce(out=sm[:, 0:1], in_=PS[b][:, 0:NE], axis=AX.X, op=ALU.max, negate=True), [('ps', b)], [R('sm2')])
        is_ctx = row0 >= CTX0
        bsel = (row0 - CTX0) // L if is_ctx else row0 // T
        c0 = 32 if (bsel == 1 and not is_ctx) else 0
        self.act(lg[:, c0:c0 + NE], PS[b][:, 0:NE], AF.Exp, [('ps', b), R('sm2')], [R('lg')], bias=sm[:, 0:1], accum_out=sm[:, 1:2])
        self.em.op('dve', lambda e: e.reciprocal(out=sm[:, 2:3], in_=sm[:, 1:2]), [R('sm2')], [R('sm2')])
        self.ts('dve', lg[:, c0:c0 + NE], lg[:, c0:c0 + NE], sm[:, 2:3], None, ALU.mult, None, [R('lg'), R('sm2')], [R('lg')])
        b2 = self.bank()
        npart = c0 + NE
        self.tr(PS[b2][0:npart, 0:128], lg[:, 0:npart], [R('lg')], [('ps', b2)])
        if is_ctx:
            col = (row0 - CTX0)
            self.cp('act', self.affTc[0:16, col:col + 128], PS[b2][0:16, 0:128], [('ps', b2)], ['affTc'])
        else:
            col = row0 - bsel * T
            self.cp('act', self.affT[c0:c0 + 16, col:col + 128], PS[b2][c0:c0 + 16, 0:128], [('ps', b2)], ['affT'])

    def alloc_pn_tiles(self, E, router):
        nc = self.nc
        tl = {'n': 0}
        tl['ht'] = [E(self.sbt("pn_ht%d" % k, [128, D], F32)) for k in range(2)]
        tl['t1'] = [E(self.sbt("pn_t1%d" % k, [128, D], F32)) for k in range(2)]
        tl['t2'] = None
        tl['st6'] = [E(self.sbt("pn_st%d" % k, [128, 2, 6], F32)) for k in range(2)]
        tl['mv'] = [E(self.sbt("pn_mv%d" % k, [128, 2], F32)) for k in range(2)]
        tl['sm'] = [E(self.sbt("pn_sm%d" % k, [128, 2], F32)) for k in range(2)]
        if router:
            tl['hmT'] = [E(self.sbt("pn_hmT%d" % k, [128, KC, 128], F32)) for k in range(2)]
            tl['lg'] = [E(self.sbt("pn_lg%d" % k, [128, 48], F32)) for k in range(2)]
            tl['sm2'] = [E(self.sbt("pn_sm2%d" % k, [128, 4], F32)) for k in range(2)]
            for k in range(2):
                self.ms('dve', tl['lg'][k][:], 0.0, [('lg', k)])
        return tl

    def conv_phase(self, i, jj, ctx_live):
        nc = self.nc
        PS = self.PS
        em = self.em
        hsrc = self.hsrc(i)
        with ExitStack() as so:
            convout = so.enter_context(self.sbt("convout", [128, KC, T], F32))
            vec = so.enter_context(self.sbt("cvec", [128, 16 + KC * CONVW + 3 * KC], F32))
            b_in = vec[:, 0:16]
            w_dw = vec[:, 16:16 + KC * CONVW].rearrange("p (k j) -> p k j", j=CONVW)
            o = 16 + KC * CONVW
            b_dw = vec[:, o:o + KC]
            cg = vec[:, o + KC:o + 2 * KC]
            cb_ = vec[:, o + 2 * KC:o + 3 * KC]
            self.dma('sp', b_in, self.conv_b_inT[jj], (), ['cvec'])
            self.dma('sp', w_dw, self.conv_w_dwT[jj], (), ['cvec'])
            self.dma('sp', b_dw, self.conv_b_dwT[jj], (), ['cvec'])
            self.dma('sp', cg, self.conv_ln_gT[jj], (), ['cvec'])
            self.dma('sp', cb_, self.conv_ln_bT[jj], (), ['cvec'])
            for (row0, nt, cond, sid) in self.streams(ctx_live):
                ntile = nt // 128
                CH = 512 if nt >= 512 else nt
                nch = nt // CH
                with ExitStack() as s:
                    E = s.enter_context
                    uT = E(self.sbt("uT", [128, KC, nt], BF16))
                    w_in = E(self.sbt("w_in", [128, KC, 2 * D], BF16))
                    xpad = [E(self.sbt("xpad%d" % k, [128, nt + 30], F32)) for k in range(2)]
                    sg = [E(self.sbt("sg%d" % k, [128, 512], F32)) for k in range(2)]
                    ht = [E(self.sbt("cht%d" % k, [128, D], F32)) for k in range(2)]
                    src = self.conv_w_in[jj].rearrange("(k p) n -> p k n", p=128)
                    for hh in range(4):
                        self.dma('pool', w_in[:, :, hh * 512:(hh + 1) * 512], src[:, :, hh * 512:(hh + 1) * 512], (), [('w_in', hh)])
                    for k in range(2):
                        self.ms('pool', xpad[k][:, 0:15], 0.0, [('xpad', k)])
                        self.ms('pool', xpad[k][:, nt + 15:nt + 30], 0.0, [('xpad', k)])
                    for tt_ in range(ntile):
                        p = tt_ % 2
                        r0 = row0 + tt_ * 128
                        self.dma('sp', ht[p][:], hsrc[r0:r0 + 128, :], [('H', r0 // 128)] if hsrc is self.H else [], [('cht', p)])
                        for half in range(2):
                            b = self.bank()
                            for kk in range(4):
                                k = half * 4 + kk
                                self.tr(PS[b][:, kk * 128:(kk + 1) * 128], ht[p][:, k * 128:(k + 1) * 128], [('cht', p)], [('ps', b)])
                            for kk in range(4):
                                k = half * 4 + kk
                                if kk % 2 == 0:
                                    self.act(uT[:, k, tt_ * 128:(tt_ + 1) * 128], PS[b][:, kk * 128:(kk + 1) * 128], AF.Identity,
                                             [('ps', b), ('modT', 0), ('modT', 1)], [('uT', tt_ * 128 // CH)],
                                             bias=self.modT[:, k, cond:cond + 1], scale=self.modT[:, 8 + k, cond:cond + 1])
                                else:
                                    self.ts('dve', uT[:, k, tt_ * 128:(tt_ + 1) * 128], PS[b][:, kk * 128:(kk + 1) * 128],
                                            self.modT[:, 8 + k, cond:cond + 1], self.modT[:, k, cond:cond + 1], ALU.mult, ALU.add,
                                            [('ps', b), ('modT', 0), ('modT', 1)], [('uT', tt_ * 128 // CH)])
                    for j in range(KC):
                        xp = xpad[j % 2]
                        for cc in range(nch):
                            tsl = slice(cc * CH, (cc + 1) * CH)
                            bg = self.bank()
                            for k in range(KC):
                                self.mm(PS[bg][:, 0:CH], w_in[:, k, (8 + j) * 128:(9 + j) * 128], uT[:, k, tsl], k == 0, k == KC - 1,
                                        [('w_in', (8 + j) // 4), ('uT', cc)], [('ps', bg)])
                            bv = self.bank()
                            for k in range(KC):
                                self.mm(PS[bv][:, 0:CH], w_in[:, k, j * 128:(j + 1) * 128], uT[:, k, tsl], k == 0, k == KC - 1,
                                        [('w_in', j // 4), ('uT', cc)], [('ps', bv)])
                            q = cc % 2
                            self.act(sg[q][:, 0:CH], PS[bg][:, 0:CH], AF.Sigmoid, [('ps', bg), 'cvec'], [('sg', q)], bias=b_in[:, 8 + j:9 + j])
                            self.stt('dve', xp[:, 15 + cc * CH:15 + (cc + 1) * CH], PS[bv][:, 0:CH], b_in[:, j:j + 1], sg[q][:, 0:CH],
                                     ALU.add, ALU.mult, [('ps', bv), ('sg', q), 'cvec'], [('xpad', j % 2)])
                        acc = convout[:, j, 0:nt]
                        self.ts('dve', acc, xp[:, 0:nt], w_dw[:, j, 0:1], b_dw[:, j:j + 1], ALU.mult, ALU.add,
                                [('xpad', j % 2), 'cvec'], [('convout', j)])
                        for tap in range(1, CONVW):
                            self.stt('dve', acc, xp[:, tap:tap + nt], w_dw[:, j, tap:tap + 1], acc, ALU.mult, ALU.add,
                                     [('xpad', j % 2), ('convout', j), 'cvec'], [('convout', j)])
                    em.flush()
                with ExitStack() as s:
                    E = s.enter_context
                    sT = E(self.sbt("sT", [128, KC, nt], BF16))
                    w_out = E(self.sbt("w_out", [128, KC, D], BF16))
                    bo_bc = E(self.sbt("bo_bc", [128, D], F32))
                    sq = [E(self.sbt("sq%d" % k, [128, 512], F32)) for k in range(2)]
                    mean = E(self.sbt("cmean", [128, 512], F32))
                    rstd = E(self.sbt("crstd", [128, 512], F32))
                    xn = [E(self.sbt("cxn%d" % k, [128, 512], F32)) for k in range(2)]
                    tl = self.alloc_pn_tiles(E, True)
                    src = self.conv_w_out[jj].rearrange("(k p) n -> p k n", p=128)
                    for hh in range(2):
                        self.dma('pool', w_out[:, :, hh * 512:(hh + 1) * 512], src[:, :, hh * 512:(hh + 1) * 512], (), [('w_out', hh)])
                    self.dma('sp', bo_bc[:], self.conv_b_out[jj:jj + 1, :].to_broadcast([128, D]), (), ['bias_bc'])
                    for cc in range(nch):
                        tsl = slice(cc * CH, (cc + 1) * CH)
                        bm = self.bank()
                        for k in range(KC):
                            self.mm(PS[bm][:, 0:CH], self.ones[:], convout[:, k, tsl], k == 0, k == KC - 1, ['ones', ('convout', k)], [('ps', bm)])
                        bq = self.bank()
                        for k in range(KC):
                            q = k % 2
                            self.act(sq[q][:, 0:CH], convout[:, k, tsl], AF.Square, [('convout', k)], [('sq', q)])
                            self.mm(PS[bq][:, 0:CH], self.ones[:], sq[q][:, 0:CH], k == 0, k == KC - 1, ['ones', ('sq', q)], [('ps', bq)])
                        self.act(mean[:, 0:CH], PS[bm][:, 0:CH], AF.Identity, [('ps', bm)], ['cmean'], scale=1.0 / D)
                        self.tt('dve', rstd[:, 0:CH], mean[:, 0:CH], mean[:, 0:CH], ALU.mult, ['cmean'], ['crstd'])
                        self.stt('dve', rstd[:, 0:CH], PS[bq][:, 0:CH], 1.0 / D, rstd[:, 0:CH], ALU.mult, ALU.subtract, [('ps', bq), 'crstd'], ['crstd'])
                        self.act(rstd[:, 0:CH], rstd[:, 0:CH], AF.Ln, ['crstd'], ['crstd'], bias=self.epsc[:, 0:1])
                        self.act(rstd[:, 0:CH], rstd[:, 0:CH], AF.Exp, ['crstd'], ['crstd'], scale=-0.5)
                        for k in range(KC):
                            q = k % 2
                            self.tt('dve', xn[q][:, 0:CH], convout[:, k, tsl], mean[:, 0:CH], ALU.subtract, [('convout', k), 'cmean'], [('cxn', q)])
                            self.tt('pool', xn[q][:, 0:CH], xn[q][:, 0:CH], rstd[:, 0:CH], ALU.mult, [('cxn', q), 'crstd'], [('cxn', q)])
                            self.act(sT[:, k, tsl], xn[q][:, 0:CH], AF.Silu, [('cxn', q), 'cvec'], [('sT', cc)],
                                     bias=cb_[:, k:k + 1], scale=cg[:, k:k + 1])
                    for tt_ in range(ntile):
                        r0 = row0 + tt_ * 128
                        bs = []
                        for nh in range(2):
                            b = self.bank()
                            bs.append(b)
                            for k in range(KC):
                                self.mm(PS[b][:, :], sT[:, k, tt_ * 128:(tt_ + 1) * 128], w_out[:, k, nh * 512:(nh + 1) * 512], k == 0, k == KC - 1,
                                        [('sT', tt_ * 128 // CH), ('w_out', nh)], [('ps', b)])
                        self.postnorm_tile(tl, r0, cond, 0, hsrc, [PS[bs[0]][:, :], PS[bs[1]][:, :]], [[('ps', bs[0])], [('ps', bs[1])]],
                                           self.H, 'H', bias_bc=bo_bc, router=True)
                    em.flush()

    def moe_phase(self, i, ctx_live, last):
        nc = self.nc
        PS = self.PS
        em = self.em
        nexp = self.n_exp
        with ExitStack() as so:
            EO = so.enter_context
            idxT = EO(self.sbt("idxT", [128, 2, 48], I32))
            gateT = EO(self.sbt("gateT", [128, 2, 48], F32))
            idxTc = EO(self.sbt("idxTc", [64, 16], I32))
            gateTc = EO(self.sbt("gateTc", [64, 16], F32))
            self.zero = EO(self.sbt("zero", [128, D], F32))
            self.ms('dve', self.zero[:], 0.0, ['zero'])
            self.load_lnp(i, 1)
            nrows = NROW if ctx_live else NB * T
            for r0 in range(0, nrows, 128):
                self.dma('sp', self.Y[r0:r0 + 128, :], self.zero[:], ['zero'], ['Y'])
            with ExitStack() as s:
                E = s.enter_context
                wk = E(self.sbt("wk", [48, T], F32))
                vals = E(self.sbt("vals", [48, CAP], F32))
                idxu = E(self.sbt("idxu", [48, CAP], U32))
                idxf = E(self.sbt("idxf", [48, CAP], F32))
                self.cp('dve', wk[:], self.affT[:], ['affT'], ['wk'])
                for it in range(CAP // 8):
                    sl = slice(it * 8, (it + 1) * 8)
                    self.em.op('dve', lambda e, sl=sl: e.max(out=vals[:, sl], in_=wk[:]), ['wk'], ['vals'])
                    self.em.op('dve', lambda e, sl=sl: e.max_index(out=idxu[:, sl], in_max=vals[:, sl], in_values=wk[:]), ['wk', 'vals'], ['idxu'])
                    if it < CAP // 8 - 1:
                        self.em.op('dve', lambda e, sl=sl: e.match_replace(out=wk[:], in_to_replace=vals[:, sl], in_values=wk[:], imm_value=-1.0),
                                   ['wk', 'vals'], ['wk'])
                self.cp('dve', idxf[:], idxu[:], ['idxu'], ['idxf'])
                self.ts('dve', idxf[32:48, :], idxf[32:48, :], float(T), None, ALU.add, None, ['idxf'], ['idxf'])
                for j in range(2):
                    b = self.bank()
                    self.tr(PS[b][:, 0:48], idxf[:, j * 128:(j + 1) * 128], ['idxf'], [('ps', b)])
                    self.cp('dve', idxT[:, j, :], PS[b][:, 0:48], [('ps', b)], ['idxT'])
                    b = self.bank()
                    self.tr(PS[b][:, 0:48], vals[:, j * 128:(j + 1) * 128], ['vals'], [('ps', b)])
                    self.cp('act', gateT[:, j, :], PS[b][:, 0:48], [('ps', b)], ['gateT'])
                if ctx_live:
                    wkc = E(self.sbt("wkc", [16, NB * L], F32))
                    valc = E(self.sbt("valc", [16, NB * CAPC], F32))
                    idxcu = E(self.sbt("idxcu", [16, NB * CAPC], U32))
                    idxcf = E(self.sbt("idxcf", [16, NB * CAPC], F32))
                    self.cp('dve', wkc[:], self.affTc[:], ['affTc'], ['wkc'])
                    for bb in range(NB):
                        for it in range(CAPC // 8):
                            sl = slice(bb * CAPC + it * 8, bb * CAPC + (it + 1) * 8)
                            wsl = slice(bb * L, (bb + 1) * L)
                            self.em.op('dve', lambda e, sl=sl, wsl=wsl: e.max(out=valc[:, sl], in_=wkc[:, wsl]), ['wkc'], ['valc'])
                            self.em.op('dve', lambda e, sl=sl, wsl=wsl: e.max_index(out=idxcu[:, sl], in_max=valc[:, sl], in_values=wkc[:, wsl]),
                                       ['wkc', 'valc'], ['idxcu'])
                            if it < CAPC // 8 - 1:
                                self.em.op('dve', lambda e, sl=sl, wsl=wsl: e.match_replace(out=wkc[:, wsl], in_to_replace=valc[:, sl],
                                                                                          in_values=wkc[:, wsl], imm_value=-1.0),
                                           ['wkc', 'valc'], ['wkc'])
                    self.cp('dve', idxcf[:], idxcu[:], ['idxcu'], ['idxcf'])
                    for bb in range(NB):
                        self.ts('dve', idxcf[:, bb * CAPC:(bb + 1) * CAPC], idxcf[:, bb * CAPC:(bb + 1) * CAPC], float(CTX0 + bb * L), None,
                                ALU.add, None, ['idxcf'], ['idxcf'])
                    b = self.bank()
                    self.tr(PS[b][0:64, 0:16], idxcf[:, :], ['idxcf'], [('ps', b)])
                    self.cp('dve', idxTc[:], PS[b][0:64, 0:16], [('ps', b)], ['idxTc'])
                    b = self.bank()
                    self.tr(PS[b][0:64, 0:16], valc[:, :], ['valc'], [('ps', b)])
                    self.cp('act', gateTc[:], PS[b][0:64, 0:16], [('ps', b)], ['gateTc'])
                em.flush()
            NTOK = 4 * 128 + (64 if ctx_live else 0)
            with ExitStack() as s:
                E = s.enter_context
                NS = 3
                G = [E(self.sbt("Gw%d" % k, [128, KC, 512], BF16)) for k in range(NS)]
                U = [E(self.sbt("Uw%d" % k, [128, KC, 512], BF16)) for k in range(NS)]
                Wd = [E(self.sbt("Wd%d" % k, [128, 16, D], BF16)) for k in range(2)]
                xg = [E(self.sbt("xg%d" % k, [128, D], F32)) for k in range(4)]
                xsT = [E(self.sbt("xsT%d" % k, [128, KC, NTOK], BF16)) for k in range(1)]
                hidT = [E(self.sbt("hidT%d" % k, [128, 16, NTOK], BF16)) for k in range(1)]
                sg = [E(self.sbt("msg%d" % k, [128, NTOK], F32)) for k in range(2)]
                ys = [E(self.sbt("ys%d" % k, [128, D], F32)) for k in range(2)]

                def load_gu(sq_):
                    if sq_ >= nexp * 4:
                        return
                    e, c = sq_ // 4, sq_ % 4
                    sl_ = sq_ % NS
                    sgw = self.moe_w_gate[i, e].rearrange("(k p) n -> p k n", p=128)
                    suw = self.moe_w_up[i, e].rearrange("(k p) n -> p k n", p=128)
                    self.dma('pool', G[sl_][:, :, :], sgw[:, :, c * 512:(c + 1) * 512], (), [('G', sl_)])
                    self.dma('pool', U[sl_][:, :, :], suw[:, :, c * 512:(c + 1) * 512], (), [('U', sl_)])

                def load_d(e):
                    sdw = self.moe_w_down[i, e].rearrange("(f p) n -> p f n", p=128)
                    for hh in range(4):
                        self.dma('pool', Wd[e % 2][:, hh * 4:(hh + 1) * 4, :], sdw[:, hh * 4:(hh + 1) * 4, :], (), [('Wd', e % 2, hh)])

                for sq_ in range(NS):
                    load_gu(sq_)
                load_d(0)
                xs = xsT[0]
                hd = hidT[0]
                allH = [('H', r) for r in range(NROW // 128)]

                def gather_tokens(e):
                    for bb in range(NB):
                        for j in range(2):
                            g = xg[bb * 2 + j]
                            col = bb * 32 + e
                            self.em.dma('pool', lambda en, g=g, j=j, col=col: en.indirect_dma_start(
                                out=g[:], out_offset=None, in_=self.H,
                                in_offset=bass.IndirectOffsetOnAxis(ap=idxT[:, j, col:col + 1], axis=0)),
                                ['idxT'] + allH, [('xg', bb * 2 + j)])

                def build_xs(e):
                    for bb in range(NB):
                        bks = [self.bank() for _ in range(4)]
                        for j in range(2):
                            g = xg[bb * 2 + j]
                            gr = ('xg', bb * 2 + j)
                            for k in range(KC):
                                bk = bks[k // 2]
                                off = (k % 2) * 256 + j * 128
                                self.tr(PS[bk][:, off:off + 128], g[:, k * 128:(k + 1) * 128], [gr], [('ps', bk)])
                        if bb == 0 and ctx_live:
                            self.em.dma('pool', lambda en, e=e: en.indirect_dma_start(
                                out=xg[0][0:64, :], out_offset=None, in_=self.H,
                                in_offset=bass.IndirectOffsetOnAxis(ap=idxTc[:, e:e + 1], axis=0)),
                                ['idxTc'] + allH, [('xg', 0)])
                        for k in range(KC):
                            bk = bks[k // 2]
                            off = (k % 2) * 256
                            dst = xs[:, k, bb * 256:(bb + 1) * 256]
                            if k % 2 == 0:
                                self.act(dst, PS[bk][:, off:off + 256], AF.Identity, [('ps', bk), ('modT', 2), ('modT', 3)], [('xsT', 0)],
                                         bias=self.modT[:, 16 + k, bb:bb + 1], scale=self.modT[:, 24 + k, bb:bb + 1])
                            else:
                                self.ts('dve', dst, PS[bk][:, off:off + 256], self.modT[:, 24 + k, bb:bb + 1], self.modT[:, 16 + k, bb:bb + 1],
                                        ALU.mult, ALU.add, [('ps', bk), ('modT', 2), ('modT', 3)], [('xsT', 0)])
                    if ctx_live:
                        g = xg[0]
                        gr = ('xg', 0)
                        bks = [self.bank() for _ in range(1)]
                        for k in range(KC):
                            self.tr(PS[bks[0]][:, k * 64:(k + 1) * 64], g[0:64, k * 128:(k + 1) * 128], [gr], [('ps', bks[0])])
                        for k in range(KC):
                            dst = xs[:, k, 512:576]
                            self.ts('dve', dst, PS[bks[0]][:, k * 64:(k + 1) * 64], self.modT[:, 24 + k, 2:3], self.modT[:, 16 + k, 2:3],
                                    ALU.mult, ALU.add, [('ps', bks[0]), ('modT', 2), ('modT', 3)], [('xsT', 0)])

                gather_tokens(0)
                build_xs(0)
                for e in range(nexp):
                    if e + 1 < nexp:
                        gather_tokens(e + 1)
                        load_d(e + 1)
                    for c in range(4):
                        for fl in range(4):
                            ft = c * 4 + fl
                            bg = self.bank()
                            bu = self.bank()
                            sl_ = (e * 4 + c) % NS
                            for (bk_, W, wr_) in ((bg, G[sl_], ('G', sl_)), (bu, U[sl_], ('U', sl_))):
                                for k in range(KC):
                                    self.mm(PS[bk_][:, 0:512], W[:, k, fl * 128:(fl + 1) * 128], xs[:, k, 0:512], k == 0, k == KC - 1,
                                            [wr_, ('xsT', 0)], [('ps', bk_)])
                            q = ft % 2
                            self.act(sg[q][:, 0:512], PS[bg][:, 0:512], AF.Silu, [('ps', bg)], [('msg', q)])
                            self.tt('dve', hd[:, ft, 0:512], sg[q][:, 0:512], PS[bu][:, 0:512], ALU.mult, [('msg', q), ('ps', bu)], [('hidT', ft)])
                            if ctx_live:
                                bc = self.bank()
                                for gi, (W, wr_) in enumerate(((G[sl_], ('G', sl_)), (U[sl_], ('U', sl_)))):
                                    for k in range(KC):
                                        self.mm(PS[bc][:, gi * 64:(gi + 1) * 64], W[:, k, fl * 128:(fl + 1) * 128], xs[:, k, 512:576], k == 0, k == KC - 1,
                                                [wr_, ('xsT', 0)], [('ps', bc)])
                                self.act(sg[q][:, 512:576], PS[bc][:, 0:64], AF.Silu, [('ps', bc)], [('msg', q)])
                                self.tt('dve', hd[:, ft, 512:576], sg[q][:, 512:576], PS[bc][:, 64:128], ALU.mult, [('msg', q), ('ps', bc)], [('hidT', ft)])
                        load_gu(e * 4 + c + NS)
                    if e + 1 < nexp:
                        build_xs(e + 1)
                    ttiles = [(bb, j, 128) for bb in range(NB) for j in range(2)] + ([(2, 0, 64)] if ctx_live else [])
                    prevY = [('Ys', (e + 1) % 2, t_) for t_ in range(5)]
                    for ti, (bb, j, np_) in enumerate(ttiles):
                        y_ = ys[ti % 2]
                        yr = ('ys', ti % 2)
                        tok0 = ti * 128
                        for nh in range(2):
                            b = self.bank()
                            for ft in range(16):
                                self.mm(PS[b][0:np_, :], hd[:, ft, tok0:tok0 + np_], Wd[e % 2][:, ft, nh * 512:(nh + 1) * 512], ft == 0, ft == 15,
                                        [('hidT', ft), ('Wd', e % 2, ft // 4)], [('ps', b)])
                            gcol = gateTc[:, e:e + 1] if bb == 2 else gateT[:, j, bb * 32 + e:bb * 32 + e + 1]
                            gres = 'gateTc' if bb == 2 else 'gateT'
                            if nh == 0:
                                self.ts('dve', y_[0:np_, 0:512], PS[b][0:np_, :], gcol[0:np_, :], None, ALU.mult, None, [('ps', b), gres], [yr])
                            else:
                                self.act(y_[0:np_, 512:1024], PS[b][0:np_, :], AF.Identity, [('ps', b), gres], [yr], scale=gcol[0:np_, :])
                        if bb == 2:
                            self.em.dma('pool', lambda en, y_=y_, e=e: en.indirect_dma_start(
                                out=self.Y, out_offset=bass.IndirectOffsetOnAxis(ap=idxTc[:, e:e + 1], axis=0),
                                in_=y_[0:64, :], in_offset=None, compute_op=ALU.add), [yr, 'idxTc', 'Y'] + prevY, [('Ys', e % 2, ti)])
                        else:
                            col = bb * 32 + e
                            self.em.dma('pool', lambda en, y_=y_, j=j, col=col: en.indirect_dma_start(
                                out=self.Y, out_offset=bass.IndirectOffsetOnAxis(ap=idxT[:, j, col:col + 1], axis=0),
                                in_=y_[:, :], in_offset=None, compute_op=ALU.add), [yr, 'idxT', 'Y'] + prevY, [('Ys', e % 2, ti)])
                em.flush()
            with ExitStack() as s:
                E = s.enter_context
                tl = self.alloc_pn_tiles(E, False)
                yt = [E(self.sbt("pn_y%d" % k, [128, D], F32)) for k in range(2)]
                for (row0, nt, cond, sid) in self.streams(ctx_live):
                    for tt_ in range(nt // 128):
                        r0 = row0 + tt_ * 128
                        p = tl['n'] % 2
                        self.dma('act', yt[p][:], self.Y[r0:r0 + 128, :], ['Y'], [('pn_y', p)])
                        if last and row0 < CTX0:
                            dst, dres = self.out, 'out'
                        else:
                            dst, dres = self.H, 'H'
                        self.postnorm_tile(tl, r0, cond, 1, self.H, [yt[p][:, 0:512], yt[p][:, 512:1024]], [[('pn_y', p)], [('pn_y', p)]],
                                           dst, dres, bias_bc=None, router=None)
                em.flush()

    def modT_tiles(self, uT, hsrc, ht, row0, nt, cond, c0, CH, res_name='uT'):
        PS = self.PS
        for tt_ in range(nt // 128):
            p = self._mt % 2
            self._mt += 1
            r0 = row0 + tt_ * 128
            col = c0 + tt_ * 128
            self.dma('sp', ht[p][:], hsrc[r0:r0 + 128, :], [('H', r0 // 128)] if hsrc is self.H else [], [('cht', p)])
            for half in range(2):
                b = self.bank()
                for kk in range(4):
                    k = half * 4 + kk
                    self.tr(PS[b][:, kk * 128:(kk + 1) * 128], ht[p][:, k * 128:(k + 1) * 128], [('cht', p)], [('ps', b)])
                for kk in range(4):
                    k = half * 4 + kk
                    if kk % 2 == 0:
                        self.act(uT[:, k, col:col + 128], PS[b][:, kk * 128:(kk + 1) * 128], AF.Identity,
                                 [('ps', b), ('modT', 0), ('modT', 1)], [(res_name, col // CH)],
                                 bias=self.modT[:, k, cond:cond + 1], scale=self.modT[:, 8 + k, cond:cond + 1])
                    else:
                        self.ts('dve', uT[:, k, col:col + 128], PS[b][:, kk * 128:(kk + 1) * 128],
                                self.modT[:, 8 + k, cond:cond + 1], self.modT[:, k, cond:cond + 1], ALU.mult, ALU.add,
                                [('ps', b), ('modT', 0), ('modT', 1)], [(res_name, col // CH)])

    def na_phase(self, i, jj, ctx_live):
        nc = self.nc
        PS = self.PS
        em = self.em
        hsrc = self.hsrc(i)
        NT = T + L
        chunks = [(0, 512), (512, 512), (1024, 512), (1536, 512), (2048, 256)]
        classes = ((0, [0]), (1, [1]), (2, list(range(2, 14))), (3, [14]), (4, [15]))
        with ExitStack() as so:
            EO = so.enter_context
            qT = EO(self.sbt("qT", [128, 8, NT], BF16))
            kT = EO(self.sbt("kT", [128, 8, NT], BF16))
            identb = EO(self.sbt("identb", [128, 128], BF16))
            self.cp('dve', identb[:], self.ident[:], ['ident'], ['identb'])
            for b in range(NB):
                with ExitStack() as s:
                    E = s.enter_context
                    uT = E(self.sbt("uT", [128, KC, NT], BF16))
                    wq = [E(self.sbt("wq%d" % k, [128, KC, D], BF16)) for k in range(2)]
                    vst = [E(self.sbt("vst%d" % k, [128, D], BF16)) for k in range(2)]
                    ht = [E(self.sbt("cht%d" % k, [128, D], F32)) for k in range(2)]

                    def loadw(blk, slot):
                        src = self.na_w_qkv[jj][:, blk * D:(blk + 1) * D].rearrange("(k p) n -> p k n", p=128)
                        for hh in range(2):
                            self.dma('pool', wq[slot][:, :, hh * 512:(hh + 1) * 512], src[:, :, hh * 512:(hh + 1) * 512], (), [('wq', slot, hh)])
                    loadw(0, 0)
                    loadw(1, 1)
                    self._mt = 0
                    self.modT_tiles(uT, hsrc, ht, b * T, T, b, 0, 512)
                    self.modT_tiles(uT, hsrc, ht, CTX0 + b * L, L, 2, T, 512)
                    for blk, dstT, nm in ((0, qT, 'qT'), (1, kT, 'kT')):
                        w = wq[blk]
                        for hp in range(8):
                            for (c0, n) in chunks:
                                bk = self.bank()
                                for k in range(KC):
                                    self.mm(PS[bk][:, 0:n], w[:, k, hp * 128:(hp + 1) * 128], uT[:, k, c0:c0 + n], k == 0, k == KC - 1,
                                            [('wq', blk, hp // 4), ('uT', c0 // 512)], [('ps', bk)])
                                if blk == 0:
                                    self.act(dstT[:, hp, c0:c0 + n], PS[bk][:, 0:n], AF.Identity, [('ps', bk)], [(nm, hp)], scale=0.125)
                                else:
                                    self.cp('dve', dstT[:, hp, c0:c0 + n], PS[bk][:, 0:n], [('ps', bk)], [(nm, hp)])
                    loadw(2, 0)
                    for t in range(NT // 128):
                        p = t % 2
                        for nh in range(2):
                            bk = self.bank()
                            for k in range(KC):
                                self.mm(PS[bk][:, :], uT[:, k, t * 128:(t + 1) * 128], wq[0][:, k, nh * 512:(nh + 1) * 512], k == 0, k == KC - 1,
                                        [('wq', 0, nh), ('uT', t * 128 // 512)], [('ps', bk)])
                            if nh == 0:
                                self.cp('act', vst[p][:, 0:512], PS[bk][:, :], [('ps', bk)], [('vst', p)])
                            else:
                                self.cp('dve', vst[p][:, 512:1024], PS[bk][:, :], [('ps', bk)], [('vst', p)])
                        self.dma('sp', self.V_d[t * 128:(t + 1) * 128, :], vst[p][:], [('vst', p)], [('V_d', t)])
                    em.flush()
                with ExitStack() as sm_:
                    oT = sm_.enter_context(self.sbt("oT", [128, 8, NT], BF16))
                    with ExitStack() as s:
                        E = s.enter_context
                        Vh = [E(self.sbt("Vh%d" % k, [128, NT // 128, 128], BF16)) for k in range(2)]
                        BM = [E(self.sbt("BM%d" % k, [128, 640], F32)) for k in range(2)]
                        S = [E(self.sbt("S%d" % k, [128, 896], F32)) for k in range(3)]
                        Pn = [E(self.sbt("Pn%d" % k, [128, 896], BF16)) for k in range(3)]
                        PT = [E(self.sbt("PT%d" % k, [128, 896], BF16)) for k in range(3)]
                        smx = [E(self.sbt("smx%d" % k, [128, 4], F32)) for k in range(3)]
                        nbm = 0
                        nq = 0
                        pend = [None]

                        def run_att(g):
                            next(g)
                            if pend[0] is not None:
                                next(pend[0], None)
                            pend[0] = g

                        vsrc = self.V_d.rearrange("(t p) c -> p t c", p=128)
                        for hp in range(8):
                            vh = Vh[hp % 2]
                            self.dma('sp', vh[:], vsrc[:, :, hp * 128:(hp + 1) * 128], [('V_d', t) for t in range(NT // 128)], [('Vh', hp % 2)])
                            for hh in range(2):
                                h = hp * 2 + hh
                                pr = slice(hh * 64, (hh + 1) * 64)

                                def attend(qcol, segs, bm, nkeys, vts, pr_a=pr, hp_a=hp, hh_a=hh, vh_a=vh):
                                    nonlocal nq
                                    m = nq % 3
                                    nq += 1
                                    pr, hp, hh, vh = pr_a, hp_a, hh_a, vh_a
                                    S_, Pn_, PT_, sx = S[m], Pn[m], PT[m], smx[m]
                                    R = lambda nm: (nm, m)
                                    bks = {}
                                    scol = 0
                                    for (bi, pc0, kc0, n, loc) in segs:
                                        if bi not in bks:
                                            bks[bi] = self.bank()
                                        bk = bks[bi]
                                        self.mm(PS[bk][:, pc0:pc0 + n], qT[pr, hp, qcol:qcol + 128], kT[pr, hp, kc0:kc0 + n], True, True,
                                                [('qT', hp), ('kT', hp)], [('ps', bk)])
                                    for (bi, pc0, kc0, n, loc) in segs:
                                        bk = bks[bi]
                                        if loc:
                                            self.tt('dve', S_[:, scol:scol + n], PS[bk][:, pc0:pc0 + n], bm[0][:, scol:scol + n], ALU.add,
                                                    [('ps', bk), bm[1]], [R('S')])
                                        else:
                                            self.cp('act', S_[:, scol:scol + n], PS[bk][:, pc0:pc0 + n], [('ps', bk)], [R('S')])
                                        scol += n
                                    self.em.op('dve', lambda e: e.tensor_reduce(out=sx[:, 0:1], in_=S_[:, 0:nkeys], axis=AX.X, op=ALU.max, negate=True),
                                               [R('S')], [R('smx')])
                                    self.act(S_[:, 0:nkeys], S_[:, 0:nkeys], AF.Exp, [R('S'), R('smx')], [R('S')], bias=sx[:, 0:1], accum_out=sx[:, 1:2])
                                    self.em.op('dve', lambda e: e.reciprocal(out=sx[:, 2:3], in_=sx[:, 1:2]), [R('smx')], [R('smx')])
                                    self.ts('dve', Pn_[:, 0:nkeys], S_[:, 0:nkeys], sx[:, 2:3], None, ALU.mult, None, [R('S'), R('smx')], [R('Pn')])
                                    yield
                                    bT = self.bank()
                                    PTv = PS[bT][:, :].bitcast(BF16)
                                    nch = nkeys // 128
                                    for c in range(nch):
                                        self.tr(PTv[:, c * 128:(c + 1) * 128], Pn_[:, c * 128:(c + 1) * 128], [R('Pn'), 'identb'], [('ps', bT)], ident=identb)
                                    if m != 1:
                                        self.cp('act', PT_[:, 0:nkeys], PTv[:, 0:nkeys], [('ps', bT)], [R('PT')])
                                    else:
                                        self.cp('dve', PT_[:, 0:nkeys], PTv[:, 0:nkeys], [('ps', bT)], [R('PT')])
                                    bO = bks[max(bks)]
                                    for c in range(nch):
                                        self.mm(PS[bO][pr, 384:512], vh[:, vts[c], hh * 64:(hh + 1) * 64], PT_[:, c * 128:(c + 1) * 128], c == 0, c == nch - 1,
                                                [('Vh', hp % 2), R('PT')], [('ps', bO)])
                                    if m == 0:
                                        self.cp('dve', oT[pr, hp, qcol:qcol + 128], PS[bO][pr, 384:512], [('ps', bO)], [('oT', qcol // 128)])
                                    else:
                                        self.cp('act', oT[pr, hp, qcol:qcol + 128], PS[bO][pr, 384:512], [('ps', bO)], [('oT', qcol // 128)])

                                for cls, tiles in classes:
                                    bmt = BM[nbm % 2]
                                    bmr = ('BM', nbm % 2)
                                    nbm += 1
                                    self.dma('sp', bmt[:], self.bm_tab[h, cls], (), [bmr])
                                    for i_ in tiles:
                                        ks = min(max(2 * i_ - 4, 0), 22)
                                        k0 = ks * 64
                                        segs = [(0, 0, k0, 512, True), (1, 0, k0 + 512, 128, True), (1, 128, T, 256, False)]
                                        vts = [ks // 2 + c for c in range(5)] + [16, 17]
                                        run_att(attend(i_ * 128, segs, (bmt, bmr), 896, vts))
                                for j in range(2):
                                    run_att(attend(T + j * 128, [(0, 0, T, 256, False)], None, 256, [16, 17]))
                        if pend[0] is not None:
                            next(pend[0], None)
                            pend[0] = None
                        em.flush()
                    with ExitStack() as s:
                        E = s.enter_context
                        w_o = E(self.sbt("w_o", [128, KC, D], BF16))
                        tl = self.alloc_pn_tiles(E, True)
                        src = self.na_w_o[jj].rearrange("(k p) n -> p k n", p=128)
                        for hh in range(2):
                            self.dma('pool', w_o[:, :, hh * 512:(hh + 1) * 512], src[:, :, hh * 512:(hh + 1) * 512], (), [('w_o', hh)])
                        for t in range(NT // 128):
                            if t < 16:
                                r0, cond = b * T + t * 128, b
                            else:
                                r0, cond = CTX0 + b * L + (t - 16) * 128, 2
                            bs = []
                            for nh in range(2):
                                bk = self.bank()
                                bs.append(bk)
                                for k in range(KC):
                                    self.mm(PS[bk][:, :], oT[:, k, t * 128:(t + 1) * 128], w_o[:, k, nh * 512:(nh + 1) * 512], k == 0, k == KC - 1,
                                            [('oT', t), ('w_o', nh)], [('ps', bk)])
                            self.postnorm_tile(tl, r0, cond, 0, hsrc, [PS[bs[0]][:, :], PS[bs[1]][:, :]], [[('ps', bs[0])], [('ps', bs[1])]],
                                               self.H, 'H', bias_bc=None, router=True)
                        em.flush()


    def rwkv_phase(self, i, jj):
        nc = self.nc
        PS = self.PS
        em = self.em
        hsrc = self.hsrc(i)
        NT = T + L
        C64 = 0.6065306597126334
        R3 = lambda ap, dd=64: ap.rearrange("p (h d) -> p h d", d=dd)

        def bc64(ap16):
            return ap16.unsqueeze(2).to_broadcast([128, 16, 64])

        for b in range(NB):
            with ExitStack() as s:
                E = s.enter_context
                wbuf = [E(self.sbt("rwW%d" % k, [128, KC, D], BF16)) for k in range(1)]
                w1c = E(self.sbt("w1c", [128, KC, 128], BF16))
                a1c = E(self.sbt("a1c", [128, KC, 128], BF16))
                g1s = E(self.sbt("g1s", [128, KC, 128], BF16))
                w2s = E(self.sbt("w2s", [128, D], BF16))
                a2s = E(self.sbt("a2s", [128, D], BF16))
                g2s = E(self.sbt("g2s", [128, D], BF16))
                muT = E(self.sbt("muT", [128, 2, 6, KC], F32))
                bc0 = E(self.sbt("bc0", [128, 2, D], F32))
                u32 = E(self.sbt("u32", [128, KC, T + 2], F32))
                lerp = E(self.sbt("lerp", [128, KC, T], BF16))
                tmp = [E(self.sbt("ltmp%d" % k, [128, T], F32)) for k in range(2)]
                loT = E(self.sbt("loT", [128, T], BF16))
                stg = [E(self.sbt("stg%d" % k, [128, D], F32)) for k in range(2)]
                ht = [E(self.sbt("cht%d" % k, [128, D], F32)) for k in range(2)]
                self.dma('sp', muT[:, 0], self.rw_mu_prevT, (), ['muT'])
                self.dma('sp', muT[:, 1], self.rw_mu_nextT, (), ['muT'])
                for d in range(2):
                    self.dma('pool', w1c[:, :, d * 64:(d + 1) * 64], self.rw_w1[d].rearrange("(k p) n -> p k n", p=128), (), ['w1c'])
                    self.dma('pool', a1c[:, :, d * 64:(d + 1) * 64], self.rw_a1[d].rearrange("(k p) n -> p k n", p=128), (), ['a1c'])
                self.dma('pool', g1s[:], self.rw_g1.rearrange("(k p) n -> p k n", p=128), (), ['g1s'])
                self.dma('pool', w2s[:], self.rw_w2.rearrange("d j n -> (d j) n"), (), ['w2s'])
                self.dma('pool', a2s[:], self.rw_a2.rearrange("d j n -> (d j) n"), (), ['a2s'])
                self.dma('pool', g2s[:], self.rw_g2, (), ['g2s'])
                nst = [0]

                def loadW(src, slot):
                    v = src.rearrange("(k p) n -> p k n", p=128)
                    for hh in range(2):
                        self.dma('pool', wbuf[slot][:, :, hh * 512:(hh + 1) * 512], v[:, :, hh * 512:(hh + 1) * 512], (), [('rwW', slot, hh)])

                def stage_out(dst, row, evs):
                    p = nst[0] % 2
                    nst[0] += 1
                    for nh in range(2):
                        evs[nh](stg[p][:, nh * 512:(nh + 1) * 512], ('stg', p))
                    self.dma('sp', dst[row:row + 128, :], stg[p][:], [('stg', p)], [(dst.tensor.name, row // 128)])

                for (row0, nt, cond, roff, is_lat) in ((b * T, T, b, 0, True), (CTX0 + b * L, L, 2, T, False)):
                    ntile = nt // 128
                    CH = min(512, nt)
                    nch = nt // CH
                    for k in range(KC):
                        self.ms('pool', u32[:, k, 0:1], 0.0, [('u32', 0)])
                        self.ms('pool', u32[:, k, nt + 1:nt + 2], 0.0, [('u32', 0)])
                    self._mt = 0
                    self.modT_tiles(u32, hsrc, ht, row0, nt, cond, 1, 1 << 30, res_name='u32')
                    lerps = (0, 1, 2, 3, 4, 5) if is_lat else (1, 2, 3, 4)
                    wsrc = {0: self.rw_w_r, 2: self.rw_w_k, 3: self.rw_w_v}
                    wslot = {0: 0, 2: 0, 3: 0}
                    for n in lerps:
                        if n in wsrc:
                            loadW(wsrc[n], wslot[n])
                        for k in range(KC):
                            uk = u32[:, k, 1:nt + 1]
                            t0, t1 = tmp[0][:, 0:nt], tmp[1][:, 0:nt]
                            self.tt('dve', t0, u32[:, k, 0:nt], uk, ALU.subtract, [('u32', 0)], [('ltmp', 0)])
                            self.stt('dve', t0, t0, muT[:, 0, n, k:k + 1], uk, ALU.mult, ALU.add, [('ltmp', 0), 'muT', ('u32', 0)], [('ltmp', 0)])
                            self.tt('pool', t1, u32[:, k, 2:nt + 2], uk, ALU.subtract, [('u32', 0)], [('ltmp', 1)])
                            self.stt('dve', lerp[:, k, 0:nt], t1, muT[:, 1, n, k:k + 1], t0, ALU.mult, ALU.add,
                                     [('ltmp', 0), ('ltmp', 1), 'muT'], [('lerp', k)])
                        lr = [('lerp', k) for k in range(KC)]
                        if n in wsrc:
                            dst = {0: self.RR, 2: self.RK, 3: self.RV}[n]
                            W = wbuf[wslot[n]]
                            for t in range(ntile):
                                bks = [self.bank(), self.bank()]
                                for nh in range(2):
                                    for k in range(KC):
                                        self.mm(PS[bks[nh]][:, :], lerp[:, k, t * 128:(t + 1) * 128], W[:, k, nh * 512:(nh + 1) * 512], k == 0, k == KC - 1,
                                                lr + [('rwW', wslot[n], nh)], [('ps', bks[nh])])
                                stage_out(dst, roff + t * 128, [
                                    lambda o, r_, bk=bks[0]: self.cp('act', o, PS[bk][:, :], [('ps', bk)], [r_]),
                                    lambda o, r_, bk=bks[1]: self.cp('dve', o, PS[bk][:, :], [('ps', bk)], [r_])])
                        else:
                            l1 = {1: w1c, 4: a1c, 5: g1s}[n]
                            l1n = {1: 'w1c', 4: 'a1c', 5: 'g1s'}[n]
                            for cc in range(nch):
                                bk = self.bank()
                                for k in range(KC):
                                    self.mm(PS[bk][:, 0:CH], l1[:, k, :], lerp[:, k, cc * CH:(cc + 1) * CH], k == 0, k == KC - 1, lr + [l1n], [('ps', bk)])
                                fn = {1: AF.Tanh, 4: AF.Identity, 5: AF.Sigmoid}[n]
                                self.act(loT[:, cc * CH:(cc + 1) * CH], PS[bk][:, 0:CH], fn, [('ps', bk)], ['loT'])
                            if n == 5:
                                for t in range(ntile):
                                    bks = [self.bank(), self.bank()]
                                    for nh in range(2):
                                        self.mm(PS[bks[nh]][:, :], loT[:, t * 128:(t + 1) * 128], g2s[:, nh * 512:(nh + 1) * 512], True, True, ['loT', 'g2s'], [('ps', bks[nh])])
                                    stage_out(self.RG, roff + t * 128, [
                                        lambda o, r_, bk=bks[0]: self.cp('act', o, PS[bk][:, :], [('ps', bk)], [r_]),
                                        lambda o, r_, bk=bks[1]: self.cp('dve', o, PS[bk][:, :], [('ps', bk)], [r_])])
                            else:
                                l2, l2n = (w2s, 'w2s') if n == 1 else (a2s, 'a2s')
                                for d in range(2):
                                    self.dma('sp', bc0[:, d, :], (self.rw_w0 if n == 1 else self.rw_a0)[d:d + 1, :].to_broadcast([128, D]), (), ['bc0'])
                                for d in range(2):
                                    pr = slice(d * 64, (d + 1) * 64)
                                    dst = (self.RLW if n == 1 else self.RAI)[d]
                                    bci = d
                                    for t in range(ntile):
                                        bks = [self.bank(), self.bank()]
                                        for nh in range(2):
                                            self.mm(PS[bks[nh]][:, :], loT[pr, t * 128:(t + 1) * 128], l2[pr, nh * 512:(nh + 1) * 512], True, True, ['loT', l2n], [('ps', bks[nh])])

                                        def ev(o, r_, bk, nh, n=n, bci=bci):
                                            self.tt('dve', o, PS[bk][:, :], bc0[:, bci, nh * 512:(nh + 1) * 512], ALU.add, [('ps', bk), 'bc0'], [r_])
                                            self.act(o, o, AF.Sigmoid, [r_], [r_])
                                            if n == 1:
                                                self.ts('pool', o, o, -C64, None, ALU.mult, None, [r_], [r_])
                                        stage_out(dst, roff + t * 128, [
                                            lambda o, r_, bk=bks[0]: ev(o, r_, bk, 0),
                                            lambda o, r_, bk=bks[1]: ev(o, r_, bk, 1)])
                em.flush()
            with ExitStack() as s:
                E = s.enter_context
                bc1 = E(self.sbt("bc1", [128, 2, D], F32))
                self.dma('sp', bc1[:, 0, :], self.rw_k_k.to_broadcast([128, D]), (), ['bc1'])
                self.dma('sp', bc1[:, 1, :], self.rw_k_a.to_broadcast([128, D]), (), ['bc1'])
                kt = [E(self.sbt("ekt%d" % k, [128, D], F32)) for k in range(2)]
                at = [[E(self.sbt("eat%d_%d" % (d, k), [128, D], F32)) for k in range(2)] for d in range(2)]
                kk = [E(self.sbt("ekk%d" % k, [128, D], F32)) for k in range(2)]
                sq = [E(self.sbt("esq%d" % k, [128, D], F32)) for k in range(2)]
                o1 = [[E(self.sbt("eo1%d_%d" % (d, k), [128, D], F32)) for k in range(2)] for d in range(2)]
                o2 = [[E(self.sbt("eo2%d_%d" % (d, k), [128, D], F32)) for k in range(2)] for d in range(2)]
                ss = [E(self.sbt("ess%d" % k, [128, 16], F32)) for k in range(2)]
                for t in range(NT // 128):
                    p = t % 2
                    row = t * 128
                    RR_ = lambda nm: (nm, p)
                    self.dma('sp', kt[p][:], self.RK[row:row + 128, :], [('RK', t)], [RR_('ekt')])
                    for d in range(2):
                        self.dma('act', at[d][p][:], self.RAI[d][row:row + 128, :], [(self.RAI[d].tensor.name, t)], [RR_('eat%d' % d)])
                    self.tt('dve', kk[p][:], kt[p][:], bc1[:, 0, :], ALU.mult, [RR_('ekt'), 'bc1'], [RR_('ekk')])
                    self.act(sq[p][:], kk[p][:], AF.Square, [RR_('ekk')], [RR_('esq')])
                    self.em.op('dve', lambda e, p=p: e.tensor_reduce(out=ss[p][:], in_=R3(sq[p][:]), axis=AX.X, op=ALU.add), [RR_('esq')], [RR_('ess')])
                    self.ts('dve', ss[p][:], ss[p][:], 1e-24, None, ALU.max, None, [RR_('ess')], [RR_('ess')])
                    self.act(ss[p][:], ss[p][:], AF.Ln, [RR_('ess')], [RR_('ess')])
                    self.act(ss[p][:], ss[p][:], AF.Exp, [RR_('ess')], [RR_('ess')], scale=-0.5)
                    self.tt('dve', R3(kk[p][:]), R3(kk[p][:]), bc64(ss[p][:]), ALU.mult, [RR_('ekk'), RR_('ess')], [RR_('ekk')])
                    self.dma('sp', self.KKN[row:row + 128, :], kk[p][:], [RR_('ekk')], [('KKN', t)])
                    for d in range(2):
                        a_ = at[d][p]
                        self.stt('dve', o1[d][p][:], a_[:], -1.0, bc1[:, 1, :], ALU.add, ALU.mult, [RR_('eat%d' % d), 'bc1'], [RR_('eo1%d' % d)])
                        self.stt('dve', o1[d][p][:], o1[d][p][:], 1.0, kt[p][:], ALU.add, ALU.mult, [RR_('eo1%d' % d), RR_('ekt')], [RR_('eo1%d' % d)])
                        self.dma('sp', self.KD[d][row:row + 128, :], o1[d][p][:], [RR_('eo1%d' % d)], [(self.KD[d].tensor.name, t)])
                        self.tt('pool', o2[d][p][:], kk[p][:], a_[:], ALU.mult, [RR_('ekk'), RR_('eat%d' % d)], [RR_('eo2%d' % d)])
                        self.dma('sp', self.BV[d][row:row + 128, :], o2[d][p][:], [RR_('eo2%d' % d)], [(self.BV[d].tensor.name, t)])
                em.flush()
            with ExitStack() as s:
                E = s.enter_context
                msk = E(self.sbt("msk", [128, 4, 128], F32))
                bdm = E(self.sbt("bdm", [128, 128], F32))
                self.dma('sp', msk[:], self.masks.rearrange("m p t -> p m t"), (), ['msk'])
                self.dma('sp', bdm[:], self.bdmask, (), ['bdm'])
                MK4 = E(self.sbt("MK4", [128, 4, 128], F32))
                ST2 = E(self.sbt("ST2", [128, 8, 128], F32R))
                ld2 = [{nm: E(self.sbt("ld%d_" % k + nm, [128, D], F32)) for nm in ('lw', 'kd', 'bv', 'kkn', 'v', 'r')} for k in range(2)]
                nchunk = [0]
                cum = E(self.sbt("cum", [128, D], F32))
                tA = [E(self.sbt("tA%d" % k, [128, D], F32)) for k in range(2)]
                kb = E(self.sbt("kbar", [128, D], F32R))
                bb = E(self.sbt("bbar", [128, D], F32R))
                QT2 = E(self.sbt("QT2", [128, 8, 2, 128], F32R))
                khT = E(self.sbt("khT", [128, 8, 128], F32R))
                bhT = E(self.sbt("bhT", [128, 8, 128], F32R))
                AT3 = E(self.sbt("AT3", [128, 16, 3, 128], F32R))
                Lb = [E(self.sbt("Lb%d" % k, [128, 16, 128], F32R)) for k in range(2)]
                Ub = [E(self.sbt("Ub%d" % k, [128, 16, 128], F32R)) for k in range(2)]
                XT = E(self.sbt("XT", [128, 16, 128], F32R))
                rhs_sb = E(self.sbt("rhs_sb", [128, D], F32R))
                sa_sb = E(self.sbt("sa_sb", [128, D], F32R))
                vr = E(self.sbt("v_r", [128, D], F32R))
                y_sb = E(self.sbt("y_sb", [128, D], F32))
                gC = E(self.sbt("gC", [128, 8], F32))
                sttmp = E(self.sbt("sttmp", [128, 4, 128], F32))
                for d in range(2):
                    mS, mI, mL = (0, 1, 2) if d == 0 else (2, 3, 0)
                    for q_, mi_ in enumerate((mS, mI, mS, mI)):
                        self.cp('dve', MK4[:, q_, :], msk[:, mi_, :], ['msk'], ['MK4'])
                    self.ts('dve', ST2[:], self.ones[:, :].unsqueeze(1).to_broadcast([128, 8, 128]), 0.0, None, ALU.mult, None, ['ones'], ['ST2'])
                    order = [16, 17] + list(range(16)) if d == 0 else [17, 16] + list(range(15, -1, -1))
                    for ci in order:
                        row = ci * 128
                        emit = ci < 16
                        names = ['lw', 'kd', 'bv', 'kkn', 'v'] + (['r'] if emit else [])
                        srcs = {'lw': self.RLW[d], 'kd': self.KD[d], 'bv': self.BV[d], 'kkn': self.KKN, 'v': self.RV, 'r': self.RR}
                        pp = nchunk[0] % 2
                        nchunk[0] += 1
                        ld = ld2[pp]
                        for qi, nm in enumerate(names):
                            self.dma('sp', ld[nm][:], srcs[nm][row:row + 128, :], [(srcs[nm].tensor.name, ci)], [('ld', nm, pp)])
                        lw = ld['lw']
                        bks = [self.bank(), self.bank()]
                        for nh in range(2):
                            self.mm(PS[bks[nh]][:, :], msk[:, mI, :], lw[:, nh * 512:(nh + 1) * 512], True, True, ['msk', ('ld', 'lw', pp)], [('ps', bks[nh])])
                            self.cp('act' if nh == 0 else 'dve', cum[:, nh * 512:(nh + 1) * 512], PS[bks[nh]][:, :], [('ps', bks[nh])], ['cum'])
                        bt = [self.bank(), self.bank()]
                        for nh in range(2):
                            self.mm(PS[bt[nh]][:, :], self.ones[:], lw[:, nh * 512:(nh + 1) * 512], True, True, ['ones', ('ld', 'lw', pp)], [('ps', bt[nh])])
                        bg = self.bank()
                        for hp in range(8):
                            self.mm(PS[bg][:, hp:hp + 1], lw[:, hp * 128:(hp + 1) * 128], self.ones[:, 0:1], True, True, ['ones', ('ld', 'lw', pp)], [('ps', bg)])
                        self.act(gC[:], PS[bg][:, 0:8], AF.Exp, [('ps', bg)], ['gC'])
                        for nh in range(2):
                            sl = slice(nh * 512, (nh + 1) * 512)
                            self.tt('dve', tA[0][:, sl], PS[bt[nh]][:, :], cum[:, sl], ALU.subtract, [('ps', bt[nh]), 'cum'], [('tA', 0)])
                        self.act(tA[0][:], tA[0][:], AF.Exp, [('tA', 0)], [('tA', 0)])
                        self.tt('dve', kb[:], ld['kd'][:], tA[0][:], ALU.mult, [('ld', 'kd', pp), ('tA', 0)], ['kbar'])
                        self.tt('pool', bb[:], ld['bv'][:], tA[0][:], ALU.mult, [('ld', 'bv', pp), ('tA', 0)], ['bbar'])

                        def transp(src, srcr, dst_fn, dstr):
                            for g4 in range(2):
                                bk = self.bank()
                                for q4 in range(4):
                                    hp = g4 * 4 + q4
                                    self.tr(PS[bk][:, q4 * 128:(q4 + 1) * 128], src[:, hp * 128:(hp + 1) * 128], [srcr], [('ps', bk)])
                                dst_fn(g4, PS[bk][:, :].rearrange("p (q t) -> p q t", t=128), bk, dstr)
                        self.tt('dve', tA[0][:], cum[:], lw[:], ALU.subtract, ['cum', ('ld', 'lw', pp)], [('tA', 0)])
                        self.act(tA[0][:], tA[0][:], AF.Exp, [('tA', 0)], [('tA', 0)])
                        self.stt('dve', tA[0][:], ld['kkn'][:], -1.0, tA[0][:], ALU.mult, ALU.mult, [('ld', 'kkn', pp), ('tA', 0)], [('tA', 0)])
                        transp(tA[0], ('tA', 0), lambda g4, pv, bk, dr: self.cp('act', QT2[:, g4 * 4:(g4 + 1) * 4, 0, :], pv, [('ps', bk)], [dr]), 'QT2')
                        if emit:
                            self.act(tA[1][:], cum[:], AF.Exp, ['cum'], [('tA', 1)])
                            self.tt('dve', tA[1][:], tA[1][:], ld['r'][:], ALU.mult, [('tA', 1), ('ld', 'r', pp)], [('tA', 1)])
                            transp(tA[1], ('tA', 1), lambda g4, pv, bk, dr: self.cp('dve', QT2[:, g4 * 4:(g4 + 1) * 4, 1, :], pv, [('ps', bk)], [dr]), 'QT2')
                        self.act(tA[0][:], cum[:], AF.Exp, ['cum'], [('tA', 0)], scale=-1.0)
                        self.tt('dve', tA[1][:], tA[0][:], ld['kd'][:], ALU.mult, [('tA', 0), ('ld', 'kd', pp)], [('tA', 1)])
                        transp(tA[1], ('tA', 1), lambda g4, pv, bk, dr: self.cp('act', khT[:, g4 * 4:(g4 + 1) * 4, :], pv, [('ps', bk)], [dr]), 'khT')
                        self.tt('pool', tA[0][:], tA[0][:], ld['bv'][:], ALU.mult, [('tA', 0), ('ld', 'bv', pp)], [('tA', 0)])
                        transp(tA[0], ('tA', 0), lambda g4, pv, bk, dr: self.cp('dve', bhT[:, g4 * 4:(g4 + 1) * 4, :], pv, [('ps', bk)], [dr]), 'bhT')
                        NQ = 256 if emit else 128
                        for h in range(16):
                            hp, hh = h // 2, h % 2
                            pr = slice(hh * 64, (hh + 1) * 64)
                            bk = self.bank()
                            q2 = QT2[pr, hp, :, :].rearrange("p a t -> p (a t)")
                            self.mm(PS[bk][:, 0:NQ], khT[pr, hp, :], q2[:, 0:NQ], True, True, ['khT', 'QT2'], [('ps', bk)], sync=True)
                            self.mm(PS[bk][:, 256:256 + NQ], bhT[pr, hp, :], q2[:, 0:NQ], True, True, ['bhT', 'QT2'], [('ps', bk)], sync=True)
                            pv = PS[bk][:, :].rearrange("p (q t) -> p q t", t=128)
                            eng = 'dve'
                            if emit:
                                self.tt(eng, AT3[:, h, 0:2, :], pv[:, 0:2, :], MK4[:, 0:2, :], ALU.mult, [('ps', bk), 'MK4'], [('AT3', h)])
                                self.tt(eng, Ub[0][:, h, :], pv[:, 2, :], MK4[:, 2, :], ALU.mult, [('ps', bk), 'MK4'], [('Ub', 0, h // 4)])
                                self.tt(eng, AT3[:, h, 2, :], pv[:, 3, :], MK4[:, 3, :], ALU.mult, [('ps', bk), 'MK4'], [('AT3', h)])
                            else:
                                self.tt(eng, AT3[:, h, 0, :], pv[:, 0, :], MK4[:, 0, :], ALU.mult, [('ps', bk), 'MK4'], [('AT3', h)])
                                self.tt(eng, Ub[0][:, h, :], pv[:, 2, :], MK4[:, 2, :], ALU.mult, [('ps', bk), 'MK4'], [('Ub', 0, h // 4)])
                        for g4 in range(4):
                            bk = self.bank()
                            for q4 in range(4):
                                h = g4 * 4 + q4
                                hp, hh = h // 2, h % 2
                                pr = slice(hh * 64, (hh + 1) * 64)
                                self.mm(PS[bk][:, q4 * 128:(q4 + 1) * 128], QT2[pr, hp, 0, :], bhT[pr, hp, :], True, True, ['QT2', 'bhT'], [('ps', bk)], sync=True)
                            pv = PS[bk][:, :].rearrange("p (q t) -> p q t", t=128)
                            self.tt('dve', Lb[0][:, g4 * 4:(g4 + 1) * 4, :], pv, msk[:, mL, :].unsqueeze(1).to_broadcast([128, 4, 128]), ALU.mult,
                                    [('ps', bk), 'msk'], [('Lb', 0, g4)])
                            self.tt('pool', XT[:, g4 * 4:(g4 + 1) * 4, :], Ub[0][:, g4 * 4:(g4 + 1) * 4, :],
                                    self.ident[:, :].unsqueeze(1).to_broadcast([128, 4, 128]), ALU.add, [('Ub', 0, g4), 'ident'], [('XT', g4)])
                        cur = 0
                        for lev in range(6):
                            nxt = 1 - cur
                            last = lev == 5
                            bL = [self.bank() for _ in range(4)]
                            for q4 in range(4):
                                for g4 in range(4):
                                    h = g4 * 4 + q4
                                    self.mm(PS[bL[g4]][:, q4 * 128:(q4 + 1) * 128], Ub[cur][:, h, :], Lb[cur][:, h, :], True, True,
                                            [('Ub', cur, g4), ('Lb', cur, g4)], [('ps', bL[g4])], sync=True)
                            for g4 in range(4):
                                self.cp('act', Lb[nxt][:, g4 * 4:(g4 + 1) * 4, :], PS[bL[g4]][:, :].rearrange("p (q t) -> p q t", t=128),
                                        [('ps', bL[g4])], [('Lb', nxt, g4)])
                            if not last:
                                bU = [self.bank() for _ in range(4)]
                                for q4 in range(4):
                                    for g4 in range(4):
                                        h = g4 * 4 + q4
                                        self.mm(PS[bU[g4]][:, q4 * 128:(q4 + 1) * 128], Lb[cur][:, h, :], Ub[cur][:, h, :], True, True,
                                                [('Ub', cur, g4), ('Lb', cur, g4)], [('ps', bU[g4])], sync=True)
                                for g4 in range(4):
                                    self.cp('dve', Ub[nxt][:, g4 * 4:(g4 + 1) * 4, :], PS[bU[g4]][:, :].rearrange("p (q t) -> p q t", t=128),
                                            [('ps', bU[g4])], [('Ub', nxt, g4)])
                            bX = [self.bank() for _ in range(4)]
                            for q4 in range(4):
                                for g4 in range(4):
                                    h = g4 * 4 + q4
                                    self.mm(PS[bX[g4]][:, q4 * 128:(q4 + 1) * 128], Lb[nxt][:, h, :], XT[:, h, :], True, True,
                                            [('Lb', nxt, g4), ('XT', g4)], [('ps', bX[g4])], sync=True)
                            for g4 in range(4):
                                self.tt('dve', XT[:, g4 * 4:(g4 + 1) * 4, :], XT[:, g4 * 4:(g4 + 1) * 4, :],
                                        PS[bX[g4]][:, :].rearrange("p (q t) -> p q t", t=128), ALU.add,
                                        [('ps', bX[g4]), ('XT', g4)], [('XT', g4)])
                            cur = nxt
                        v_ = vr
                        self.cp('pool', vr[:], ld['v'][:], [('ld', 'v', pp)], ['vr'])
                        br = [self.bank(), self.bank()]
                        for hp in range(8):
                            bk = br[hp // 4]
                            c0 = (hp % 4) * 128
                            self.mm(PS[bk][:, c0:c0 + 128], QT2[:, hp, 0, :], ST2[:, hp, :], True, False, ['QT2', 'ST2'], [('ps', bk)], sync=True)
                            for hh in range(2):
                                h = hp * 2 + hh
                                self.mm(PS[bk][:, c0 + hh * 64:c0 + (hh + 1) * 64], AT3[:, h, 0, :], v_[:, h * 64:(h + 1) * 64], False, hh == 1,
                                        [('AT3', h), 'vr'], [('ps', bk)], sync=True)
                        self.cp('act', rhs_sb[:, 0:512], PS[br[0]][:, :], [('ps', br[0])], ['rhs_sb'])
                        self.cp('dve', rhs_sb[:, 512:1024], PS[br[1]][:, :], [('ps', br[1])], ['rhs_sb'])
                        bs_ = [self.bank(), self.bank()]
                        for h in range(16):
                            bk = bs_[h // 8]
                            c0 = (h % 8) * 64
                            self.mm(PS[bk][:, c0:c0 + 64], XT[:, h, :], rhs_sb[:, h * 64:(h + 1) * 64], True, True, [('XT', h // 4), 'rhs_sb'], [('ps', bk)], sync=True)
                        self.cp('act', sa_sb[:, 0:512], PS[bs_[0]][:, :], [('ps', bs_[0])], ['sa_sb'])
                        self.cp('dve', sa_sb[:, 512:1024], PS[bs_[1]][:, :], [('ps', bs_[1])], ['sa_sb'])
                        if emit:
                            by = [self.bank(), self.bank()]
                            for hp in range(8):
                                bk = by[hp // 4]
                                c0 = (hp % 4) * 128
                                self.mm(PS[bk][:, c0:c0 + 128], QT2[:, hp, 1, :], ST2[:, hp, :], True, False, ['QT2', 'ST2'], [('ps', bk)], sync=True)
                                for hh in range(2):
                                    h = hp * 2 + hh
                                    self.mm(PS[bk][:, c0 + hh * 64:c0 + (hh + 1) * 64], AT3[:, h, 1, :], v_[:, h * 64:(h + 1) * 64], False, False,
                                            [('AT3', h), 'vr'], [('ps', bk)], sync=True)
                                    self.mm(PS[bk][:, c0 + hh * 64:c0 + (hh + 1) * 64], AT3[:, h, 2, :], sa_sb[:, h * 64:(h + 1) * 64], False, hh == 1,
                                            [('AT3', h), 'sa_sb'], [('ps', bk)], sync=True)
                            self.cp('act', y_sb[:, 0:512], PS[by[0]][:, :], [('ps', by[0])], ['y_sb'])
                            self.cp('dve', y_sb[:, 512:1024], PS[by[1]][:, :], [('ps', by[1])], ['y_sb'])
                            self.dma('sp', self.YS[d][row:row + 128, :], y_sb[:], ['y_sb'], [(self.YS[d].tensor.name, ci)])
                        bst = [self.bank(), self.bank()]
                        for hp in range(8):
                            bk = bst[hp // 4]
                            c0 = (hp % 4) * 128
                            self.mm(PS[bk][:, c0:c0 + 128], kb[:, hp * 128:(hp + 1) * 128], v_[:, hp * 128:(hp + 1) * 128], True, False,
                                    ['kbar', 'vr'], [('ps', bk)], sync=True)
                            self.mm(PS[bk][:, c0:c0 + 128], bb[:, hp * 128:(hp + 1) * 128], sa_sb[:, hp * 128:(hp + 1) * 128], False, True,
                                    ['bbar', 'sa_sb'], [('ps', bk)], sync=True)
                        for g4 in range(2):
                            self.tt('dve', sttmp[:], PS[bst[g4]][:, :].rearrange("p (q t) -> p q t", t=128),
                                    bdm[:, :].unsqueeze(1).to_broadcast([128, 4, 128]), ALU.mult, [('ps', bst[g4]), 'bdm'], ['sttmp'])
                            self.tt('pool', ST2[:, g4 * 4:(g4 + 1) * 4, :], ST2[:, g4 * 4:(g4 + 1) * 4, :],
                                    gC[:, g4 * 4:(g4 + 1) * 4].unsqueeze(2).to_broadcast([128, 4, 128]), ALU.mult, ['ST2', 'gC'], ['ST2'])
                            self.tt('dve', ST2[:, g4 * 4:(g4 + 1) * 4, :], ST2[:, g4 * 4:(g4 + 1) * 4, :], sttmp[:], ALU.add, ['ST2', 'sttmp'], ['ST2'])
                em.flush()
            with ExitStack() as s:
                E = s.enter_context
                w_o = E(self.sbt("rw_wo", [128, KC, D], BF16))
                bc2 = E(self.sbt("bc2", [128, 3, D], F32))
                self.dma('sp', bc2[:, 0, :], self.rw_r_k.to_broadcast([128, D]), (), ['bc2'])
                self.dma('sp', bc2[:, 1, :], self.rw_gn_g.to_broadcast([128, D]), (), ['bc2'])
                self.dma('sp', bc2[:, 2, :], self.rw_gn_b.to_broadcast([128, D]), (), ['bc2'])
                src = self.rw_w_o.rearrange("(k p) n -> p k n", p=128)
                for hh in range(2):
                    self.dma('pool', w_o[:, :, hh * 512:(hh + 1) * 512], src[:, :, hh * 512:(hh + 1) * 512], (), [('w_o', hh)])
                tl = self.alloc_pn_tiles(E, True)
                L_ = {nm: E(self.sbt("d_" + nm, [128, D], F32)) for nm in ('y0', 'y1', 'r', 'kd0', 'kd1', 'v', 'g')}
                acc = E(self.sbt("d_acc", [128, D], F32))
                tq = E(self.sbt("d_tq", [128, D], F32))
                st16 = E(self.sbt("d_st", [128, 4, 16], F32))
                zT = E(self.sbt("d_zT", [128, KC, 128], BF16))
                for t in range(T // 128):
                    row = t * 128
                    srcs = {'y0': self.YS[0], 'y1': self.YS[1], 'r': self.RR, 'kd0': self.KD[0], 'kd1': self.KD[1], 'v': self.RV, 'g': self.RG}
                    for qi, nm in enumerate(L_):
                        self.dma('sp' if qi % 2 == 0 else 'act', L_[nm][:], srcs[nm][row:row + 128, :], [(srcs[nm].tensor.name, t)], [('dl', nm)])
                    for d in range(2):
                        y = L_['y%d' % d]
                        yr = ('dl', 'y%d' % d)
                        self.em.op('dve', lambda e, y=y: e.tensor_reduce(out=st16[:, 0, :], in_=R3(y[:]), axis=AX.X, op=ALU.add), [yr], ['st16'])
                        self.act(tq[:], y[:], AF.Square, [yr], ['tq'])
                        self.em.op('dve', lambda e: e.tensor_reduce(out=st16[:, 1, :], in_=R3(tq[:]), axis=AX.X, op=ALU.add), ['tq'], ['st16'])
                        self.ts('dve', st16[:, 0, :], st16[:, 0, :], 1.0 / 64, None, ALU.mult, None, ['st16'], ['st16'])
                        self.tt('dve', st16[:, 2, :], st16[:, 0, :], st16[:, 0, :], ALU.mult, ['st16'], ['st16'])
                        self.stt('dve', st16[:, 1, :], st16[:, 1, :], 1.0 / 64, st16[:, 2, :], ALU.mult, ALU.subtract, ['st16'], ['st16'])
                        self.act(st16[:, 1, :], st16[:, 1, :], AF.Ln, ['st16', 'epsc'], ['st16'], bias=self.epsc[:, 1:2])
                        self.act(st16[:, 1, :], st16[:, 1, :], AF.Exp, ['st16'], ['st16'], scale=-0.5)
                        self.tt('dve', R3(y[:]), R3(y[:]), bc64(st16[:, 0, :]), ALU.subtract, [yr, 'st16'], [yr])
                        self.tt('dve', R3(y[:]), R3(y[:]), bc64(st16[:, 1, :]), ALU.mult, [yr, 'st16'], [yr])
                        self.tt('pool', y[:], y[:], bc2[:, 1, :], ALU.mult, [yr, 'bc2'], [yr])
                        if d == 0:
                            self.tt('pool', acc[:], y[:], bc2[:, 2, :], ALU.add, [yr, 'bc2'], ['acc'])
                        else:
                            self.tt('pool', y[:], y[:], bc2[:, 2, :], ALU.add, [yr, 'bc2'], [yr])
                            self.tt('pool', acc[:], acc[:], y[:], ALU.add, [yr, 'acc'], ['acc'])
                        kd_ = L_['kd%d' % d]
                        kr_ = ('dl', 'kd%d' % d)
                        self.tt('dve', tq[:], L_['r'][:], kd_[:], ALU.mult, [('dl', 'r'), kr_], ['tq'])
                        self.tt('dve', tq[:], tq[:], bc2[:, 0, :], ALU.mult, ['tq', 'bc2'], ['tq'])
                        self.em.op('dve', lambda e: e.tensor_reduce(out=st16[:, 3, :], in_=R3(tq[:]), axis=AX.X, op=ALU.add), ['tq'], ['st16'])
                        self.tt('dve', R3(tq[:]), R3(L_['v'][:]), bc64(st16[:, 3, :]), ALU.mult, [('dl', 'v'), 'st16'], ['tq'])
                        self.tt('dve', acc[:], acc[:], tq[:], ALU.add, ['acc', 'tq'], ['acc'])
                    self.tt('dve', acc[:], acc[:], L_['g'][:], ALU.mult, ['acc', ('dl', 'g')], ['acc'])
                    for half in range(2):
                        bk = self.bank()
                        for kk_ in range(4):
                            k = half * 4 + kk_
                            self.tr(PS[bk][:, kk_ * 128:(kk_ + 1) * 128], acc[:, k * 128:(k + 1) * 128], ['acc'], [('ps', bk)])
                        self.cp('act' if half == 0 else 'dve', zT[:, half * 4:(half + 1) * 4, :], PS[bk][:, :].rearrange("p (q t) -> p q t", t=128),
                                [('ps', bk)], ['zT'])
                    bs = []
                    for nh in range(2):
                        bk = self.bank()
                        bs.append(bk)
                        for k in range(KC):
                            self.mm(PS[bk][:, :], zT[:, k, :], w_o[:, k, nh * 512:(nh + 1) * 512], k == 0, k == KC - 1, ['zT', ('w_o', nh)], [('ps', bk)])
                    self.postnorm_tile(tl, b * T + row, b, 0, hsrc, [PS[bs[0]][:, :], PS[bs[1]][:, :]], [[('ps', bs[0])], [('ps', bs[1])]],
                                       self.H, 'H', bias_bc=None, router=True)
                em.flush()


def tr128(v, nchunk):
    return np.ascontiguousarray(v.reshape(nchunk, 128).T)


def make_bm_tab(rpb):
    NEG = np.float32(-30000.0)
    out = np.full((16, 5, 2, 64, 10, 64), NEG, np.float32)
    rep = {0: 0, 1: 1, 2: 2, 3: 14, 4: 15}
    c = np.arange(64)
    kc = np.arange(64)
    win_c0 = np.clip(c - 8, 0, 48)
    col_ok = (kc[None, :] >= win_c0[:, None]) & (kc[None, :] < win_c0[:, None] + 16)
    cidx = np.clip(kc[None, :] - c[:, None] + 15, 0, 30)
    for cls, i_ in rep.items():
        ks = min(max(2 * i_ - 4, 0), 22)
        for rr in range(2):
            r = 2 * i_ + rr
            row0 = min(max(r - 4, 0), 24)
            for krel in range(10):
                kr = ks + krel
                if not (row0 <= kr < row0 + 8):
                    continue
                ridx = kr - r + 7
                vals = rpb[:, ridx, :][:, cidx]
                out[:, cls, rr, :, krel, :] = np.where(col_ok[None], vals, NEG)
    return np.ascontiguousarray(out.reshape(16, 5, 128, 640))


def prep_shared(inp):
    f = lambda a: np.ascontiguousarray(np.asarray(a, dtype=np.float32))
    sh = {}
    sh['ident'] = np.eye(128, dtype=np.float32)
    sh['ada_w'] = f(inp['ada_w'])
    sh['ada_b'] = f(inp['ada_b'])
    sh['ada_bT'] = np.stack([tr128(f(inp['ada_b'])[i], 48) for i in range(DEPTH)])
    sh['ln_g'] = f(inp['ln_g'])
    sh['ln_b'] = f(inp['ln_b'])
    sh['conv_w_in'] = f(inp['conv_w_in'])
    sh['conv_b_inT'] = np.stack([tr128(f(inp['conv_b_in'])[i], 16) for i in range(2)])
    wdw = f(inp['conv_w_dw'])
    sh['conv_w_dwT'] = np.ascontiguousarray(wdw.reshape(2, CONVW, KC, 128).transpose(0, 3, 2, 1))
    sh['conv_b_dwT'] = np.stack([tr128(f(inp['conv_b_dw'])[i], KC) for i in range(2)])
    sh['conv_ln_gT'] = np.stack([tr128(f(inp['conv_ln_g'])[i], KC) for i in range(2)])
    sh['conv_ln_bT'] = np.stack([tr128(f(inp['conv_ln_b'])[i], KC) for i in range(2)])
    sh['conv_w_out'] = f(inp['conv_w_out'])
    sh['conv_b_out'] = f(inp['conv_b_out'])
    sh['na_w_qkv'] = f(inp['na_w_qkv'])
    sh['na_w_o'] = f(inp['na_w_o'])
    sh['bm_tab'] = make_bm_tab(f(inp['na_rpb'])[0])
    mp = f(inp['rw_mu_prev'])[0]
    mn = f(inp['rw_mu_next'])[0]
    sh['rw_mu_prevT'] = np.ascontiguousarray(mp.reshape(6, KC, 128).transpose(2, 0, 1))
    sh['rw_mu_nextT'] = np.ascontiguousarray(mn.reshape(6, KC, 128).transpose(2, 0, 1))
    for nm in ("rw_w_r", "rw_w_k", "rw_w_v", "rw_w_o", "rw_w0", "rw_a0", "rw_w1", "rw_a1", "rw_w2", "rw_a2", "rw_g1", "rw_g2"):
        sh[nm] = np.ascontiguousarray(f(inp[nm])[0])
    for nm in ("rw_k_k", "rw_k_a", "rw_r_k", "rw_gn_g", "rw_gn_b"):
        sh[nm] = np.ascontiguousarray(f(inp[nm])[0].reshape(1, D))
    tri = np.triu(np.ones((128, 128), np.float32))
    sh['masks'] = np.stack([np.triu(tri, 1), tri, np.tril(tri.T, -1), tri.T]).astype(np.float32)
    bd = np.zeros((128, 128), np.float32)
    bd[:64, :64] = 1
    bd[64:, 64:] = 1
    sh['bdmask'] = bd
    sh['moe_router'] = f(inp['moe_router'])
    sh['moe_w_gate'] = f(inp['moe_w_gate'])
    sh['moe_w_up'] = f(inp['moe_w_up'])
    sh['moe_w_down'] = f(inp['moe_w_down'])
    return sh


def prep_core(inp, core):
    f = lambda a: np.asarray(a, dtype=np.float32)
    b0 = core * NB
    x = f(inp['x'])[b0:b0 + NB].reshape(NB * T, D)
    cx = f(inp['ctx'])[b0:b0 + NB].reshape(NB * L, D)
    hin = np.ascontiguousarray(np.concatenate([x, cx], axis=0))
    conds = np.stack([f(inp['c'])[b0], f(inp['c'])[b0 + 1], f(inp['c_ctx'])], axis=1)
    cT = np.ascontiguousarray(conds.reshape(KC, 128, 3).transpose(1, 0, 2))
    return {'hin': hin, 'cT': cT}


_NC_CACHE = {}


def kernel(**inputs):
    kb = KB()
    nc = kb.build()
    sh = prep_shared(inputs)
    in_maps = []
    for c in range(NCORES):
        m = dict(sh)
        m.update(prep_core(inputs, c))
        in_maps.append(m)
    res = run_bass_kernel_spmd(nc, in_maps, core_ids=list(range(NCORES)))
    outs = [r["out"].reshape(NB, T, D) for r in res.results]
    return np.concatenate(outs, axis=0).astype(np.float32)
```
